# Optimizing a Trainium2 kernel written in Bass

```python
import math
import jax
import jax.numpy as jnp
from jax import lax
import numpy as np

D_MODEL = 1024
BATCH = 4
SEQ = 4096
DEPTH = 2

CTX_LEN = 256
GRID_W = 64
BLOCK = 128
WINDOW = 128
ROPE_BASE = 10000.0
NORM_EPS = 1e-6
NEG_INF = -1e30
F32 = jnp.float32

MIX_W = D_MODEL
RW_HEAD = 64
RW_W = MIX_W // 4
RW_HEADS = RW_W // RW_HEAD
RW_DECAY_LORA = 32
RW_ICLR_LORA = 32
RW_GATE_LORA = 64
RW_GN_EPS = 64e-5
DF_W = MIX_W // 4
DF_HEADS = 4
DF_V = DF_W // DF_HEADS
DF_QK = DF_V // 2
GQ_W = MIX_W - RW_W - DF_W
GQ_HEAD = 64
GQ_HEADS = GQ_W // GQ_HEAD
GQ_KV = 2
GQ_GROUP = GQ_HEADS // GQ_KV
D_FF = ((8 * D_MODEL // 3 + 127) // 128) * 128

IN_SIZES = (RW_W, RW_W, RW_W, RW_GATE_LORA, 2 * RW_DECAY_LORA, 2 * RW_ICLR_LORA,
            DF_HEADS * 2 * DF_QK, DF_HEADS * 2 * DF_QK, DF_W,
            GQ_HEADS * GQ_HEAD, GQ_KV * GQ_HEAD, GQ_KV * GQ_HEAD)
IN_W = sum(IN_SIZES)

kernel_name = 'hybrid_rwkv7_diffattn_swa_prefix_dit'


def rms_norm(x, g):
    xf = x.astype(F32)
    y = xf * lax.rsqrt(jnp.mean(xf * xf, axis=-1, keepdims=True) + NORM_EPS)
    return (y * g.astype(F32)).astype(x.dtype)


def dwconv3(x, w):
    xp = jnp.pad(x, ((0, 0), (1, 1), (0, 0)))
    return xp[:, :-2] * w[0] + xp[:, 1:-1] * w[1] + xp[:, 2:] * w[2]


def split_cols(p):
    cuts, off = [], 0
    for n in IN_SIZES[:-1]:
        off += n
        cuts.append(off)
    return jnp.split(p, cuts, axis=-1)


def axial_rope_tables(rows, dim):
    row = jnp.repeat(jnp.arange(rows, dtype=F32), GRID_W)
    col = jnp.tile(jnp.arange(GRID_W, dtype=F32), rows)
    n_freq = dim // 4
    inv = ROPE_BASE ** (-jnp.arange(n_freq, dtype=F32) / n_freq)
    ang = jnp.stack([row, col], axis=-1)[:, :, None] * inv
    return jnp.cos(ang), jnp.sin(ang)


def apply_axial_rope(x, cos, sin):
    shp = x.shape
    xs = x.astype(F32).reshape(shp[:3] + (2, 2, shp[-1] // 4))
    x1, x2 = xs[..., 0, :], xs[..., 1, :]
    cs, sn = cos[None, :, None], sin[None, :, None]
    out = jnp.stack([x1 * cs - x2 * sn, x2 * cs + x1 * sn], axis=-2)
    return out.reshape(shp).astype(x.dtype)


def rwkv_features(r, k, v, wd, ad, conv_w, w0, w_up, a0, a_up, k_k, k_a):
    B, T, _ = r.shape
    r, k, v = jnp.split(dwconv3(jnp.concatenate([r, k, v], axis=-1), conv_w).astype(F32), 3, axis=-1)
    wl = w0 + jnp.einsum('btdr,drc->btdc', jnp.tanh(wd.reshape(B, T, 2, RW_DECAY_LORA)), w_up)
    decay = jnp.exp(-jnp.exp(-jax.nn.softplus(-wl.astype(F32)) - 0.5))
    a = jax.nn.sigmoid((a0 + jnp.einsum('btdr,drc->btdc', ad.reshape(B, T, 2, RW_ICLR_LORA), a_up)).astype(F32))
    kk = (k * k_k).reshape(B, T, RW_HEADS, RW_HEAD)
    kk = kk / jnp.maximum(jnp.linalg.norm(kk, axis=-1, keepdims=True), 1e-12)
    kd = k[:, :, None] * (1.0 + (a - 1.0) * k_a)
    hd = lambda t: t.reshape(t.shape[:-1] + (RW_HEADS, RW_HEAD))
    return hd(r), hd(decay), hd(kd), hd(v), kk, hd(a)


def rwkv_scan(s0, decay, kd, v, kk, a, r):
    dir_tm = lambda t: jnp.stack([t[:, :, 0], t[:, ::-1, 1]], 0).transpose(2, 0, 1, 3, 4)
    both_tm = lambda t: jnp.stack([t, t[:, ::-1]], 0).transpose(2, 0, 1, 3, 4)
    emit = r is not None

    def step(state, inp):
        w_t, k_t, v_t, kk_t, a_t = inp[:5]
        sk = jnp.einsum('dbhvk,dbhk->dbhv', state, kk_t)
        state = (state * w_t[..., None, :] - sk[..., :, None] * (kk_t * a_t)[..., None, :]
                 + v_t[..., :, None] * k_t[..., None, :])
        out = jnp.einsum('dbhvk,dbhk->dbhv', state, inp[5]) if emit else None
        return state, out

    xs = (dir_tm(decay), dir_tm(kd), both_tm(v), both_tm(kk), dir_tm(a))
    if emit:
        xs = xs + (both_tm(r),)
    s_fin, o = lax.scan(step, s0, xs)
    if not emit:
        return s_fin, None
    return s_fin, (o[:, 0] + o[::-1, 1]).transpose(1, 0, 2, 3)


def rwkv_readout(o, r, kd, v, gd, g_up, r_k, ln_g, ln_b):
    B, T = o.shape[:2]
    mu = jnp.mean(o, axis=-1, keepdims=True)
    var = jnp.mean(jnp.square(o - mu), axis=-1, keepdims=True)
    o = ((o - mu) * lax.rsqrt(var + RW_GN_EPS)).reshape(B, T, RW_W) * ln_g + ln_b
    bonus = jnp.einsum('bthn,btdhn,hn->bth', r, kd, r_k.astype(F32))[..., None] * v
    g = jax.nn.sigmoid(gd) @ g_up
    return (o + bonus.reshape(B, T, RW_W)) * g


def diff_core(q, k, v, lam):
    s = jnp.einsum('bqhmd,bkhmd->bhmqk', q, k, preferred_element_type=F32) * (DF_QK ** -0.5)
    p = jax.nn.softmax(s, axis=-1)
    p = p[:, :, 0] - lam * p[:, :, 1]
    return jnp.einsum('bhqk,bkhd->bqhd', p.astype(v.dtype), v)


def diff_attention(q, k, v, qc, kc, vc, cos, sin, lam_vecs, norm_g, lam_init, emit_ctx):
    B, S, _ = q.shape
    nb = S // BLOCK
    qk_shape = lambda t: t.reshape(t.shape[:2] + (DF_HEADS, 2, DF_QK))
    v_shape = lambda t: t.reshape(t.shape[:2] + (DF_HEADS, DF_V))
    rope = lambda t: apply_axial_rope(t.reshape(B, S, DF_HEADS * 2, DF_QK), cos, sin).reshape(B, S, DF_HEADS, 2, DF_QK)
    lv = lam_vecs.astype(F32)
    lam = jnp.exp(jnp.sum(lv[0] * lv[1])) - jnp.exp(jnp.sum(lv[2] * lv[3])) + lam_init
    kc, vc = qk_shape(kc), v_shape(vc)
    k_all = jnp.concatenate([kc, rope(k)], axis=1)
    v_all = jnp.concatenate([vc, v_shape(v)], axis=1)
    q_blocks = rope(q).reshape(B, nb, BLOCK, DF_HEADS, 2, DF_QK).swapaxes(0, 1)
    o = lax.map(lambda qb: diff_core(qb, k_all, v_all, lam), q_blocks)
    o = o.swapaxes(0, 1).reshape(B, S, DF_HEADS, DF_V)
    head_out = lambda t: (rms_norm(t, norm_g) * (1.0 - lam_init)).reshape(t.shape[:2] + (DF_W,))
    oc = head_out(diff_core(qk_shape(qc), kc, vc, lam)) if emit_ctx else None
    return head_out(o), oc


def band_blocks(t, nb):
    tb = t.reshape((t.shape[0], nb, BLOCK) + t.shape[2:])
    tp = jnp.pad(tb, ((0, 0), (1, 1), (0, 0), (0, 0), (0, 0)))
    return jnp.concatenate([tp[:, :-2], tp[:, 1:-1], tp[:, 2:]], axis=2)


def window_gqa(q, k, v, qc, kc, vc, cos, sin, sink, emit_ctx):
    B, S, _ = q.shape
    C = kc.shape[1]
    nb = S // BLOCK
    scale = GQ_HEAD ** -0.5
    q = apply_axial_rope(q.reshape(B, S, GQ_HEADS, GQ_HEAD), cos, sin).reshape(B, nb, BLOCK, GQ_KV, GQ_GROUP, GQ_HEAD)
    k = apply_axial_rope(k.reshape(B, S, GQ_KV, GQ_HEAD), cos, sin)
    v = v.reshape(B, S, GQ_KV, GQ_HEAD)
    kc = kc.reshape(B, C, GQ_KV, GQ_HEAD)
    vc = vc.reshape(B, C, GQ_KV, GQ_HEAD)
    sink_l = sink.reshape(GQ_KV, GQ_GROUP)[..., None, None].astype(F32)
    kw, vw = band_blocks(k, nb), band_blocks(v, nb)
    s_loc = jnp.einsum('bnqkgd,bnskd->bnkgqs', q, kw, preferred_element_type=F32) * scale
    blk = jnp.arange(nb)[:, None, None]
    qpos = blk * BLOCK + jnp.arange(BLOCK)[None, :, None]
    kpos = (blk - 1) * BLOCK + jnp.arange(3 * BLOCK)[None, None, :]
    mask = (jnp.abs(kpos - qpos) <= WINDOW) & (kpos >= 0) & (kpos < S)
    s_loc = jnp.where(mask[None, :, None, None], s_loc, NEG_INF)
    s_ctx = jnp.einsum('bnqkgd,bckd->bnkgqc', q, kc, preferred_element_type=F32) * scale
    s_snk = jnp.broadcast_to(sink_l, s_ctx.shape[:-1] + (1,))
    p = jax.nn.softmax(jnp.concatenate([s_snk, s_ctx, s_loc], axis=-1), axis=-1).astype(v.dtype)
    o = (jnp.einsum('bnkgqc,bckd->bnqkgd', p[..., 1:1 + C], vc)
         + jnp.einsum('bnkgqs,bnskd->bnqkgd', p[..., 1 + C:], vw))
    oc = None
    if emit_ctx:
        qcs = qc.reshape(B, C, GQ_KV, GQ_GROUP, GQ_HEAD)
        sc = jnp.einsum('bqkgd,bckd->bkgqc', qcs, kc, preferred_element_type=F32) * scale
        snk = jnp.broadcast_to(sink_l, sc.shape[:-1] + (1,))
        pc = jax.nn.softmax(jnp.concatenate([snk, sc], axis=-1), axis=-1)[..., 1:].astype(vc.dtype)
        oc = jnp.einsum('bkgqc,bckd->bqkgd', pc, vc).reshape(B, C, GQ_W)
    return o.reshape(B, S, GQ_W), oc


def conv_ffn(h, w_gate, w_up, w_down, conv_w, conv_b):
    gate = dwconv3(h @ w_gate, conv_w) + conv_b
    return (jax.nn.silu(gate) * (h @ w_up)) @ w_down


def setup_inputs(seed: int = 0) -> dict:
    key = jax.random.key(seed)
    ks = iter(jax.random.split(key, 28))
    nrm = lambda shape, s: jax.random.normal(next(ks), shape, F32) * s
    L, D = DEPTH, D_MODEL
    return {
        'x': nrm((BATCH, SEQ, D), 1.0),
        'c': nrm((BATCH, D), 1.0),
        'ctx': nrm((BATCH, CTX_LEN, D), 1.0),
        'c_ctx': nrm((D,), 1.0),
        'ada_w': nrm((L, D, 6 * D), 0.5 * D ** -0.5),
        'ada_b': nrm((L, 6 * D), 0.02),
        'norm_g': 1.0 + nrm((L, 4, D), 0.05),
        'w_in': nrm((L, D, IN_W), D ** -0.5),
        'w_out': nrm((L, MIX_W, D), MIX_W ** -0.5),
        'rw_conv': nrm((L, 3, 3 * RW_W), 0.6),
        'rw_w0': -2.5 + nrm((L, 2, RW_W), 1.5),
        'rw_w_up': nrm((L, 2, RW_DECAY_LORA, RW_W), 0.5 * RW_DECAY_LORA ** -0.5),
        'rw_a0': nrm((L, 2, RW_W), 0.5),
        'rw_a_up': nrm((L, 2, RW_ICLR_LORA, RW_W), 0.5 * RW_ICLR_LORA ** -0.5),
        'rw_g_up': nrm((L, RW_GATE_LORA, RW_W), RW_GATE_LORA ** -0.5),
        'rw_k_k': 0.85 + nrm((L, RW_W), 0.05),
        'rw_k_a': 1.0 + nrm((L, RW_W), 0.05),
        'rw_r_k': nrm((L, RW_HEADS, RW_HEAD), 0.1),
        'rw_ln_g': 1.0 + nrm((L, RW_W), 0.05),
        'rw_ln_b': nrm((L, RW_W), 0.02),
        'df_lambda': nrm((L, 4, DF_QK), 0.1),
        'df_norm_g': 1.0 + nrm((L, DF_V), 0.05),
        'gq_sink': nrm((L, GQ_HEADS), 0.5),
        'ff_w_gate': nrm((L, D, D_FF), D ** -0.5),
        'ff_w_up': nrm((L, D, D_FF), D ** -0.5),
        'ff_w_down': nrm((L, D_FF, D), D_FF ** -0.5),
        'ff_conv_w': nrm((L, 3, D_FF), 0.6),
        'ff_conv_b': nrm((L, D_FF), 0.02),
    }


def reference(x, c, ctx, c_ctx, ada_w, ada_b, norm_g, w_in, w_out,
              rw_conv, rw_w0, rw_w_up, rw_a0, rw_a_up, rw_g_up, rw_k_k, rw_k_a, rw_r_k, rw_ln_g, rw_ln_b,
              df_lambda, df_norm_g, gq_sink,
              ff_w_gate, ff_w_up, ff_w_down, ff_conv_w, ff_conv_b):
    B, S, D = x.shape
    rows = S // GRID_W
    cos_df, sin_df = axial_rope_tables(rows, DF_QK)
    cos_gq, sin_gq = axial_rope_tables(rows, GQ_HEAD)
    s_zero = jnp.zeros((2, B, RW_HEADS, RW_HEAD, RW_HEAD), F32)
    xc = ctx
    for l in range(DEPTH):
        last = l == DEPTH - 1
        lam_init = 0.8 - 0.6 * math.exp(-0.3 * l)
        mod = (jax.nn.silu(c) @ ada_w[l] + ada_b[l]).reshape(B, 1, 6, D)
        mod_c = (jax.nn.silu(c_ctx) @ ada_w[l] + ada_b[l]).reshape(6, D)
        sh1, sc1, ga1, sh2, sc2, ga2 = (mod[:, :, i] for i in range(6))
        sh1c, sc1c, ga1c, sh2c, sc2c, ga2c = (mod_c[i] for i in range(6))

        h = rms_norm(x, norm_g[l, 0]) * (1.0 + sc1) + sh1
        hc = rms_norm(xc, norm_g[l, 0]) * (1.0 + sc1c) + sh1c
        p = split_cols(h @ w_in[l])
        pc = split_cols(hc @ w_in[l])

        rw_args = (rw_conv[l], rw_w0[l], rw_w_up[l], rw_a0[l], rw_a_up[l], rw_k_k[l], rw_k_a[l])
        rd_args = (rw_g_up[l], rw_r_k[l], rw_ln_g[l], rw_ln_b[l])
        r_c, dec_c, kd_c, v_c, kk_c, a_c = rwkv_features(pc[0], pc[1], pc[2], pc[4], pc[5], *rw_args)
        s_ctx, o_c = rwkv_scan(s_zero, dec_c, kd_c, v_c, kk_c, a_c, None if last else r_c)
        r_x, dec_x, kd_x, v_x, kk_x, a_x = rwkv_features(p[0], p[1], p[2], p[4], p[5], *rw_args)
        _, o_x = rwkv_scan(s_ctx, dec_x, kd_x, v_x, kk_x, a_x, r_x)
        y_a = rwkv_readout(o_x, r_x, kd_x, v_x, p[3], *rd_args)

        y_b, yc_b = diff_attention(p[6], p[7], p[8], pc[6], pc[7], pc[8], cos_df, sin_df,
                                   df_lambda[l], df_norm_g[l], lam_init, not last)
        y_c, yc_c = window_gqa(p[9], p[10], p[11], pc[9], pc[10], pc[11], cos_gq, sin_gq,
                               gq_sink[l], not last)

        y = jnp.concatenate([y_a.astype(x.dtype), y_b.astype(x.dtype), y_c.astype(x.dtype)], axis=-1) @ w_out[l]
        x = x + ga1 * rms_norm(y, norm_g[l, 1])

        h2 = rms_norm(x, norm_g[l, 2]) * (1.0 + sc2) + sh2
        x = x + ga2 * rms_norm(conv_ffn(h2, ff_w_gate[l], ff_w_up[l], ff_w_down[l], ff_conv_w[l], ff_conv_b[l]),
                               norm_g[l, 3])

        if not last:
            y_ac = rwkv_readout(o_c, r_c, kd_c, v_c, pc[3], *rd_args)
            yc = jnp.concatenate([y_ac.astype(xc.dtype), yc_b.astype(xc.dtype), yc_c.astype(xc.dtype)], axis=-1) @ w_out[l]
            xc = xc + ga1c * rms_norm(yc, norm_g[l, 1])
            h2c = rms_norm(xc, norm_g[l, 2]) * (1.0 + sc2c) + sh2c
            xc = xc + ga2c * rms_norm(conv_ffn(h2c, ff_w_gate[l], ff_w_up[l], ff_w_down[l], ff_conv_w[l], ff_conv_b[l]),
                                      norm_g[l, 3])
    return x
```

```python
import math
from contextlib import ExitStack
import numpy as np
import concourse.bass as bass
import concourse.mybir as mybir
from concourse.bass_utils import run_bass_kernel_spmd

F32 = mybir.dt.float32
BF16 = mybir.dt.bfloat16
AF = mybir.ActivationFunctionType
ALU = mybir.AluOpType
AX = mybir.AxisListType

D = 1024
NB = 4
SEQ = 4096
CTX = 256
T = CTX + SEQ
L = 2
DFF = 2816
GRID_W = 64
EPS = 1e-6

ENG_NAMES = ("pe", "act", "dve", "pool", "sp")


class Prog:
    N_DMA_SEMS = 12

    def __init__(self, nc):
        self.nc = nc
        self.streams = {e: [] for e in ENG_NAMES}
        self.cnt = {e: 0 for e in ENG_NAMES}
        self.sems = {e: nc.alloc_semaphore(f"c_{e}") for e in ENG_NAMES}
        self.dsems = [nc.alloc_semaphore(f"d_{i}") for i in range(self.N_DMA_SEMS)]
        self.dcnt = [0] * self.N_DMA_SEMS
        self.dnext = 0
        self.waited = {e: {} for e in ENG_NAMES}
        self.bufs = {}
        self.n_instr = 0
        self.split_stores = True

    def _sem(self, key):
        return self.sems[key] if isinstance(key, str) else self.dsems[key]

    def _need(self, eng, tok):
        if tok is None:
            return
        key, val = tok
        if key == eng and val > self.cnt[eng]:
            return
        if self.waited[eng].get(key, 0) >= val:
            return
        self.waited[eng][key] = val
        sem = self._sem(key)
        self.streams[eng].append(lambda e, sem=sem, val=val: e.wait_ge(sem, val))
        self.n_instr += 1

    def _deps(self, eng, reads, writes):
        for r in reads:
            st = self.bufs.get(r)
            if st is not None:
                self._need(eng, st["w"])
        for w in writes:
            st = self.bufs.get(w)
            if st is not None:
                self._need(eng, st["w"])
                for t in st["r"]:
                    self._need(eng, t)

    def _commit(self, tok, reads, writes):
        for r in reads:
            st = self.bufs.setdefault(r, {"w": None, "r": []})
            st["r"].append(tok)
            if len(st["r"]) > 24:
                best = {}
                for k, v in st["r"]:
                    best[k] = max(best.get(k, 0), v)
                st["r"] = list(best.items())
        for w in writes:
            self.bufs[w] = {"w": tok, "r": []}

    def op(self, eng, fn, reads=(), writes=(), inc=True):
        self._deps(eng, reads, writes)
        sem = self.sems[eng]
        if inc:
            self.cnt[eng] += 1
            self.streams[eng].append(lambda e, fn=fn, sem=sem: fn(e).then_inc(sem, 1))
            tok = (eng, self.cnt[eng])
        else:
            self.streams[eng].append(lambda e, fn=fn: fn(e))
            tok = (eng, self.cnt[eng] + 1)
        self.n_instr += 1
        self._commit(tok, reads, writes)

    def call(self, eng, method, reads=(), writes=(), inc=True, **kw):
        self.op(eng, lambda e: getattr(e, method)(**kw), reads, writes, inc=inc)

    def dma(self, q, out, in_, reads=(), writes=(), **kw):
        if q == "sp" and self.split_stores and str(out.space) == "DRAM" and str(in_.space) != "DRAM":
            q = "pool"
        i = self.dnext
        self.dnext = (self.dnext + 1) % self.N_DMA_SEMS
        if self.dcnt[i] > 0:
            self._need(q, (i, 16 * self.dcnt[i]))
        self._deps(q, reads, writes)
        self.dcnt[i] += 1
        sem = self.dsems[i]
        self.streams[q].append(
            lambda e, out=out, in_=in_, sem=sem, kw=kw: e.dma_start(out=out, in_=in_, **kw).then_inc(sem, 16))
        self.n_instr += 1
        self._commit((i, 16 * self.dcnt[i]), reads, writes)

    def mark(self, name):
        if not hasattr(self, "marks"):
            self.marks = []
        self.marks.append((name, dict(self.cnt)))

    def barrier(self):
        toks = [(e, self.cnt[e]) for e in ENG_NAMES if self.cnt[e] > 0]
        toks += [(i, 16 * self.dcnt[i]) for i in range(self.N_DMA_SEMS) if self.dcnt[i] > 0]
        for e in ENG_NAMES:
            for tok in toks:
                if tok[0] != e:
                    self._need(e, tok)

    def finish(self, final_keys):
        for k in final_keys:
            st = self.bufs.get(k)
            if st is not None:
                self._need("sp", st["w"])
        for i in range(self.N_DMA_SEMS):
            if self.dcnt[i] > 0:
                self._need("sp", (i, 16 * self.dcnt[i]))
        nc = self.nc
        with nc.Block() as block:
            @block.tensor
            def _(e):
                for f in self.streams["pe"]:
                    f(e)

            @block.scalar
            def _(e):
                for f in self.streams["act"]:
                    f(e)

            @block.vector
            def _(e):
                for f in self.streams["dve"]:
                    f(e)

            @block.gpsimd
            def _(e):
                for f in self.streams["pool"]:
                    f(e)

            @block.sync
            def _(e):
                for f in self.streams["sp"]:
                    f(e)


class Ctx:
    pass


class Uniq:
    def __init__(self, nc, suffix):
        self._nc = nc
        self._sfx = suffix

    def sbuf_tensor(self, name, shape, dtype):
        return self._nc.sbuf_tensor(name + self._sfx, shape, dtype)

    def psum_tensor(self, name, shape, dtype):
        return self._nc.psum_tensor(name + self._sfx, shape, dtype)

    def __getattr__(self, k):
        return getattr(self._nc, k)


def phase_mod(P, nc, K):
    GW = 768
    NG = 6144 // GW
    with (
        nc.sbuf_tensor("m_c", [128, 8, 2], F32) as craw,
        nc.sbuf_tensor("m_cs", [128, 8, 2], F32) as cs,
        nc.sbuf_tensor("m_w0", [128, 8, GW], F32) as w0,
        nc.sbuf_tensor("m_w1", [128, 8, GW], F32) as w1,
        nc.sbuf_tensor("m_b", [128, 48], F32) as bcol,
        nc.psum_tensor("m_ps", [128, 256, 2], F32) as ps,
        nc.psum_tensor("m_pst", [128, 512], F32) as pst,
        nc.sbuf_tensor("m_mcw", [128, 48], F32) as mcw,
        nc.sbuf_tensor("m_mrow", [48, 128], F32) as mrow,
    ):
        wb = [w0, w1]
        P.dma("sp", craw[:, :, 0], K.c_in.rearrange("(c p) -> p c", p=128), writes=["m_c"],
              allow_slow_non_contiguous=True)
        P.dma("sp", craw[:, :, 1], K.cctx_in.rearrange("(c p) -> p c", p=128), writes=["m_c"],
              allow_slow_non_contiguous=True)
        P.op("act", lambda e: e.activation(out=cs[:], in_=craw[:], func=AF.Silu), reads=["m_c"], writes=["m_cs"])
        for l in range(L):
            P.dma("sp", bcol[:], K.ada_b[l].rearrange("(j p) -> p j", p=128), writes=["m_b"],
                  allow_slow_non_contiguous=True)
            for gi in range(NG):
                wt = wb[gi % 2]
                wk = f"m_w{gi % 2}"
                src = K.ada_w[l, :, gi * GW:(gi + 1) * GW].rearrange("(kc p) n -> p kc n", p=128)
                P.dma("sp", wt[:], src, writes=[wk])
                for jj in range(GW // 128):
                    j = gi * (GW // 128) + jj
                    for kc in range(8):
                        P.op("pe", lambda e, wt=wt, jj=jj, kc=kc, j=j: e.matmul(
                            ps[:, j, :], lhsT=wt[:, kc, jj * 128:(jj + 1) * 128], rhs=cs[:, kc, :],
                            start=(kc == 0), stop=(kc == 7)),
                            reads=[wk, "m_cs"], writes=["m_ps"])
            mc = K.modcol[l]
            for who in range(2):
                P.op("dve", lambda e, mc=mc, who=who: e.tensor_tensor(
                    out=mc[:, :, who], in0=ps[:, 0:48, who], in1=bcol[:], op=ALU.add),
                    reads=["m_ps", "m_b"], writes=[f"modcol{l}"])
            for who in range(2):
                P.call("dve", "tensor_copy", reads=[f"modcol{l}"], writes=["m_mcw"], out=mcw[:], in_=mc[:, :, who])
                P.call("pe", "transpose", reads=["m_mcw", "ident"], writes=["m_pst"], out=pst[0:48, 0:128], in_=mcw[:], identity=K.ident[:])
                P.call("act", "activation", reads=["m_pst"], writes=["m_mrow"], out=mrow[:], in_=pst[0:48, 0:128], func=AF.Copy)
                P.dma("sp", K.modrow[l, who].rearrange("j (c p) -> (j c) p", p=128), mrow[:], reads=["m_mrow"], writes=["modrow"])


def _swap_idx(du):
    nf = du // 4
    idx = np.arange(du)
    axis = idx // (2 * nf); half = (idx // nf) % 2; f = idx % nf
    return axis * 2 * nf + (1 - half) * nf + f


def w_in_cols():
    sw32 = _swap_idx(32); sw64 = _swap_idx(64)
    cols = list(range(0, 768))
    cols += list(range(832, 896)) + list(range(768, 832))
    cols += list(range(896, 960)) * 2
    dfq = np.arange(960, 1216); dfk = np.arange(1216, 1472)
    sw = lambda base, du: np.concatenate([base[u * du:(u + 1) * du][_swap_idx(du)] for u in range(len(base) // du)])
    cols += list(dfq) + list(sw(dfq, 32)) + list(dfk) + list(sw(dfk, 32))
    gqq = np.arange(1728, 2240); gqk = np.arange(2240, 2368)
    cols += list(gqq) + list(sw(gqq, 64)) + list(gqk) + list(sw(gqk, 64))
    cols += list(range(1472, 1728)) + list(range(2368, 2496))
    return np.asarray(cols, dtype=np.int64)


NFM = 26
WCOLS = NFM * 128 + 384


def rope_tables():
    out = np.zeros((4, 128, T), np.float32)
    tt = np.arange(SEQ)
    pos = np.stack([(tt // GRID_W).astype(np.float32), (tt % GRID_W).astype(np.float32)], 0)
    for ti, du in ((0, 32), (2, 64)):
        nf = du // 4
        inv = (np.float32(10000.0) ** (-np.arange(nf, dtype=np.float32) / np.float32(nf))).astype(np.float32)
        i = np.arange(du)
        axis = i // (2 * nf); half = (i // nf) % 2; f = i % nf
        ang = (pos[axis] * inv[f][:, None]).astype(np.float32)
        c = np.cos(ang).astype(np.float32); s_ = np.sin(ang).astype(np.float32)
        s_ = np.where(half[:, None] == 0, -s_, s_)
        rep = 128 // du
        out[ti, :, :CTX] = 1.0
        out[ti, :, CTX:] = np.tile(c, (rep, 1))
        out[ti + 1, :, CTX:] = np.tile(s_, (rep, 1))
    return out


def tiles_512():
    return [(0, CTX)] + [(CTX + i * 512, 512) for i in range(SEQ // 512)]


def norm_block(P, xt, xk, ss, rs, junk, xn, xnk, pfx="nb"):
    P.call("act", "activation", reads=[xk], writes=[pfx + "_junk", pfx + "_ss"],
           out=junk[:], in_=xt[:], func=AF.Square, accum_out=ss[:])
    P.call("dve", "tensor_scalar", reads=[pfx + "_ss"], writes=[pfx + "_rs"],
           out=rs[:], in0=ss[:], scalar1=1.0 / D, scalar2=EPS, op0=ALU.mult, op1=ALU.add)
    P.call("act", "activation", reads=[pfx + "_rs"], writes=[pfx + "_rs"], out=rs[:], in_=rs[:], func=AF.Sqrt)
    P.call("dve", "reciprocal", reads=[pfx + "_rs"], writes=[pfx + "_rs"], out=rs[:], in_=rs[:])
    P.call("dve", "tensor_scalar", reads=[xk, pfx + "_rs"], writes=[xnk],
           out=xn[:], in0=xt[:], scalar1=rs[:], scalar2=None, op0=ALU.mult)


def mod_AB(P, nc, K, l, gi, jsc, jsh, A, Bv, gcol, name):
    P.dma("sp", gcol[:], K.norm_g[l, gi].rearrange("(c p) -> p c", p=128), writes=[name + "_g"],
          allow_slow_non_contiguous=True)
    mc = K.modcol[l]
    for who in range(2):
        P.call("dve", "scalar_tensor_tensor", reads=[f"modcol{l}", name + "_g"], writes=[name + "_A"],
               out=A[:, :, who], in0=mc[:, jsc * 8:(jsc + 1) * 8, who], scalar=1.0, in1=gcol[:],
               op0=ALU.add, op1=ALU.mult)
        P.call("dve", "tensor_copy", reads=[f"modcol{l}"], writes=[name + "_B"],
               out=Bv[:, :, who], in_=mc[:, jsh * 8:(jsh + 1) * 8, who])


def load_weight_bf16(P, nc, src, W, wkey, nk, ncols, stages, skeys):
    i = 0
    for c0 in range(0, ncols, 512):
        c1 = min(ncols, c0 + 512)
        key = f"{wkey}_{c0 // 512}"
        for kc in range(nk):
            st = stages[i % len(stages)]; sk = skeys[i % len(stages)]
            P.dma("sp", st[:, :c1 - c0], src[kc * 128:(kc + 1) * 128, c0:c1], writes=[sk])
            eng = ("pool", "dve", "act")[i % 3]
            i += 1
            if eng == "act":
                P.call("act", "activation", reads=[sk], writes=[key], out=W[:, kc, c0:c1], in_=st[:, :c1 - c0], func=AF.Copy)
            else:
                P.call(eng, "tensor_copy", reads=[sk], writes=[key], out=W[:, kc, c0:c1], in_=st[:, :c1 - c0])


def wkeys(wkey, c0, c1):
    return [f"{wkey}_{c}" for c in range(c0 // 512, (c1 - 1) // 512 + 1)]


def phase_inproj(P, nc, K, l, xsrc):
    with ExitStack() as es:
        W = es.enter_context(nc.sbuf_tensor("p1_w", [128, 8, WCOLS], BF16))
        x0 = es.enter_context(nc.sbuf_tensor("p1_x0", [128, D], F32))
        x1 = es.enter_context(nc.sbuf_tensor("p1_x1", [128, D], F32))
        junk = es.enter_context(nc.sbuf_tensor("p1_junk", [128, D], F32))
        xn0 = es.enter_context(nc.sbuf_tensor("p1_xn0", [128, D], F32))
        xn1 = es.enter_context(nc.sbuf_tensor("p1_xn1", [128, D], F32))
        ss = es.enter_context(nc.sbuf_tensor("p1_ss", [128, 1], F32))
        rs = es.enter_context(nc.sbuf_tensor("p1_rs", [128, 1], F32))
        hT0 = es.enter_context(nc.sbuf_tensor("p1_hT0", [128, 8, 512], BF16))
        hT1 = es.enter_context(nc.sbuf_tensor("p1_hT1", [128, 8, 512], BF16))
        A = es.enter_context(nc.sbuf_tensor("p1_A", [128, 8, 2], F32))
        Bv = es.enter_context(nc.sbuf_tensor("p1_B", [128, 8, 2], F32))
        gcol = es.enter_context(nc.sbuf_tensor("p1_g", [128, 8], F32))
        tab = es.enter_context(nc.sbuf_tensor("p1_tab", [128, 4, 512], F32))
        st0 = es.enter_context(nc.sbuf_tensor("p1_st0", [128, 512], F32))
        st1 = es.enter_context(nc.sbuf_tensor("p1_st1", [128, 512], F32))
        tm1 = es.enter_context(nc.sbuf_tensor("p1_t1", [128, 512], F32))
        tm2 = es.enter_context(nc.sbuf_tensor("p1_t2", [128, 512], F32))
        ro0 = es.enter_context(nc.sbuf_tensor("p1_ro0", [128, 512], BF16))
        ro1 = es.enter_context(nc.sbuf_tensor("p1_ro1", [128, 512], BF16))
        vs0 = es.enter_context(nc.sbuf_tensor("p1_vs0", [128, 384], BF16))
        vs1 = es.enter_context(nc.sbuf_tensor("p1_vs1", [128, 384], BF16))
        pt = es.enter_context(nc.psum_tensor("p1_pt", [128, 8, 128], F32))
        pf0 = es.enter_context(nc.psum_tensor("p1_pf0", [128, 512], F32))
        pf1 = es.enter_context(nc.psum_tensor("p1_pf1", [128, 512], F32))
        pf2 = es.enter_context(nc.psum_tensor("p1_pf2", [128, 512], F32))
        pf3 = es.enter_context(nc.psum_tensor("p1_pf3", [128, 512], F32))
        pv = es.enter_context(nc.psum_tensor("p1_pv", [128, 512], F32))
        xb = [x0, x1]; xnb = [xn0, xn1]; hTb = [hT0, hT1]; stb = [st0, st1]; rob = [ro0, ro1]; vsb = [vs0, vs1]
        pfb = [pf0, pf1, pf2, pf3]
        load_weight_bf16(P, nc, K.w_in[l], W, "p1_w", 8, WCOLS, [st0, st1, tm1, tm2],
                         ["p1_st0", "p1_st1", "p1_t1", "p1_t2"])
        mod_AB(P, nc, K, l, 0, 1, 0, A, Bv, gcol, "p1")
        cnt = {"blk": 0, "pf": 0, "st": 0, "ro": 0}
        xnt = [[es.enter_context(nc.sbuf_tensor(f"p1_xnt{i}_{j}", [128, D], F32)) for j in range(4)] for i in range(2)]
        tl = tiles_512()

        def prepA(ti):
            t0, n = tl[ti]
            for blk in range(n // 128):
                tb = t0 + blk * 128
                i2 = cnt["blk"] % 2
                cnt["blk"] += 1
                xt = xb[i2]; xk = f"p1_x{i2}"
                P.dma("sp", xt[:], xsrc[tb:tb + 128, :], reads=["xres"], writes=[xk])
                norm_block(P, xt, xk, ss, rs, junk, xnt[ti % 2][blk], f"p1_xnt{ti % 2}_{blk}")

        def prepB(ti):
            t0, n = tl[ti]
            who = 1 if t0 < CTX else 0
            hT = hTb[ti % 2]; hk = f"p1_hT{ti % 2}"
            for blk in range(n // 128):
                xn = xnt[ti % 2][blk]; xnk = f"p1_xnt{ti % 2}_{blk}"
                for dc in range(8):
                    P.call("pe", "transpose", reads=[xnk, "ident"], writes=["p1_pt"], inc=(dc == 7),
                           out=pt[:, dc, :], in_=xn[:, dc * 128:(dc + 1) * 128], identity=K.ident[:])
                for dc in range(8):
                    P.call("act", "activation", reads=["p1_pt", "p1_A", "p1_B"], writes=[hk],
                           out=hT[:, dc, blk * 128:(blk + 1) * 128], in_=pt[:, dc, :], func=AF.Identity,
                           scale=A[:, dc, who:who + 1], bias=Bv[:, dc, who:who + 1])

        def proj(ti):
            t0, n = tl[ti]
            hT = hTb[ti % 2]; hk = f"p1_hT{ti % 2}"
            P.dma("sp", tab[:, :, :n], K.rope[:, :, t0:t0 + n].rearrange("f p t -> p f t"), writes=["p1_tab"])

            def fm(m):
                i4 = cnt["pf"] % 4
                cnt["pf"] += 1
                ps = pfb[i4]; pk = f"p1_pf{i4}"
                for kc in range(8):
                    P.call("pe", "matmul", reads=wkeys("p1_w", m * 128, (m + 1) * 128) + [hk], writes=[pk], inc=(kc == 7),
                           out=ps[:, :n], lhsT=W[:, kc, m * 128:(m + 1) * 128], rhs=hT[:, kc, :n],
                           start=(kc == 0), stop=(kc == 7))
                return ps, pk

            for m in range(8):
                ps, pk = fm(m)
                i2 = cnt["st"] % 2
                cnt["st"] += 1
                st = stb[i2]; sk = f"p1_st{i2}"
                P.call("act", "activation", reads=[pk], writes=[sk], out=st[:, :n], in_=ps[:, :n], func=AF.Copy)
                P.dma("sp", K.fm32[l][m * 128:(m + 1) * 128, t0:t0 + n], st[:, :n], reads=[sk], writes=["fm32"])
            pairs = [(8, 10, 0), (9, 11, 0), (12, 14, 0), (13, 15, 0), (16, 20, 2), (17, 21, 2), (18, 22, 2),
                     (19, 23, 2), (24, 25, 2)]
            for oi, (ma, mb, tbi) in enumerate(pairs):
                psa, pka = fm(ma)
                psb, pkb = fm(mb)
                i2 = cnt["ro"] % 2
                cnt["ro"] += 1
                ro = rob[i2]; rk = f"p1_ro{i2}"
                P.call("dve", "tensor_tensor", reads=[pka, "p1_tab"], writes=["p1_t1"],
                       out=tm1[:, :n], in0=psa[:, :n], in1=tab[:, tbi, :n], op=ALU.mult)
                P.call("dve", "tensor_tensor", reads=[pkb, "p1_tab"], writes=["p1_t2"],
                       out=tm2[:, :n], in0=psb[:, :n], in1=tab[:, tbi + 1, :n], op=ALU.mult)
                P.call("pool", "tensor_tensor", reads=["p1_t1", "p1_t2"], writes=[rk],
                       out=ro[:, :n], in0=tm1[:, :n], in1=tm2[:, :n], op=ALU.add)
                P.dma("sp", K.ropeT[l][oi * 128:(oi + 1) * 128, t0:t0 + n], ro[:, :n], reads=[rk], writes=["ropeT"])
            for blk in range(n // 128):
                tb = t0 + blk * 128
                vs = vsb[blk % 2]; vk = f"p1_vs{blk % 2}"
                for kc in range(8):
                    P.call("pe", "matmul", reads=wkeys("p1_w", NFM * 128, WCOLS) + [hk], writes=["p1_pv"], inc=(kc == 7),
                           out=pv[:, 0:384], lhsT=hT[:, kc, blk * 128:(blk + 1) * 128], rhs=W[:, kc, NFM * 128:WCOLS],
                           start=(kc == 0), stop=(kc == 7))
                P.call("act", "activation", reads=["p1_pv"], writes=[vk], out=vs[:], in_=pv[:, 0:384], func=AF.Copy)
                P.dma("sp", K.vtm[l][tb:tb + 128, :], vs[:], reads=[vk], writes=["vtm"])

        prepA(0)
        prepB(0)
        for ti in range(len(tl)):
            if ti + 1 < len(tl):
                prepA(ti + 1)
            proj(ti)
            if ti + 1 < len(tl):
                prepB(ti + 1)


def phase_rwprep(P, nc, K, l):
    with ExitStack() as es:
        sb = lambda name, shape, dt=F32: es.enter_context(nc.sbuf_tensor(name, shape, dt))
        pp = lambda name, shape, dt=F32: es.enter_context(nc.psum_tensor(name, shape, dt))
        cw = sb("p2_cw", [128, 6, 3]); kkc = sb("p2_kkc", [128, 2]); kac = sb("p2_kac", [128, 2])
        omka = sb("p2_omka", [128, 2]); w0c = sb("p2_w0", [128, 2, 2]); a0c = sb("p2_a0", [128, 2, 2])
        wup = sb("p2_wup", [64, 256]); aup = sb("p2_aup", [64, 256]); gup = sb("p2_gup", [128, 256])
        bones = sb("p2_bones", [128, 128])
        xr = sb("p2_xr", [128, 6, 514]); cv = sb("p2_cv", [128, 6, 512])
        lg = sb("p2_lg", [128, 512]); la = sb("p2_la", [64, 512]); thw = sb("p2_thw", [64, 512]); sgd = sb("p2_sgd", [128, 512])
        kkr = sb("p2_kkr", [128, 512]); sq = sb("p2_sq", [128, 512]); nr = sb("p2_nr", [128, 512])
        kk = sb("p2_kk", [128, 2, 512]); sgw = sb("p2_sgw", [128, 512])
        dec0 = sb("p2_dec0", [128, 512]); dec1 = sb("p2_dec1", [128, 512])
        av = sb("p2_a", [128, 512]); tt = sb("p2_t", [128, 512])
        NKf = sb("p2_NKf", [128, 2, 2, 512]); KDf = sb("p2_KDf", [128, 2, 2, 512])
        tm0 = sb("p2_tm0", [128, 7, 256]); tm1 = sb("p2_tm1", [128, 7, 256])
        pT = pp("p2_pT", [128, 12, 128]); pg_full = pp("p2_pg", [128, 512]); pg = pg_full[:, 0:256]
        px0 = pp("p2_px0", [128, 512]); px1 = pp("p2_px1", [128, 512]); px2 = pp("p2_px2", [128, 512])
        pxb = [px0, px1, px2]; decb = [dec0, dec1]; tmb = [tm0, tm1]
        NS = dict(allow_slow_non_contiguous=True)
        for j in range(3):
            P.dma("sp", cw[:, :, j], K.rw_conv[l, j].rearrange("(c p) -> p c", p=128), writes=["p2_cw"], **NS)
        P.dma("sp", kkc[:], K.rw_k_k[l].rearrange("(h p) -> p h", p=128), writes=["p2_kkc"], **NS)
        P.dma("sp", kac[:], K.rw_k_a[l].rearrange("(h p) -> p h", p=128), writes=["p2_kac"], **NS)
        for d in range(2):
            P.dma("sp", w0c[:, d, :], K.rw_w0[l, d].rearrange("(h p) -> p h", p=128), writes=["p2_w0"], **NS)
            P.dma("sp", a0c[:, d, :], K.rw_a0[l, d].rearrange("(h p) -> p h", p=128), writes=["p2_a0"], **NS)
        for d in range(2):
            P.dma("sp", wup[32 * d:32 * d + 32, :], K.rw_w_up[l, d], writes=["p2_wup"])
            P.dma("sp", aup[32 * d:32 * d + 32, :], K.rw_a_up[l, d], writes=["p2_aup"])
        P.dma("sp", gup[64:128, :], K.rw_g_up[l], writes=["p2_gup"])
        P.dma("sp", bones[:], K.bones_d, writes=["p2_bones"])
        P.call("dve", "tensor_scalar", reads=["p2_kac"], writes=["p2_omka"], out=omka[:], in0=kac[:],
               scalar1=-1.0, scalar2=1.0, op0=ALU.mult, op1=ALU.add)
        cnt = {"px": 0, "dec": 0, "tm": 0}

        def px():
            i = cnt["px"] % 3
            cnt["px"] += 1
            return pxb[i], f"p2_px{i}"

        for (t0, n) in tiles_512():
            s0, s1 = (0, CTX) if t0 < CTX else (CTX, T)
            src = lambda a, b: K.fm32[l][0:768, a:b].rearrange("(c p) t -> p c t", p=128)
            P.dma("sp", xr[:, :, 1:n + 1], src(t0, t0 + n), reads=["fm32"], writes=["p2_xr"])
            if t0 > s0:
                P.dma("sp", xr[:, :, 0:1], src(t0 - 1, t0), reads=["fm32"], writes=["p2_xr"], **NS)
            else:
                P.call("pool", "memset", writes=["p2_xr"], ap=xr[:, :, 0:1], constant=0.0)
            if t0 + n < s1:
                P.dma("sp", xr[:, :, n + 1:n + 2], src(t0 + n, t0 + n + 1), reads=["fm32"], writes=["p2_xr"], **NS)
            else:
                P.call("pool", "memset", writes=["p2_xr"], ap=xr[:, :, n + 1:n + 2], constant=0.0)
            P.dma("sp", lg[:, :n], K.fm32[l][768:896, t0:t0 + n], reads=["fm32"], writes=["p2_lg"])
            P.dma("sp", la[:, :n], K.fm32[l][896:960, t0:t0 + n], reads=["fm32"], writes=["p2_la"])
            for c in range(6):
                P.call("act", "activation", reads=["p2_xr", "p2_cw"], writes=["p2_cv"],
                       out=cv[:, c, :n], in_=xr[:, c, 1:n + 1], func=AF.Identity, scale=cw[:, c, 1:2])
                P.call("dve", "scalar_tensor_tensor", reads=["p2_xr", "p2_cw", "p2_cv"], writes=["p2_cv"],
                       out=cv[:, c, :n], in0=xr[:, c, 0:n], scalar=cw[:, c, 0:1], in1=cv[:, c, :n],
                       op0=ALU.mult, op1=ALU.add)
                P.call("dve", "scalar_tensor_tensor", reads=["p2_xr", "p2_cw", "p2_cv"], writes=["p2_cv"],
                       out=cv[:, c, :n], in0=xr[:, c, 2:n + 2], scalar=cw[:, c, 2:3], in1=cv[:, c, :n],
                       op0=ALU.mult, op1=ALU.add)
            P.call("act", "activation", reads=["p2_lg"], writes=["p2_thw"], out=thw[:, :n], in_=lg[0:64, :n], func=AF.Tanh)
            P.call("act", "activation", reads=["p2_lg"], writes=["p2_sgd"], out=sgd[64:128, :n], in_=lg[64:128, :n], func=AF.Sigmoid)
            for hp in range(2):
                kf = cv[:, 2 + hp, :n]
                P.call("dve", "tensor_scalar", reads=["p2_cv", "p2_kkc"], writes=["p2_kkr"], out=kkr[:, :n], in0=kf,
                       scalar1=kkc[:, hp:hp + 1], scalar2=None, op0=ALU.mult)
                P.call("act", "activation", reads=["p2_kkr"], writes=["p2_sq"], out=sq[:, :n], in_=kkr[:, :n], func=AF.Square)
                ps, pk = px()
                P.call("pe", "matmul", reads=["p2_bones", "p2_sq"], writes=[pk], out=ps[:, :n], lhsT=bones[:], rhs=sq[:, :n],
                       start=True, stop=True)
                P.call("act", "activation", reads=[pk], writes=["p2_nr"], out=nr[:, :n], in_=ps[:, :n], func=AF.Sqrt)
                P.call("dve", "tensor_scalar", reads=["p2_nr"], writes=["p2_nr"], out=nr[:, :n], in0=nr[:, :n],
                       scalar1=1e-12, scalar2=None, op0=ALU.max)
                P.call("dve", "reciprocal", reads=["p2_nr"], writes=["p2_nr"], out=nr[:, :n], in_=nr[:, :n])
                P.call("dve", "tensor_tensor", reads=["p2_kkr", "p2_nr"], writes=["p2_kk"], out=kk[:, hp, :n],
                       in0=kkr[:, :n], in1=nr[:, :n], op=ALU.mult)
                P.dma("sp", K.col_kr[l][hp][:, t0:t0 + n], kk[:, hp, :n], reads=["p2_kk"], writes=["col_kr"])
                P.dma("sp", K.col_kr[l][2 + hp][:, t0:t0 + n], cv[:, hp, :n], reads=["p2_cv"], writes=["col_kr"])
                for d in range(2):
                    ps, pk = px()
                    P.call("pe", "matmul", reads=["p2_wup", "p2_thw"], writes=[pk], out=ps[:, :n],
                           lhsT=wup[32 * d:32 * d + 32, hp * 128:(hp + 1) * 128], rhs=thw[32 * d:32 * d + 32, :n],
                           start=True, stop=True)
                    P.call("act", "activation", reads=[pk, "p2_w0"], writes=["p2_sgw"], out=sgw[:, :n], in_=ps[:, :n],
                           func=AF.Sigmoid, bias=w0c[:, d, hp:hp + 1])
                    i2 = cnt["dec"] % 2
                    cnt["dec"] += 1
                    dec = decb[i2]; dk = f"p2_dec{i2}"
                    P.call("act", "activation", reads=["p2_sgw"], writes=[dk], out=dec[:, :n], in_=sgw[:, :n],
                           func=AF.Exp, scale=-math.exp(-0.5))
                    P.dma("sp", K.col_w[l][2 * d + hp][:, t0:t0 + n], dec[:, :n], reads=[dk], writes=["col_w"])
                    ps, pk = px()
                    P.call("pe", "matmul", reads=["p2_aup", "p2_la"], writes=[pk], out=ps[:, :n],
                           lhsT=aup[32 * d:32 * d + 32, hp * 128:(hp + 1) * 128], rhs=la[32 * d:32 * d + 32, :n],
                           start=True, stop=True)
                    P.call("act", "activation", reads=[pk, "p2_a0"], writes=["p2_a"], out=av[:, :n], in_=ps[:, :n],
                           func=AF.Sigmoid, bias=a0c[:, d, hp:hp + 1])
                    P.call("dve", "tensor_scalar", reads=["p2_a", "p2_kac", "p2_omka"], writes=["p2_t"], out=tt[:, :n],
                           in0=av[:, :n], scalar1=kac[:, hp:hp + 1], scalar2=omka[:, hp:hp + 1], op0=ALU.mult, op1=ALU.add)
                    P.call("dve", "tensor_tensor", reads=["p2_cv", "p2_t"], writes=["p2_KDf"], out=KDf[:, d, hp, :n],
                           in0=kf, in1=tt[:, :n], op=ALU.mult)
                    P.call("dve", "scalar_tensor_tensor", reads=["p2_kk", "p2_a"], writes=["p2_NKf"], out=NKf[:, d, hp, :n],
                           in0=kk[:, hp, :n], scalar=-1.0, in1=av[:, :n], op0=ALU.mult, op1=ALU.mult)
                    P.dma("sp", K.col_nk[l][2 * d + hp][:, t0:t0 + n], NKf[:, d, hp, :n], reads=["p2_NKf"], writes=["col_nk"])
                    P.dma("sp", K.col_kd[l][2 * d + hp][:, t0:t0 + n], KDf[:, d, hp, :n], reads=["p2_KDf"], writes=["col_kd"])
            for blk in range(n // 128):
                tb = t0 + blk * 128
                bs = slice(blk * 128, (blk + 1) * 128)
                srcs = []
                for d in range(2):
                    for hp in range(2):
                        srcs.append((NKf[:, d, hp, bs], "p2_NKf"))
                for d in range(2):
                    for hp in range(2):
                        srcs.append((KDf[:, d, hp, bs], "p2_KDf"))
                for hp in range(2):
                    srcs.append((cv[:, 4 + hp, bs], "p2_cv"))
                for hp in range(2):
                    srcs.append((cv[:, hp, bs], "p2_cv"))
                for j, (sap, skey) in enumerate(srcs):
                    P.call("pe", "transpose", reads=[skey, "ident"], writes=["p2_pT"], out=pT[:, j, :], in_=sap,
                           identity=K.ident[:])
                P.call("pe", "matmul", reads=["p2_sgd", "p2_gup"], writes=["p2_pg"], out=pg, lhsT=sgd[64:128, bs], rhs=gup[64:128, :],
                       start=True, stop=True)
                i2 = cnt["tm"] % 2
                cnt["tm"] += 1
                tm = tmb[i2]; tk = f"p2_tm{i2}"
                for q in range(3):
                    eng = "act" if q == 1 else "dve"
                    o_ap = tm[:, 2 * q:2 * q + 2, :].rearrange("p a b -> p (a b)")
                    i_ap = pT[:, 4 * q:4 * q + 4, :].rearrange("p a b -> p (a b)")
                    if eng == "act":
                        P.call("act", "activation", reads=["p2_pT"], writes=[tk], out=o_ap, in_=i_ap, func=AF.Copy)
                    else:
                        P.call("dve", "tensor_copy", reads=["p2_pT"], writes=[tk], out=o_ap, in_=i_ap)
                P.call("act", "activation", reads=["p2_pg"], writes=[tk], out=tm[:, 6, :], in_=pg, func=AF.Copy)
                P.dma("sp", K.rw_tm[l][tb:tb + 128], tm[:], reads=[tk], writes=["rw_tm"])


TC = 32


def phase_scan(P, nc, K, l):
    nchunk = T // TC
    nctx = CTX // TC
    fwd = list(range(nchunk))
    bwd = list(range(nctx - 1, -1, -1)) + list(range(nchunk - 1, nctx - 1, -1))
    with ExitStack() as es:
        sb = lambda name, shape, dt=F32: es.enter_context(nc.sbuf_tensor(name, shape, dt))
        pp = lambda name, shape, dt=F32: es.enter_context(nc.psum_tensor(name, shape, dt))
        wcol = [sb(f"p3_w{b}", [128, 4, TC]) for b in range(2)]
        kkc = [sb(f"p3_kkc{b}", [128, 4, TC]) for b in range(2)]
        rc = [sb(f"p3_rc{b}", [128, 4, TC]) for b in range(2)]
        KKbd = [sb(f"p3_KK{b}", [128, 4, TC, 8]) for b in range(2)]
        Rbd = [sb(f"p3_R{b}", [128, 4, TC, 8]) for b in range(2)]
        LH = [[sb(f"p3_LH{b}{hp}", [128, TC, 128]) for hp in range(2)] for b in range(2)]
        Vr = [sb(f"p3_V{b}", [128, TC, 64]) for b in range(2)]
        Orows = [sb(f"p3_O{b}", [128, TC, 64]) for b in range(2)]
        SKV = sb("p3_SKV", [128, 64])
        S = sb("p3_S", [128, 4, 64])
        ps_sk = pp("p3_psk", [128, 512])[:, 0:64]; ps_o = pp("p3_po", [128, 512])[:, 0:64]
        ps_u = [pp(f"p3_pu{p}", [128, 512])[:, 0:64] for p in range(4)]
        for b in range(2):
            tiles = [(KKbd[b], f"p3_KK{b}"), (Rbd[b], f"p3_R{b}"), (Vr[b], f"p3_V{b}"), (Orows[b], f"p3_O{b}"),
                     (LH[b][0], f"p3_LH{b}"), (LH[b][1], f"p3_LH{b}")]
            for t_, nm in tiles:
                P.call("pool", "memset", writes=[nm + "_g0", nm + "_g1"], ap=t_[:], constant=0.0)
        P.call("pool", "memset", writes=["p3_S0", "p3_S1", "p3_S2", "p3_S3"], ap=S[:], constant=0.0)
        P.call("pool", "memset", writes=["p3_SKV_g0", "p3_SKV_g1"], ap=SKV[:], constant=0.0)
        row = lambda ap: ap.rearrange("(o t) k -> o t k", o=1)
        for c in range(nchunk):
            b = c % 2
            t0s = (fwd[c] * TC, bwd[c] * TC)
            for g in range(2):
                t0 = t0s[g]
                for hp in range(2):
                    p = 2 * g + hp
                    r0 = 64 * g + 4 * hp
                    P.dma("sp", wcol[b][:, p, :], K.col_w[l][p][:, t0:t0 + TC], reads=["col_w"], writes=[f"p3_w{b}_g{g}"])
                    P.dma("sp", kkc[b][:, p, :], K.col_kr[l][hp][:, t0:t0 + TC], reads=["col_kr"], writes=[f"p3_kkc{b}_g{g}"])
                    P.dma("sp", rc[b][:, p, :], K.col_kr[l][2 + hp][:, t0:t0 + TC], reads=["col_kr"], writes=[f"p3_rc{b}_g{g}"])
                    for j in range(2):
                        f0 = (2 * hp + j) * 64
                        P.dma("sp", LH[b][hp][r0 + j:r0 + j + 1, :, 64 * j:64 * j + 64],
                              row(K.rw_tm[l][t0:t0 + TC, g, f0:f0 + 64]), reads=["rw_tm"], writes=[f"p3_LH{b}_g{g}"])
                        P.dma("sp", LH[b][hp][r0 + 2 + j:r0 + 3 + j, :, 64 * j:64 * j + 64],
                              row(K.rw_tm[l][t0:t0 + TC, 2 + g, f0:f0 + 64]), reads=["rw_tm"], writes=[f"p3_LH{b}_g{g}"])
                        P.dma("sp", Vr[b][r0 + 2 + j:r0 + 3 + j, :, :],
                              row(K.rw_tm[l][t0:t0 + TC, 4, f0:f0 + 64]), reads=["rw_tm"], writes=[f"p3_V{b}_g{g}"])
                    for half in range(2):
                        hs = slice(64 * half, 64 * half + 64)
                        P.call("pool", "tensor_copy", reads=[f"p3_kkc{b}_g{g}"], writes=[f"p3_KK{b}_g{g}"],
                               out=KKbd[b][hs, p, :, 4 * hp + half], in_=kkc[b][hs, p, :])
                        P.call("pool", "tensor_copy", reads=[f"p3_rc{b}_g{g}"], writes=[f"p3_R{b}_g{g}"],
                               out=Rbd[b][hs, p, :, 4 * hp + half], in_=rc[b][hs, p, :])
            if getattr(K, "scan_dbg", False) and c == 0:
                for nm, t_, keys in (("dbg_LH0", LH[0][0], ["p3_LH0_g0", "p3_LH0_g1"]), ("dbg_LH1", LH[0][1], ["p3_LH0_g0", "p3_LH0_g1"]),
                                     ("dbg_V", Vr[0], ["p3_V0_g0", "p3_V0_g1"]), ("dbg_KK", KKbd[0], ["p3_KK0_g0", "p3_KK0_g1"]),
                                     ("dbg_R", Rbd[0], ["p3_R0_g0", "p3_R0_g1"]), ("dbg_w", wcol[0], ["p3_w0_g0", "p3_w0_g1"])):
                    shp = list(t_.shape)
                    o_ = nc.dram_tensor(nm, shp, F32, kind="ExternalOutput").ap()
                    P.dma("sp", o_, t_[:], reads=keys, writes=[nm])
            for i in range(TC):
                idxs = (i, TC - 1 - i)
                for g in range(2):
                    idx = idxs[g]
                    gs = slice(64 * g, 64 * g + 8)
                    for hp in range(2):
                        p = 2 * g + hp
                        P.call("pe", "matmul", reads=[f"p3_KK{b}_g{g}", f"p3_S{p}"], writes=[f"p3_psk_g{g}"],
                               out=ps_sk[gs, :], lhsT=KKbd[b][:, p, idx, :], rhs=S[:, p, :],
                               start=(hp == 0), stop=(hp == 1))
                    P.call("dve", "tensor_tensor", reads=[f"p3_psk_g{g}", f"p3_V{b}_g{g}"], writes=[f"p3_SKV_g{g}"],
                           out=SKV[gs, :], in0=ps_sk[gs, :], in1=Vr[b][gs, idx, :], op=ALU.add)
                    for hp in range(2):
                        p = 2 * g + hp
                        P.call("pe", "matmul", reads=[f"p3_LH{b}_g{g}", f"p3_SKV_g{g}"], writes=[f"p3_pu{p}"],
                               out=ps_u[p], lhsT=LH[b][hp][gs, idx, :], rhs=SKV[gs, :], start=True, stop=True)
                    for hp in range(2):
                        p = 2 * g + hp
                        P.call("dve", "scalar_tensor_tensor", reads=[f"p3_S{p}", f"p3_w{b}_g{g}", f"p3_pu{p}"],
                               writes=[f"p3_S{p}"], out=S[:, p, :], in0=S[:, p, :], scalar=wcol[b][:, p, idx:idx + 1],
                               in1=ps_u[p], op0=ALU.mult, op1=ALU.add)
                    for hp in range(2):
                        p = 2 * g + hp
                        P.call("pe", "matmul", reads=[f"p3_R{b}_g{g}", f"p3_S{p}"], writes=[f"p3_po_g{g}"],
                               out=ps_o[gs, :], lhsT=Rbd[b][:, p, idx, :], rhs=S[:, p, :],
                               start=(hp == 0), stop=(hp == 1))
                    P.call("act", "activation", reads=[f"p3_po_g{g}"], writes=[f"p3_O{b}_g{g}"],
                           out=Orows[b][gs, idx, :], in_=ps_o[gs, :], func=AF.Copy)
            if getattr(K, "scan_dbg", False) and c == 0:
                o_ = nc.dram_tensor("dbg_O", list(Orows[0].shape), F32, kind="ExternalOutput").ap()
                P.dma("sp", o_, Orows[0][:], reads=["p3_O0_g0", "p3_O0_g1"], writes=["dbg_O"])
                o_ = nc.dram_tensor("dbg_S", list(S.shape), F32, kind="ExternalOutput").ap()
                P.dma("sp", o_, S[:], reads=["p3_S0", "p3_S1", "p3_S2", "p3_S3"], writes=["dbg_S"])
                return
            for g in range(2):
                t0 = t0s[g]
                for hp in range(2):
                    r0 = 64 * g + 4 * hp
                    for j in range(2):
                        f0 = (2 * hp + j) * 64
                        P.dma("sp", row(K.o_tm[l][t0:t0 + TC, g, f0:f0 + 64]), Orows[b][r0 + j:r0 + j + 1, :, :],
                              reads=[f"p3_O{b}_g{g}"], writes=["o_tm"])


CH = 64


def phase_scan_chunked(P, nc, K, l):
    nchunk = T // CH
    nctx = CTX // CH
    order = [list(range(nchunk)), list(range(nctx - 1, -1, -1)) + list(range(nchunk - 1, nctx - 1, -1))]
    with ExitStack() as es:
        sb = lambda name, shape, dt=F32: es.enter_context(nc.sbuf_tensor(name, shape, dt))
        NBUF = 2
        names_in = ["w", "kk", "nk", "kd", "r"]
        tin = {nm: [[sb(f"c3_{nm}{p}{b}", [128, CH]) for b in range(NBUF)] for p in range(4)] for nm in names_in}
        Vtm = [[sb(f"c3_V{p}{b}", [128, 64]) for b in range(NBUF)] for p in range(4)]
        bdn = ["KH", "AH", "KD", "RH", "ANs", "KDs"]
        bd = {nm: [[sb(f"c3_{nm}{p}{b}", [128, 128]) for b in range(NBUF)] for p in range(4)] for nm in bdn}
        sqn = ["An", "AnT", "B", "Apn", "Bp", "X", "XT", "N", "ANtm", "KDtm"]
        sq = {nm: [[sb(f"c3_{nm}{p}{b}", [128, 128]) for b in range(NBUF)] for p in range(4)] for nm in sqn}
        X2 = [sb(f"c3_X2_{p}", [128, 128]) for p in range(4)]; X2T = [sb(f"c3_X2T_{p}", [128, 128]) for p in range(4)]
        Pc = [sb(f"c3_Pc{p}", [128, CH]) for p in range(4)]; Pm1 = [sb(f"c3_Pm{p}", [128, CH]) for p in range(4)]
        rP = [sb(f"c3_rP{p}", [128, CH]) for p in range(4)]; tmpv = [sb(f"c3_tv{p}", [128, CH]) for p in range(4)]
        PCc = [[sb(f"c3_PC{p}{b}", [128, 1]) for b in range(NBUF)] for p in range(4)]
        zer = sb("c3_zero", [128, CH])
        S = [sb(f"c3_S{p}", [128, 64]) for p in range(4)]
        Rt = [sb(f"c3_Rt{p}", [128, 64]) for p in range(4)]; Ut = [sb(f"c3_Ut{p}", [128, 64]) for p in range(4)]
        Ot = [[sb(f"c3_Ot{p}{b}", [128, 64]) for b in range(NBUF)] for p in range(4)]
        msk = sb("c3_msk", [128, 4, 128])
        psb = [es.enter_context(nc.psum_tensor(f"c3_ps{i}", [128, 512], F32)) for i in range(8)]
        pcnt = {"i": 0}

        def ps():
            i = pcnt["i"] % 8
            pcnt["i"] += 1
            return psb[i], f"c3_ps{i}"

        P.dma("sp", msk[:], K.cmask_d.rearrange("m a b -> a m b"), writes=["c3_msk"])
        P.call("pool", "memset", writes=["c3_zero"], ap=zer[:], constant=0.0)
        for p in range(4):
            P.call("pool", "memset", writes=[f"c3_S{p}"], ap=S[p][:], constant=0.0)
            for b in range(NBUF):
                for nm in bdn:
                    P.call("pool", "memset", writes=[f"c3_{nm}{p}{b}"], ap=bd[nm][p][b][:], constant=0.0)

        def mm(out, lhsT, rhs, reads, pk, start=True, stop=True, fast=False, inc=True):
            P.call("pe", "matmul", reads=reads, writes=[pk], inc=inc, out=out, lhsT=lhsT, rhs=rhs, start=start, stop=stop)

        def stage_a(ci, p):
            b = ci % NBUF
            d = p // 2; hp = p % 2
            t0 = order[d][ci] * CH
            k = lambda nm: f"c3_{nm}{p}{b}"
            srcs = {"w": K.col_w[l][p], "kk": K.col_kr[l][hp], "nk": K.col_nk[l][p], "kd": K.col_kd[l][p], "r": K.col_kr[l][2 + hp]}
            rkeys = {"w": "col_w", "kk": "col_kr", "nk": "col_nk", "kd": "col_kd", "r": "col_kr"}
            for nm in names_in:
                P.dma("sp", tin[nm][p][b][:], srcs[nm][:, t0:t0 + CH], reads=[rkeys[nm]], writes=[k(nm)])
            for j in range(2):
                f0 = (2 * hp + j) * 64
                P.dma("sp", Vtm[p][b][64 * j:64 * j + 64, :], K.rw_tm[l][t0:t0 + CH, 4, f0:f0 + 64], reads=["rw_tm"], writes=[k("V")])
            w_ = tin["w"][p][b]
            P.call("dve", "tensor_tensor_scan", reads=[k("w"), "c3_zero"], writes=[f"c3_Pc{p}"], out=Pc[p][:], data0=w_[:], data1=zer[:],
                   initial=1.0, op0=ALU.mult, op1=ALU.add)
            P.call("pool", "memset", writes=[f"c3_Pm{p}"], ap=Pm1[p][:, 0:1], constant=1.0)
            P.call("pool", "tensor_copy", reads=[f"c3_Pc{p}"], writes=[f"c3_Pm{p}"], out=Pm1[p][:, 1:CH], in_=Pc[p][:, 0:CH - 1])
            P.call("act", "activation", reads=[f"c3_Pc{p}"], writes=[k("PC")], out=PCc[p][b][:], in_=Pc[p][:, CH - 1:CH], func=AF.Copy)
            if d == 1:
                P.call("dve", "reciprocal", reads=[f"c3_Pm{p}"], writes=[f"c3_tv{p}"], out=tmpv[p][:], in_=Pm1[p][:])
                P.call("dve", "reciprocal", reads=[f"c3_Pc{p}"], writes=[f"c3_rP{p}"], out=rP[p][:], in_=Pc[p][:])
                P.call("dve", "tensor_scalar", reads=[f"c3_tv{p}", k("PC")], writes=[f"c3_Pc{p}"], out=Pc[p][:], in0=tmpv[p][:],
                       scalar1=PCc[p][b][:, 0:1], scalar2=None, op0=ALU.mult)
                P.call("dve", "tensor_scalar", reads=[f"c3_rP{p}", k("PC")], writes=[f"c3_Pm{p}"], out=Pm1[p][:], in0=rP[p][:],
                       scalar1=PCc[p][b][:, 0:1], scalar2=None, op0=ALU.mult)
            P.call("dve", "reciprocal", reads=[f"c3_Pc{p}"], writes=[f"c3_rP{p}"], out=rP[p][:], in_=Pc[p][:])
            for j in range(2):
                hs = slice(64 * j, 64 * j + 64)
                eng = "dve" if j == 0 else "pool"
                P.call(eng, "tensor_tensor", reads=[k("kk"), f"c3_Pm{p}"], writes=[k("KH")], out=bd["KH"][p][b][hs, hs],
                       in0=tin["kk"][p][b][hs, :], in1=Pm1[p][hs, :], op=ALU.mult)
                P.call(eng, "tensor_tensor", reads=[k("nk"), f"c3_rP{p}"], writes=[k("AH")], out=bd["AH"][p][b][hs, hs],
                       in0=tin["nk"][p][b][hs, :], in1=rP[p][hs, :], op=ALU.mult)
                P.call(eng, "tensor_tensor", reads=[k("kd"), f"c3_rP{p}"], writes=[k("KD")], out=bd["KD"][p][b][hs, hs],
                       in0=tin["kd"][p][b][hs, :], in1=rP[p][hs, :], op=ALU.mult)
                P.call(eng, "tensor_tensor", reads=[k("r"), f"c3_Pc{p}"], writes=[k("RH")], out=bd["RH"][p][b][hs, hs],
                       in0=tin["r"][p][b][hs, :], in1=Pc[p][hs, :], op=ALU.mult)
                P.call("dve", "tensor_scalar", reads=[k("AH"), k("PC")], writes=[k("ANs")], out=bd["ANs"][p][b][hs, hs],
                       in0=bd["AH"][p][b][hs, hs], scalar1=PCc[p][b][hs, 0:1], scalar2=None, op0=ALU.mult)
                P.call("dve", "tensor_scalar", reads=[k("KD"), k("PC")], writes=[k("KDs")], out=bd["KDs"][p][b][hs, hs],
                       in0=bd["KD"][p][b][hs, hs], scalar1=PCc[p][b][hs, 0:1], scalar2=None, op0=ALU.mult)

        def chain_stage(ci, st):
            b = ci % NBUF
            kf = lambda p, nm: f"c3_{nm}{p}{b}"
            if st == 0:
                for p in range(4):
                    pt_, pk = ps()
                    mm(pt_[:, 0:64], bd["KH"][p][b][:], S[p][:], [kf(p, "KH"), f"c3_S{p}"], pk, start=True, stop=False, inc=False)
                    mm(pt_[:, 0:64], sq["B"][p][b][:], Vtm[p][b][:], [kf(p, "B"), kf(p, "V")], pk, start=False, stop=True)
                    P.call("act", "activation", reads=[pk], writes=[f"c3_Rt{p}"], out=Rt[p][:], in_=pt_[:, 0:64], func=AF.Copy)
            elif st == 1:
                for p in range(4):
                    pt2, pk2 = ps()
                    mm(pt2[:, 0:64], sq["N"][p][b][:], Rt[p][:], [kf(p, "N"), f"c3_Rt{p}"], pk2)
                    P.call("dve", "tensor_copy", reads=[pk2], writes=[f"c3_Ut{p}"], out=Ut[p][:], in_=pt2[:, 0:64])
            else:
                for p in range(4):
                    sk_ = f"c3_S{p}"
                    pt3, pk3 = ps()
                    mm(pt3[:, 0:64], bd["RH"][p][b][:], S[p][:], [kf(p, "RH"), sk_], pk3, start=True, stop=False, inc=False)
                    mm(pt3[:, 0:64], sq["Apn"][p][b][:], Ut[p][:], [kf(p, "Apn"), f"c3_Ut{p}"], pk3, start=False, stop=False, inc=False)
                    mm(pt3[:, 0:64], sq["Bp"][p][b][:], Vtm[p][b][:], [kf(p, "Bp"), kf(p, "V")], pk3, start=False, stop=True)
                    P.call("act", "activation", reads=[pk3], writes=[kf(p, "Ot")], out=Ot[p][b][:], in_=pt3[:, 0:64], func=AF.Copy)
                    pt4, pk4 = ps()
                    mm(pt4[:, 0:64], sq["ANtm"][p][b][:], Ut[p][:], [kf(p, "ANtm"), f"c3_Ut{p}"], pk4, start=True, stop=False, inc=False)
                    mm(pt4[:, 0:64], sq["KDtm"][p][b][:], Vtm[p][b][:], [kf(p, "KDtm"), kf(p, "V")], pk4, start=False, stop=True)
                    P.call("dve", "scalar_tensor_tensor", reads=[sk_, kf(p, "PC"), pk4], writes=[sk_], out=S[p][:], in0=S[p][:],
                           scalar=PCc[p][b][:, 0:1], in1=pt4[:, 0:64], op0=ALU.mult, op1=ALU.add)
                for p in range(4):
                    d = p // 2; hp = p % 2
                    t0 = order[d][ci] * CH
                    for j in range(2):
                        f0 = (2 * hp + j) * 64
                        P.dma("sp", K.o_tm[l][t0:t0 + CH, d, f0:f0 + 64], Ot[p][b][64 * j:64 * j + 64, :], reads=[kf(p, "Ot")], writes=["o_tm"])

        for p in range(4):
            stage_a(0, p)
        for ci in range(nchunk):
            b = ci % NBUF
            kf = lambda p, nm: f"c3_{nm}{p}{b}"
            for p in range(4):
                d = p // 2
                k = lambda nm, p=p: f"c3_{nm}{p}{b}"
                for nm_s, nm_d in (("ANs", "ANtm"), ("KDs", "KDtm")):
                    pt_, pk = ps()
                    P.call("pe", "transpose", reads=[k(nm_s), "ident"], writes=[pk], out=pt_[:, 0:128], in_=bd[nm_s][p][b][:], identity=K.ident[:])
                    P.call("act", "activation", reads=[pk], writes=[k(nm_d)], out=sq[nm_d][p][b][:], in_=pt_[:, 0:128], func=AF.Copy)
                ms, mi = (0, 1) if d == 0 else (2, 3)
                for (dst, lh, rh, mk) in (("An", "AH", "KH", ms), ("B", "KD", "KH", ms), ("Apn", "AH", "RH", mi), ("Bp", "KD", "RH", mi)):
                    pt_, pk = ps()
                    mm(pt_[:, 0:128], bd[lh][p][b][:], bd[rh][p][b][:], [k(lh), k(rh)], pk)
                    P.call("dve", "tensor_tensor", reads=[pk, "c3_msk"], writes=[k(dst)], out=sq[dst][p][b][:], in0=pt_[:, 0:128],
                           in1=msk[:, mk, :], op=ALU.mult)
            for p in range(4):
                k = lambda nm, p=p: f"c3_{nm}{p}{b}"
                pt_, pk = ps()
                P.call("pe", "transpose", reads=[k("An"), "ident"], writes=[pk], out=pt_[:, 0:128], in_=sq["An"][p][b][:], identity=K.ident[:])
                P.call("act", "activation", reads=[pk], writes=[k("AnT")], out=sq["AnT"][p][b][:], in_=pt_[:, 0:128], func=AF.Copy)
                P.call("dve", "tensor_tensor", reads=[k("An"), "ident"], writes=[k("N")], out=sq["N"][p][b][:], in0=sq["An"][p][b][:], in1=K.ident[:], op=ALU.add)
            cur = {p: (sq["An"][p][b], sq["AnT"][p][b], kf(p, "An"), kf(p, "AnT")) for p in range(4)}
            nround = 5
            for rnd in range(nround):
                lastr = rnd == nround - 1
                nxt = {}
                for p in range(4):
                    Xc, XTc, xk, xtk = cur[p]
                    pt_, pk = ps()
                    mm(pt_[:, 0:128], Xc[:], XTc[:], [xk, xtk], pk)
                    x2t = X2T[p] if rnd % 2 == 0 else sq["XT"][p][b]
                    x2tk = f"c3_X2T_{p}" if rnd % 2 == 0 else kf(p, "XT")
                    P.call("act", "activation", reads=[pk], writes=[x2tk], out=x2t[:], in_=pt_[:, 0:128], func=AF.Copy)
                    nxt[p] = (x2t, x2tk)
                for p in range(4):
                    x2t, x2tk = nxt[p]
                    Nt = sq["N"][p][b]
                    pt3, pk3 = ps()
                    mm(pt3[:, 0:128], x2t[:], Nt[:], [x2tk, kf(p, "N")], pk3)
                    P.call("dve", "tensor_tensor", reads=[pk3, kf(p, "N")], writes=[kf(p, "N")], out=Nt[:], in0=pt3[:, 0:128], in1=Nt[:], op=ALU.add)
                    if not lastr:
                        pt2, pk2 = ps()
                        P.call("pe", "transpose", reads=[x2tk, "ident"], writes=[pk2], out=pt2[:, 0:128], in_=x2t[:], identity=K.ident[:])
                        x2 = X2[p] if rnd % 2 == 0 else sq["X"][p][b]
                        x2k = f"c3_X2_{p}" if rnd % 2 == 0 else kf(p, "X")
                        P.call("act", "activation", reads=[pk2], writes=[x2k], out=x2[:], in_=pt2[:, 0:128], func=AF.Copy)
                        cur[p] = (x2, x2t, x2k, x2tk)
                if rnd < 3 and ci >= 1:
                    chain_stage(ci - 1, rnd)
                if rnd >= 3 and ci + 1 < nchunk:
                    stage_a(ci + 1, 2 * (rnd - 3))
                    stage_a(ci + 1, 2 * (rnd - 3) + 1)
            if ci == nchunk - 1:
                P.mark(f"L{l}_scan_pre_last")
        for st in range(3):
            chain_stage(nchunk - 1, st)


def phase_readout(P, nc, K, l):
    with ExitStack() as es:
        sb = lambda name, shape, dt=F32: es.enter_context(nc.sbuf_tensor(name, shape, dt))
        lng = sb("p4_lng", [128, 256]); lnb = sb("p4_lnb", [128, 256]); rkr = sb("p4_rkr", [128, 256])
        o2 = [sb(f"p4_o2{b}", [128, 2, 256]) for b in range(2)]
        tm = [sb(f"p4_tm{b}", [128, 7, 256]) for b in range(2)]
        o = sb("p4_o", [128, 4, 64]); xc = sb("p4_xc", [128, 4, 64]); sq = sb("p4_sq", [128, 4, 64])
        mu = sb("p4_mu", [128, 4]); var = sb("p4_var", [128, 4]); bs = sb("p4_bs", [128, 4])
        kds = sb("p4_kds", [128, 256]); y = [sb(f"p4_y{b}", [128, 256]) for b in range(2)]
        P.dma("sp", lng[:], K.rw_ln_g[l].partition_broadcast(128), writes=["p4_lng"])
        P.dma("sp", lnb[:], K.rw_ln_b[l].partition_broadcast(128), writes=["p4_lnb"])
        P.dma("sp", rkr[:], K.rw_r_k[l].partition_broadcast(128), writes=["p4_rkr"])
        f3 = lambda ap: ap.rearrange("p (h n) -> p h n", h=4)
        f2 = lambda ap: ap.rearrange("p h n -> p (h n)")
        bc = lambda ap: ap.unsqueeze(2).broadcast_to([128, 4, 64])
        for bi in range(T // 128):
            tb = bi * 128
            b = bi % 2
            ok = f"p4_o2{b}"; tk = f"p4_tm{b}"; yk = f"p4_y{b}"
            P.dma("sp", o2[b][:], K.o_tm[l][tb:tb + 128], reads=["o_tm"], writes=[ok])
            P.dma("sp", tm[b][:], K.rw_tm[l][tb:tb + 128], reads=["rw_tm"], writes=[tk])
            P.call("dve", "tensor_tensor", reads=[ok], writes=["p4_o"], out=f2(o[:]), in0=o2[b][:, 0, :], in1=o2[b][:, 1, :], op=ALU.add)
            P.call("dve", "tensor_reduce", reads=["p4_o"], writes=["p4_mu"], out=mu[:], in_=o[:], axis=AX.X, op=ALU.add)
            P.call("dve", "tensor_scalar", reads=["p4_mu"], writes=["p4_mu"], out=mu[:], in0=mu[:], scalar1=1.0 / 64, scalar2=None, op0=ALU.mult)
            P.call("dve", "tensor_tensor", reads=["p4_o", "p4_mu"], writes=["p4_xc"], out=xc[:], in0=o[:], in1=bc(mu[:]), op=ALU.subtract)
            P.call("pool", "tensor_tensor", reads=["p4_xc"], writes=["p4_sq"], out=sq[:], in0=xc[:], in1=xc[:], op=ALU.mult)
            P.call("dve", "tensor_reduce", reads=["p4_sq"], writes=["p4_var"], out=var[:], in_=sq[:], axis=AX.X, op=ALU.add)
            P.call("dve", "tensor_scalar", reads=["p4_var"], writes=["p4_var"], out=var[:], in0=var[:], scalar1=1.0 / 64, scalar2=64e-5,
                   op0=ALU.mult, op1=ALU.add)
            P.call("act", "activation", reads=["p4_var"], writes=["p4_var"], out=var[:], in_=var[:], func=AF.Sqrt)
            P.call("dve", "reciprocal", reads=["p4_var"], writes=["p4_var"], out=var[:], in_=var[:])
            P.call("dve", "tensor_tensor", reads=["p4_xc", "p4_var"], writes=["p4_xc"], out=xc[:], in0=xc[:], in1=bc(var[:]), op=ALU.mult)
            P.call("dve", "tensor_tensor", reads=["p4_xc", "p4_lng"], writes=["p4_xc"], out=f2(xc[:]), in0=f2(xc[:]), in1=lng[:], op=ALU.mult)
            P.call("dve", "tensor_tensor", reads=["p4_xc", "p4_lnb"], writes=["p4_xc"], out=f2(xc[:]), in0=f2(xc[:]), in1=lnb[:], op=ALU.add)
            P.call("pool", "tensor_tensor", reads=[tk], writes=["p4_kds"], out=kds[:], in0=tm[b][:, 2, :], in1=tm[b][:, 3, :], op=ALU.add)
            P.call("pool", "tensor_tensor", reads=[tk, "p4_kds"], writes=["p4_kds"], out=kds[:], in0=kds[:], in1=tm[b][:, 5, :], op=ALU.mult)
            P.call("pool", "tensor_tensor", reads=["p4_kds", "p4_rkr"], writes=["p4_kds"], out=kds[:], in0=kds[:], in1=rkr[:], op=ALU.mult)
            P.call("dve", "tensor_reduce", reads=["p4_kds"], writes=["p4_bs"], out=bs[:], in_=f3(kds[:]), axis=AX.X, op=ALU.add)
            P.call("dve", "tensor_tensor", reads=[tk, "p4_bs"], writes=["p4_sq"], out=sq[:], in0=f3(tm[b][:, 4, :]), in1=bc(bs[:]), op=ALU.mult)
            P.call("dve", "tensor_tensor", reads=["p4_sq", "p4_xc"], writes=["p4_sq"], out=sq[:], in0=sq[:], in1=xc[:], op=ALU.add)
            P.call("dve", "tensor_tensor", reads=["p4_sq", tk], writes=[yk], out=y[b][:], in0=f2(sq[:]), in1=tm[b][:, 6, :], op=ALU.mult)
            P.dma("sp", K.ytm[l][tb:tb + 128, 0:256], y[b][:], reads=[yk], writes=["ytm"])


def phase_attn(P, nc, K, l):
    lam_init = 0.8 - 0.6 * math.exp(-0.3 * l)
    NBLK = T // 128
    with ExitStack() as es:
        sb = lambda name, shape, dt=F32: es.enter_context(nc.sbuf_tensor(name, shape, dt))
        pp = lambda name: es.enter_context(nc.psum_tensor(name, [128, 512], F32))
        kdf = sb("p5_kdf", [128, 2, T], BF16)
        Vaug = sb("p5_V", [128, NBLK, 6, 65], BF16)
        Kd = [sb(f"p5_Kd{g}", [128, T], BF16) for g in range(2)]
        Qm = [[sb(f"p5_Qm{b}_{u}", [128, 512], BF16) for u in range(8)] for b in range(2)]
        Eb = [sb(f"p5_E{i}", [128, 512], BF16) for i in range(3)]
        oT = [sb(f"p5_oT{m}", [65, 512]) for m in range(2)]
        rz = [sb(f"p5_rz{m}", [64, 512]) for m in range(2)]
        dd = sb("p5_dd", [64, 512]); sqt = sb("p5_sq", [64, 512]); rs = sb("p5_rs", [64, 512])
        ybt = [sb(f"p5_yb{i}", [64, 512], BF16) for i in range(2)]
        sel = sb("p5_sel", [65, 64]); ones64 = sb("p5_ones", [64, 64])
        lamt = sb("p5_lamt", [64, 128]); lp = sb("p5_lp", [64, 64]); e12 = sb("p5_e12", [64, 2]); neglam = sb("p5_nl", [64, 1])
        gdf = sb("p5_gdf", [64, 1])
        msk = sb("p5_msk", [128, 2, 128])
        exps = sb("p5_exps", [128, 8])
        qg = [sb(f"p5_qg{b}", [128, 4, 128], BF16) for b in range(2)]
        Eg = [sb(f"p5_Eg{b}", [128, 5, 128], BF16) for b in range(2)]
        zt = sb("p5_zt", [128, 1]); ycst = [sb(f"p5_yc{b}", [128, 512]) for b in range(2)]
        ps_s = [pp(f"p5_ps{i}") for i in range(3)]
        ps_o = [pp(f"p5_po{m}") for m in range(2)]
        ps_z = [pp(f"p5_pz{m}") for m in range(2)]
        ps_g = pp("p5_pg")
        NS = dict(allow_slow_non_contiguous=True)
        for c in range(2):
            P.dma("sp", kdf[:, c, :], K.ropeT[l][256 + c * 128:256 + (c + 1) * 128, :], reads=["ropeT"], writes=["p5_kdf"])
        for g in range(2):
            for half in range(2):
                P.dma("sp", Kd[g][64 * half:64 * half + 64, :], K.ropeT[l][1024 + 64 * g:1024 + 64 * g + 64, :],
                      reads=["ropeT"], writes=[f"p5_Kd{g}"])
        P.call("pool", "memset", writes=["p5_V"], ap=Vaug[:], constant=1.0)
        for kb in range(NBLK):
            P.dma("sp", Vaug[:, kb, :, 0:64], K.vtm[l][kb * 128:(kb + 1) * 128, :].rearrange("p (h d) -> p h d", h=6),
                  reads=["vtm"], writes=["p5_V"])
        for b in range(2):
            for u in range(8):
                P.call("pool", "memset", writes=[f"p5_Qm{b}"], ap=Qm[b][u][:], constant=0.0)
        P.dma("sp", sel[:], K.sel_d, writes=["p5_sel"])
        P.dma("sp", ones64[:], K.bones_d[0:64, 0:64], writes=["p5_ones"])
        P.dma("sp", msk[:], K.msk_d.rearrange("m a b -> a m b"), writes=["p5_msk"])
        P.dma("sp", lamt[:], K.df_lambda[l].partition_broadcast(64), writes=["p5_lamt"])
        P.dma("sp", gdf[:], K.df_norm_g[l].rearrange("(p o) -> p o", o=1), writes=["p5_gdf"], **NS)
        P.dma("sp", exps[:], K.gq_sink[l].partition_broadcast(128), writes=["p5_exps"])
        P.call("act", "activation", reads=["p5_exps"], writes=["p5_exps"], out=exps[:], in_=exps[:], func=AF.Exp)
        P.call("dve", "tensor_scalar", reads=["p5_gdf"], writes=["p5_gdf"], out=gdf[:], in0=gdf[:], scalar1=1.0 - lam_init,
               scalar2=None, op0=ALU.mult)
        for i in range(2):
            P.call("dve", "tensor_tensor", reads=["p5_lamt"], writes=["p5_lp"], out=lp[:, 32 * i:32 * i + 32],
                   in0=lamt[:, 64 * i:64 * i + 32], in1=lamt[:, 64 * i + 32:64 * i + 64], op=ALU.mult)
        P.call("dve", "tensor_reduce", reads=["p5_lp"], writes=["p5_e12"], out=e12[:],
               in_=lp[:].rearrange("p (a b) -> p a b", a=2), axis=AX.X, op=ALU.add)
        P.call("act", "activation", reads=["p5_e12"], writes=["p5_e12"], out=e12[:], in_=e12[:], func=AF.Exp)
        P.call("dve", "tensor_tensor", reads=["p5_e12"], writes=["p5_nl"], out=neglam[:], in0=e12[:, 1:2], in1=e12[:, 0:1], op=ALU.subtract)
        P.call("dve", "tensor_scalar", reads=["p5_nl"], writes=["p5_nl"], out=neglam[:], in0=neglam[:], scalar1=-lam_init,
               scalar2=None, op0=ALU.add)
        cnt = {"s": 0, "yb": 0}
        for ti, (t0, nq) in enumerate(tiles_512()):
            b = ti % 2
            kbs = list(range(0, CTX // 128)) if t0 < CTX else list(range(NBLK))
            for u in range(8):
                r0 = 32 * (u % 4)
                P.dma("sp", Qm[b][u][r0:r0 + 32, :nq], K.ropeT[l][(u // 4) * 128 + r0:(u // 4) * 128 + r0 + 32, t0:t0 + nq],
                      reads=["ropeT"], writes=[f"p5_Qm{b}"])
            seq = [(h, m, ki, kb) for h in range(4) for m in range(2) for ki, kb in enumerate(kbs)]

            def emit_S(i):
                h, m, ki, kb = seq[i]
                u = 2 * h + m
                i3 = i % 3
                P.call("pe", "matmul", reads=["p5_kdf", f"p5_Qm{b}"], writes=[f"p5_ps{i3}"], out=ps_s[i3][:, :nq],
                       lhsT=kdf[:, u // 4, kb * 128:(kb + 1) * 128], rhs=Qm[b][u][:, :nq], start=True, stop=True)
                P.call("act", "activation", reads=[f"p5_ps{i3}"], writes=[f"p5_E{i3}"], out=Eb[i3][:, :nq],
                       in_=ps_s[i3][:, :nq], func=AF.Exp, scale=32 ** -0.5)

            def emit_PV(i):
                h, m, ki, kb = seq[i]
                i3 = i % 3
                P.call("pe", "matmul", reads=["p5_V", f"p5_E{i3}"], writes=[f"p5_po{m}"], out=ps_o[m][0:65, :nq],
                       lhsT=Vaug[:, kb, h, :], rhs=Eb[i3][:, :nq], start=(ki == 0), stop=(ki == len(kbs) - 1))

            def post(h):
                for m in range(2):
                    P.call("act", "activation", reads=[f"p5_po{m}"], writes=[f"p5_oT{m}"], out=oT[m][:, :nq], in_=ps_o[m][0:65, :nq],
                           func=AF.Copy)
                    P.call("pe", "matmul", reads=["p5_sel", f"p5_oT{m}"], writes=[f"p5_pz{m}"], out=ps_z[m][0:64, :nq],
                           lhsT=sel[:], rhs=oT[m][:, :nq], start=True, stop=True)
                    P.call("dve", "reciprocal", reads=[f"p5_pz{m}"], writes=[f"p5_rz{m}"], out=rz[m][:, :nq], in_=ps_z[m][0:64, :nq])
                    P.call("dve", "tensor_tensor", reads=[f"p5_oT{m}", f"p5_rz{m}"], writes=[f"p5_rz{m}"], out=rz[m][:, :nq],
                           in0=oT[m][0:64, :nq], in1=rz[m][:, :nq], op=ALU.mult)
                P.call("dve", "scalar_tensor_tensor", reads=["p5_rz0", "p5_rz1", "p5_nl"], writes=["p5_dd"], out=dd[:, :nq],
                       in0=rz[1][:, :nq], scalar=neglam[:, 0:1], in1=rz[0][:, :nq], op0=ALU.mult, op1=ALU.add)
                P.call("act", "activation", reads=["p5_dd"], writes=["p5_sq"], out=sqt[:, :nq], in_=dd[:, :nq], func=AF.Square)
                P.call("pe", "matmul", reads=["p5_ones", "p5_sq"], writes=["p5_pz0"], out=ps_z[0][0:64, :nq], lhsT=ones64[:],
                       rhs=sqt[:, :nq], start=True, stop=True)
                P.call("dve", "tensor_scalar", reads=["p5_pz0"], writes=["p5_rs"], out=rs[:, :nq], in0=ps_z[0][0:64, :nq],
                       scalar1=1.0 / 64, scalar2=EPS, op0=ALU.mult, op1=ALU.add)
                P.call("act", "activation", reads=["p5_rs"], writes=["p5_rs"], out=rs[:, :nq], in_=rs[:, :nq], func=AF.Sqrt)
                P.call("dve", "reciprocal", reads=["p5_rs"], writes=["p5_rs"], out=rs[:, :nq], in_=rs[:, :nq])
                i2 = cnt["yb"] % 2
                cnt["yb"] += 1
                P.call("dve", "scalar_tensor_tensor", reads=["p5_dd", "p5_gdf", "p5_rs"], writes=[f"p5_yb{i2}"], out=ybt[i2][:, :nq],
                       in0=dd[:, :nq], scalar=gdf[:, 0:1], in1=rs[:, :nq], op0=ALU.mult, op1=ALU.mult)
                P.dma("sp", K.yT[l][256 + 64 * h:256 + 64 * h + 64, t0:t0 + nq], ybt[i2][:, :nq], reads=[f"p5_yb{i2}"], writes=["yT"])

            LOOK = 2
            for i in range(min(LOOK, len(seq))):
                emit_S(i)
            for i in range(len(seq)):
                if i + LOOK < len(seq):
                    emit_S(i + LOOK)
                emit_PV(i)
                h, m, ki, kb = seq[i]
                if m == 1 and ki == len(kbs) - 1:
                    post(h)
        P.mark(f"L{l}_diff_end")
        for tb in range(NBLK):
            b = tb % 2
            P.dma("sp", qg[b][:], K.ropeT[l][512:1024, tb * 128:(tb + 1) * 128].rearrange("(c p) t -> p c t", p=128),
                  reads=["ropeT"], writes=[f"p5_qg{b}"])
            keyblocks = [0, 1]
            if tb >= 2:
                keyblocks += [kb for kb in (tb - 1, tb, tb + 1) if 2 <= kb < NBLK]
            nk = len(keyblocks)
            psA = [(ps_s[0], "p5_ps0", ps_s[1], "p5_ps1"), (ps_s[2], "p5_ps2", ps_o[0], "p5_po0")]
            psG = [(ps_g, "p5_pg"), (ps_o[1], "p5_po1")]

            def g_scores(hd):
                g = hd // 4; c = hd // 2; base = 64 * (hd % 2)
                bs = slice(base, base + 64)
                pa, pak, pb, pbk = psA[hd % 2]
                for j, kb in enumerate(keyblocks):
                    pst, pstk = (pa, pak) if j < 4 else (pb, pbk)
                    P.call("pe", "matmul", reads=[f"p5_Kd{g}", f"p5_qg{b}"], writes=[pstk], inc=(j == nk - 1 or j == 3),
                           out=pst[:, (j % 4) * 128:(j % 4 + 1) * 128], lhsT=Kd[g][bs, kb * 128:(kb + 1) * 128], rhs=qg[b][bs, c, :],
                           start=True, stop=True)
                eg = Eg[hd % 2]; ek = f"p5_Eg{hd % 2}"
                n0 = min(nk, 4)
                P.call("act", "activation", reads=[pak], writes=[ek], out=eg[:, 0:n0, :].rearrange("p a b -> p (a b)"),
                       in_=pa[:, 0:n0 * 128], func=AF.Exp, scale=64 ** -0.5)
                if nk > 4:
                    P.call("act", "activation", reads=[pbk], writes=[ek], out=eg[:, 4, :], in_=pb[:, 0:128],
                           func=AF.Exp, scale=64 ** -0.5)
                for j, kb in enumerate(keyblocks):
                    if tb >= 2 and kb == tb - 1 and kb >= 2:
                        P.call("pool", "tensor_tensor", reads=[ek, "p5_msk"], writes=[ek], out=eg[:, j, :], in0=eg[:, j, :], in1=msk[:, 0, :], op=ALU.mult)
                    if tb >= 2 and kb == tb + 1:
                        P.call("pool", "tensor_tensor", reads=[ek, "p5_msk"], writes=[ek], out=eg[:, j, :], in0=eg[:, j, :], in1=msk[:, 1, :], op=ALU.mult)

            def g_pv(hd):
                g = hd // 4
                eg = Eg[hd % 2]; ek = f"p5_Eg{hd % 2}"
                pgt, pgk = psG[hd % 2]
                for j, kb in enumerate(keyblocks):
                    P.call("pe", "matmul", reads=[ek, "p5_V"], writes=[pgk], inc=(j == nk - 1), out=pgt[:, 0:65], lhsT=eg[:, j, :],
                           rhs=Vaug[:, kb, 4 + g, :], start=(j == 0), stop=(j == nk - 1))
                P.call("dve", "tensor_scalar", reads=[pgk, "p5_exps"], writes=["p5_zt"], out=zt[:], in0=pgt[:, 64:65],
                       scalar1=exps[:, hd:hd + 1], scalar2=None, op0=ALU.add)
                P.call("dve", "reciprocal", reads=["p5_zt"], writes=["p5_zt"], out=zt[:], in_=zt[:])
                P.call("dve", "tensor_scalar", reads=[pgk, "p5_zt"], writes=[f"p5_yc{b}"], out=ycst[b][:, hd * 64:(hd + 1) * 64],
                       in0=pgt[:, 0:64], scalar1=zt[:, 0:1], scalar2=None, op0=ALU.mult)

            g_scores(0)
            for hd in range(8):
                if hd + 1 < 8:
                    g_scores(hd + 1)
                g_pv(hd)
            P.dma("sp", K.ytm[l][tb * 128:(tb + 1) * 128, 512:1024], ycst[b][:], reads=[f"p5_yc{b}"], writes=["ytm"])


def row_gain(P, nc, K, l, gi, jga, G, tmp, name):
    for who in range(2):
        P.dma("sp", G[who][:], K.norm_g[l, gi].partition_broadcast(128), writes=[f"{name}_G{who}"])
        P.dma("sp", tmp[:], K.modrow[l, who, jga].partition_broadcast(128), reads=["modrow"], writes=[name + "_tmp"])
        P.call("dve", "tensor_tensor", reads=[f"{name}_G{who}", name + "_tmp"], writes=[f"{name}_G{who}"],
               out=G[who][:], in0=G[who][:], in1=tmp[:], op=ALU.mult)


def norm_rows(P, ps2, pkeys, ss, ss2, rs, junk, pfx):
    P.call("act", "activation", reads=[pkeys[0]], writes=[pfx + "_junk", pfx + "_ss"], out=junk[:, 0:512], in_=ps2[0][:, :],
           func=AF.Square, accum_out=ss[:])
    P.call("act", "activation", reads=[pkeys[1]], writes=[pfx + "_junk", pfx + "_ss2"], out=junk[:, 512:1024], in_=ps2[1][:, :],
           func=AF.Square, accum_out=ss2[:])
    P.call("dve", "tensor_tensor", reads=[pfx + "_ss", pfx + "_ss2"], writes=[pfx + "_rs"], out=rs[:], in0=ss[:], in1=ss2[:], op=ALU.add)
    P.call("dve", "tensor_scalar", reads=[pfx + "_rs"], writes=[pfx + "_rs"], out=rs[:], in0=rs[:], scalar1=1.0 / D, scalar2=EPS,
           op0=ALU.mult, op1=ALU.add)
    P.call("act", "activation", reads=[pfx + "_rs"], writes=[pfx + "_rs"], out=rs[:], in_=rs[:], func=AF.Sqrt)
    P.call("dve", "reciprocal", reads=[pfx + "_rs"], writes=[pfx + "_rs"], out=rs[:], in_=rs[:])


def phase_outproj(P, nc, K, l, xsrc):
    NBLK = T // 128
    with ExitStack() as es:
        sb = lambda name, shape, dt=F32: es.enter_context(nc.sbuf_tensor(name, shape, dt))
        pp = lambda name: es.enter_context(nc.psum_tensor(name, [128, 512], F32))
        W = sb("p6_w", [128, 8, D], BF16)
        stg = [sb(f"p6_stg{i}", [128, 512]) for i in range(6)]
        G = [sb(f"p6_G{who}", [128, D]) for who in range(2)]
        tmp = sb("p6_tmp", [128, D])
        A = sb("p6_A", [128, 8, 2]); Bv = sb("p6_B", [128, 8, 2]); gcol = sb("p6_g", [128, 8])
        yt = [sb(f"p6_yt{b}", [128, D]) for b in range(2)]
        yTb = [sb(f"p6_yT{b}", [128, 8, 128], BF16) for b in range(2)]
        xt = [sb(f"p6_x{b}", [128, D]) for b in range(2)]
        xm = [sb(f"p6_xm{b}", [128, D]) for b in range(2)]
        xn = sb("p6_xn", [128, D]); junk = sb("p6_junk", [128, D])
        ss = sb("p6_ss", [128, 1]); ss2 = sb("p6_ss2", [128, 1]); rs = sb("p6_rs", [128, 1])
        hT = [sb(f"p6_hT{b}", [128, 8, 128], BF16) for b in range(2)]
        pt = es.enter_context(nc.psum_tensor("p6_pt", [128, 8, 128], F32))
        po4 = [pp(f"p6_po{i}") for i in range(4)]
        load_weight_bf16(P, nc, K.w_out[l], W, "p6_w", 8, D, stg, [f"p6_stg{i}" for i in range(6)])
        row_gain(P, nc, K, l, 1, 2, G, tmp, "p6")
        mod_AB(P, nc, K, l, 2, 4, 3, A, Bv, gcol, "p6")
        for tb in range(NBLK):
            b = tb % 2
            who = 1 if tb * 128 < CTX else 0
            ts = slice(tb * 128, (tb + 1) * 128)
            po = po4[2 * b:2 * b + 2]; pok = [f"p6_po{2 * b}", f"p6_po{2 * b + 1}"]
            P.dma("sp", yt[b][:], K.ytm[l][ts, :], reads=["ytm"], writes=[f"p6_yt{b}"])
            P.dma("sp", xt[b][:], xsrc[ts, :], reads=["xres"], writes=[f"p6_x{b}"])
            P.dma("sp", yTb[b][:, 2:4, :], K.yT[l][256:512, ts].rearrange("(c p) t -> p c t", p=128), reads=["yT"], writes=[f"p6_yT{b}"])
            for c in (0, 1, 4, 5, 6, 7):
                P.call("pe", "transpose", reads=[f"p6_yt{b}", "ident"], writes=["p6_pt"], inc=(c == 7), out=pt[:, c, :],
                       in_=yt[b][:, c * 128:(c + 1) * 128], identity=K.ident[:])
            P.call("act", "activation", reads=["p6_pt"], writes=[f"p6_yT{b}"], out=yTb[b][:, 0:2, :], in_=pt[:, 0:2, :], func=AF.Copy)
            P.call("dve", "tensor_copy", reads=["p6_pt"], writes=[f"p6_yT{b}"], out=yTb[b][:, 4:8, :], in_=pt[:, 4:8, :])
            for half in range(2):
                for kc in range(8):
                    P.call("pe", "matmul", reads=[f"p6_yT{b}", f"p6_w_{half}"], writes=[pok[half]], inc=(kc == 7), out=po[half][:, :],
                           lhsT=yTb[b][:, kc, :], rhs=W[:, kc, half * 512:(half + 1) * 512], start=(kc == 0), stop=(kc == 7))
            norm_rows(P, po, pok, ss, ss2, rs, junk, "p6")
            for half in range(2):
                hs = slice(half * 512, (half + 1) * 512)
                P.call("dve", "scalar_tensor_tensor", reads=[pok[half], "p6_rs", f"p6_G{who}"], writes=[f"p6_xm{b}"],
                       out=xm[b][:, hs], in0=po[half][:, :], scalar=rs[:, 0:1], in1=G[who][:, hs], op0=ALU.mult, op1=ALU.mult)
                P.call("pool", "tensor_tensor", reads=[f"p6_xm{b}", f"p6_x{b}"], writes=[f"p6_xm{b}"], out=xm[b][:, hs],
                       in0=xm[b][:, hs], in1=xt[b][:, hs], op=ALU.add)
            P.dma("sp", K.xmid[l][ts, :], xm[b][:], reads=[f"p6_xm{b}"], writes=["xmid"])
            norm_block(P, xm[b], f"p6_xm{b}", ss, rs, junk, xn, "p6_xn", pfx="p6")
            for dc in range(8):
                P.call("pe", "transpose", reads=["p6_xn", "ident"], writes=["p6_pt"], inc=(dc == 7), out=pt[:, dc, :],
                       in_=xn[:, dc * 128:(dc + 1) * 128], identity=K.ident[:])
            for dc in range(8):
                P.call("act", "activation", reads=["p6_pt", "p6_A", "p6_B"], writes=[f"p6_hT{b}"], out=hT[b][:, dc, :],
                       in_=pt[:, dc, :], func=AF.Identity, scale=A[:, dc, who:who + 1], bias=Bv[:, dc, who:who + 1])
            P.dma("sp", K.h2T[l][:, ts].rearrange("(c p) t -> p c t", p=128), hT[b][:], reads=[f"p6_hT{b}"], writes=["h2T"])


def phase_ffn_up(P, nc, K, l):
    NF = DFF // 128
    with ExitStack() as es:
        sb = lambda name, shape, dt=F32: es.enter_context(nc.sbuf_tensor(name, shape, dt))
        pp = lambda name: es.enter_context(nc.psum_tensor(name, [128, 512], F32))
        Wg = sb("p7_wg", [128, 8, DFF], BF16); Wu = sb("p7_wu", [128, 8, DFF], BF16)
        stg = [sb(f"p7_stg{i}", [128, 512]) for i in range(6)]
        cw = sb("p7_cw", [128, NF, 3]); cb = sb("p7_cb", [128, NF])
        hT = [sb(f"p7_hT{b}", [128, 8, 514], BF16) for b in range(2)]
        gsb = sb("p7_g", [128, 514]); tt = sb("p7_t", [128, 512]); sg = sb("p7_s", [128, 512])
        zt = [sb(f"p7_z{i}", [128, 512], BF16) for i in range(2)]
        pg = [pp(f"p7_pg{i}") for i in range(2)]; ph = [pp(f"p7_ph{i}") for i in range(2)]; pu = [pp(f"p7_pu{i}") for i in range(2)]
        NS = dict(allow_slow_non_contiguous=True)
        load_weight_bf16(P, nc, K.ff_w_gate[l], Wg, "p7_wg", 8, DFF, stg, [f"p7_stg{i}" for i in range(6)])
        load_weight_bf16(P, nc, K.ff_w_up[l], Wu, "p7_wu", 8, DFF, stg, [f"p7_stg{i}" for i in range(6)])
        for j in range(3):
            P.dma("sp", cw[:, :, j], K.ff_conv_w[l, j].rearrange("(c p) -> p c", p=128), writes=["p7_cw"], **NS)
        P.dma("sp", cb[:], K.ff_conv_b[l].rearrange("(c p) -> p c", p=128), writes=["p7_cb"], **NS)
        NSEAM = SEQ // 512 - 1
        hH = sb("p7_hH", [128, 8, 2 * NSEAM], BF16); HG = sb("p7_HG", [128, NF, 2 * NSEAM])
        for k_ in range(NSEAM):
            tk = CTX + 512 * (k_ + 1)
            P.dma("sp", hH[:, :, 2 * k_:2 * k_ + 2], K.h2T[l][:, tk - 1:tk + 1].rearrange("(c p) t -> p c t", p=128),
                  reads=["h2T"], writes=["p7_hH"], **NS)
        for fc in range(NF):
            fs = slice(fc * 128, (fc + 1) * 128)
            i2 = fc % 2
            for kc in range(8):
                P.call("pe", "matmul", reads=[f"p7_wg_{fc // 4}", "p7_hH"], writes=[f"p7_ph{i2}"], out=ph[i2][:, 0:2 * NSEAM], lhsT=Wg[:, kc, fs],
                       rhs=hH[:, kc, :], start=(kc == 0), stop=(kc == 7))
            P.call("act", "activation", reads=[f"p7_ph{i2}"], writes=["p7_HG"], out=HG[:, fc, :], in_=ph[i2][:, 0:2 * NSEAM], func=AF.Copy)
        cnt = {"i": 0}
        for ti, (t0, n) in enumerate(tiles_512()):
            b = ti % 2
            xi = ti - 1
            hk = f"p7_hT{b}"
            s0, s1 = (0, CTX) if t0 < CTX else (CTX, T)
            src = lambda a, b_: K.h2T[l][:, a:b_].rearrange("(c p) t -> p c t", p=128)
            P.dma("sp", hT[b][:, :, 1:n + 1], src(t0, t0 + n), reads=["h2T"], writes=[hk])
            if t0 > s0:
                P.dma("sp", hT[b][:, :, 0:1], src(t0 - 1, t0), reads=["h2T"], writes=[hk], **NS)
            else:
                P.call("pool", "memset", writes=[hk], ap=hT[b][:, :, 0:1], constant=0.0)
            if t0 + n < s1:
                P.dma("sp", hT[b][:, :, n + 1:n + 2], src(t0 + n, t0 + n + 1), reads=["h2T"], writes=[hk], **NS)
            else:
                P.call("pool", "memset", writes=[hk], ap=hT[b][:, :, n + 1:n + 2], constant=0.0)
            for fc in range(NF):
                i2 = cnt["i"] % 2
                cnt["i"] += 1
                fs = slice(fc * 128, (fc + 1) * 128)
                for kc in range(8):
                    P.call("pe", "matmul", reads=[f"p7_wg_{fc // 4}", hk], writes=[f"p7_pg{i2}"], inc=(kc == 7), out=pg[i2][:, :n], lhsT=Wg[:, kc, fs],
                           rhs=hT[b][:, kc, 1:n + 1], start=(kc == 0), stop=(kc == 7))
                for kc in range(8):
                    P.call("pe", "matmul", reads=[f"p7_wu_{fc // 4}", hk], writes=[f"p7_pu{i2}"], inc=(kc == 7), out=pu[i2][:, :n], lhsT=Wu[:, kc, fs],
                           rhs=hT[b][:, kc, 1:n + 1], start=(kc == 0), stop=(kc == 7))
                P.call("act", "activation", reads=[f"p7_pg{i2}"], writes=["p7_g"], out=gsb[:, 1:n + 1], in_=pg[i2][:, :n], func=AF.Copy)
                if t0 > s0:
                    P.call("dve", "tensor_copy", reads=["p7_HG"], writes=["p7_g"], out=gsb[:, 0:1], in_=HG[:, fc, 2 * (xi - 1):2 * (xi - 1) + 1])
                else:
                    P.call("pool", "memset", writes=["p7_g"], ap=gsb[:, 0:1], constant=0.0)
                if t0 + n < s1:
                    P.call("dve", "tensor_copy", reads=["p7_HG"], writes=["p7_g"], out=gsb[:, n + 1:n + 2], in_=HG[:, fc, 2 * xi + 1:2 * xi + 2])
                else:
                    P.call("pool", "memset", writes=["p7_g"], ap=gsb[:, n + 1:n + 2], constant=0.0)
                P.call("act", "activation", reads=["p7_g", "p7_cw", "p7_cb"], writes=["p7_t"], out=tt[:, :n], in_=gsb[:, 1:n + 1],
                       func=AF.Identity, scale=cw[:, fc, 1:2], bias=cb[:, fc:fc + 1])
                P.call("dve", "scalar_tensor_tensor", reads=["p7_g", "p7_cw", "p7_t"], writes=["p7_t"], out=tt[:, :n],
                       in0=gsb[:, 0:n], scalar=cw[:, fc, 0:1], in1=tt[:, :n], op0=ALU.mult, op1=ALU.add)
                P.call("dve", "scalar_tensor_tensor", reads=["p7_g", "p7_cw", "p7_t"], writes=["p7_t"], out=tt[:, :n],
                       in0=gsb[:, 2:n + 2], scalar=cw[:, fc, 2:3], in1=tt[:, :n], op0=ALU.mult, op1=ALU.add)
                P.call("act", "activation", reads=["p7_t"], writes=["p7_s"], out=sg[:, :n], in_=tt[:, :n], func=AF.Silu)
                P.call("dve", "tensor_tensor", reads=["p7_s", f"p7_pu{i2}"], writes=[f"p7_z{i2}"], out=zt[i2][:, :n], in0=sg[:, :n],
                       in1=pu[i2][:, :n], op=ALU.mult)
                P.dma("sp", K.zT[l][fs, t0:t0 + n], zt[i2][:, :n], reads=[f"p7_z{i2}"], writes=["zT"])


def phase_ffn_down(P, nc, K, l, xdst, last):
    NF = DFF // 128
    NBLK = T // 128
    with ExitStack() as es:
        sb = lambda name, shape, dt=F32: es.enter_context(nc.sbuf_tensor(name, shape, dt))
        pp = lambda name: es.enter_context(nc.psum_tensor(name, [128, 512], F32))
        Wd = sb("p8_wd", [128, NF, D], BF16)
        stg = [sb(f"p8_stg{i}", [128, 512]) for i in range(6)]
        G = [sb(f"p8_G{who}", [128, D]) for who in range(2)]
        tmp = sb("p8_tmp", [128, D]); junk = sb("p8_junk", [128, D])
        zb = [sb(f"p8_z{b}", [128, NF, 128], BF16) for b in range(2)]
        xm = [sb(f"p8_xm{b}", [128, D]) for b in range(2)]
        xo = [sb(f"p8_xo{b}", [128, D]) for b in range(2)]
        ss = sb("p8_ss", [128, 1]); ss2 = sb("p8_ss2", [128, 1]); rs = sb("p8_rs", [128, 1])
        po4 = [pp(f"p8_po{i}") for i in range(4)]
        load_weight_bf16(P, nc, K.ff_w_down[l], Wd, "p8_wd", NF, D, stg, [f"p8_stg{i}" for i in range(6)])
        row_gain(P, nc, K, l, 3, 5, G, tmp, "p8")
        for tb in range(NBLK):
            if last and tb * 128 < CTX:
                continue
            b = tb % 2
            who = 1 if tb * 128 < CTX else 0
            ts = slice(tb * 128, (tb + 1) * 128)
            po = po4[2 * b:2 * b + 2]; pok = [f"p8_po{2 * b}", f"p8_po{2 * b + 1}"]
            P.dma("sp", zb[b][:], K.zT[l][:, ts].rearrange("(c p) t -> p c t", p=128), reads=["zT"], writes=[f"p8_z{b}"])
            P.dma("sp", xm[b][:], K.xmid[l][ts, :], reads=["xmid"], writes=[f"p8_xm{b}"])
            for half in range(2):
                for fc in range(NF):
                    P.call("pe", "matmul", reads=[f"p8_z{b}", f"p8_wd_{half}"], writes=[pok[half]], inc=(fc == NF - 1), out=po[half][:, :],
                           lhsT=zb[b][:, fc, :], rhs=Wd[:, fc, half * 512:(half + 1) * 512], start=(fc == 0), stop=(fc == NF - 1))
            norm_rows(P, po, pok, ss, ss2, rs, junk, "p8")
            for half in range(2):
                hs = slice(half * 512, (half + 1) * 512)
                P.call("dve", "scalar_tensor_tensor", reads=[pok[half], "p8_rs", f"p8_G{who}"], writes=[f"p8_xo{b}"],
                       out=xo[b][:, hs], in0=po[half][:, :], scalar=rs[:, 0:1], in1=G[who][:, hs], op0=ALU.mult, op1=ALU.mult)
                P.call("pool", "tensor_tensor", reads=[f"p8_xo{b}", f"p8_xm{b}"], writes=[f"p8_xo{b}"], out=xo[b][:, hs],
                       in0=xo[b][:, hs], in1=xm[b][:, hs], op=ALU.add)
            if last:
                P.dma("sp", K.out[tb * 128 - CTX:(tb + 1) * 128 - CTX, :], xo[b][:], reads=[f"p8_xo{b}"], writes=["out"])
            else:
                P.dma("sp", xdst[ts, :], xo[b][:], reads=[f"p8_xo{b}"], writes=["xres"])


def build(dbg=(), upto=99, nlayers=L, skip=()):
    nc = bass.Bass("TRN2", target_bir_lowering=False)
    K = Ctx()
    K.scan_dbg = 'scandbg' in dbg
    K.chunked = 'seqscan' not in dbg
    K.f32r = False
    dt = lambda name, shape, dtype=F32, kind="ExternalInput": nc.dram_tensor(name, shape, dtype, kind=kind).ap()
    scr = lambda name, shape, dtype=F32: dt(name, shape, dtype, "ExternalOutput" if name in dbg else "Internal")
    K.xin = dt("xin", [T, D])
    K.c_in = dt("c_in", [D])
    K.cctx_in = dt("cctx_in", [D])
    K.ada_w = dt("ada_w", [L, D, 6 * D])
    K.ada_b = dt("ada_b", [L, 6 * D])
    K.norm_g = dt("norm_g", [L, 4, D])
    K.w_in = dt("w_in", [L, D, WCOLS])
    K.rope = dt("rope", [4, 128, T])
    K.ident_d = dt("ident", [128, 128])
    K.out = dt("out", [SEQ, D], kind="ExternalOutput")
    K.fm32 = [scr(f"fm32_{l}", [1024, T]) for l in range(L)]
    K.ropeT = [scr(f"ropeT_{l}", [1152, T], BF16) for l in range(L)]
    K.vtm = [scr(f"vtm_{l}", [T, 384], BF16) for l in range(L)]
    for nm, shp in (("rw_conv", [L, 3, 768]), ("rw_w0", [L, 2, 256]), ("rw_w_up", [L, 2, 32, 256]), ("rw_a0", [L, 2, 256]),
                    ("rw_a_up", [L, 2, 32, 256]), ("rw_g_up", [L, 64, 256]), ("rw_k_k", [L, 256]), ("rw_k_a", [L, 256]),
                    ("rw_r_k", [L, 256]), ("rw_ln_g", [L, 256]), ("rw_ln_b", [L, 256])):
        setattr(K, nm, dt(nm, shp))
    K.bones_d = dt("bones", [128, 128])
    K.col_w = [[scr(f"col_w_{l}_{i}", [128, T]) for i in range(4)] for l in range(L)]
    K.col_kr = [[scr(f"col_kr_{l}_{i}", [128, T]) for i in range(4)] for l in range(L)]
    K.rw_tm = [scr(f"rw_tm_{l}", [T, 7, 256]) for l in range(L)]
    K.col_nk = [[scr(f"col_nk_{l}_{i}", [128, T]) for i in range(4)] for l in range(L)]
    K.col_kd = [[scr(f"col_kd_{l}_{i}", [128, T]) for i in range(4)] for l in range(L)]
    K.cmask_d = dt("cmask", [4, 128, 128])
    K.o_tm = [scr(f"o_tm_{l}", [T, 2, 256]) for l in range(L)]
    K.ytm = [scr(f"ytm_{l}", [T, 1024]) for l in range(L)]
    K.yT = [scr(f"yT_{l}", [1024, T], BF16) for l in range(L)]
    K.modrow = scr("modrow", [L, 2, 6, D])
    K.xmid = [scr(f"xmid_{l}", [T, D]) for l in range(L)]
    K.h2T = [scr(f"h2T_{l}", [D, T], BF16) for l in range(L)]
    K.zT = [scr(f"zT_{l}", [DFF, T], BF16) for l in range(L)]
    K.xres = [scr(f"xres_{l}", [T, D]) for l in range(L)]
    K.w_out = dt("w_out", [L, D, D])
    K.ff_w_gate = dt("ff_w_gate", [L, D, DFF]); K.ff_w_up = dt("ff_w_up", [L, D, DFF]); K.ff_w_down = dt("ff_w_down", [L, DFF, D])
    K.ff_conv_w = dt("ff_conv_w", [L, 3, DFF]); K.ff_conv_b = dt("ff_conv_b", [L, DFF])
    K.df_lambda = dt("df_lambda", [L, 128])
    K.df_norm_g = dt("df_norm_g", [L, 64])
    K.gq_sink = dt("gq_sink", [L, 8])
    K.sel_d = dt("sel65", [65, 64])
    K.msk_d = dt("msk", [2, 128, 128])

    P = Prog(nc)
    with (
        nc.sbuf_tensor("modcol0", [128, 48, 2], F32) as mc0,
        nc.sbuf_tensor("modcol1", [128, 48, 2], F32) as mc1,
        nc.sbuf_tensor("ident_sb", [128, 128], F32) as ident,
    ):
        K.modcol = [mc0, mc1]
        K.ident = ident
        P.dma("sp", ident[:], K.ident_d, writes=["ident"])
        phase_mod(P, nc, K)
        P.barrier()
        for l in range(nlayers):
            xsrc = K.xin if l == 0 else K.xres[l - 1]
            last = (l == L - 1)
            nc0 = nc
            nc = Uniq(nc0, f"_L{l}")
            phases = [lambda: phase_inproj(P, nc, K, l, xsrc), lambda: phase_rwprep(P, nc, K, l), lambda: (phase_scan_chunked if K.chunked else phase_scan)(P, nc, K, l),
                      lambda: phase_readout(P, nc, K, l), lambda: phase_attn(P, nc, K, l), lambda: phase_outproj(P, nc, K, l, xsrc),
                      lambda: phase_ffn_up(P, nc, K, l), lambda: phase_ffn_down(P, nc, K, l, K.xres[l], last)]
            for pi, ph in enumerate(phases):
                if upto >= pi + 1 and pi + 1 not in skip:
                    ph()
                    P.mark(f"L{l}_ph{pi + 1}")
                    P.barrier()
            nc = nc0
        P.finish(["out"])
    return nc, P


def make_in_maps(inp):
    f = lambda a: np.ascontiguousarray(np.asarray(a, dtype=np.float32))
    cols = w_in_cols()
    shared = {
        "cctx_in": f(inp["c_ctx"]),
        "ada_w": f(inp["ada_w"]),
        "ada_b": f(inp["ada_b"]),
        "norm_g": f(inp["norm_g"]),
        "w_in": f(np.asarray(inp["w_in"])[:, :, cols]),
        "rope": rope_tables(),
        "ident": np.eye(128, dtype=np.float32),
        "bones": np.kron(np.eye(2, dtype=np.float32), np.ones((64, 64), np.float32)),
    }
    for nm in ("rw_conv", "rw_w0", "rw_w_up", "rw_a0", "rw_a_up", "rw_g_up", "rw_k_k", "rw_k_a", "rw_ln_g", "rw_ln_b"):
        shared[nm] = f(inp[nm])
    shared["rw_r_k"] = f(np.asarray(inp["rw_r_k"]).reshape(L, 256))
    shared["df_lambda"] = f(np.asarray(inp["df_lambda"]).reshape(L, 128))
    tau = np.arange(128) % 64
    shared["cmask"] = np.stack([tau[:, None] < tau[None, :], tau[:, None] <= tau[None, :],
                                tau[:, None] > tau[None, :], tau[:, None] >= tau[None, :]]).astype(np.float32)
    shared["df_norm_g"] = f(inp["df_norm_g"])
    for nm in ("w_out", "ff_w_gate", "ff_w_up", "ff_w_down", "ff_conv_w", "ff_conv_b"):
        shared[nm] = f(inp[nm])
    shared["gq_sink"] = f(inp["gq_sink"])
    sel = np.zeros((65, 64), np.float32); sel[64, :] = 1.0
    shared["sel65"] = sel
    a = np.arange(128)
    shared["msk"] = np.stack([(a[:, None] >= a[None, :]), (a[:, None] <= a[None, :])]).astype(np.float32)
    maps = []
    for core in range(8):
        b = core % NB
        m = dict(shared)
        m["xin"] = f(np.concatenate([inp["ctx"][b], inp["x"][b]], axis=0))
        m["c_in"] = f(inp["c"][b])
        maps.append(m)
    return maps


def kernel(**inputs):
    inp = {k: np.asarray(v) for k, v in inputs.items()}
    nc, _ = build()
    maps = make_in_maps(inp)
    res = run_bass_kernel_spmd(nc, maps, core_ids=list(range(8)))
    out = np.stack([np.asarray(res.results[b]["out"], dtype=np.float32) for b in range(NB)], axis=0)
    return out
```

```python
import math
from contextlib import ExitStack
import numpy as np
import concourse.bass as bass
import concourse.mybir as mybir
from concourse.bass_utils import run_bass_kernel_spmd

F32 = mybir.dt.float32
BF16 = mybir.dt.bfloat16
AF = mybir.ActivationFunctionType
ALU = mybir.AluOpType
AX = mybir.AxisListType

D = 1024
NB = 4
SEQ = 4096
CTX = 256
T = CTX + SEQ
L = 2
DFF = 2816
GRID_W = 64
EPS = 1e-6

ENG_NAMES = ("pe", "act", "dve", "pool", "sp")


class Prog:
    N_DMA_SEMS = 12

    def __init__(self, nc):
        self.nc = nc
        self.streams = {e: [] for e in ENG_NAMES}
        self.cnt = {e: 0 for e in ENG_NAMES}
        self.sems = {e: nc.alloc_semaphore(f"c_{e}") for e in ENG_NAMES}
        self.dsems = [nc.alloc_semaphore(f"d_{i}") for i in range(self.N_DMA_SEMS)]
        self.dcnt = [0] * self.N_DMA_SEMS
        self.dnext = 0
        self.waited = {e: {} for e in ENG_NAMES}
        self.bufs = {}
        self.n_instr = 0
        self.split_stores = True

    def _sem(self, key):
        return self.sems[key] if isinstance(key, str) else self.dsems[key]

    def _need(self, eng, tok):
        if tok is None:
            return
        key, val = tok
        if key == eng and val > self.cnt[eng]:
            return
        if self.waited[eng].get(key, 0) >= val:
            return
        self.waited[eng][key] = val
        sem = self._sem(key)
        self.streams[eng].append(lambda e, sem=sem, val=val: e.wait_ge(sem, val))
        self.n_instr += 1

    def _deps(self, eng, reads, writes):
        for r in reads:
            st = self.bufs.get(r)
            if st is not None:
                self._need(eng, st["w"])
        for w in writes:
            st = self.bufs.get(w)
            if st is not None:
                self._need(eng, st["w"])
                for t in st["r"]:
                    self._need(eng, t)

    def _commit(self, tok, reads, writes):
        for r in reads:
            st = self.bufs.setdefault(r, {"w": None, "r": []})
            st["r"].append(tok)
            if len(st["r"]) > 24:
                best = {}
                for k, v in st["r"]:
                    best[k] = max(best.get(k, 0), v)
                st["r"] = list(best.items())
        for w in writes:
            self.bufs[w] = {"w": tok, "r": []}

    def op(self, eng, fn, reads=(), writes=(), inc=True):
        self._deps(eng, reads, writes)
        sem = self.sems[eng]
        if inc:
            self.cnt[eng] += 1
            self.streams[eng].append(lambda e, fn=fn, sem=sem: fn(e).then_inc(sem, 1))
            tok = (eng, self.cnt[eng])
        else:
            self.streams[eng].append(lambda e, fn=fn: fn(e))
            tok = (eng, self.cnt[eng] + 1)
        self.n_instr += 1
        self._commit(tok, reads, writes)

    def call(self, eng, method, reads=(), writes=(), inc=True, **kw):
        self.op(eng, lambda e: getattr(e, method)(**kw), reads, writes, inc=inc)

    def dma(self, q, out, in_, reads=(), writes=(), **kw):
        if q == "sp" and self.split_stores and str(out.space) == "DRAM" and str(in_.space) != "DRAM":
            q = "pool"
        i = self.dnext
        self.dnext = (self.dnext + 1) % self.N_DMA_SEMS
        if self.dcnt[i] > 0:
            self._need(q, (i, 16 * self.dcnt[i]))
        self._deps(q, reads, writes)
        self.dcnt[i] += 1
        sem = self.dsems[i]
        self.streams[q].append(
            lambda e, out=out, in_=in_, sem=sem, kw=kw: e.dma_start(out=out, in_=in_, **kw).then_inc(sem, 16))
        self.n_instr += 1
        self._commit((i, 16 * self.dcnt[i]), reads, writes)

    def mark(self, name):
        if not hasattr(self, "marks"):
            self.marks = []
        self.marks.append((name, dict(self.cnt)))

    def barrier(self):
        toks = [(e, self.cnt[e]) for e in ENG_NAMES if self.cnt[e] > 0]
        toks += [(i, 16 * self.dcnt[i]) for i in range(self.N_DMA_SEMS) if self.dcnt[i] > 0]
        for e in ENG_NAMES:
            for tok in toks:
                if tok[0] != e:
                    self._need(e, tok)

    def finish(self, final_keys):
        for k in final_keys:
            st = self.bufs.get(k)
            if st is not None:
                self._need("sp", st["w"])
        for i in range(self.N_DMA_SEMS):
            if self.dcnt[i] > 0:
                self._need("sp", (i, 16 * self.dcnt[i]))
        nc = self.nc
        with nc.Block() as block:
            @block.tensor
            def _(e):
                for f in self.streams["pe"]:
                    f(e)

            @block.scalar
            def _(e):
                for f in self.streams["act"]:
                    f(e)

            @block.vector
            def _(e):
                for f in self.streams["dve"]:
                    f(e)

            @block.gpsimd
            def _(e):
                for f in self.streams["pool"]:
                    f(e)

            @block.sync
            def _(e):
                for f in self.streams["sp"]:
                    f(e)


class Ctx:
    pass


class Uniq:
    def __init__(self, nc, suffix):
        self._nc = nc
        self._sfx = suffix

    def sbuf_tensor(self, name, shape, dtype):
        return self._nc.sbuf_tensor(name + self._sfx, shape, dtype)

    def psum_tensor(self, name, shape, dtype):
        return self._nc.psum_tensor(name + self._sfx, shape, dtype)

    def __getattr__(self, k):
        return getattr(self._nc, k)


def phase_mod(P, nc, K):
    GW = 768
    NG = 6144 // GW
    with (
        nc.sbuf_tensor("m_c", [128, 8, 2], F32) as craw,
        nc.sbuf_tensor("m_cs", [128, 8, 2], F32) as cs,
        nc.sbuf_tensor("m_w0", [128, 8, GW], F32) as w0,
        nc.sbuf_tensor("m_w1", [128, 8, GW], F32) as w1,
        nc.sbuf_tensor("m_b", [128, 48], F32) as bcol,
        nc.psum_tensor("m_ps", [128, 256, 2], F32) as ps,
        nc.psum_tensor("m_pst", [128, 512], F32) as pst,
        nc.sbuf_tensor("m_mcw", [128, 48], F32) as mcw,
        nc.sbuf_tensor("m_mrow", [48, 128], F32) as mrow,
    ):
        wb = [w0, w1]
        P.dma("sp", craw[:, :, 0], K.c_in.rearrange("(c p) -> p c", p=128), writes=["m_c"],
              allow_slow_non_contiguous=True)
        P.dma("sp", craw[:, :, 1], K.cctx_in.rearrange("(c p) -> p c", p=128), writes=["m_c"],
              allow_slow_non_contiguous=True)
        P.op("act", lambda e: e.activation(out=cs[:], in_=craw[:], func=AF.Silu), reads=["m_c"], writes=["m_cs"])
        for l in range(L):
            P.dma("sp", bcol[:], K.ada_b[l].rearrange("(j p) -> p j", p=128), writes=["m_b"],
                  allow_slow_non_contiguous=True)
            for gi in range(NG):
                wt = wb[gi % 2]
                wk = f"m_w{gi % 2}"
                src = K.ada_w[l, :, gi * GW:(gi + 1) * GW].rearrange("(kc p) n -> p kc n", p=128)
                P.dma("sp", wt[:], src, writes=[wk])
                for jj in range(GW // 128):
                    j = gi * (GW // 128) + jj
                    for kc in range(8):
                        P.op("pe", lambda e, wt=wt, jj=jj, kc=kc, j=j: e.matmul(
                            ps[:, j, :], lhsT=wt[:, kc, jj * 128:(jj + 1) * 128], rhs=cs[:, kc, :],
                            start=(kc == 0), stop=(kc == 7)),
                            reads=[wk, "m_cs"], writes=["m_ps"])
            mc = K.modcol[l]
            for who in range(2):
                P.op("dve", lambda e, mc=mc, who=who: e.tensor_tensor(
                    out=mc[:, :, who], in0=ps[:, 0:48, who], in1=bcol[:], op=ALU.add),
                    reads=["m_ps", "m_b"], writes=[f"modcol{l}"])
            for who in range(2):
                P.call("dve", "tensor_copy", reads=[f"modcol{l}"], writes=["m_mcw"], out=mcw[:], in_=mc[:, :, who])
                P.call("pe", "transpose", reads=["m_mcw", "ident"], writes=["m_pst"], out=pst[0:48, 0:128], in_=mcw[:], identity=K.ident[:])
                P.call("act", "activation", reads=["m_pst"], writes=["m_mrow"], out=mrow[:], in_=pst[0:48, 0:128], func=AF.Copy)
                P.dma("sp", K.modrow[l, who].rearrange("j (c p) -> (j c) p", p=128), mrow[:], reads=["m_mrow"], writes=["modrow"])


def _swap_idx(du):
    nf = du // 4
    idx = np.arange(du)
    axis = idx // (2 * nf); half = (idx // nf) % 2; f = idx % nf
    return axis * 2 * nf + (1 - half) * nf + f


def w_in_cols():
    sw32 = _swap_idx(32); sw64 = _swap_idx(64)
    cols = list(range(0, 768))
    cols += list(range(832, 896)) + list(range(768, 832))
    cols += list(range(896, 960)) * 2
    dfq = np.arange(960, 1216); dfk = np.arange(1216, 1472)
    sw = lambda base, du: np.concatenate([base[u * du:(u + 1) * du][_swap_idx(du)] for u in range(len(base) // du)])
    cols += list(dfq) + list(sw(dfq, 32)) + list(dfk) + list(sw(dfk, 32))
    gqq = np.arange(1728, 2240); gqk = np.arange(2240, 2368)
    cols += list(gqq) + list(sw(gqq, 64)) + list(gqk) + list(sw(gqk, 64))
    cols += list(range(1472, 1728)) + list(range(2368, 2496))
    return np.asarray(cols, dtype=np.int64)


NFM = 26
WCOLS = NFM * 128 + 384


def rope_tables():
    out = np.zeros((4, 128, T), np.float32)
    tt = np.arange(SEQ)
    pos = np.stack([(tt // GRID_W).astype(np.float32), (tt % GRID_W).astype(np.float32)], 0)
    for ti, du in ((0, 32), (2, 64)):
        nf = du // 4
        inv = (np.float32(10000.0) ** (-np.arange(nf, dtype=np.float32) / np.float32(nf))).astype(np.float32)
        i = np.arange(du)
        axis = i // (2 * nf); half = (i // nf) % 2; f = i % nf
        ang = (pos[axis] * inv[f][:, None]).astype(np.float32)
        c = np.cos(ang).astype(np.float32); s_ = np.sin(ang).astype(np.float32)
        s_ = np.where(half[:, None] == 0, -s_, s_)
        rep = 128 // du
        out[ti, :, :CTX] = 1.0
        out[ti, :, CTX:] = np.tile(c, (rep, 1))
        out[ti + 1, :, CTX:] = np.tile(s_, (rep, 1))
    return out


def tiles_512():
    return [(0, CTX)] + [(CTX + i * 512, 512) for i in range(SEQ // 512)]


def norm_block(P, xt, xk, ss, rs, junk, xn, xnk, pfx="nb"):
    P.call("act", "activation", reads=[xk], writes=[pfx + "_junk", pfx + "_ss"],
           out=junk[:], in_=xt[:], func=AF.Square, accum_out=ss[:])
    P.call("dve", "tensor_scalar", reads=[pfx + "_ss"], writes=[pfx + "_rs"],
           out=rs[:], in0=ss[:], scalar1=1.0 / D, scalar2=EPS, op0=ALU.mult, op1=ALU.add)
    P.call("act", "activation", reads=[pfx + "_rs"], writes=[pfx + "_rs"], out=rs[:], in_=rs[:], func=AF.Sqrt)
    P.call("dve", "reciprocal", reads=[pfx + "_rs"], writes=[pfx + "_rs"], out=rs[:], in_=rs[:])
    P.call("dve", "tensor_scalar", reads=[xk, pfx + "_rs"], writes=[xnk],
           out=xn[:], in0=xt[:], scalar1=rs[:], scalar2=None, op0=ALU.mult)


def mod_AB(P, nc, K, l, gi, jsc, jsh, A, Bv, gcol, name):
    P.dma("sp", gcol[:], K.norm_g[l, gi].rearrange("(c p) -> p c", p=128), writes=[name + "_g"],
          allow_slow_non_contiguous=True)
    mc = K.modcol[l]
    for who in range(2):
        P.call("dve", "scalar_tensor_tensor", reads=[f"modcol{l}", name + "_g"], writes=[name + "_A"],
               out=A[:, :, who], in0=mc[:, jsc * 8:(jsc + 1) * 8, who], scalar=1.0, in1=gcol[:],
               op0=ALU.add, op1=ALU.mult)
        P.call("dve", "tensor_copy", reads=[f"modcol{l}"], writes=[name + "_B"],
               out=Bv[:, :, who], in_=mc[:, jsh * 8:(jsh + 1) * 8, who])


def load_weight_bf16(P, nc, src, W, wkey, nk, ncols, stages, skeys):
    i = 0
    for c0 in range(0, ncols, 512):
        c1 = min(ncols, c0 + 512)
        key = f"{wkey}_{c0 // 512}"
        for kc in range(nk):
            st = stages[i % len(stages)]; sk = skeys[i % len(stages)]
            P.dma("sp", st[:, :c1 - c0], src[kc * 128:(kc + 1) * 128, c0:c1], writes=[sk])
            eng = ("pool", "dve", "act")[i % 3]
            i += 1
            if eng == "act":
                P.call("act", "activation", reads=[sk], writes=[key], out=W[:, kc, c0:c1], in_=st[:, :c1 - c0], func=AF.Copy)
            else:
                P.call(eng, "tensor_copy", reads=[sk], writes=[key], out=W[:, kc, c0:c1], in_=st[:, :c1 - c0])


def wkeys(wkey, c0, c1):
    return [f"{wkey}_{c}" for c in range(c0 // 512, (c1 - 1) // 512 + 1)]


def phase_inproj(P, nc, K, l, xsrc):
    with ExitStack() as es:
        W = es.enter_context(nc.sbuf_tensor("p1_w", [128, 8, WCOLS], BF16))
        x0 = es.enter_context(nc.sbuf_tensor("p1_x0", [128, D], F32))
        x1 = es.enter_context(nc.sbuf_tensor("p1_x1", [128, D], F32))
        junk = es.enter_context(nc.sbuf_tensor("p1_junk", [128, D], F32))
        xn0 = es.enter_context(nc.sbuf_tensor("p1_xn0", [128, D], F32))
        xn1 = es.enter_context(nc.sbuf_tensor("p1_xn1", [128, D], F32))
        ss = es.enter_context(nc.sbuf_tensor("p1_ss", [128, 1], F32))
        rs = es.enter_context(nc.sbuf_tensor("p1_rs", [128, 1], F32))
        hT0 = es.enter_context(nc.sbuf_tensor("p1_hT0", [128, 8, 512], BF16))
        hT1 = es.enter_context(nc.sbuf_tensor("p1_hT1", [128, 8, 512], BF16))
        A = es.enter_context(nc.sbuf_tensor("p1_A", [128, 8, 2], F32))
        Bv = es.enter_context(nc.sbuf_tensor("p1_B", [128, 8, 2], F32))
        gcol = es.enter_context(nc.sbuf_tensor("p1_g", [128, 8], F32))
        tab = es.enter_context(nc.sbuf_tensor("p1_tab", [128, 4, 512], F32))
        st0 = es.enter_context(nc.sbuf_tensor("p1_st0", [128, 512], F32))
        st1 = es.enter_context(nc.sbuf_tensor("p1_st1", [128, 512], F32))
        tm1 = es.enter_context(nc.sbuf_tensor("p1_t1", [128, 512], F32))
        tm2 = es.enter_context(nc.sbuf_tensor("p1_t2", [128, 512], F32))
        ro0 = es.enter_context(nc.sbuf_tensor("p1_ro0", [128, 512], BF16))
        ro1 = es.enter_context(nc.sbuf_tensor("p1_ro1", [128, 512], BF16))
        vs0 = es.enter_context(nc.sbuf_tensor("p1_vs0", [128, 384], BF16))
        vs1 = es.enter_context(nc.sbuf_tensor("p1_vs1", [128, 384], BF16))
        pt = es.enter_context(nc.psum_tensor("p1_pt", [128, 8, 128], F32))
        pf0 = es.enter_context(nc.psum_tensor("p1_pf0", [128, 512], F32))
        pf1 = es.enter_context(nc.psum_tensor("p1_pf1", [128, 512], F32))
        pf2 = es.enter_context(nc.psum_tensor("p1_pf2", [128, 512], F32))
        pf3 = es.enter_context(nc.psum_tensor("p1_pf3", [128, 512], F32))
        pv = es.enter_context(nc.psum_tensor("p1_pv", [128, 512], F32))
        xb = [x0, x1]; xnb = [xn0, xn1]; hTb = [hT0, hT1]; stb = [st0, st1]; rob = [ro0, ro1]; vsb = [vs0, vs1]
        pfb = [pf0, pf1, pf2, pf3]
        load_weight_bf16(P, nc, K.w_in[l], W, "p1_w", 8, WCOLS, [st0, st1, tm1, tm2],
                         ["p1_st0", "p1_st1", "p1_t1", "p1_t2"])
        mod_AB(P, nc, K, l, 0, 1, 0, A, Bv, gcol, "p1")
        cnt = {"blk": 0, "pf": 0, "st": 0, "ro": 0}
        xnt = [[es.enter_context(nc.sbuf_tensor(f"p1_xnt{i}_{j}", [128, D], F32)) for j in range(4)] for i in range(2)]
        tl = tiles_512()

        def prepA(ti):
            t0, n = tl[ti]
            for blk in range(n // 128):
                tb = t0 + blk * 128
                i2 = cnt["blk"] % 2
                cnt["blk"] += 1
                xt = xb[i2]; xk = f"p1_x{i2}"
                P.dma("sp", xt[:], xsrc[tb:tb + 128, :], reads=["xres"], writes=[xk])
                norm_block(P, xt, xk, ss, rs, junk, xnt[ti % 2][blk], f"p1_xnt{ti % 2}_{blk}")

        def prepB(ti):
            t0, n = tl[ti]
            who = 1 if t0 < CTX else 0
            hT = hTb[ti % 2]; hk = f"p1_hT{ti % 2}"
            for blk in range(n // 128):
                xn = xnt[ti % 2][blk]; xnk = f"p1_xnt{ti % 2}_{blk}"
                for dc in range(8):
                    P.call("pe", "transpose", reads=[xnk, "ident"], writes=["p1_pt"], inc=(dc == 7),
                           out=pt[:, dc, :], in_=xn[:, dc * 128:(dc + 1) * 128], identity=K.ident[:])
                for dc in range(8):
                    P.call("act", "activation", reads=["p1_pt", "p1_A", "p1_B"], writes=[hk],
                           out=hT[:, dc, blk * 128:(blk + 1) * 128], in_=pt[:, dc, :], func=AF.Identity,
                           scale=A[:, dc, who:who + 1], bias=Bv[:, dc, who:who + 1])

        def proj(ti):
            t0, n = tl[ti]
            hT = hTb[ti % 2]; hk = f"p1_hT{ti % 2}"
            P.dma("sp", tab[:, :, :n], K.rope[:, :, t0:t0 + n].rearrange("f p t -> p f t"), writes=["p1_tab"])

            def fm(m):
                i4 = cnt["pf"] % 4
                cnt["pf"] += 1
                ps = pfb[i4]; pk = f"p1_pf{i4}"
                for kc in range(8):
                    P.call("pe", "matmul", reads=wkeys("p1_w", m * 128, (m + 1) * 128) + [hk], writes=[pk], inc=(kc == 7),
                           out=ps[:, :n], lhsT=W[:, kc, m * 128:(m + 1) * 128], rhs=hT[:, kc, :n],
                           start=(kc == 0), stop=(kc == 7))
                return ps, pk

            for m in range(8):
                ps, pk = fm(m)
                i2 = cnt["st"] % 2
                cnt["st"] += 1
                st = stb[i2]; sk = f"p1_st{i2}"
                P.call("act", "activation", reads=[pk], writes=[sk], out=st[:, :n], in_=ps[:, :n], func=AF.Copy)
                P.dma("sp", K.fm32[l][m * 128:(m + 1) * 128, t0:t0 + n], st[:, :n], reads=[sk], writes=["fm32"])
            pairs = [(8, 10, 0), (9, 11, 0), (12, 14, 0), (13, 15, 0), (16, 20, 2), (17, 21, 2), (18, 22, 2),
                     (19, 23, 2), (24, 25, 2)]
            for oi, (ma, mb, tbi) in enumerate(pairs):
                psa, pka = fm(ma)
                psb, pkb = fm(mb)
                i2 = cnt["ro"] % 2
                cnt["ro"] += 1
                ro = rob[i2]; rk = f"p1_ro{i2}"
                P.call("dve", "tensor_tensor", reads=[pka, "p1_tab"], writes=["p1_t1"],
                       out=tm1[:, :n], in0=psa[:, :n], in1=tab[:, tbi, :n], op=ALU.mult)
                P.call("dve", "tensor_tensor", reads=[pkb, "p1_tab"], writes=["p1_t2"],
                       out=tm2[:, :n], in0=psb[:, :n], in1=tab[:, tbi + 1, :n], op=ALU.mult)
                P.call("pool", "tensor_tensor", reads=["p1_t1", "p1_t2"], writes=[rk],
                       out=ro[:, :n], in0=tm1[:, :n], in1=tm2[:, :n], op=ALU.add)
                P.dma("sp", K.ropeT[l][oi * 128:(oi + 1) * 128, t0:t0 + n], ro[:, :n], reads=[rk], writes=["ropeT"])
            for blk in range(n // 128):
                tb = t0 + blk * 128
                vs = vsb[blk % 2]; vk = f"p1_vs{blk % 2}"
                for kc in range(8):
                    P.call("pe", "matmul", reads=wkeys("p1_w", NFM * 128, WCOLS) + [hk], writes=["p1_pv"], inc=(kc == 7),
                           out=pv[:, 0:384], lhsT=hT[:, kc, blk * 128:(blk + 1) * 128], rhs=W[:, kc, NFM * 128:WCOLS],
                           start=(kc == 0), stop=(kc == 7))
                P.call("act", "activation", reads=["p1_pv"], writes=[vk], out=vs[:], in_=pv[:, 0:384], func=AF.Copy)
                P.dma("sp", K.vtm[l][tb:tb + 128, :], vs[:], reads=[vk], writes=["vtm"])

        prepA(0)
        prepB(0)
        for ti in range(len(tl)):
            if ti + 1 < len(tl):
                prepA(ti + 1)
            proj(ti)
            if ti + 1 < len(tl):
                prepB(ti + 1)


def phase_rwprep(P, nc, K, l):
    with ExitStack() as es:
        sb = lambda name, shape, dt=F32: es.enter_context(nc.sbuf_tensor(name, shape, dt))
        pp = lambda name, shape, dt=F32: es.enter_context(nc.psum_tensor(name, shape, dt))
        cw = sb("p2_cw", [128, 6, 3]); kkc = sb("p2_kkc", [128, 2]); kac = sb("p2_kac", [128, 2])
        omka = sb("p2_omka", [128, 2]); w0c = sb("p2_w0", [128, 2, 2]); a0c = sb("p2_a0", [128, 2, 2])
        wup = sb("p2_wup", [64, 256]); aup = sb("p2_aup", [64, 256]); gup = sb("p2_gup", [128, 256])
        bones = sb("p2_bones", [128, 128])
        xr = sb("p2_xr", [128, 6, 514]); cv = sb("p2_cv", [128, 6, 512])
        lg = sb("p2_lg", [128, 512]); la = sb("p2_la", [64, 512]); thw = sb("p2_thw", [64, 512]); sgd = sb("p2_sgd", [128, 512])
        kkr = sb("p2_kkr", [128, 512]); sq = sb("p2_sq", [128, 512]); nr = sb("p2_nr", [128, 512])
        kk = sb("p2_kk", [128, 2, 512]); sgw = sb("p2_sgw", [128, 512])
        dec0 = sb("p2_dec0", [128, 512]); dec1 = sb("p2_dec1", [128, 512])
        av = sb("p2_a", [128, 512]); tt = sb("p2_t", [128, 512])
        NKf = sb("p2_NKf", [128, 2, 2, 512]); KDf = sb("p2_KDf", [128, 2, 2, 512])
        tm0 = sb("p2_tm0", [128, 7, 256]); tm1 = sb("p2_tm1", [128, 7, 256])
        pT = pp("p2_pT", [128, 12, 128]); pg_full = pp("p2_pg", [128, 512]); pg = pg_full[:, 0:256]
        px0 = pp("p2_px0", [128, 512]); px1 = pp("p2_px1", [128, 512]); px2 = pp("p2_px2", [128, 512])
        pxb = [px0, px1, px2]; decb = [dec0, dec1]; tmb = [tm0, tm1]
        NS = dict(allow_slow_non_contiguous=True)
        for j in range(3):
            P.dma("sp", cw[:, :, j], K.rw_conv[l, j].rearrange("(c p) -> p c", p=128), writes=["p2_cw"], **NS)
        P.dma("sp", kkc[:], K.rw_k_k[l].rearrange("(h p) -> p h", p=128), writes=["p2_kkc"], **NS)
        P.dma("sp", kac[:], K.rw_k_a[l].rearrange("(h p) -> p h", p=128), writes=["p2_kac"], **NS)
        for d in range(2):
            P.dma("sp", w0c[:, d, :], K.rw_w0[l, d].rearrange("(h p) -> p h", p=128), writes=["p2_w0"], **NS)
            P.dma("sp", a0c[:, d, :], K.rw_a0[l, d].rearrange("(h p) -> p h", p=128), writes=["p2_a0"], **NS)
        for d in range(2):
            P.dma("sp", wup[32 * d:32 * d + 32, :], K.rw_w_up[l, d], writes=["p2_wup"])
            P.dma("sp", aup[32 * d:32 * d + 32, :], K.rw_a_up[l, d], writes=["p2_aup"])
        P.dma("sp", gup[64:128, :], K.rw_g_up[l], writes=["p2_gup"])
        P.dma("sp", bones[:], K.bones_d, writes=["p2_bones"])
        P.call("dve", "tensor_scalar", reads=["p2_kac"], writes=["p2_omka"], out=omka[:], in0=kac[:],
               scalar1=-1.0, scalar2=1.0, op0=ALU.mult, op1=ALU.add)
        cnt = {"px": 0, "dec": 0, "tm": 0}

        def px():
            i = cnt["px"] % 3
            cnt["px"] += 1
            return pxb[i], f"p2_px{i}"

        for (t0, n) in tiles_512():
            s0, s1 = (0, CTX) if t0 < CTX else (CTX, T)
            src = lambda a, b: K.fm32[l][0:768, a:b].rearrange("(c p) t -> p c t", p=128)
            P.dma("sp", xr[:, :, 1:n + 1], src(t0, t0 + n), reads=["fm32"], writes=["p2_xr"])
            if t0 > s0:
                P.dma("sp", xr[:, :, 0:1], src(t0 - 1, t0), reads=["fm32"], writes=["p2_xr"], **NS)
            else:
                P.call("pool", "memset", writes=["p2_xr"], ap=xr[:, :, 0:1], constant=0.0)
            if t0 + n < s1:
                P.dma("sp", xr[:, :, n + 1:n + 2], src(t0 + n, t0 + n + 1), reads=["fm32"], writes=["p2_xr"], **NS)
            else:
                P.call("pool", "memset", writes=["p2_xr"], ap=xr[:, :, n + 1:n + 2], constant=0.0)
            P.dma("sp", lg[:, :n], K.fm32[l][768:896, t0:t0 + n], reads=["fm32"], writes=["p2_lg"])
            P.dma("sp", la[:, :n], K.fm32[l][896:960, t0:t0 + n], reads=["fm32"], writes=["p2_la"])
            for c in range(6):
                P.call("act", "activation", reads=["p2_xr", "p2_cw"], writes=["p2_cv"],
                       out=cv[:, c, :n], in_=xr[:, c, 1:n + 1], func=AF.Identity, scale=cw[:, c, 1:2])
                P.call("dve", "scalar_tensor_tensor", reads=["p2_xr", "p2_cw", "p2_cv"], writes=["p2_cv"],
                       out=cv[:, c, :n], in0=xr[:, c, 0:n], scalar=cw[:, c, 0:1], in1=cv[:, c, :n],
                       op0=ALU.mult, op1=ALU.add)
                P.call("dve", "scalar_tensor_tensor", reads=["p2_xr", "p2_cw", "p2_cv"], writes=["p2_cv"],
                       out=cv[:, c, :n], in0=xr[:, c, 2:n + 2], scalar=cw[:, c, 2:3], in1=cv[:, c, :n],
                       op0=ALU.mult, op1=ALU.add)
            P.call("act", "activation", reads=["p2_lg"], writes=["p2_thw"], out=thw[:, :n], in_=lg[0:64, :n], func=AF.Tanh)
            P.call("act", "activation", reads=["p2_lg"], writes=["p2_sgd"], out=sgd[64:128, :n], in_=lg[64:128, :n], func=AF.Sigmoid)
            for hp in range(2):
                kf = cv[:, 2 + hp, :n]
                P.call("dve", "tensor_scalar", reads=["p2_cv", "p2_kkc"], writes=["p2_kkr"], out=kkr[:, :n], in0=kf,
                       scalar1=kkc[:, hp:hp + 1], scalar2=None, op0=ALU.mult)
                P.call("act", "activation", reads=["p2_kkr"], writes=["p2_sq"], out=sq[:, :n], in_=kkr[:, :n], func=AF.Square)
                ps, pk = px()
                P.call("pe", "matmul", reads=["p2_bones", "p2_sq"], writes=[pk], out=ps[:, :n], lhsT=bones[:], rhs=sq[:, :n],
                       start=True, stop=True)
                P.call("act", "activation", reads=[pk], writes=["p2_nr"], out=nr[:, :n], in_=ps[:, :n], func=AF.Sqrt)
                P.call("dve", "tensor_scalar", reads=["p2_nr"], writes=["p2_nr"], out=nr[:, :n], in0=nr[:, :n],
                       scalar1=1e-12, scalar2=None, op0=ALU.max)
                P.call("dve", "reciprocal", reads=["p2_nr"], writes=["p2_nr"], out=nr[:, :n], in_=nr[:, :n])
                P.call("dve", "tensor_tensor", reads=["p2_kkr", "p2_nr"], writes=["p2_kk"], out=kk[:, hp, :n],
                       in0=kkr[:, :n], in1=nr[:, :n], op=ALU.mult)
                P.dma("sp", K.col_kr[l][hp][:, t0:t0 + n], kk[:, hp, :n], reads=["p2_kk"], writes=["col_kr"])
                P.dma("sp", K.col_kr[l][2 + hp][:, t0:t0 + n], cv[:, hp, :n], reads=["p2_cv"], writes=["col_kr"])
                for d in range(2):
                    ps, pk = px()
                    P.call("pe", "matmul", reads=["p2_wup", "p2_thw"], writes=[pk], out=ps[:, :n],
                           lhsT=wup[32 * d:32 * d + 32, hp * 128:(hp + 1) * 128], rhs=thw[32 * d:32 * d + 32, :n],
                           start=True, stop=True)
                    P.call("act", "activation", reads=[pk, "p2_w0"], writes=["p2_sgw"], out=sgw[:, :n], in_=ps[:, :n],
                           func=AF.Sigmoid, bias=w0c[:, d, hp:hp + 1])
                    i2 = cnt["dec"] % 2
                    cnt["dec"] += 1
                    dec = decb[i2]; dk = f"p2_dec{i2}"
                    P.call("act", "activation", reads=["p2_sgw"], writes=[dk], out=dec[:, :n], in_=sgw[:, :n],
                           func=AF.Exp, scale=-math.exp(-0.5))
                    P.dma("sp", K.col_w[l][2 * d + hp][:, t0:t0 + n], dec[:, :n], reads=[dk], writes=["col_w"])
                    ps, pk = px()
                    P.call("pe", "matmul", reads=["p2_aup", "p2_la"], writes=[pk], out=ps[:, :n],
                           lhsT=aup[32 * d:32 * d + 32, hp * 128:(hp + 1) * 128], rhs=la[32 * d:32 * d + 32, :n],
                           start=True, stop=True)
                    P.call("act", "activation", reads=[pk, "p2_a0"], writes=["p2_a"], out=av[:, :n], in_=ps[:, :n],
                           func=AF.Sigmoid, bias=a0c[:, d, hp:hp + 1])
                    P.call("dve", "tensor_scalar", reads=["p2_a", "p2_kac", "p2_omka"], writes=["p2_t"], out=tt[:, :n],
                           in0=av[:, :n], scalar1=kac[:, hp:hp + 1], scalar2=omka[:, hp:hp + 1], op0=ALU.mult, op1=ALU.add)
                    P.call("dve", "tensor_tensor", reads=["p2_cv", "p2_t"], writes=["p2_KDf"], out=KDf[:, d, hp, :n],
                           in0=kf, in1=tt[:, :n], op=ALU.mult)
                    P.call("dve", "scalar_tensor_tensor", reads=["p2_kk", "p2_a"], writes=["p2_NKf"], out=NKf[:, d, hp, :n],
                           in0=kk[:, hp, :n], scalar=-1.0, in1=av[:, :n], op0=ALU.mult, op1=ALU.mult)
                    P.dma("sp", K.col_nk[l][2 * d + hp][:, t0:t0 + n], NKf[:, d, hp, :n], reads=["p2_NKf"], writes=["col_nk"])
                    P.dma("sp", K.col_kd[l][2 * d + hp][:, t0:t0 + n], KDf[:, d, hp, :n], reads=["p2_KDf"], writes=["col_kd"])
            for blk in range(n // 128):
                tb = t0 + blk * 128
                bs = slice(blk * 128, (blk + 1) * 128)
                srcs = []
                for d in range(2):
                    for hp in range(2):
                        srcs.append((NKf[:, d, hp, bs], "p2_NKf"))
                for d in range(2):
                    for hp in range(2):
                        srcs.append((KDf[:, d, hp, bs], "p2_KDf"))
                for hp in range(2):
                    srcs.append((cv[:, 4 + hp, bs], "p2_cv"))
                for hp in range(2):
                    srcs.append((cv[:, hp, bs], "p2_cv"))
                for j, (sap, skey) in enumerate(srcs):
                    P.call("pe", "transpose", reads=[skey, "ident"], writes=["p2_pT"], out=pT[:, j, :], in_=sap,
                           identity=K.ident[:])
                P.call("pe", "matmul", reads=["p2_sgd", "p2_gup"], writes=["p2_pg"], out=pg, lhsT=sgd[64:128, bs], rhs=gup[64:128, :],
                       start=True, stop=True)
                i2 = cnt["tm"] % 2
                cnt["tm"] += 1
                tm = tmb[i2]; tk = f"p2_tm{i2}"
                for q in range(3):
                    eng = "act" if q == 1 else "dve"
                    o_ap = tm[:, 2 * q:2 * q + 2, :].rearrange("p a b -> p (a b)")
                    i_ap = pT[:, 4 * q:4 * q + 4, :].rearrange("p a b -> p (a b)")
                    if eng == "act":
                        P.call("act", "activation", reads=["p2_pT"], writes=[tk], out=o_ap, in_=i_ap, func=AF.Copy)
                    else:
                        P.call("dve", "tensor_copy", reads=["p2_pT"], writes=[tk], out=o_ap, in_=i_ap)
                P.call("act", "activation", reads=["p2_pg"], writes=[tk], out=tm[:, 6, :], in_=pg, func=AF.Copy)
                P.dma("sp", K.rw_tm[l][tb:tb + 128], tm[:], reads=[tk], writes=["rw_tm"])


TC = 32


def phase_scan(P, nc, K, l):
    nchunk = T // TC
    nctx = CTX // TC
    fwd = list(range(nchunk))
    bwd = list(range(nctx - 1, -1, -1)) + list(range(nchunk - 1, nctx - 1, -1))
    with ExitStack() as es:
        sb = lambda name, shape, dt=F32: es.enter_context(nc.sbuf_tensor(name, shape, dt))
        pp = lambda name, shape, dt=F32: es.enter_context(nc.psum_tensor(name, shape, dt))
        wcol = [sb(f"p3_w{b}", [128, 4, TC]) for b in range(2)]
        kkc = [sb(f"p3_kkc{b}", [128, 4, TC]) for b in range(2)]
        rc = [sb(f"p3_rc{b}", [128, 4, TC]) for b in range(2)]
        KKbd = [sb(f"p3_KK{b}", [128, 4, TC, 8]) for b in range(2)]
        Rbd = [sb(f"p3_R{b}", [128, 4, TC, 8]) for b in range(2)]
        LH = [[sb(f"p3_LH{b}{hp}", [128, TC, 128]) for hp in range(2)] for b in range(2)]
        Vr = [sb(f"p3_V{b}", [128, TC, 64]) for b in range(2)]
        Orows = [sb(f"p3_O{b}", [128, TC, 64]) for b in range(2)]
        SKV = sb("p3_SKV", [128, 64])
        S = sb("p3_S", [128, 4, 64])
        ps_sk = pp("p3_psk", [128, 512])[:, 0:64]; ps_o = pp("p3_po", [128, 512])[:, 0:64]
        ps_u = [pp(f"p3_pu{p}", [128, 512])[:, 0:64] for p in range(4)]
        for b in range(2):
            tiles = [(KKbd[b], f"p3_KK{b}"), (Rbd[b], f"p3_R{b}"), (Vr[b], f"p3_V{b}"), (Orows[b], f"p3_O{b}"),
                     (LH[b][0], f"p3_LH{b}"), (LH[b][1], f"p3_LH{b}")]
            for t_, nm in tiles:
                P.call("pool", "memset", writes=[nm + "_g0", nm + "_g1"], ap=t_[:], constant=0.0)
        P.call("pool", "memset", writes=["p3_S0", "p3_S1", "p3_S2", "p3_S3"], ap=S[:], constant=0.0)
        P.call("pool", "memset", writes=["p3_SKV_g0", "p3_SKV_g1"], ap=SKV[:], constant=0.0)
        row = lambda ap: ap.rearrange("(o t) k -> o t k", o=1)
        for c in range(nchunk):
            b = c % 2
            t0s = (fwd[c] * TC, bwd[c] * TC)
            for g in range(2):
                t0 = t0s[g]
                for hp in range(2):
                    p = 2 * g + hp
                    r0 = 64 * g + 4 * hp
                    P.dma("sp", wcol[b][:, p, :], K.col_w[l][p][:, t0:t0 + TC], reads=["col_w"], writes=[f"p3_w{b}_g{g}"])
                    P.dma("sp", kkc[b][:, p, :], K.col_kr[l][hp][:, t0:t0 + TC], reads=["col_kr"], writes=[f"p3_kkc{b}_g{g}"])
                    P.dma("sp", rc[b][:, p, :], K.col_kr[l][2 + hp][:, t0:t0 + TC], reads=["col_kr"], writes=[f"p3_rc{b}_g{g}"])
                    for j in range(2):
                        f0 = (2 * hp + j) * 64
                        P.dma("sp", LH[b][hp][r0 + j:r0 + j + 1, :, 64 * j:64 * j + 64],
                              row(K.rw_tm[l][t0:t0 + TC, g, f0:f0 + 64]), reads=["rw_tm"], writes=[f"p3_LH{b}_g{g}"])
                        P.dma("sp", LH[b][hp][r0 + 2 + j:r0 + 3 + j, :, 64 * j:64 * j + 64],
                              row(K.rw_tm[l][t0:t0 + TC, 2 + g, f0:f0 + 64]), reads=["rw_tm"], writes=[f"p3_LH{b}_g{g}"])
                        P.dma("sp", Vr[b][r0 + 2 + j:r0 + 3 + j, :, :],
                              row(K.rw_tm[l][t0:t0 + TC, 4, f0:f0 + 64]), reads=["rw_tm"], writes=[f"p3_V{b}_g{g}"])
                    for half in range(2):
                        hs = slice(64 * half, 64 * half + 64)
                        P.call("pool", "tensor_copy", reads=[f"p3_kkc{b}_g{g}"], writes=[f"p3_KK{b}_g{g}"],
                               out=KKbd[b][hs, p, :, 4 * hp + half], in_=kkc[b][hs, p, :])
                        P.call("pool", "tensor_copy", reads=[f"p3_rc{b}_g{g}"], writes=[f"p3_R{b}_g{g}"],
                               out=Rbd[b][hs, p, :, 4 * hp + half], in_=rc[b][hs, p, :])
            if getattr(K, "scan_dbg", False) and c == 0:
                for nm, t_, keys in (("dbg_LH0", LH[0][0], ["p3_LH0_g0", "p3_LH0_g1"]), ("dbg_LH1", LH[0][1], ["p3_LH0_g0", "p3_LH0_g1"]),
                                     ("dbg_V", Vr[0], ["p3_V0_g0", "p3_V0_g1"]), ("dbg_KK", KKbd[0], ["p3_KK0_g0", "p3_KK0_g1"]),
                                     ("dbg_R", Rbd[0], ["p3_R0_g0", "p3_R0_g1"]), ("dbg_w", wcol[0], ["p3_w0_g0", "p3_w0_g1"])):
                    shp = list(t_.shape)
                    o_ = nc.dram_tensor(nm, shp, F32, kind="ExternalOutput").ap()
                    P.dma("sp", o_, t_[:], reads=keys, writes=[nm])
            for i in range(TC):
                idxs = (i, TC - 1 - i)
                for g in range(2):
                    idx = idxs[g]
                    gs = slice(64 * g, 64 * g + 8)
                    for hp in range(2):
                        p = 2 * g + hp
                        P.call("pe", "matmul", reads=[f"p3_KK{b}_g{g}", f"p3_S{p}"], writes=[f"p3_psk_g{g}"],
                               out=ps_sk[gs, :], lhsT=KKbd[b][:, p, idx, :], rhs=S[:, p, :],
                               start=(hp == 0), stop=(hp == 1))
                    P.call("dve", "tensor_tensor", reads=[f"p3_psk_g{g}", f"p3_V{b}_g{g}"], writes=[f"p3_SKV_g{g}"],
                           out=SKV[gs, :], in0=ps_sk[gs, :], in1=Vr[b][gs, idx, :], op=ALU.add)
                    for hp in range(2):
                        p = 2 * g + hp
                        P.call("pe", "matmul", reads=[f"p3_LH{b}_g{g}", f"p3_SKV_g{g}"], writes=[f"p3_pu{p}"],
                               out=ps_u[p], lhsT=LH[b][hp][gs, idx, :], rhs=SKV[gs, :], start=True, stop=True)
                    for hp in range(2):
                        p = 2 * g + hp
                        P.call("dve", "scalar_tensor_tensor", reads=[f"p3_S{p}", f"p3_w{b}_g{g}", f"p3_pu{p}"],
                               writes=[f"p3_S{p}"], out=S[:, p, :], in0=S[:, p, :], scalar=wcol[b][:, p, idx:idx + 1],
                               in1=ps_u[p], op0=ALU.mult, op1=ALU.add)
                    for hp in range(2):
                        p = 2 * g + hp
                        P.call("pe", "matmul", reads=[f"p3_R{b}_g{g}", f"p3_S{p}"], writes=[f"p3_po_g{g}"],
                               out=ps_o[gs, :], lhsT=Rbd[b][:, p, idx, :], rhs=S[:, p, :],
                               start=(hp == 0), stop=(hp == 1))
                    P.call("act", "activation", reads=[f"p3_po_g{g}"], writes=[f"p3_O{b}_g{g}"],
                           out=Orows[b][gs, idx, :], in_=ps_o[gs, :], func=AF.Copy)
            if getattr(K, "scan_dbg", False) and c == 0:
                o_ = nc.dram_tensor("dbg_O", list(Orows[0].shape), F32, kind="ExternalOutput").ap()
                P.dma("sp", o_, Orows[0][:], reads=["p3_O0_g0", "p3_O0_g1"], writes=["dbg_O"])
                o_ = nc.dram_tensor("dbg_S", list(S.shape), F32, kind="ExternalOutput").ap()
                P.dma("sp", o_, S[:], reads=["p3_S0", "p3_S1", "p3_S2", "p3_S3"], writes=["dbg_S"])
                return
            for g in range(2):
                t0 = t0s[g]
                for hp in range(2):
                    r0 = 64 * g + 4 * hp
                    for j in range(2):
                        f0 = (2 * hp + j) * 64
                        P.dma("sp", row(K.o_tm[l][t0:t0 + TC, g, f0:f0 + 64]), Orows[b][r0 + j:r0 + j + 1, :, :],
                              reads=[f"p3_O{b}_g{g}"], writes=["o_tm"])


CH = 64


def phase_scan_chunked(P, nc, K, l):
    nchunk = T // CH
    nctx = CTX // CH
    order = [list(range(nchunk)), list(range(nctx - 1, -1, -1)) + list(range(nchunk - 1, nctx - 1, -1))]
    with ExitStack() as es:
        sb = lambda name, shape, dt=F32: es.enter_context(nc.sbuf_tensor(name, shape, dt))
        NBUF = 2
        names_in = ["w", "kk", "nk", "kd", "r"]
        tin = {nm: [[sb(f"c3_{nm}{p}{b}", [128, CH]) for b in range(NBUF)] for p in range(4)] for nm in names_in}
        Vtm = [[sb(f"c3_V{p}{b}", [128, 64]) for b in range(NBUF)] for p in range(4)]
        bdn = ["KH", "AH", "KD", "RH", "ANs", "KDs"]
        bd = {nm: [[sb(f"c3_{nm}{p}{b}", [128, 128]) for b in range(NBUF)] for p in range(4)] for nm in bdn}
        sqn = ["An", "AnT", "B", "Apn", "Bp", "X", "XT", "N", "ANtm", "KDtm"]
        sq = {nm: [[sb(f"c3_{nm}{p}{b}", [128, 128]) for b in range(NBUF)] for p in range(4)] for nm in sqn}
        X2 = [sb(f"c3_X2_{p}", [128, 128]) for p in range(4)]; X2T = [sb(f"c3_X2T_{p}", [128, 128]) for p in range(4)]
        Pc = [sb(f"c3_Pc{p}", [128, CH]) for p in range(4)]; Pm1 = [sb(f"c3_Pm{p}", [128, CH]) for p in range(4)]
        rP = [sb(f"c3_rP{p}", [128, CH]) for p in range(4)]; tmpv = [sb(f"c3_tv{p}", [128, CH]) for p in range(4)]
        PCc = [[sb(f"c3_PC{p}{b}", [128, 1]) for b in range(NBUF)] for p in range(4)]
        zer = sb("c3_zero", [128, CH])
        S = [sb(f"c3_S{p}", [128, 64]) for p in range(4)]
        Rt = [sb(f"c3_Rt{p}", [128, 64]) for p in range(4)]; Ut = [sb(f"c3_Ut{p}", [128, 64]) for p in range(4)]
        Ot = [[sb(f"c3_Ot{p}{b}", [128, 64]) for b in range(NBUF)] for p in range(4)]
        msk = sb("c3_msk", [128, 4, 128])
        psb = [es.enter_context(nc.psum_tensor(f"c3_ps{i}", [128, 512], F32)) for i in range(8)]
        pcnt = {"i": 0}

        def ps():
            i = pcnt["i"] % 8
            pcnt["i"] += 1
            return psb[i], f"c3_ps{i}"

        P.dma("sp", msk[:], K.cmask_d.rearrange("m a b -> a m b"), writes=["c3_msk"])
        P.call("pool", "memset", writes=["c3_zero"], ap=zer[:], constant=0.0)
        for p in range(4):
            P.call("pool", "memset", writes=[f"c3_S{p}"], ap=S[p][:], constant=0.0)
            for b in range(NBUF):
                for nm in bdn:
                    P.call("pool", "memset", writes=[f"c3_{nm}{p}{b}"], ap=bd[nm][p][b][:], constant=0.0)

        def mm(out, lhsT, rhs, reads, pk, start=True, stop=True, fast=False, inc=True):
            P.call("pe", "matmul", reads=reads, writes=[pk], inc=inc, out=out, lhsT=lhsT, rhs=rhs, start=start, stop=stop)

        def stage_a(ci, p):
            b = ci % NBUF
            d = p // 2; hp = p % 2
            t0 = order[d][ci] * CH
            k = lambda nm: f"c3_{nm}{p}{b}"
            srcs = {"w": K.col_w[l][p], "kk": K.col_kr[l][hp], "nk": K.col_nk[l][p], "kd": K.col_kd[l][p], "r": K.col_kr[l][2 + hp]}
            rkeys = {"w": "col_w", "kk": "col_kr", "nk": "col_nk", "kd": "col_kd", "r": "col_kr"}
            for nm in names_in:
                P.dma("sp", tin[nm][p][b][:], srcs[nm][:, t0:t0 + CH], reads=[rkeys[nm]], writes=[k(nm)])
            for j in range(2):
                f0 = (2 * hp + j) * 64
                P.dma("sp", Vtm[p][b][64 * j:64 * j + 64, :], K.rw_tm[l][t0:t0 + CH, 4, f0:f0 + 64], reads=["rw_tm"], writes=[k("V")])
            w_ = tin["w"][p][b]
            P.call("dve", "tensor_tensor_scan", reads=[k("w"), "c3_zero"], writes=[f"c3_Pc{p}"], out=Pc[p][:], data0=w_[:], data1=zer[:],
                   initial=1.0, op0=ALU.mult, op1=ALU.add)
            P.call("pool", "memset", writes=[f"c3_Pm{p}"], ap=Pm1[p][:, 0:1], constant=1.0)
            P.call("pool", "tensor_copy", reads=[f"c3_Pc{p}"], writes=[f"c3_Pm{p}"], out=Pm1[p][:, 1:CH], in_=Pc[p][:, 0:CH - 1])
            P.call("act", "activation", reads=[f"c3_Pc{p}"], writes=[k("PC")], out=PCc[p][b][:], in_=Pc[p][:, CH - 1:CH], func=AF.Copy)
            if d == 1:
                P.call("dve", "reciprocal", reads=[f"c3_Pm{p}"], writes=[f"c3_tv{p}"], out=tmpv[p][:], in_=Pm1[p][:])
                P.call("dve", "reciprocal", reads=[f"c3_Pc{p}"], writes=[f"c3_rP{p}"], out=rP[p][:], in_=Pc[p][:])
                P.call("dve", "tensor_scalar", reads=[f"c3_tv{p}", k("PC")], writes=[f"c3_Pc{p}"], out=Pc[p][:], in0=tmpv[p][:],
                       scalar1=PCc[p][b][:, 0:1], scalar2=None, op0=ALU.mult)
                P.call("dve", "tensor_scalar", reads=[f"c3_rP{p}", k("PC")], writes=[f"c3_Pm{p}"], out=Pm1[p][:], in0=rP[p][:],
                       scalar1=PCc[p][b][:, 0:1], scalar2=None, op0=ALU.mult)
            P.call("dve", "reciprocal", reads=[f"c3_Pc{p}"], writes=[f"c3_rP{p}"], out=rP[p][:], in_=Pc[p][:])
            for j in range(2):
                hs = slice(64 * j, 64 * j + 64)
                eng = "dve" if j == 0 else "pool"
                P.call(eng, "tensor_tensor", reads=[k("kk"), f"c3_Pm{p}"], writes=[k("KH")], out=bd["KH"][p][b][hs, hs],
                       in0=tin["kk"][p][b][hs, :], in1=Pm1[p][hs, :], op=ALU.mult)
                P.call(eng, "tensor_tensor", reads=[k("nk"), f"c3_rP{p}"], writes=[k("AH")], out=bd["AH"][p][b][hs, hs],
                       in0=tin["nk"][p][b][hs, :], in1=rP[p][hs, :], op=ALU.mult)
                P.call(eng, "tensor_tensor", reads=[k("kd"), f"c3_rP{p}"], writes=[k("KD")], out=bd["KD"][p][b][hs, hs],
                       in0=tin["kd"][p][b][hs, :], in1=rP[p][hs, :], op=ALU.mult)
                P.call(eng, "tensor_tensor", reads=[k("r"), f"c3_Pc{p}"], writes=[k("RH")], out=bd["RH"][p][b][hs, hs],
                       in0=tin["r"][p][b][hs, :], in1=Pc[p][hs, :], op=ALU.mult)
                P.call("dve", "tensor_scalar", reads=[k("AH"), k("PC")], writes=[k("ANs")], out=bd["ANs"][p][b][hs, hs],
                       in0=bd["AH"][p][b][hs, hs], scalar1=PCc[p][b][hs, 0:1], scalar2=None, op0=ALU.mult)
                P.call("dve", "tensor_scalar", reads=[k("KD"), k("PC")], writes=[k("KDs")], out=bd["KDs"][p][b][hs, hs],
                       in0=bd["KD"][p][b][hs, hs], scalar1=PCc[p][b][hs, 0:1], scalar2=None, op0=ALU.mult)

        for p in range(4):
            stage_a(0, p)
        for ci in range(nchunk):
            b = ci % NBUF
            kf = lambda p, nm: f"c3_{nm}{p}{b}"
            for p in range(4):
                d = p // 2
                k = lambda nm, p=p: f"c3_{nm}{p}{b}"
                for nm_s, nm_d in (("ANs", "ANtm"), ("KDs", "KDtm")):
                    pt_, pk = ps()
                    P.call("pe", "transpose", reads=[k(nm_s), "ident"], writes=[pk], out=pt_[:, 0:128], in_=bd[nm_s][p][b][:], identity=K.ident[:])
                    P.call("act", "activation", reads=[pk], writes=[k(nm_d)], out=sq[nm_d][p][b][:], in_=pt_[:, 0:128], func=AF.Copy)
                ms, mi = (0, 1) if d == 0 else (2, 3)
                for (dst, lh, rh, mk) in (("An", "AH", "KH", ms), ("B", "KD", "KH", ms), ("Apn", "AH", "RH", mi), ("Bp", "KD", "RH", mi)):
                    pt_, pk = ps()
                    mm(pt_[:, 0:128], bd[lh][p][b][:], bd[rh][p][b][:], [k(lh), k(rh)], pk)
                    P.call("dve", "tensor_tensor", reads=[pk, "c3_msk"], writes=[k(dst)], out=sq[dst][p][b][:], in0=pt_[:, 0:128],
                           in1=msk[:, mk, :], op=ALU.mult)
            for p in range(4):
                k = lambda nm, p=p: f"c3_{nm}{p}{b}"
                pt_, pk = ps()
                P.call("pe", "transpose", reads=[k("An"), "ident"], writes=[pk], out=pt_[:, 0:128], in_=sq["An"][p][b][:], identity=K.ident[:])
                P.call("act", "activation", reads=[pk], writes=[k("AnT")], out=sq["AnT"][p][b][:], in_=pt_[:, 0:128], func=AF.Copy)
                P.call("dve", "tensor_tensor", reads=[k("An"), "ident"], writes=[k("N")], out=sq["N"][p][b][:], in0=sq["An"][p][b][:], in1=K.ident[:], op=ALU.add)
            cur = {p: (sq["An"][p][b], sq["AnT"][p][b], kf(p, "An"), kf(p, "AnT")) for p in range(4)}
            nround = 5
            for rnd in range(nround):
                lastr = rnd == nround - 1
                nxt = {}
                for p in range(4):
                    Xc, XTc, xk, xtk = cur[p]
                    pt_, pk = ps()
                    mm(pt_[:, 0:128], Xc[:], XTc[:], [xk, xtk], pk)
                    x2t = X2T[p] if rnd % 2 == 0 else sq["XT"][p][b]
                    x2tk = f"c3_X2T_{p}" if rnd % 2 == 0 else kf(p, "XT")
                    P.call("act", "activation", reads=[pk], writes=[x2tk], out=x2t[:], in_=pt_[:, 0:128], func=AF.Copy)
                    nxt[p] = (x2t, x2tk)
                for p in range(4):
                    x2t, x2tk = nxt[p]
                    Nt = sq["N"][p][b]
                    pt3, pk3 = ps()
                    mm(pt3[:, 0:128], x2t[:], Nt[:], [x2tk, kf(p, "N")], pk3)
                    P.call("dve", "tensor_tensor", reads=[pk3, kf(p, "N")], writes=[kf(p, "N")], out=Nt[:], in0=pt3[:, 0:128], in1=Nt[:], op=ALU.add)
                    if not lastr:
                        pt2, pk2 = ps()
                        P.call("pe", "transpose", reads=[x2tk, "ident"], writes=[pk2], out=pt2[:, 0:128], in_=x2t[:], identity=K.ident[:])
                        x2 = X2[p] if rnd % 2 == 0 else sq["X"][p][b]
                        x2k = f"c3_X2_{p}" if rnd % 2 == 0 else kf(p, "X")
                        P.call("act", "activation", reads=[pk2], writes=[x2k], out=x2[:], in_=pt2[:, 0:128], func=AF.Copy)
                        cur[p] = (x2, x2t, x2k, x2tk)
                if rnd < 4 and ci + 1 < nchunk:
                    stage_a(ci + 1, rnd)
            if ci == nchunk - 1:
                P.mark(f"L{l}_scan_pre_last")
            kf = lambda p, nm: f"c3_{nm}{p}{b}"
            held = {}
            for p in range(4):
                pt_, pk = ps()
                mm(pt_[:, 0:64], bd["KH"][p][b][:], S[p][:], [kf(p, "KH"), f"c3_S{p}"], pk, start=True, stop=False, inc=False)
                mm(pt_[:, 0:64], sq["B"][p][b][:], Vtm[p][b][:], [kf(p, "B"), kf(p, "V")], pk, start=False, stop=True)
                P.call("act", "activation", reads=[pk], writes=[f"c3_Rt{p}"], out=Rt[p][:], in_=pt_[:, 0:64], func=AF.Copy)
            for p in range(4):
                pt2, pk2 = ps()
                mm(pt2[:, 0:64], sq["N"][p][b][:], Rt[p][:], [kf(p, "N"), f"c3_Rt{p}"], pk2)
                P.call("dve", "tensor_copy", reads=[pk2], writes=[f"c3_Ut{p}"], out=Ut[p][:], in_=pt2[:, 0:64])
            for p in range(4):
                sk_ = f"c3_S{p}"
                pt3, pk3 = ps()
                mm(pt3[:, 0:64], bd["RH"][p][b][:], S[p][:], [kf(p, "RH"), sk_], pk3, start=True, stop=False, inc=False)
                mm(pt3[:, 0:64], sq["Apn"][p][b][:], Ut[p][:], [kf(p, "Apn"), f"c3_Ut{p}"], pk3, start=False, stop=False, inc=False)
                mm(pt3[:, 0:64], sq["Bp"][p][b][:], Vtm[p][b][:], [kf(p, "Bp"), kf(p, "V")], pk3, start=False, stop=True)
                P.call("act", "activation", reads=[pk3], writes=[kf(p, "Ot")], out=Ot[p][b][:], in_=pt3[:, 0:64], func=AF.Copy)
                pt4, pk4 = ps()
                mm(pt4[:, 0:64], sq["ANtm"][p][b][:], Ut[p][:], [kf(p, "ANtm"), f"c3_Ut{p}"], pk4, start=True, stop=False, inc=False)
                mm(pt4[:, 0:64], sq["KDtm"][p][b][:], Vtm[p][b][:], [kf(p, "KDtm"), kf(p, "V")], pk4, start=False, stop=True)
                P.call("dve", "scalar_tensor_tensor", reads=[sk_, kf(p, "PC"), pk4], writes=[sk_], out=S[p][:], in0=S[p][:],
                       scalar=PCc[p][b][:, 0:1], in1=pt4[:, 0:64], op0=ALU.mult, op1=ALU.add)
            for p in range(4):
                d = p // 2; hp = p % 2
                t0 = order[d][ci] * CH
                for j in range(2):
                    f0 = (2 * hp + j) * 64
                    P.dma("sp", K.o_tm[l][t0:t0 + CH, d, f0:f0 + 64], Ot[p][b][64 * j:64 * j + 64, :], reads=[kf(p, "Ot")], writes=["o_tm"])


def phase_readout(P, nc, K, l):
    with ExitStack() as es:
        sb = lambda name, shape, dt=F32: es.enter_context(nc.sbuf_tensor(name, shape, dt))
        lng = sb("p4_lng", [128, 256]); lnb = sb("p4_lnb", [128, 256]); rkr = sb("p4_rkr", [128, 256])
        o2 = [sb(f"p4_o2{b}", [128, 2, 256]) for b in range(2)]
        tm = [sb(f"p4_tm{b}", [128, 7, 256]) for b in range(2)]
        o = sb("p4_o", [128, 4, 64]); xc = sb("p4_xc", [128, 4, 64]); sq = sb("p4_sq", [128, 4, 64])
        mu = sb("p4_mu", [128, 4]); var = sb("p4_var", [128, 4]); bs = sb("p4_bs", [128, 4])
        kds = sb("p4_kds", [128, 256]); y = [sb(f"p4_y{b}", [128, 256]) for b in range(2)]
        P.dma("sp", lng[:], K.rw_ln_g[l].partition_broadcast(128), writes=["p4_lng"])
        P.dma("sp", lnb[:], K.rw_ln_b[l].partition_broadcast(128), writes=["p4_lnb"])
        P.dma("sp", rkr[:], K.rw_r_k[l].partition_broadcast(128), writes=["p4_rkr"])
        f3 = lambda ap: ap.rearrange("p (h n) -> p h n", h=4)
        f2 = lambda ap: ap.rearrange("p h n -> p (h n)")
        bc = lambda ap: ap.unsqueeze(2).broadcast_to([128, 4, 64])
        for bi in range(T // 128):
            tb = bi * 128
            b = bi % 2
            ok = f"p4_o2{b}"; tk = f"p4_tm{b}"; yk = f"p4_y{b}"
            P.dma("sp", o2[b][:], K.o_tm[l][tb:tb + 128], reads=["o_tm"], writes=[ok])
            P.dma("sp", tm[b][:], K.rw_tm[l][tb:tb + 128], reads=["rw_tm"], writes=[tk])
            P.call("dve", "tensor_tensor", reads=[ok], writes=["p4_o"], out=f2(o[:]), in0=o2[b][:, 0, :], in1=o2[b][:, 1, :], op=ALU.add)
            P.call("dve", "tensor_reduce", reads=["p4_o"], writes=["p4_mu"], out=mu[:], in_=o[:], axis=AX.X, op=ALU.add)
            P.call("dve", "tensor_scalar", reads=["p4_mu"], writes=["p4_mu"], out=mu[:], in0=mu[:], scalar1=1.0 / 64, scalar2=None, op0=ALU.mult)
            P.call("dve", "tensor_tensor", reads=["p4_o", "p4_mu"], writes=["p4_xc"], out=xc[:], in0=o[:], in1=bc(mu[:]), op=ALU.subtract)
            P.call("pool", "tensor_tensor", reads=["p4_xc"], writes=["p4_sq"], out=sq[:], in0=xc[:], in1=xc[:], op=ALU.mult)
            P.call("dve", "tensor_reduce", reads=["p4_sq"], writes=["p4_var"], out=var[:], in_=sq[:], axis=AX.X, op=ALU.add)
            P.call("dve", "tensor_scalar", reads=["p4_var"], writes=["p4_var"], out=var[:], in0=var[:], scalar1=1.0 / 64, scalar2=64e-5,
                   op0=ALU.mult, op1=ALU.add)
            P.call("act", "activation", reads=["p4_var"], writes=["p4_var"], out=var[:], in_=var[:], func=AF.Sqrt)
            P.call("dve", "reciprocal", reads=["p4_var"], writes=["p4_var"], out=var[:], in_=var[:])
            P.call("dve", "tensor_tensor", reads=["p4_xc", "p4_var"], writes=["p4_xc"], out=xc[:], in0=xc[:], in1=bc(var[:]), op=ALU.mult)
            P.call("dve", "tensor_tensor", reads=["p4_xc", "p4_lng"], writes=["p4_xc"], out=f2(xc[:]), in0=f2(xc[:]), in1=lng[:], op=ALU.mult)
            P.call("dve", "tensor_tensor", reads=["p4_xc", "p4_lnb"], writes=["p4_xc"], out=f2(xc[:]), in0=f2(xc[:]), in1=lnb[:], op=ALU.add)
            P.call("pool", "tensor_tensor", reads=[tk], writes=["p4_kds"], out=kds[:], in0=tm[b][:, 2, :], in1=tm[b][:, 3, :], op=ALU.add)
            P.call("pool", "tensor_tensor", reads=[tk, "p4_kds"], writes=["p4_kds"], out=kds[:], in0=kds[:], in1=tm[b][:, 5, :], op=ALU.mult)
            P.call("pool", "tensor_tensor", reads=["p4_kds", "p4_rkr"], writes=["p4_kds"], out=kds[:], in0=kds[:], in1=rkr[:], op=ALU.mult)
            P.call("dve", "tensor_reduce", reads=["p4_kds"], writes=["p4_bs"], out=bs[:], in_=f3(kds[:]), axis=AX.X, op=ALU.add)
            P.call("dve", "tensor_tensor", reads=[tk, "p4_bs"], writes=["p4_sq"], out=sq[:], in0=f3(tm[b][:, 4, :]), in1=bc(bs[:]), op=ALU.mult)
            P.call("dve", "tensor_tensor", reads=["p4_sq", "p4_xc"], writes=["p4_sq"], out=sq[:], in0=sq[:], in1=xc[:], op=ALU.add)
            P.call("dve", "tensor_tensor", reads=["p4_sq", tk], writes=[yk], out=y[b][:], in0=f2(sq[:]), in1=tm[b][:, 6, :], op=ALU.mult)
            P.dma("sp", K.ytm[l][tb:tb + 128, 0:256], y[b][:], reads=[yk], writes=["ytm"])


def phase_attn(P, nc, K, l):
    lam_init = 0.8 - 0.6 * math.exp(-0.3 * l)
    NBLK = T // 128
    with ExitStack() as es:
        sb = lambda name, shape, dt=F32: es.enter_context(nc.sbuf_tensor(name, shape, dt))
        pp = lambda name: es.enter_context(nc.psum_tensor(name, [128, 512], F32))
        kdf = sb("p5_kdf", [128, 2, T], BF16)
        Vaug = sb("p5_V", [128, NBLK, 6, 65], BF16)
        Kd = [sb(f"p5_Kd{g}", [128, T], BF16) for g in range(2)]
        Qm = [[sb(f"p5_Qm{b}_{u}", [128, 512], BF16) for u in range(8)] for b in range(2)]
        Eb = [sb(f"p5_E{i}", [128, 512], BF16) for i in range(3)]
        oT = [sb(f"p5_oT{m}", [65, 512]) for m in range(2)]
        rz = [sb(f"p5_rz{m}", [64, 512]) for m in range(2)]
        dd = sb("p5_dd", [64, 512]); sqt = sb("p5_sq", [64, 512]); rs = sb("p5_rs", [64, 512])
        ybt = [sb(f"p5_yb{i}", [64, 512], BF16) for i in range(2)]
        sel = sb("p5_sel", [65, 64]); ones64 = sb("p5_ones", [64, 64])
        lamt = sb("p5_lamt", [64, 128]); lp = sb("p5_lp", [64, 64]); e12 = sb("p5_e12", [64, 2]); neglam = sb("p5_nl", [64, 1])
        gdf = sb("p5_gdf", [64, 1])
        msk = sb("p5_msk", [128, 2, 128])
        exps = sb("p5_exps", [128, 8])
        qg = [sb(f"p5_qg{b}", [128, 4, 128], BF16) for b in range(2)]
        Eg = [sb(f"p5_Eg{b}", [128, 5, 128], BF16) for b in range(2)]
        zt = sb("p5_zt", [128, 1]); ycst = [sb(f"p5_yc{b}", [128, 512]) for b in range(2)]
        ps_s = [pp(f"p5_ps{i}") for i in range(3)]
        ps_o = [pp(f"p5_po{m}") for m in range(2)]
        ps_z = [pp(f"p5_pz{m}") for m in range(2)]
        ps_g = pp("p5_pg")
        NS = dict(allow_slow_non_contiguous=True)
        for c in range(2):
            P.dma("sp", kdf[:, c, :], K.ropeT[l][256 + c * 128:256 + (c + 1) * 128, :], reads=["ropeT"], writes=["p5_kdf"])
        for g in range(2):
            for half in range(2):
                P.dma("sp", Kd[g][64 * half:64 * half + 64, :], K.ropeT[l][1024 + 64 * g:1024 + 64 * g + 64, :],
                      reads=["ropeT"], writes=[f"p5_Kd{g}"])
        P.call("pool", "memset", writes=["p5_V"], ap=Vaug[:], constant=1.0)
        for kb in range(NBLK):
            P.dma("sp", Vaug[:, kb, :, 0:64], K.vtm[l][kb * 128:(kb + 1) * 128, :].rearrange("p (h d) -> p h d", h=6),
                  reads=["vtm"], writes=["p5_V"])
        for b in range(2):
            for u in range(8):
                P.call("pool", "memset", writes=[f"p5_Qm{b}"], ap=Qm[b][u][:], constant=0.0)
        P.dma("sp", sel[:], K.sel_d, writes=["p5_sel"])
        P.dma("sp", ones64[:], K.bones_d[0:64, 0:64], writes=["p5_ones"])
        P.dma("sp", msk[:], K.msk_d.rearrange("m a b -> a m b"), writes=["p5_msk"])
        P.dma("sp", lamt[:], K.df_lambda[l].partition_broadcast(64), writes=["p5_lamt"])
        P.dma("sp", gdf[:], K.df_norm_g[l].rearrange("(p o) -> p o", o=1), writes=["p5_gdf"], **NS)
        P.dma("sp", exps[:], K.gq_sink[l].partition_broadcast(128), writes=["p5_exps"])
        P.call("act", "activation", reads=["p5_exps"], writes=["p5_exps"], out=exps[:], in_=exps[:], func=AF.Exp)
        P.call("dve", "tensor_scalar", reads=["p5_gdf"], writes=["p5_gdf"], out=gdf[:], in0=gdf[:], scalar1=1.0 - lam_init,
               scalar2=None, op0=ALU.mult)
        for i in range(2):
            P.call("dve", "tensor_tensor", reads=["p5_lamt"], writes=["p5_lp"], out=lp[:, 32 * i:32 * i + 32],
                   in0=lamt[:, 64 * i:64 * i + 32], in1=lamt[:, 64 * i + 32:64 * i + 64], op=ALU.mult)
        P.call("dve", "tensor_reduce", reads=["p5_lp"], writes=["p5_e12"], out=e12[:],
               in_=lp[:].rearrange("p (a b) -> p a b", a=2), axis=AX.X, op=ALU.add)
        P.call("act", "activation", reads=["p5_e12"], writes=["p5_e12"], out=e12[:], in_=e12[:], func=AF.Exp)
        P.call("dve", "tensor_tensor", reads=["p5_e12"], writes=["p5_nl"], out=neglam[:], in0=e12[:, 1:2], in1=e12[:, 0:1], op=ALU.subtract)
        P.call("dve", "tensor_scalar", reads=["p5_nl"], writes=["p5_nl"], out=neglam[:], in0=neglam[:], scalar1=-lam_init,
               scalar2=None, op0=ALU.add)
        cnt = {"s": 0, "yb": 0}
        for ti, (t0, nq) in enumerate(tiles_512()):
            b = ti % 2
            kbs = list(range(0, CTX // 128)) if t0 < CTX else list(range(NBLK))
            for u in range(8):
                r0 = 32 * (u % 4)
                P.dma("sp", Qm[b][u][r0:r0 + 32, :nq], K.ropeT[l][(u // 4) * 128 + r0:(u // 4) * 128 + r0 + 32, t0:t0 + nq],
                      reads=["ropeT"], writes=[f"p5_Qm{b}"])
            seq = [(h, m, ki, kb) for h in range(4) for m in range(2) for ki, kb in enumerate(kbs)]

            def emit_S(i):
                h, m, ki, kb = seq[i]
                u = 2 * h + m
                i3 = i % 3
                P.call("pe", "matmul", reads=["p5_kdf", f"p5_Qm{b}"], writes=[f"p5_ps{i3}"], out=ps_s[i3][:, :nq],
                       lhsT=kdf[:, u // 4, kb * 128:(kb + 1) * 128], rhs=Qm[b][u][:, :nq], start=True, stop=True)
                P.call("act", "activation", reads=[f"p5_ps{i3}"], writes=[f"p5_E{i3}"], out=Eb[i3][:, :nq],
                       in_=ps_s[i3][:, :nq], func=AF.Exp, scale=32 ** -0.5)

            def emit_PV(i):
                h, m, ki, kb = seq[i]
                i3 = i % 3
                P.call("pe", "matmul", reads=["p5_V", f"p5_E{i3}"], writes=[f"p5_po{m}"], out=ps_o[m][0:65, :nq],
                       lhsT=Vaug[:, kb, h, :], rhs=Eb[i3][:, :nq], start=(ki == 0), stop=(ki == len(kbs) - 1))

            def post(h):
                for m in range(2):
                    P.call("act", "activation", reads=[f"p5_po{m}"], writes=[f"p5_oT{m}"], out=oT[m][:, :nq], in_=ps_o[m][0:65, :nq],
                           func=AF.Copy)
                    P.call("pe", "matmul", reads=["p5_sel", f"p5_oT{m}"], writes=[f"p5_pz{m}"], out=ps_z[m][0:64, :nq],
                           lhsT=sel[:], rhs=oT[m][:, :nq], start=True, stop=True)
                    P.call("dve", "reciprocal", reads=[f"p5_pz{m}"], writes=[f"p5_rz{m}"], out=rz[m][:, :nq], in_=ps_z[m][0:64, :nq])
                    P.call("dve", "tensor_tensor", reads=[f"p5_oT{m}", f"p5_rz{m}"], writes=[f"p5_rz{m}"], out=rz[m][:, :nq],
                           in0=oT[m][0:64, :nq], in1=rz[m][:, :nq], op=ALU.mult)
                P.call("dve", "scalar_tensor_tensor", reads=["p5_rz0", "p5_rz1", "p5_nl"], writes=["p5_dd"], out=dd[:, :nq],
                       in0=rz[1][:, :nq], scalar=neglam[:, 0:1], in1=rz[0][:, :nq], op0=ALU.mult, op1=ALU.add)
                P.call("act", "activation", reads=["p5_dd"], writes=["p5_sq"], out=sqt[:, :nq], in_=dd[:, :nq], func=AF.Square)
                P.call("pe", "matmul", reads=["p5_ones", "p5_sq"], writes=["p5_pz0"], out=ps_z[0][0:64, :nq], lhsT=ones64[:],
                       rhs=sqt[:, :nq], start=True, stop=True)
                P.call("dve", "tensor_scalar", reads=["p5_pz0"], writes=["p5_rs"], out=rs[:, :nq], in0=ps_z[0][0:64, :nq],
                       scalar1=1.0 / 64, scalar2=EPS, op0=ALU.mult, op1=ALU.add)
                P.call("act", "activation", reads=["p5_rs"], writes=["p5_rs"], out=rs[:, :nq], in_=rs[:, :nq], func=AF.Sqrt)
                P.call("dve", "reciprocal", reads=["p5_rs"], writes=["p5_rs"], out=rs[:, :nq], in_=rs[:, :nq])
                i2 = cnt["yb"] % 2
                cnt["yb"] += 1
                P.call("dve", "scalar_tensor_tensor", reads=["p5_dd", "p5_gdf", "p5_rs"], writes=[f"p5_yb{i2}"], out=ybt[i2][:, :nq],
                       in0=dd[:, :nq], scalar=gdf[:, 0:1], in1=rs[:, :nq], op0=ALU.mult, op1=ALU.mult)
                P.dma("sp", K.yT[l][256 + 64 * h:256 + 64 * h + 64, t0:t0 + nq], ybt[i2][:, :nq], reads=[f"p5_yb{i2}"], writes=["yT"])

            LOOK = 2
            for i in range(min(LOOK, len(seq))):
                emit_S(i)
            for i in range(len(seq)):
                if i + LOOK < len(seq):
                    emit_S(i + LOOK)
                emit_PV(i)
                h, m, ki, kb = seq[i]
                if m == 1 and ki == len(kbs) - 1:
                    post(h)
        P.mark(f"L{l}_diff_end")
        for tb in range(NBLK):
            b = tb % 2
            P.dma("sp", qg[b][:], K.ropeT[l][512:1024, tb * 128:(tb + 1) * 128].rearrange("(c p) t -> p c t", p=128),
                  reads=["ropeT"], writes=[f"p5_qg{b}"])
            keyblocks = [0, 1]
            if tb >= 2:
                keyblocks += [kb for kb in (tb - 1, tb, tb + 1) if 2 <= kb < NBLK]
            nk = len(keyblocks)
            psA = [(ps_s[0], "p5_ps0", ps_s[1], "p5_ps1"), (ps_s[2], "p5_ps2", ps_o[0], "p5_po0")]
            psG = [(ps_g, "p5_pg"), (ps_o[1], "p5_po1")]

            def g_scores(hd):
                g = hd // 4; c = hd // 2; base = 64 * (hd % 2)
                bs = slice(base, base + 64)
                pa, pak, pb, pbk = psA[hd % 2]
                for j, kb in enumerate(keyblocks):
                    pst, pstk = (pa, pak) if j < 4 else (pb, pbk)
                    P.call("pe", "matmul", reads=[f"p5_Kd{g}", f"p5_qg{b}"], writes=[pstk], inc=(j == nk - 1 or j == 3),
                           out=pst[:, (j % 4) * 128:(j % 4 + 1) * 128], lhsT=Kd[g][bs, kb * 128:(kb + 1) * 128], rhs=qg[b][bs, c, :],
                           start=True, stop=True)
                eg = Eg[hd % 2]; ek = f"p5_Eg{hd % 2}"
                n0 = min(nk, 4)
                P.call("act", "activation", reads=[pak], writes=[ek], out=eg[:, 0:n0, :].rearrange("p a b -> p (a b)"),
                       in_=pa[:, 0:n0 * 128], func=AF.Exp, scale=64 ** -0.5)
                if nk > 4:
                    P.call("act", "activation", reads=[pbk], writes=[ek], out=eg[:, 4, :], in_=pb[:, 0:128],
                           func=AF.Exp, scale=64 ** -0.5)
                for j, kb in enumerate(keyblocks):
                    if tb >= 2 and kb == tb - 1 and kb >= 2:
                        P.call("pool", "tensor_tensor", reads=[ek, "p5_msk"], writes=[ek], out=eg[:, j, :], in0=eg[:, j, :], in1=msk[:, 0, :], op=ALU.mult)
                    if tb >= 2 and kb == tb + 1:
                        P.call("pool", "tensor_tensor", reads=[ek, "p5_msk"], writes=[ek], out=eg[:, j, :], in0=eg[:, j, :], in1=msk[:, 1, :], op=ALU.mult)

            def g_pv(hd):
                g = hd // 4
                eg = Eg[hd % 2]; ek = f"p5_Eg{hd % 2}"
                pgt, pgk = psG[hd % 2]
                for j, kb in enumerate(keyblocks):
                    P.call("pe", "matmul", reads=[ek, "p5_V"], writes=[pgk], inc=(j == nk - 1), out=pgt[:, 0:65], lhsT=eg[:, j, :],
                           rhs=Vaug[:, kb, 4 + g, :], start=(j == 0), stop=(j == nk - 1))
                P.call("dve", "tensor_scalar", reads=[pgk, "p5_exps"], writes=["p5_zt"], out=zt[:], in0=pgt[:, 64:65],
                       scalar1=exps[:, hd:hd + 1], scalar2=None, op0=ALU.add)
                P.call("dve", "reciprocal", reads=["p5_zt"], writes=["p5_zt"], out=zt[:], in_=zt[:])
                P.call("dve", "tensor_scalar", reads=[pgk, "p5_zt"], writes=[f"p5_yc{b}"], out=ycst[b][:, hd * 64:(hd + 1) * 64],
                       in0=pgt[:, 0:64], scalar1=zt[:, 0:1], scalar2=None, op0=ALU.mult)

            g_scores(0)
            for hd in range(8):
                if hd + 1 < 8:
                    g_scores(hd + 1)
                g_pv(hd)
            P.dma("sp", K.ytm[l][tb * 128:(tb + 1) * 128, 512:1024], ycst[b][:], reads=[f"p5_yc{b}"], writes=["ytm"])


def row_gain(P, nc, K, l, gi, jga, G, tmp, name):
    for who in range(2):
        P.dma("sp", G[who][:], K.norm_g[l, gi].partition_broadcast(128), writes=[f"{name}_G{who}"])
        P.dma("sp", tmp[:], K.modrow[l, who, jga].partition_broadcast(128), reads=["modrow"], writes=[name + "_tmp"])
        P.call("dve", "tensor_tensor", reads=[f"{name}_G{who}", name + "_tmp"], writes=[f"{name}_G{who}"],
               out=G[who][:], in0=G[who][:], in1=tmp[:], op=ALU.mult)


def norm_rows(P, ps2, pkeys, ss, ss2, rs, junk, pfx):
    P.call("act", "activation", reads=[pkeys[0]], writes=[pfx + "_junk", pfx + "_ss"], out=junk[:, 0:512], in_=ps2[0][:, :],
           func=AF.Square, accum_out=ss[:])
    P.call("act", "activation", reads=[pkeys[1]], writes=[pfx + "_junk", pfx + "_ss2"], out=junk[:, 512:1024], in_=ps2[1][:, :],
           func=AF.Square, accum_out=ss2[:])
    P.call("dve", "tensor_tensor", reads=[pfx + "_ss", pfx + "_ss2"], writes=[pfx + "_rs"], out=rs[:], in0=ss[:], in1=ss2[:], op=ALU.add)
    P.call("dve", "tensor_scalar", reads=[pfx + "_rs"], writes=[pfx + "_rs"], out=rs[:], in0=rs[:], scalar1=1.0 / D, scalar2=EPS,
           op0=ALU.mult, op1=ALU.add)
    P.call("act", "activation", reads=[pfx + "_rs"], writes=[pfx + "_rs"], out=rs[:], in_=rs[:], func=AF.Sqrt)
    P.call("dve", "reciprocal", reads=[pfx + "_rs"], writes=[pfx + "_rs"], out=rs[:], in_=rs[:])


def phase_outproj(P, nc, K, l, xsrc):
    NBLK = T // 128
    with ExitStack() as es:
        sb = lambda name, shape, dt=F32: es.enter_context(nc.sbuf_tensor(name, shape, dt))
        pp = lambda name: es.enter_context(nc.psum_tensor(name, [128, 512], F32))
        W = sb("p6_w", [128, 8, D], BF16)
        stg = [sb(f"p6_stg{i}", [128, 512]) for i in range(6)]
        G = [sb(f"p6_G{who}", [128, D]) for who in range(2)]
        tmp = sb("p6_tmp", [128, D])
        A = sb("p6_A", [128, 8, 2]); Bv = sb("p6_B", [128, 8, 2]); gcol = sb("p6_g", [128, 8])
        yt = [sb(f"p6_yt{b}", [128, D]) for b in range(2)]
        yTb = [sb(f"p6_yT{b}", [128, 8, 128], BF16) for b in range(2)]
        xt = [sb(f"p6_x{b}", [128, D]) for b in range(2)]
        xm = [sb(f"p6_xm{b}", [128, D]) for b in range(2)]
        xn = sb("p6_xn", [128, D]); junk = sb("p6_junk", [128, D])
        ss = sb("p6_ss", [128, 1]); ss2 = sb("p6_ss2", [128, 1]); rs = sb("p6_rs", [128, 1])
        hT = [sb(f"p6_hT{b}", [128, 8, 128], BF16) for b in range(2)]
        pt = es.enter_context(nc.psum_tensor("p6_pt", [128, 8, 128], F32))
        po4 = [pp(f"p6_po{i}") for i in range(4)]
        load_weight_bf16(P, nc, K.w_out[l], W, "p6_w", 8, D, stg, [f"p6_stg{i}" for i in range(6)])
        row_gain(P, nc, K, l, 1, 2, G, tmp, "p6")
        mod_AB(P, nc, K, l, 2, 4, 3, A, Bv, gcol, "p6")
        pt2 = es.enter_context(nc.psum_tensor("p6_pt2", [128, 8, 128], F32))
        ssb = sb("p6_ssb", [128, 1]); rsb = sb("p6_rsb", [128, 1]); junkb = sb("p6_junkb", [128, D])

        def part1(tb):
            b = tb % 2
            who = 1 if tb * 128 < CTX else 0
            ts = slice(tb * 128, (tb + 1) * 128)
            po = po4[2 * b:2 * b + 2]; pok = [f"p6_po{2 * b}", f"p6_po{2 * b + 1}"]
            P.dma("sp", yt[b][:], K.ytm[l][ts, :], reads=["ytm"], writes=[f"p6_yt{b}"])
            P.dma("sp", xt[b][:], xsrc[ts, :], reads=["xres"], writes=[f"p6_x{b}"])
            P.dma("sp", yTb[b][:, 2:4, :], K.yT[l][256:512, ts].rearrange("(c p) t -> p c t", p=128), reads=["yT"], writes=[f"p6_yT{b}"])
            for c in (0, 1, 4, 5, 6, 7):
                P.call("pe", "transpose", reads=[f"p6_yt{b}", "ident"], writes=["p6_pt"], inc=(c == 7), out=pt[:, c, :],
                       in_=yt[b][:, c * 128:(c + 1) * 128], identity=K.ident[:])
            P.call("act", "activation", reads=["p6_pt"], writes=[f"p6_yT{b}"], out=yTb[b][:, 0:2, :], in_=pt[:, 0:2, :], func=AF.Copy)
            P.call("dve", "tensor_copy", reads=["p6_pt"], writes=[f"p6_yT{b}"], out=yTb[b][:, 4:8, :], in_=pt[:, 4:8, :])
            for half in range(2):
                for kc in range(8):
                    P.call("pe", "matmul", reads=[f"p6_yT{b}", f"p6_w_{half}"], writes=[pok[half]], inc=(kc == 7), out=po[half][:, :],
                           lhsT=yTb[b][:, kc, :], rhs=W[:, kc, half * 512:(half + 1) * 512], start=(kc == 0), stop=(kc == 7))
            norm_rows(P, po, pok, ss, ss2, rs, junk, "p6")
            for half in range(2):
                hs = slice(half * 512, (half + 1) * 512)
                P.call("dve", "scalar_tensor_tensor", reads=[pok[half], "p6_rs", f"p6_G{who}"], writes=[f"p6_xm{b}"],
                       out=xm[b][:, hs], in0=po[half][:, :], scalar=rs[:, 0:1], in1=G[who][:, hs], op0=ALU.mult, op1=ALU.mult)
                P.call("pool", "tensor_tensor", reads=[f"p6_xm{b}", f"p6_x{b}"], writes=[f"p6_xm{b}"], out=xm[b][:, hs],
                       in0=xm[b][:, hs], in1=xt[b][:, hs], op=ALU.add)
            P.dma("sp", K.xmid[l][ts, :], xm[b][:], reads=[f"p6_xm{b}"], writes=["xmid"])

        def part2(tb):
            b = tb % 2
            who = 1 if tb * 128 < CTX else 0
            ts = slice(tb * 128, (tb + 1) * 128)
            norm_block(P, xm[b], f"p6_xm{b}", ssb, rsb, junkb, xn, "p6_xn", pfx="p6b")
            for dc in range(8):
                P.call("pe", "transpose", reads=["p6_xn", "ident"], writes=["p6_pt2"], inc=(dc == 7), out=pt2[:, dc, :],
                       in_=xn[:, dc * 128:(dc + 1) * 128], identity=K.ident[:])
            for dc in range(8):
                P.call("act", "activation", reads=["p6_pt2", "p6_A", "p6_B"], writes=[f"p6_hT{b}"], out=hT[b][:, dc, :],
                       in_=pt2[:, dc, :], func=AF.Identity, scale=A[:, dc, who:who + 1], bias=Bv[:, dc, who:who + 1])
            P.dma("sp", K.h2T[l][:, ts].rearrange("(c p) t -> p c t", p=128), hT[b][:], reads=[f"p6_hT{b}"], writes=["h2T"])

        part1(0)
        for tb in range(NBLK):
            if tb + 1 < NBLK:
                part1(tb + 1)
            part2(tb)


def phase_ffn_up(P, nc, K, l):
    NF = DFF // 128
    with ExitStack() as es:
        sb = lambda name, shape, dt=F32: es.enter_context(nc.sbuf_tensor(name, shape, dt))
        pp = lambda name: es.enter_context(nc.psum_tensor(name, [128, 512], F32))
        Wg = sb("p7_wg", [128, 8, DFF], BF16); Wu = sb("p7_wu", [128, 8, DFF], BF16)
        stg = [sb(f"p7_stg{i}", [128, 512]) for i in range(6)]
        cw = sb("p7_cw", [128, NF, 3]); cb = sb("p7_cb", [128, NF])
        hT = [sb(f"p7_hT{b}", [128, 8, 514], BF16) for b in range(2)]
        gsb = sb("p7_g", [128, 514]); tt = sb("p7_t", [128, 512]); sg = sb("p7_s", [128, 512])
        zt = [sb(f"p7_z{i}", [128, 512], BF16) for i in range(2)]
        pg = [pp(f"p7_pg{i}") for i in range(2)]; ph = [pp(f"p7_ph{i}") for i in range(2)]; pu = [pp(f"p7_pu{i}") for i in range(2)]
        NS = dict(allow_slow_non_contiguous=True)
        load_weight_bf16(P, nc, K.ff_w_gate[l], Wg, "p7_wg", 8, DFF, stg, [f"p7_stg{i}" for i in range(6)])
        load_weight_bf16(P, nc, K.ff_w_up[l], Wu, "p7_wu", 8, DFF, stg, [f"p7_stg{i}" for i in range(6)])
        for j in range(3):
            P.dma("sp", cw[:, :, j], K.ff_conv_w[l, j].rearrange("(c p) -> p c", p=128), writes=["p7_cw"], **NS)
        P.dma("sp", cb[:], K.ff_conv_b[l].rearrange("(c p) -> p c", p=128), writes=["p7_cb"], **NS)
        NSEAM = SEQ // 512 - 1
        hH = sb("p7_hH", [128, 8, 2 * NSEAM], BF16); HG = sb("p7_HG", [128, NF, 2 * NSEAM])
        for k_ in range(NSEAM):
            tk = CTX + 512 * (k_ + 1)
            P.dma("sp", hH[:, :, 2 * k_:2 * k_ + 2], K.h2T[l][:, tk - 1:tk + 1].rearrange("(c p) t -> p c t", p=128),
                  reads=["h2T"], writes=["p7_hH"], **NS)
        for fc in range(NF):
            fs = slice(fc * 128, (fc + 1) * 128)
            i2 = fc % 2
            for kc in range(8):
                P.call("pe", "matmul", reads=[f"p7_wg_{fc // 4}", "p7_hH"], writes=[f"p7_ph{i2}"], out=ph[i2][:, 0:2 * NSEAM], lhsT=Wg[:, kc, fs],
                       rhs=hH[:, kc, :], start=(kc == 0), stop=(kc == 7))
            P.call("act", "activation", reads=[f"p7_ph{i2}"], writes=["p7_HG"], out=HG[:, fc, :], in_=ph[i2][:, 0:2 * NSEAM], func=AF.Copy)
        cnt = {"i": 0}
        for ti, (t0, n) in enumerate(tiles_512()):
            b = ti % 2
            xi = ti - 1
            hk = f"p7_hT{b}"
            s0, s1 = (0, CTX) if t0 < CTX else (CTX, T)
            src = lambda a, b_: K.h2T[l][:, a:b_].rearrange("(c p) t -> p c t", p=128)
            P.dma("sp", hT[b][:, :, 1:n + 1], src(t0, t0 + n), reads=["h2T"], writes=[hk])
            if t0 > s0:
                P.dma("sp", hT[b][:, :, 0:1], src(t0 - 1, t0), reads=["h2T"], writes=[hk], **NS)
            else:
                P.call("pool", "memset", writes=[hk], ap=hT[b][:, :, 0:1], constant=0.0)
            if t0 + n < s1:
                P.dma("sp", hT[b][:, :, n + 1:n + 2], src(t0 + n, t0 + n + 1), reads=["h2T"], writes=[hk], **NS)
            else:
                P.call("pool", "memset", writes=[hk], ap=hT[b][:, :, n + 1:n + 2], constant=0.0)
            for fc in range(NF):
                i2 = cnt["i"] % 2
                cnt["i"] += 1
                fs = slice(fc * 128, (fc + 1) * 128)
                for kc in range(8):
                    P.call("pe", "matmul", reads=[f"p7_wg_{fc // 4}", hk], writes=[f"p7_pg{i2}"], inc=(kc == 7), out=pg[i2][:, :n], lhsT=Wg[:, kc, fs],
                           rhs=hT[b][:, kc, 1:n + 1], start=(kc == 0), stop=(kc == 7))
                for kc in range(8):
                    P.call("pe", "matmul", reads=[f"p7_wu_{fc // 4}", hk], writes=[f"p7_pu{i2}"], inc=(kc == 7), out=pu[i2][:, :n], lhsT=Wu[:, kc, fs],
                           rhs=hT[b][:, kc, 1:n + 1], start=(kc == 0), stop=(kc == 7))
                P.call("act", "activation", reads=[f"p7_pg{i2}"], writes=["p7_g"], out=gsb[:, 1:n + 1], in_=pg[i2][:, :n], func=AF.Copy)
                if t0 > s0:
                    P.call("dve", "tensor_copy", reads=["p7_HG"], writes=["p7_g"], out=gsb[:, 0:1], in_=HG[:, fc, 2 * (xi - 1):2 * (xi - 1) + 1])
                else:
                    P.call("pool", "memset", writes=["p7_g"], ap=gsb[:, 0:1], constant=0.0)
                if t0 + n < s1:
                    P.call("dve", "tensor_copy", reads=["p7_HG"], writes=["p7_g"], out=gsb[:, n + 1:n + 2], in_=HG[:, fc, 2 * xi + 1:2 * xi + 2])
                else:
                    P.call("pool", "memset", writes=["p7_g"], ap=gsb[:, n + 1:n + 2], constant=0.0)
                P.call("act", "activation", reads=["p7_g", "p7_cw", "p7_cb"], writes=["p7_t"], out=tt[:, :n], in_=gsb[:, 1:n + 1],
                       func=AF.Identity, scale=cw[:, fc, 1:2], bias=cb[:, fc:fc + 1])
                P.call("dve", "scalar_tensor_tensor", reads=["p7_g", "p7_cw", "p7_t"], writes=["p7_t"], out=tt[:, :n],
                       in0=gsb[:, 0:n], scalar=cw[:, fc, 0:1], in1=tt[:, :n], op0=ALU.mult, op1=ALU.add)
                P.call("dve", "scalar_tensor_tensor", reads=["p7_g", "p7_cw", "p7_t"], writes=["p7_t"], out=tt[:, :n],
                       in0=gsb[:, 2:n + 2], scalar=cw[:, fc, 2:3], in1=tt[:, :n], op0=ALU.mult, op1=ALU.add)
                P.call("act", "activation", reads=["p7_t"], writes=["p7_s"], out=sg[:, :n], in_=tt[:, :n], func=AF.Silu)
                P.call("dve", "tensor_tensor", reads=["p7_s", f"p7_pu{i2}"], writes=[f"p7_z{i2}"], out=zt[i2][:, :n], in0=sg[:, :n],
                       in1=pu[i2][:, :n], op=ALU.mult)
                P.dma("sp", K.zT[l][fs, t0:t0 + n], zt[i2][:, :n], reads=[f"p7_z{i2}"], writes=["zT"])


def phase_ffn_down(P, nc, K, l, xdst, last):
    NF = DFF // 128
    NBLK = T // 128
    with ExitStack() as es:
        sb = lambda name, shape, dt=F32: es.enter_context(nc.sbuf_tensor(name, shape, dt))
        pp = lambda name: es.enter_context(nc.psum_tensor(name, [128, 512], F32))
        Wd = sb("p8_wd", [128, NF, D], BF16)
        stg = [sb(f"p8_stg{i}", [128, 512]) for i in range(6)]
        G = [sb(f"p8_G{who}", [128, D]) for who in range(2)]
        tmp = sb("p8_tmp", [128, D]); junk = sb("p8_junk", [128, D])
        zb = [sb(f"p8_z{b}", [128, NF, 128], BF16) for b in range(2)]
        xm = [sb(f"p8_xm{b}", [128, D]) for b in range(2)]
        xo = [sb(f"p8_xo{b}", [128, D]) for b in range(2)]
        ss = sb("p8_ss", [128, 1]); ss2 = sb("p8_ss2", [128, 1]); rs = sb("p8_rs", [128, 1])
        po4 = [pp(f"p8_po{i}") for i in range(4)]
        load_weight_bf16(P, nc, K.ff_w_down[l], Wd, "p8_wd", NF, D, stg, [f"p8_stg{i}" for i in range(6)])
        row_gain(P, nc, K, l, 3, 5, G, tmp, "p8")
        for tb in range(NBLK):
            if last and tb * 128 < CTX:
                continue
            b = tb % 2
            who = 1 if tb * 128 < CTX else 0
            ts = slice(tb * 128, (tb + 1) * 128)
            po = po4[2 * b:2 * b + 2]; pok = [f"p8_po{2 * b}", f"p8_po{2 * b + 1}"]
            P.dma("sp", zb[b][:], K.zT[l][:, ts].rearrange("(c p) t -> p c t", p=128), reads=["zT"], writes=[f"p8_z{b}"])
            P.dma("sp", xm[b][:], K.xmid[l][ts, :], reads=["xmid"], writes=[f"p8_xm{b}"])
            for half in range(2):
                for fc in range(NF):
                    P.call("pe", "matmul", reads=[f"p8_z{b}", f"p8_wd_{half}"], writes=[pok[half]], inc=(fc == NF - 1), out=po[half][:, :],
                           lhsT=zb[b][:, fc, :], rhs=Wd[:, fc, half * 512:(half + 1) * 512], start=(fc == 0), stop=(fc == NF - 1))
            norm_rows(P, po, pok, ss, ss2, rs, junk, "p8")
            for half in range(2):
                hs = slice(half * 512, (half + 1) * 512)
                P.call("dve", "scalar_tensor_tensor", reads=[pok[half], "p8_rs", f"p8_G{who}"], writes=[f"p8_xo{b}"],
                       out=xo[b][:, hs], in0=po[half][:, :], scalar=rs[:, 0:1], in1=G[who][:, hs], op0=ALU.mult, op1=ALU.mult)
                P.call("pool", "tensor_tensor", reads=[f"p8_xo{b}", f"p8_xm{b}"], writes=[f"p8_xo{b}"], out=xo[b][:, hs],
                       in0=xo[b][:, hs], in1=xm[b][:, hs], op=ALU.add)
            if last:
                P.dma("sp", K.out[tb * 128 - CTX:(tb + 1) * 128 - CTX, :], xo[b][:], reads=[f"p8_xo{b}"], writes=["out"])
            else:
                P.dma("sp", xdst[ts, :], xo[b][:], reads=[f"p8_xo{b}"], writes=["xres"])


def build(dbg=(), upto=99, nlayers=L, skip=()):
    nc = bass.Bass("TRN2", target_bir_lowering=False)
    K = Ctx()
    K.scan_dbg = 'scandbg' in dbg
    K.chunked = 'seqscan' not in dbg
    K.f32r = False
    dt = lambda name, shape, dtype=F32, kind="ExternalInput": nc.dram_tensor(name, shape, dtype, kind=kind).ap()
    scr = lambda name, shape, dtype=F32: dt(name, shape, dtype, "ExternalOutput" if name in dbg else "Internal")
    K.xin = dt("xin", [T, D])
    K.c_in = dt("c_in", [D])
    K.cctx_in = dt("cctx_in", [D])
    K.ada_w = dt("ada_w", [L, D, 6 * D])
    K.ada_b = dt("ada_b", [L, 6 * D])
    K.norm_g = dt("norm_g", [L, 4, D])
    K.w_in = dt("w_in", [L, D, WCOLS])
    K.rope = dt("rope", [4, 128, T])
    K.ident_d = dt("ident", [128, 128])
    K.out = dt("out", [SEQ, D], kind="ExternalOutput")
    K.fm32 = [scr(f"fm32_{l}", [1024, T]) for l in range(L)]
    K.ropeT = [scr(f"ropeT_{l}", [1152, T], BF16) for l in range(L)]
    K.vtm = [scr(f"vtm_{l}", [T, 384], BF16) for l in range(L)]
    for nm, shp in (("rw_conv", [L, 3, 768]), ("rw_w0", [L, 2, 256]), ("rw_w_up", [L, 2, 32, 256]), ("rw_a0", [L, 2, 256]),
                    ("rw_a_up", [L, 2, 32, 256]), ("rw_g_up", [L, 64, 256]), ("rw_k_k", [L, 256]), ("rw_k_a", [L, 256]),
                    ("rw_r_k", [L, 256]), ("rw_ln_g", [L, 256]), ("rw_ln_b", [L, 256])):
        setattr(K, nm, dt(nm, shp))
    K.bones_d = dt("bones", [128, 128])
    K.col_w = [[scr(f"col_w_{l}_{i}", [128, T]) for i in range(4)] for l in range(L)]
    K.col_kr = [[scr(f"col_kr_{l}_{i}", [128, T]) for i in range(4)] for l in range(L)]
    K.rw_tm = [scr(f"rw_tm_{l}", [T, 7, 256]) for l in range(L)]
    K.col_nk = [[scr(f"col_nk_{l}_{i}", [128, T]) for i in range(4)] for l in range(L)]
    K.col_kd = [[scr(f"col_kd_{l}_{i}", [128, T]) for i in range(4)] for l in range(L)]
    K.cmask_d = dt("cmask", [4, 128, 128])
    K.o_tm = [scr(f"o_tm_{l}", [T, 2, 256]) for l in range(L)]
    K.ytm = [scr(f"ytm_{l}", [T, 1024]) for l in range(L)]
    K.yT = [scr(f"yT_{l}", [1024, T], BF16) for l in range(L)]
    K.modrow = scr("modrow", [L, 2, 6, D])
    K.xmid = [scr(f"xmid_{l}", [T, D]) for l in range(L)]
    K.h2T = [scr(f"h2T_{l}", [D, T], BF16) for l in range(L)]
    K.zT = [scr(f"zT_{l}", [DFF, T], BF16) for l in range(L)]
    K.xres = [scr(f"xres_{l}", [T, D]) for l in range(L)]
    K.w_out = dt("w_out", [L, D, D])
    K.ff_w_gate = dt("ff_w_gate", [L, D, DFF]); K.ff_w_up = dt("ff_w_up", [L, D, DFF]); K.ff_w_down = dt("ff_w_down", [L, DFF, D])
    K.ff_conv_w = dt("ff_conv_w", [L, 3, DFF]); K.ff_conv_b = dt("ff_conv_b", [L, DFF])
    K.df_lambda = dt("df_lambda", [L, 128])
    K.df_norm_g = dt("df_norm_g", [L, 64])
    K.gq_sink = dt("gq_sink", [L, 8])
    K.sel_d = dt("sel65", [65, 64])
    K.msk_d = dt("msk", [2, 128, 128])

    P = Prog(nc)
    with (
        nc.sbuf_tensor("modcol0", [128, 48, 2], F32) as mc0,
        nc.sbuf_tensor("modcol1", [128, 48, 2], F32) as mc1,
        nc.sbuf_tensor("ident_sb", [128, 128], F32) as ident,
    ):
        K.modcol = [mc0, mc1]
        K.ident = ident
        P.dma("sp", ident[:], K.ident_d, writes=["ident"])
        phase_mod(P, nc, K)
        P.barrier()
        for l in range(nlayers):
            xsrc = K.xin if l == 0 else K.xres[l - 1]
            last = (l == L - 1)
            nc0 = nc
            nc = Uniq(nc0, f"_L{l}")
            phases = [lambda: phase_inproj(P, nc, K, l, xsrc), lambda: phase_rwprep(P, nc, K, l), lambda: (phase_scan_chunked if K.chunked else phase_scan)(P, nc, K, l),
                      lambda: phase_readout(P, nc, K, l), lambda: phase_attn(P, nc, K, l), lambda: phase_outproj(P, nc, K, l, xsrc),
                      lambda: phase_ffn_up(P, nc, K, l), lambda: phase_ffn_down(P, nc, K, l, K.xres[l], last)]
            for pi, ph in enumerate(phases):
                if upto >= pi + 1 and pi + 1 not in skip:
                    ph()
                    P.mark(f"L{l}_ph{pi + 1}")
                    P.barrier()
            nc = nc0
        P.finish(["out"])
    return nc, P


def make_in_maps(inp):
    f = lambda a: np.ascontiguousarray(np.asarray(a, dtype=np.float32))
    cols = w_in_cols()
    shared = {
        "cctx_in": f(inp["c_ctx"]),
        "ada_w": f(inp["ada_w"]),
        "ada_b": f(inp["ada_b"]),
        "norm_g": f(inp["norm_g"]),
        "w_in": f(np.asarray(inp["w_in"])[:, :, cols]),
        "rope": rope_tables(),
        "ident": np.eye(128, dtype=np.float32),
        "bones": np.kron(np.eye(2, dtype=np.float32), np.ones((64, 64), np.float32)),
    }
    for nm in ("rw_conv", "rw_w0", "rw_w_up", "rw_a0", "rw_a_up", "rw_g_up", "rw_k_k", "rw_k_a", "rw_ln_g", "rw_ln_b"):
        shared[nm] = f(inp[nm])
    shared["rw_r_k"] = f(np.asarray(inp["rw_r_k"]).reshape(L, 256))
    shared["df_lambda"] = f(np.asarray(inp["df_lambda"]).reshape(L, 128))
    tau = np.arange(128) % 64
    shared["cmask"] = np.stack([tau[:, None] < tau[None, :], tau[:, None] <= tau[None, :],
                                tau[:, None] > tau[None, :], tau[:, None] >= tau[None, :]]).astype(np.float32)
    shared["df_norm_g"] = f(inp["df_norm_g"])
    for nm in ("w_out", "ff_w_gate", "ff_w_up", "ff_w_down", "ff_conv_w", "ff_conv_b"):
        shared[nm] = f(inp[nm])
    shared["gq_sink"] = f(inp["gq_sink"])
    sel = np.zeros((65, 64), np.float32); sel[64, :] = 1.0
    shared["sel65"] = sel
    a = np.arange(128)
    shared["msk"] = np.stack([(a[:, None] >= a[None, :]), (a[:, None] <= a[None, :])]).astype(np.float32)
    maps = []
    for core in range(8):
        b = core % NB
        m = dict(shared)
        m["xin"] = f(np.concatenate([inp["ctx"][b], inp["x"][b]], axis=0))
        m["c_in"] = f(inp["c"][b])
        maps.append(m)
    return maps


def kernel(**inputs):
    inp = {k: np.asarray(v) for k, v in inputs.items()}
    nc, _ = build()
    maps = make_in_maps(inp)
    res = run_bass_kernel_spmd(nc, maps, core_ids=list(range(8)))
    out = np.stack([np.asarray(res.results[b]["out"], dtype=np.float32) for b in range(NB)], axis=0)
    return out
```

```python
import math
from contextlib import ExitStack
import numpy as np
import concourse.bass as bass
import concourse.mybir as mybir
from concourse.bass_utils import run_bass_kernel_spmd

F32 = mybir.dt.float32
BF16 = mybir.dt.bfloat16
AF = mybir.ActivationFunctionType
ALU = mybir.AluOpType
AX = mybir.AxisListType

D = 1024
NB = 4
SEQ = 4096
CTX = 256
T = CTX + SEQ
L = 2
DFF = 2816
GRID_W = 64
EPS = 1e-6

ENG_NAMES = ("pe", "act", "dve", "pool", "sp")


class Prog:
    N_DMA_SEMS = 12

    def __init__(self, nc):
        self.nc = nc
        self.streams = {e: [] for e in ENG_NAMES}
        self.cnt = {e: 0 for e in ENG_NAMES}
        self.sems = {e: nc.alloc_semaphore(f"c_{e}") for e in ENG_NAMES}
        self.dsems = [nc.alloc_semaphore(f"d_{i}") for i in range(self.N_DMA_SEMS)]
        self.dcnt = [0] * self.N_DMA_SEMS
        self.dnext = 0
        self.waited = {e: {} for e in ENG_NAMES}
        self.bufs = {}
        self.n_instr = 0
        self.split_stores = True

    def _sem(self, key):
        return self.sems[key] if isinstance(key, str) else self.dsems[key]

    def _need(self, eng, tok):
        if tok is None:
            return
        key, val = tok
        if key == eng and val > self.cnt[eng]:
            return
        if self.waited[eng].get(key, 0) >= val:
            return
        self.waited[eng][key] = val
        sem = self._sem(key)
        self.streams[eng].append(lambda e, sem=sem, val=val: e.wait_ge(sem, val))
        self.n_instr += 1

    def _deps(self, eng, reads, writes):
        for r in reads:
            st = self.bufs.get(r)
            if st is not None:
                self._need(eng, st["w"])
        for w in writes:
            st = self.bufs.get(w)
            if st is not None:
                self._need(eng, st["w"])
                for t in st["r"]:
                    self._need(eng, t)

    def _commit(self, tok, reads, writes):
        for r in reads:
            st = self.bufs.setdefault(r, {"w": None, "r": []})
            st["r"].append(tok)
            if len(st["r"]) > 24:
                best = {}
                for k, v in st["r"]:
                    best[k] = max(best.get(k, 0), v)
                st["r"] = list(best.items())
        for w in writes:
            self.bufs[w] = {"w": tok, "r": []}

    def op(self, eng, fn, reads=(), writes=(), inc=True):
        self._deps(eng, reads, writes)
        sem = self.sems[eng]
        if inc:
            self.cnt[eng] += 1
            self.streams[eng].append(lambda e, fn=fn, sem=sem: fn(e).then_inc(sem, 1))
            tok = (eng, self.cnt[eng])
        else:
            self.streams[eng].append(lambda e, fn=fn: fn(e))
            tok = (eng, self.cnt[eng] + 1)
        self.n_instr += 1
        self._commit(tok, reads, writes)

    def call(self, eng, method, reads=(), writes=(), inc=True, **kw):
        self.op(eng, lambda e: getattr(e, method)(**kw), reads, writes, inc=inc)

    def dma(self, q, out, in_, reads=(), writes=(), **kw):
        if q == "sp" and self.split_stores and str(out.space) == "DRAM" and str(in_.space) != "DRAM":
            q = "pool"
        i = self.dnext
        self.dnext = (self.dnext + 1) % self.N_DMA_SEMS
        if self.dcnt[i] > 0:
            self._need(q, (i, 16 * self.dcnt[i]))
        self._deps(q, reads, writes)
        self.dcnt[i] += 1
        sem = self.dsems[i]
        self.streams[q].append(
            lambda e, out=out, in_=in_, sem=sem, kw=kw: e.dma_start(out=out, in_=in_, **kw).then_inc(sem, 16))
        self.n_instr += 1
        self._commit((i, 16 * self.dcnt[i]), reads, writes)

    def mark(self, name):
        if not hasattr(self, "marks"):
            self.marks = []
        self.marks.append((name, dict(self.cnt)))

    def barrier(self):
        toks = [(e, self.cnt[e]) for e in ENG_NAMES if self.cnt[e] > 0]
        toks += [(i, 16 * self.dcnt[i]) for i in range(self.N_DMA_SEMS) if self.dcnt[i] > 0]
        for e in ENG_NAMES:
            for tok in toks:
                if tok[0] != e:
                    self._need(e, tok)

    def finish(self, final_keys):
        for k in final_keys:
            st = self.bufs.get(k)
            if st is not None:
                self._need("sp", st["w"])
        for i in range(self.N_DMA_SEMS):
            if self.dcnt[i] > 0:
                self._need("sp", (i, 16 * self.dcnt[i]))
        nc = self.nc
        with nc.Block() as block:
            @block.tensor
            def _(e):
                for f in self.streams["pe"]:
                    f(e)

            @block.scalar
            def _(e):
                for f in self.streams["act"]:
                    f(e)

            @block.vector
            def _(e):
                for f in self.streams["dve"]:
                    f(e)

            @block.gpsimd
            def _(e):
                for f in self.streams["pool"]:
                    f(e)

            @block.sync
            def _(e):
                for f in self.streams["sp"]:
                    f(e)


class Ctx:
    pass


class Uniq:
    def __init__(self, nc, suffix):
        self._nc = nc
        self._sfx = suffix

    def sbuf_tensor(self, name, shape, dtype):
        return self._nc.sbuf_tensor(name + self._sfx, shape, dtype)

    def psum_tensor(self, name, shape, dtype):
        return self._nc.psum_tensor(name + self._sfx, shape, dtype)

    def __getattr__(self, k):
        return getattr(self._nc, k)


def phase_mod(P, nc, K):
    GW = 768
    NG = 6144 // GW
    with (
        nc.sbuf_tensor("m_c", [128, 8, 2], F32) as craw,
        nc.sbuf_tensor("m_cs", [128, 8, 2], F32) as cs,
        nc.sbuf_tensor("m_w0", [128, 8, GW], F32) as w0,
        nc.sbuf_tensor("m_w1", [128, 8, GW], F32) as w1,
        nc.sbuf_tensor("m_b", [128, 48], F32) as bcol,
        nc.psum_tensor("m_ps", [128, 256, 2], F32) as ps,
        nc.psum_tensor("m_pst", [128, 512], F32) as pst,
        nc.sbuf_tensor("m_mcw", [128, 48], F32) as mcw,
        nc.sbuf_tensor("m_mrow", [48, 128], F32) as mrow,
    ):
        wb = [w0, w1]
        P.dma("sp", craw[:, :, 0], K.c_in.rearrange("(c p) -> p c", p=128), writes=["m_c"],
              allow_slow_non_contiguous=True)
        P.dma("sp", craw[:, :, 1], K.cctx_in.rearrange("(c p) -> p c", p=128), writes=["m_c"],
              allow_slow_non_contiguous=True)
        P.op("act", lambda e: e.activation(out=cs[:], in_=craw[:], func=AF.Silu), reads=["m_c"], writes=["m_cs"])
        for l in range(L):
            P.dma("sp", bcol[:], K.ada_b[l].rearrange("(j p) -> p j", p=128), writes=["m_b"],
                  allow_slow_non_contiguous=True)
            for gi in range(NG):
                wt = wb[gi % 2]
                wk = f"m_w{gi % 2}"
                src = K.ada_w[l, :, gi * GW:(gi + 1) * GW].rearrange("(kc p) n -> p kc n", p=128)
                P.dma("sp", wt[:], src, writes=[wk])
                for jj in range(GW // 128):
                    j = gi * (GW // 128) + jj
                    for kc in range(8):
                        P.op("pe", lambda e, wt=wt, jj=jj, kc=kc, j=j: e.matmul(
                            ps[:, j, :], lhsT=wt[:, kc, jj * 128:(jj + 1) * 128], rhs=cs[:, kc, :],
                            start=(kc == 0), stop=(kc == 7)),
                            reads=[wk, "m_cs"], writes=["m_ps"])
            mc = K.modcol[l]
            for who in range(2):
                P.op("dve", lambda e, mc=mc, who=who: e.tensor_tensor(
                    out=mc[:, :, who], in0=ps[:, 0:48, who], in1=bcol[:], op=ALU.add),
                    reads=["m_ps", "m_b"], writes=[f"modcol{l}"])
            for who in range(2):
                P.call("dve", "tensor_copy", reads=[f"modcol{l}"], writes=["m_mcw"], out=mcw[:], in_=mc[:, :, who])
                P.call("pe", "transpose", reads=["m_mcw", "ident"], writes=["m_pst"], out=pst[0:48, 0:128], in_=mcw[:], identity=K.ident[:])
                P.call("act", "activation", reads=["m_pst"], writes=["m_mrow"], out=mrow[:], in_=pst[0:48, 0:128], func=AF.Copy)
                P.dma("sp", K.modrow[l, who].rearrange("j (c p) -> (j c) p", p=128), mrow[:], reads=["m_mrow"], writes=["modrow"])


def _swap_idx(du):
    nf = du // 4
    idx = np.arange(du)
    axis = idx // (2 * nf); half = (idx // nf) % 2; f = idx % nf
    return axis * 2 * nf + (1 - half) * nf + f


def w_in_cols():
    sw32 = _swap_idx(32); sw64 = _swap_idx(64)
    cols = list(range(0, 768))
    cols += list(range(832, 896)) + list(range(768, 832))
    cols += list(range(896, 960)) * 2
    dfq = np.arange(960, 1216); dfk = np.arange(1216, 1472)
    sw = lambda base, du: np.concatenate([base[u * du:(u + 1) * du][_swap_idx(du)] for u in range(len(base) // du)])
    cols += list(dfq) + list(sw(dfq, 32)) + list(dfk) + list(sw(dfk, 32))
    gqq = np.arange(1728, 2240); gqk = np.arange(2240, 2368)
    cols += list(gqq) + list(sw(gqq, 64)) + list(gqk) + list(sw(gqk, 64))
    cols += list(range(1472, 1728)) + list(range(2368, 2496))
    return np.asarray(cols, dtype=np.int64)


NFM = 26
WCOLS = NFM * 128 + 384


def rope_tables():
    out = np.zeros((4, 128, T), np.float32)
    tt = np.arange(SEQ)
    pos = np.stack([(tt // GRID_W).astype(np.float32), (tt % GRID_W).astype(np.float32)], 0)
    for ti, du in ((0, 32), (2, 64)):
        nf = du // 4
        inv = (np.float32(10000.0) ** (-np.arange(nf, dtype=np.float32) / np.float32(nf))).astype(np.float32)
        i = np.arange(du)
        axis = i // (2 * nf); half = (i // nf) % 2; f = i % nf
        ang = (pos[axis] * inv[f][:, None]).astype(np.float32)
        c = np.cos(ang).astype(np.float32); s_ = np.sin(ang).astype(np.float32)
        s_ = np.where(half[:, None] == 0, -s_, s_)
        rep = 128 // du
        out[ti, :, :CTX] = 1.0
        out[ti, :, CTX:] = np.tile(c, (rep, 1))
        out[ti + 1, :, CTX:] = np.tile(s_, (rep, 1))
    return out


def tiles_512():
    return [(0, CTX)] + [(CTX + i * 512, 512) for i in range(SEQ // 512)]


def norm_block(P, xt, xk, ss, rs, junk, xn, xnk, pfx="nb"):
    P.call("act", "activation", reads=[xk], writes=[pfx + "_junk", pfx + "_ss"],
           out=junk[:], in_=xt[:], func=AF.Square, accum_out=ss[:])
    P.call("dve", "tensor_scalar", reads=[pfx + "_ss"], writes=[pfx + "_rs"],
           out=rs[:], in0=ss[:], scalar1=1.0 / D, scalar2=EPS, op0=ALU.mult, op1=ALU.add)
    P.call("act", "activation", reads=[pfx + "_rs"], writes=[pfx + "_rs"], out=rs[:], in_=rs[:], func=AF.Sqrt)
    P.call("dve", "reciprocal", reads=[pfx + "_rs"], writes=[pfx + "_rs"], out=rs[:], in_=rs[:])
    P.call("dve", "tensor_scalar", reads=[xk, pfx + "_rs"], writes=[xnk],
           out=xn[:], in0=xt[:], scalar1=rs[:], scalar2=None, op0=ALU.mult)


def mod_AB(P, nc, K, l, gi, jsc, jsh, A, Bv, gcol, name):
    P.dma("sp", gcol[:], K.norm_g[l, gi].rearrange("(c p) -> p c", p=128), writes=[name + "_g"],
          allow_slow_non_contiguous=True)
    mc = K.modcol[l]
    for who in range(2):
        P.call("dve", "scalar_tensor_tensor", reads=[f"modcol{l}", name + "_g"], writes=[name + "_A"],
               out=A[:, :, who], in0=mc[:, jsc * 8:(jsc + 1) * 8, who], scalar=1.0, in1=gcol[:],
               op0=ALU.add, op1=ALU.mult)
        P.call("dve", "tensor_copy", reads=[f"modcol{l}"], writes=[name + "_B"],
               out=Bv[:, :, who], in_=mc[:, jsh * 8:(jsh + 1) * 8, who])


def load_weight_bf16(P, nc, src, W, wkey, nk, ncols, stages, skeys):
    i = 0
    for c0 in range(0, ncols, 512):
        c1 = min(ncols, c0 + 512)
        key = f"{wkey}_{c0 // 512}"
        for kc in range(nk):
            st = stages[i % len(stages)]; sk = skeys[i % len(stages)]
            P.dma("sp", st[:, :c1 - c0], src[kc * 128:(kc + 1) * 128, c0:c1], writes=[sk])
            eng = ("pool", "dve", "act")[i % 3]
            i += 1
            if eng == "act":
                P.call("act", "activation", reads=[sk], writes=[key], out=W[:, kc, c0:c1], in_=st[:, :c1 - c0], func=AF.Copy)
            else:
                P.call(eng, "tensor_copy", reads=[sk], writes=[key], out=W[:, kc, c0:c1], in_=st[:, :c1 - c0])


def wkeys(wkey, c0, c1):
    return [f"{wkey}_{c}" for c in range(c0 // 512, (c1 - 1) // 512 + 1)]


def phase_inproj(P, nc, K, l, xsrc):
    with ExitStack() as es:
        W = es.enter_context(nc.sbuf_tensor("p1_w", [128, 8, WCOLS], BF16))
        x0 = es.enter_context(nc.sbuf_tensor("p1_x0", [128, D], F32))
        x1 = es.enter_context(nc.sbuf_tensor("p1_x1", [128, D], F32))
        junk = es.enter_context(nc.sbuf_tensor("p1_junk", [128, D], F32))
        xn0 = es.enter_context(nc.sbuf_tensor("p1_xn0", [128, D], F32))
        xn1 = es.enter_context(nc.sbuf_tensor("p1_xn1", [128, D], F32))
        ss = es.enter_context(nc.sbuf_tensor("p1_ss", [128, 1], F32))
        rs = es.enter_context(nc.sbuf_tensor("p1_rs", [128, 1], F32))
        hT0 = es.enter_context(nc.sbuf_tensor("p1_hT0", [128, 8, 512], BF16))
        hT1 = es.enter_context(nc.sbuf_tensor("p1_hT1", [128, 8, 512], BF16))
        A = es.enter_context(nc.sbuf_tensor("p1_A", [128, 8, 2], F32))
        Bv = es.enter_context(nc.sbuf_tensor("p1_B", [128, 8, 2], F32))
        gcol = es.enter_context(nc.sbuf_tensor("p1_g", [128, 8], F32))
        tab = es.enter_context(nc.sbuf_tensor("p1_tab", [128, 4, 512], F32))
        st0 = es.enter_context(nc.sbuf_tensor("p1_st0", [128, 512], F32))
        st1 = es.enter_context(nc.sbuf_tensor("p1_st1", [128, 512], F32))
        tm1 = es.enter_context(nc.sbuf_tensor("p1_t1", [128, 512], F32))
        tm2 = es.enter_context(nc.sbuf_tensor("p1_t2", [128, 512], F32))
        ro0 = es.enter_context(nc.sbuf_tensor("p1_ro0", [128, 512], BF16))
        ro1 = es.enter_context(nc.sbuf_tensor("p1_ro1", [128, 512], BF16))
        vs0 = es.enter_context(nc.sbuf_tensor("p1_vs0", [128, 384], BF16))
        vs1 = es.enter_context(nc.sbuf_tensor("p1_vs1", [128, 384], BF16))
        pt = es.enter_context(nc.psum_tensor("p1_pt", [128, 8, 128], F32))
        pf0 = es.enter_context(nc.psum_tensor("p1_pf0", [128, 512], F32))
        pf1 = es.enter_context(nc.psum_tensor("p1_pf1", [128, 512], F32))
        pf2 = es.enter_context(nc.psum_tensor("p1_pf2", [128, 512], F32))
        pf3 = es.enter_context(nc.psum_tensor("p1_pf3", [128, 512], F32))
        pv = es.enter_context(nc.psum_tensor("p1_pv", [128, 512], F32))
        xb = [x0, x1]; xnb = [xn0, xn1]; hTb = [hT0, hT1]; stb = [st0, st1]; rob = [ro0, ro1]; vsb = [vs0, vs1]
        pfb = [pf0, pf1, pf2, pf3]
        load_weight_bf16(P, nc, K.w_in[l], W, "p1_w", 8, WCOLS, [st0, st1, tm1, tm2],
                         ["p1_st0", "p1_st1", "p1_t1", "p1_t2"])
        mod_AB(P, nc, K, l, 0, 1, 0, A, Bv, gcol, "p1")
        cnt = {"blk": 0, "pf": 0, "st": 0, "ro": 0}
        xnt = [[es.enter_context(nc.sbuf_tensor(f"p1_xnt{i}_{j}", [128, D], F32)) for j in range(4)] for i in range(2)]
        tl = tiles_512()

        def prepA(ti):
            t0, n = tl[ti]
            for blk in range(n // 128):
                tb = t0 + blk * 128
                i2 = cnt["blk"] % 2
                cnt["blk"] += 1
                xt = xb[i2]; xk = f"p1_x{i2}"
                P.dma("sp", xt[:], xsrc[tb:tb + 128, :], reads=["xres"], writes=[xk])
                norm_block(P, xt, xk, ss, rs, junk, xnt[ti % 2][blk], f"p1_xnt{ti % 2}_{blk}")

        def prepB(ti):
            t0, n = tl[ti]
            who = 1 if t0 < CTX else 0
            hT = hTb[ti % 2]; hk = f"p1_hT{ti % 2}"
            for blk in range(n // 128):
                xn = xnt[ti % 2][blk]; xnk = f"p1_xnt{ti % 2}_{blk}"
                for dc in range(8):
                    P.call("pe", "transpose", reads=[xnk, "ident"], writes=["p1_pt"], inc=(dc == 7),
                           out=pt[:, dc, :], in_=xn[:, dc * 128:(dc + 1) * 128], identity=K.ident[:])
                for dc in range(8):
                    P.call("act", "activation", reads=["p1_pt", "p1_A", "p1_B"], writes=[hk],
                           out=hT[:, dc, blk * 128:(blk + 1) * 128], in_=pt[:, dc, :], func=AF.Identity,
                           scale=A[:, dc, who:who + 1], bias=Bv[:, dc, who:who + 1])

        def proj(ti):
            t0, n = tl[ti]
            hT = hTb[ti % 2]; hk = f"p1_hT{ti % 2}"
            P.dma("sp", tab[:, :, :n], K.rope[:, :, t0:t0 + n].rearrange("f p t -> p f t"), writes=["p1_tab"])

            def fm(m):
                i4 = cnt["pf"] % 4
                cnt["pf"] += 1
                ps = pfb[i4]; pk = f"p1_pf{i4}"
                for kc in range(8):
                    P.call("pe", "matmul", reads=wkeys("p1_w", m * 128, (m + 1) * 128) + [hk], writes=[pk], inc=(kc == 7),
                           out=ps[:, :n], lhsT=W[:, kc, m * 128:(m + 1) * 128], rhs=hT[:, kc, :n],
                           start=(kc == 0), stop=(kc == 7))
                return ps, pk

            for m in range(8):
                ps, pk = fm(m)
                i2 = cnt["st"] % 2
                cnt["st"] += 1
                st = stb[i2]; sk = f"p1_st{i2}"
                P.call("act", "activation", reads=[pk], writes=[sk], out=st[:, :n], in_=ps[:, :n], func=AF.Copy)
                P.dma("sp", K.fm32[l][m * 128:(m + 1) * 128, t0:t0 + n], st[:, :n], reads=[sk], writes=["fm32"])
            pairs = [(8, 10, 0), (9, 11, 0), (12, 14, 0), (13, 15, 0), (16, 20, 2), (17, 21, 2), (18, 22, 2),
                     (19, 23, 2), (24, 25, 2)]
            for oi, (ma, mb, tbi) in enumerate(pairs):
                psa, pka = fm(ma)
                psb, pkb = fm(mb)
                i2 = cnt["ro"] % 2
                cnt["ro"] += 1
                ro = rob[i2]; rk = f"p1_ro{i2}"
                P.call("dve", "tensor_tensor", reads=[pka, "p1_tab"], writes=["p1_t1"],
                       out=tm1[:, :n], in0=psa[:, :n], in1=tab[:, tbi, :n], op=ALU.mult)
                P.call("dve", "tensor_tensor", reads=[pkb, "p1_tab"], writes=["p1_t2"],
                       out=tm2[:, :n], in0=psb[:, :n], in1=tab[:, tbi + 1, :n], op=ALU.mult)
                P.call("pool", "tensor_tensor", reads=["p1_t1", "p1_t2"], writes=[rk],
                       out=ro[:, :n], in0=tm1[:, :n], in1=tm2[:, :n], op=ALU.add)
                P.dma("sp", K.ropeT[l][oi * 128:(oi + 1) * 128, t0:t0 + n], ro[:, :n], reads=[rk], writes=["ropeT"])
            for blk in range(n // 128):
                tb = t0 + blk * 128
                vs = vsb[blk % 2]; vk = f"p1_vs{blk % 2}"
                for kc in range(8):
                    P.call("pe", "matmul", reads=wkeys("p1_w", NFM * 128, WCOLS) + [hk], writes=["p1_pv"], inc=(kc == 7),
                           out=pv[:, 0:384], lhsT=hT[:, kc, blk * 128:(blk + 1) * 128], rhs=W[:, kc, NFM * 128:WCOLS],
                           start=(kc == 0), stop=(kc == 7))
                P.call("act", "activation", reads=["p1_pv"], writes=[vk], out=vs[:], in_=pv[:, 0:384], func=AF.Copy)
                P.dma("sp", K.vtm[l][tb:tb + 128, :], vs[:], reads=[vk], writes=["vtm"])

        prepA(0)
        prepB(0)
        for ti in range(len(tl)):
            if ti + 1 < len(tl):
                prepA(ti + 1)
            proj(ti)
            if ti + 1 < len(tl):
                prepB(ti + 1)


def phase_rwprep(P, nc, K, l):
    with ExitStack() as es:
        sb = lambda name, shape, dt=F32: es.enter_context(nc.sbuf_tensor(name, shape, dt))
        pp = lambda name, shape, dt=F32: es.enter_context(nc.psum_tensor(name, shape, dt))
        cw = sb("p2_cw", [128, 6, 3]); kkc = sb("p2_kkc", [128, 2]); kac = sb("p2_kac", [128, 2])
        omka = sb("p2_omka", [128, 2]); w0c = sb("p2_w0", [128, 2, 2]); a0c = sb("p2_a0", [128, 2, 2])
        wup = sb("p2_wup", [64, 256]); aup = sb("p2_aup", [64, 256]); gup = sb("p2_gup", [128, 256])
        bones = sb("p2_bones", [128, 128])
        xr = sb("p2_xr", [128, 6, 514]); cv = sb("p2_cv", [128, 6, 512])
        lg = sb("p2_lg", [128, 512]); la = sb("p2_la", [64, 512]); thw = sb("p2_thw", [64, 512]); sgd = sb("p2_sgd", [128, 512])
        kkr = sb("p2_kkr", [128, 512]); sq = sb("p2_sq", [128, 512]); nr = sb("p2_nr", [128, 512])
        kk = sb("p2_kk", [128, 2, 512]); sgw = sb("p2_sgw", [128, 512])
        dec0 = sb("p2_dec0", [128, 512]); dec1 = sb("p2_dec1", [128, 512])
        av = sb("p2_a", [128, 512]); tt = sb("p2_t", [128, 512])
        NKf = sb("p2_NKf", [128, 2, 2, 512]); KDf = sb("p2_KDf", [128, 2, 2, 512])
        tm0 = sb("p2_tm0", [128, 7, 256]); tm1 = sb("p2_tm1", [128, 7, 256])
        pT = pp("p2_pT", [128, 12, 128]); pg_full = pp("p2_pg", [128, 512]); pg = pg_full[:, 0:256]
        px0 = pp("p2_px0", [128, 512]); px1 = pp("p2_px1", [128, 512]); px2 = pp("p2_px2", [128, 512])
        pxb = [px0, px1, px2]; decb = [dec0, dec1]; tmb = [tm0, tm1]
        NS = dict(allow_slow_non_contiguous=True)
        for j in range(3):
            P.dma("sp", cw[:, :, j], K.rw_conv[l, j].rearrange("(c p) -> p c", p=128), writes=["p2_cw"], **NS)
        P.dma("sp", kkc[:], K.rw_k_k[l].rearrange("(h p) -> p h", p=128), writes=["p2_kkc"], **NS)
        P.dma("sp", kac[:], K.rw_k_a[l].rearrange("(h p) -> p h", p=128), writes=["p2_kac"], **NS)
        for d in range(2):
            P.dma("sp", w0c[:, d, :], K.rw_w0[l, d].rearrange("(h p) -> p h", p=128), writes=["p2_w0"], **NS)
            P.dma("sp", a0c[:, d, :], K.rw_a0[l, d].rearrange("(h p) -> p h", p=128), writes=["p2_a0"], **NS)
        for d in range(2):
            P.dma("sp", wup[32 * d:32 * d + 32, :], K.rw_w_up[l, d], writes=["p2_wup"])
            P.dma("sp", aup[32 * d:32 * d + 32, :], K.rw_a_up[l, d], writes=["p2_aup"])
        P.dma("sp", gup[64:128, :], K.rw_g_up[l], writes=["p2_gup"])
        P.dma("sp", bones[:], K.bones_d, writes=["p2_bones"])
        P.call("dve", "tensor_scalar", reads=["p2_kac"], writes=["p2_omka"], out=omka[:], in0=kac[:],
               scalar1=-1.0, scalar2=1.0, op0=ALU.mult, op1=ALU.add)
        cnt = {"px": 0, "dec": 0, "tm": 0}

        def px():
            i = cnt["px"] % 3
            cnt["px"] += 1
            return pxb[i], f"p2_px{i}"

        for (t0, n) in tiles_512():
            s0, s1 = (0, CTX) if t0 < CTX else (CTX, T)
            src = lambda a, b: K.fm32[l][0:768, a:b].rearrange("(c p) t -> p c t", p=128)
            P.dma("sp", xr[:, :, 1:n + 1], src(t0, t0 + n), reads=["fm32"], writes=["p2_xr"])
            if t0 > s0:
                P.dma("sp", xr[:, :, 0:1], src(t0 - 1, t0), reads=["fm32"], writes=["p2_xr"], **NS)
            else:
                P.call("pool", "memset", writes=["p2_xr"], ap=xr[:, :, 0:1], constant=0.0)
            if t0 + n < s1:
                P.dma("sp", xr[:, :, n + 1:n + 2], src(t0 + n, t0 + n + 1), reads=["fm32"], writes=["p2_xr"], **NS)
            else:
                P.call("pool", "memset", writes=["p2_xr"], ap=xr[:, :, n + 1:n + 2], constant=0.0)
            P.dma("sp", lg[:, :n], K.fm32[l][768:896, t0:t0 + n], reads=["fm32"], writes=["p2_lg"])
            P.dma("sp", la[:, :n], K.fm32[l][896:960, t0:t0 + n], reads=["fm32"], writes=["p2_la"])
            for c in range(6):
                P.call("act", "activation", reads=["p2_xr", "p2_cw"], writes=["p2_cv"],
                       out=cv[:, c, :n], in_=xr[:, c, 1:n + 1], func=AF.Identity, scale=cw[:, c, 1:2])
                P.call("dve", "scalar_tensor_tensor", reads=["p2_xr", "p2_cw", "p2_cv"], writes=["p2_cv"],
                       out=cv[:, c, :n], in0=xr[:, c, 0:n], scalar=cw[:, c, 0:1], in1=cv[:, c, :n],
                       op0=ALU.mult, op1=ALU.add)
                P.call("dve", "scalar_tensor_tensor", reads=["p2_xr", "p2_cw", "p2_cv"], writes=["p2_cv"],
                       out=cv[:, c, :n], in0=xr[:, c, 2:n + 2], scalar=cw[:, c, 2:3], in1=cv[:, c, :n],
                       op0=ALU.mult, op1=ALU.add)
            P.call("act", "activation", reads=["p2_lg"], writes=["p2_thw"], out=thw[:, :n], in_=lg[0:64, :n], func=AF.Tanh)
            P.call("act", "activation", reads=["p2_lg"], writes=["p2_sgd"], out=sgd[64:128, :n], in_=lg[64:128, :n], func=AF.Sigmoid)
            for hp in range(2):
                kf = cv[:, 2 + hp, :n]
                P.call("dve", "tensor_scalar", reads=["p2_cv", "p2_kkc"], writes=["p2_kkr"], out=kkr[:, :n], in0=kf,
                       scalar1=kkc[:, hp:hp + 1], scalar2=None, op0=ALU.mult)
                P.call("act", "activation", reads=["p2_kkr"], writes=["p2_sq"], out=sq[:, :n], in_=kkr[:, :n], func=AF.Square)
                ps, pk = px()
                P.call("pe", "matmul", reads=["p2_bones", "p2_sq"], writes=[pk], out=ps[:, :n], lhsT=bones[:], rhs=sq[:, :n],
                       start=True, stop=True)
                P.call("act", "activation", reads=[pk], writes=["p2_nr"], out=nr[:, :n], in_=ps[:, :n], func=AF.Sqrt)
                P.call("dve", "tensor_scalar", reads=["p2_nr"], writes=["p2_nr"], out=nr[:, :n], in0=nr[:, :n],
                       scalar1=1e-12, scalar2=None, op0=ALU.max)
                P.call("dve", "reciprocal", reads=["p2_nr"], writes=["p2_nr"], out=nr[:, :n], in_=nr[:, :n])
                P.call("dve", "tensor_tensor", reads=["p2_kkr", "p2_nr"], writes=["p2_kk"], out=kk[:, hp, :n],
                       in0=kkr[:, :n], in1=nr[:, :n], op=ALU.mult)
                P.dma("sp", K.col_kr[l][hp][:, t0:t0 + n], kk[:, hp, :n], reads=["p2_kk"], writes=["col_kr"])
                P.dma("sp", K.col_kr[l][2 + hp][:, t0:t0 + n], cv[:, hp, :n], reads=["p2_cv"], writes=["col_kr"])
                for d in range(2):
                    ps, pk = px()
                    P.call("pe", "matmul", reads=["p2_wup", "p2_thw"], writes=[pk], out=ps[:, :n],
                           lhsT=wup[32 * d:32 * d + 32, hp * 128:(hp + 1) * 128], rhs=thw[32 * d:32 * d + 32, :n],
                           start=True, stop=True)
                    P.call("act", "activation", reads=[pk, "p2_w0"], writes=["p2_sgw"], out=sgw[:, :n], in_=ps[:, :n],
                           func=AF.Sigmoid, bias=w0c[:, d, hp:hp + 1])
                    i2 = cnt["dec"] % 2
                    cnt["dec"] += 1
                    dec = decb[i2]; dk = f"p2_dec{i2}"
                    P.call("act", "activation", reads=["p2_sgw"], writes=[dk], out=dec[:, :n], in_=sgw[:, :n],
                           func=AF.Exp, scale=-math.exp(-0.5))
                    P.dma("sp", K.col_w[l][2 * d + hp][:, t0:t0 + n], dec[:, :n], reads=[dk], writes=["col_w"])
                    ps, pk = px()
                    P.call("pe", "matmul", reads=["p2_aup", "p2_la"], writes=[pk], out=ps[:, :n],
                           lhsT=aup[32 * d:32 * d + 32, hp * 128:(hp + 1) * 128], rhs=la[32 * d:32 * d + 32, :n],
                           start=True, stop=True)
                    P.call("act", "activation", reads=[pk, "p2_a0"], writes=["p2_a"], out=av[:, :n], in_=ps[:, :n],
                           func=AF.Sigmoid, bias=a0c[:, d, hp:hp + 1])
                    P.call("dve", "tensor_scalar", reads=["p2_a", "p2_kac", "p2_omka"], writes=["p2_t"], out=tt[:, :n],
                           in0=av[:, :n], scalar1=kac[:, hp:hp + 1], scalar2=omka[:, hp:hp + 1], op0=ALU.mult, op1=ALU.add)
                    P.call("dve", "tensor_tensor", reads=["p2_cv", "p2_t"], writes=["p2_KDf"], out=KDf[:, d, hp, :n],
                           in0=kf, in1=tt[:, :n], op=ALU.mult)
                    P.call("dve", "scalar_tensor_tensor", reads=["p2_kk", "p2_a"], writes=["p2_NKf"], out=NKf[:, d, hp, :n],
                           in0=kk[:, hp, :n], scalar=-1.0, in1=av[:, :n], op0=ALU.mult, op1=ALU.mult)
                    P.dma("sp", K.col_nk[l][2 * d + hp][:, t0:t0 + n], NKf[:, d, hp, :n], reads=["p2_NKf"], writes=["col_nk"])
                    P.dma("sp", K.col_kd[l][2 * d + hp][:, t0:t0 + n], KDf[:, d, hp, :n], reads=["p2_KDf"], writes=["col_kd"])
            for blk in range(n // 128):
                tb = t0 + blk * 128
                bs = slice(blk * 128, (blk + 1) * 128)
                srcs = []
                for d in range(2):
                    for hp in range(2):
                        srcs.append((NKf[:, d, hp, bs], "p2_NKf"))
                for d in range(2):
                    for hp in range(2):
                        srcs.append((KDf[:, d, hp, bs], "p2_KDf"))
                for hp in range(2):
                    srcs.append((cv[:, 4 + hp, bs], "p2_cv"))
                for hp in range(2):
                    srcs.append((cv[:, hp, bs], "p2_cv"))
                for j, (sap, skey) in enumerate(srcs):
                    P.call("pe", "transpose", reads=[skey, "ident"], writes=["p2_pT"], out=pT[:, j, :], in_=sap,
                           identity=K.ident[:])
                P.call("pe", "matmul", reads=["p2_sgd", "p2_gup"], writes=["p2_pg"], out=pg, lhsT=sgd[64:128, bs], rhs=gup[64:128, :],
                       start=True, stop=True)
                i2 = cnt["tm"] % 2
                cnt["tm"] += 1
                tm = tmb[i2]; tk = f"p2_tm{i2}"
                for q in range(3):
                    eng = "act" if q == 1 else "dve"
                    o_ap = tm[:, 2 * q:2 * q + 2, :].rearrange("p a b -> p (a b)")
                    i_ap = pT[:, 4 * q:4 * q + 4, :].rearrange("p a b -> p (a b)")
                    if eng == "act":
                        P.call("act", "activation", reads=["p2_pT"], writes=[tk], out=o_ap, in_=i_ap, func=AF.Copy)
                    else:
                        P.call("dve", "tensor_copy", reads=["p2_pT"], writes=[tk], out=o_ap, in_=i_ap)
                P.call("act", "activation", reads=["p2_pg"], writes=[tk], out=tm[:, 6, :], in_=pg, func=AF.Copy)
                P.dma("sp", K.rw_tm[l][tb:tb + 128], tm[:], reads=[tk], writes=["rw_tm"])


TC = 32


def phase_scan(P, nc, K, l):
    nchunk = T // TC
    nctx = CTX // TC
    fwd = list(range(nchunk))
    bwd = list(range(nctx - 1, -1, -1)) + list(range(nchunk - 1, nctx - 1, -1))
    with ExitStack() as es:
        sb = lambda name, shape, dt=F32: es.enter_context(nc.sbuf_tensor(name, shape, dt))
        pp = lambda name, shape, dt=F32: es.enter_context(nc.psum_tensor(name, shape, dt))
        wcol = [sb(f"p3_w{b}", [128, 4, TC]) for b in range(2)]
        kkc = [sb(f"p3_kkc{b}", [128, 4, TC]) for b in range(2)]
        rc = [sb(f"p3_rc{b}", [128, 4, TC]) for b in range(2)]
        KKbd = [sb(f"p3_KK{b}", [128, 4, TC, 8]) for b in range(2)]
        Rbd = [sb(f"p3_R{b}", [128, 4, TC, 8]) for b in range(2)]
        LH = [[sb(f"p3_LH{b}{hp}", [128, TC, 128]) for hp in range(2)] for b in range(2)]
        Vr = [sb(f"p3_V{b}", [128, TC, 64]) for b in range(2)]
        Orows = [sb(f"p3_O{b}", [128, TC, 64]) for b in range(2)]
        SKV = sb("p3_SKV", [128, 64])
        S = sb("p3_S", [128, 4, 64])
        ps_sk = pp("p3_psk", [128, 512])[:, 0:64]; ps_o = pp("p3_po", [128, 512])[:, 0:64]
        ps_u = [pp(f"p3_pu{p}", [128, 512])[:, 0:64] for p in range(4)]
        for b in range(2):
            tiles = [(KKbd[b], f"p3_KK{b}"), (Rbd[b], f"p3_R{b}"), (Vr[b], f"p3_V{b}"), (Orows[b], f"p3_O{b}"),
                     (LH[b][0], f"p3_LH{b}"), (LH[b][1], f"p3_LH{b}")]
            for t_, nm in tiles:
                P.call("pool", "memset", writes=[nm + "_g0", nm + "_g1"], ap=t_[:], constant=0.0)
        P.call("pool", "memset", writes=["p3_S0", "p3_S1", "p3_S2", "p3_S3"], ap=S[:], constant=0.0)
        P.call("pool", "memset", writes=["p3_SKV_g0", "p3_SKV_g1"], ap=SKV[:], constant=0.0)
        row = lambda ap: ap.rearrange("(o t) k -> o t k", o=1)
        for c in range(nchunk):
            b = c % 2
            t0s = (fwd[c] * TC, bwd[c] * TC)
            for g in range(2):
                t0 = t0s[g]
                for hp in range(2):
                    p = 2 * g + hp
                    r0 = 64 * g + 4 * hp
                    P.dma("sp", wcol[b][:, p, :], K.col_w[l][p][:, t0:t0 + TC], reads=["col_w"], writes=[f"p3_w{b}_g{g}"])
                    P.dma("sp", kkc[b][:, p, :], K.col_kr[l][hp][:, t0:t0 + TC], reads=["col_kr"], writes=[f"p3_kkc{b}_g{g}"])
                    P.dma("sp", rc[b][:, p, :], K.col_kr[l][2 + hp][:, t0:t0 + TC], reads=["col_kr"], writes=[f"p3_rc{b}_g{g}"])
                    for j in range(2):
                        f0 = (2 * hp + j) * 64
                        P.dma("sp", LH[b][hp][r0 + j:r0 + j + 1, :, 64 * j:64 * j + 64],
                              row(K.rw_tm[l][t0:t0 + TC, g, f0:f0 + 64]), reads=["rw_tm"], writes=[f"p3_LH{b}_g{g}"])
                        P.dma("sp", LH[b][hp][r0 + 2 + j:r0 + 3 + j, :, 64 * j:64 * j + 64],
                              row(K.rw_tm[l][t0:t0 + TC, 2 + g, f0:f0 + 64]), reads=["rw_tm"], writes=[f"p3_LH{b}_g{g}"])
                        P.dma("sp", Vr[b][r0 + 2 + j:r0 + 3 + j, :, :],
                              row(K.rw_tm[l][t0:t0 + TC, 4, f0:f0 + 64]), reads=["rw_tm"], writes=[f"p3_V{b}_g{g}"])
                    for half in range(2):
                        hs = slice(64 * half, 64 * half + 64)
                        P.call("pool", "tensor_copy", reads=[f"p3_kkc{b}_g{g}"], writes=[f"p3_KK{b}_g{g}"],
                               out=KKbd[b][hs, p, :, 4 * hp + half], in_=kkc[b][hs, p, :])
                        P.call("pool", "tensor_copy", reads=[f"p3_rc{b}_g{g}"], writes=[f"p3_R{b}_g{g}"],
                               out=Rbd[b][hs, p, :, 4 * hp + half], in_=rc[b][hs, p, :])
            if getattr(K, "scan_dbg", False) and c == 0:
                for nm, t_, keys in (("dbg_LH0", LH[0][0], ["p3_LH0_g0", "p3_LH0_g1"]), ("dbg_LH1", LH[0][1], ["p3_LH0_g0", "p3_LH0_g1"]),
                                     ("dbg_V", Vr[0], ["p3_V0_g0", "p3_V0_g1"]), ("dbg_KK", KKbd[0], ["p3_KK0_g0", "p3_KK0_g1"]),
                                     ("dbg_R", Rbd[0], ["p3_R0_g0", "p3_R0_g1"]), ("dbg_w", wcol[0], ["p3_w0_g0", "p3_w0_g1"])):
                    shp = list(t_.shape)
                    o_ = nc.dram_tensor(nm, shp, F32, kind="ExternalOutput").ap()
                    P.dma("sp", o_, t_[:], reads=keys, writes=[nm])
            for i in range(TC):
                idxs = (i, TC - 1 - i)
                for g in range(2):
                    idx = idxs[g]
                    gs = slice(64 * g, 64 * g + 8)
                    for hp in range(2):
                        p = 2 * g + hp
                        P.call("pe", "matmul", reads=[f"p3_KK{b}_g{g}", f"p3_S{p}"], writes=[f"p3_psk_g{g}"],
                               out=ps_sk[gs, :], lhsT=KKbd[b][:, p, idx, :], rhs=S[:, p, :],
                               start=(hp == 0), stop=(hp == 1))
                    P.call("dve", "tensor_tensor", reads=[f"p3_psk_g{g}", f"p3_V{b}_g{g}"], writes=[f"p3_SKV_g{g}"],
                           out=SKV[gs, :], in0=ps_sk[gs, :], in1=Vr[b][gs, idx, :], op=ALU.add)
                    for hp in range(2):
                        p = 2 * g + hp
                        P.call("pe", "matmul", reads=[f"p3_LH{b}_g{g}", f"p3_SKV_g{g}"], writes=[f"p3_pu{p}"],
                               out=ps_u[p], lhsT=LH[b][hp][gs, idx, :], rhs=SKV[gs, :], start=True, stop=True)
                    for hp in range(2):
                        p = 2 * g + hp
                        P.call("dve", "scalar_tensor_tensor", reads=[f"p3_S{p}", f"p3_w{b}_g{g}", f"p3_pu{p}"],
                               writes=[f"p3_S{p}"], out=S[:, p, :], in0=S[:, p, :], scalar=wcol[b][:, p, idx:idx + 1],
                               in1=ps_u[p], op0=ALU.mult, op1=ALU.add)
                    for hp in range(2):
                        p = 2 * g + hp
                        P.call("pe", "matmul", reads=[f"p3_R{b}_g{g}", f"p3_S{p}"], writes=[f"p3_po_g{g}"],
                               out=ps_o[gs, :], lhsT=Rbd[b][:, p, idx, :], rhs=S[:, p, :],
                               start=(hp == 0), stop=(hp == 1))
                    P.call("act", "activation", reads=[f"p3_po_g{g}"], writes=[f"p3_O{b}_g{g}"],
                           out=Orows[b][gs, idx, :], in_=ps_o[gs, :], func=AF.Copy)
            if getattr(K, "scan_dbg", False) and c == 0:
                o_ = nc.dram_tensor("dbg_O", list(Orows[0].shape), F32, kind="ExternalOutput").ap()
                P.dma("sp", o_, Orows[0][:], reads=["p3_O0_g0", "p3_O0_g1"], writes=["dbg_O"])
                o_ = nc.dram_tensor("dbg_S", list(S.shape), F32, kind="ExternalOutput").ap()
                P.dma("sp", o_, S[:], reads=["p3_S0", "p3_S1", "p3_S2", "p3_S3"], writes=["dbg_S"])
                return
            for g in range(2):
                t0 = t0s[g]
                for hp in range(2):
                    r0 = 64 * g + 4 * hp
                    for j in range(2):
                        f0 = (2 * hp + j) * 64
                        P.dma("sp", row(K.o_tm[l][t0:t0 + TC, g, f0:f0 + 64]), Orows[b][r0 + j:r0 + j + 1, :, :],
                              reads=[f"p3_O{b}_g{g}"], writes=["o_tm"])


CH = 64


def phase_scan_chunked(P, nc, K, l):
    nchunk = T // CH
    nctx = CTX // CH
    order = [list(range(nchunk)), list(range(nctx - 1, -1, -1)) + list(range(nchunk - 1, nctx - 1, -1))]
    with ExitStack() as es:
        sb = lambda name, shape, dt=F32: es.enter_context(nc.sbuf_tensor(name, shape, dt))
        NBUF = 2
        names_in = ["w", "kk", "nk", "kd", "r"]
        tin = {nm: [[sb(f"c3_{nm}{p}{b}", [128, CH]) for b in range(NBUF)] for p in range(4)] for nm in names_in}
        Vtm = [[sb(f"c3_V{p}{b}", [128, 64]) for b in range(NBUF)] for p in range(4)]
        bdn = ["KH", "AH", "KD", "RH", "ANs", "KDs"]
        bd = {nm: [[sb(f"c3_{nm}{p}{b}", [128, 128]) for b in range(NBUF)] for p in range(4)] for nm in bdn}
        sqn = ["An", "AnT", "B", "Apn", "Bp", "X", "XT", "N", "ANtm", "KDtm"]
        sq = {nm: [[sb(f"c3_{nm}{p}{b}", [128, 128]) for b in range(NBUF)] for p in range(4)] for nm in sqn}
        X2 = [sb(f"c3_X2_{p}", [128, 128]) for p in range(4)]; X2T = [sb(f"c3_X2T_{p}", [128, 128]) for p in range(4)]
        Pc = [sb(f"c3_Pc{p}", [128, CH]) for p in range(4)]; Pm1 = [sb(f"c3_Pm{p}", [128, CH]) for p in range(4)]
        rP = [sb(f"c3_rP{p}", [128, CH]) for p in range(4)]; tmpv = [sb(f"c3_tv{p}", [128, CH]) for p in range(4)]
        PCc = [[sb(f"c3_PC{p}{b}", [128, 1]) for b in range(NBUF)] for p in range(4)]
        zer = sb("c3_zero", [128, CH])
        S = [sb(f"c3_S{p}", [128, 64]) for p in range(4)]
        Rt = [sb(f"c3_Rt{p}", [128, 64]) for p in range(4)]; Ut = [sb(f"c3_Ut{p}", [128, 64]) for p in range(4)]
        Ot = [[sb(f"c3_Ot{p}{b}", [128, 64]) for b in range(NBUF)] for p in range(4)]
        msk = sb("c3_msk", [128, 4, 128])
        psb = [es.enter_context(nc.psum_tensor(f"c3_ps{i}", [128, 512], F32)) for i in range(8)]
        pcnt = {"i": 0}

        def ps():
            i = pcnt["i"] % 8
            pcnt["i"] += 1
            return psb[i], f"c3_ps{i}"

        P.dma("sp", msk[:], K.cmask_d.rearrange("m a b -> a m b"), writes=["c3_msk"])
        P.call("pool", "memset", writes=["c3_zero"], ap=zer[:], constant=0.0)
        for p in range(4):
            P.call("pool", "memset", writes=[f"c3_S{p}"], ap=S[p][:], constant=0.0)
            for b in range(NBUF):
                for nm in bdn:
                    P.call("pool", "memset", writes=[f"c3_{nm}{p}{b}"], ap=bd[nm][p][b][:], constant=0.0)

        def mm(out, lhsT, rhs, reads, pk, start=True, stop=True, fast=False, inc=True):
            P.call("pe", "matmul", reads=reads, writes=[pk], inc=inc, out=out, lhsT=lhsT, rhs=rhs, start=start, stop=stop)

        def stage_a(ci, p):
            b = ci % NBUF
            d = p // 2; hp = p % 2
            t0 = order[d][ci] * CH
            k = lambda nm: f"c3_{nm}{p}{b}"
            srcs = {"w": K.col_w[l][p], "kk": K.col_kr[l][hp], "nk": K.col_nk[l][p], "kd": K.col_kd[l][p], "r": K.col_kr[l][2 + hp]}
            rkeys = {"w": "col_w", "kk": "col_kr", "nk": "col_nk", "kd": "col_kd", "r": "col_kr"}
            for nm in names_in:
                P.dma("sp", tin[nm][p][b][:], srcs[nm][:, t0:t0 + CH], reads=[rkeys[nm]], writes=[k(nm)])
            for j in range(2):
                f0 = (2 * hp + j) * 64
                P.dma("sp", Vtm[p][b][64 * j:64 * j + 64, :], K.rw_tm[l][t0:t0 + CH, 4, f0:f0 + 64], reads=["rw_tm"], writes=[k("V")])
            w_ = tin["w"][p][b]
            P.call("dve", "tensor_tensor_scan", reads=[k("w"), "c3_zero"], writes=[f"c3_Pc{p}"], out=Pc[p][:], data0=w_[:], data1=zer[:],
                   initial=1.0, op0=ALU.mult, op1=ALU.add)
            P.call("pool", "memset", writes=[f"c3_Pm{p}"], ap=Pm1[p][:, 0:1], constant=1.0)
            P.call("pool", "tensor_copy", reads=[f"c3_Pc{p}"], writes=[f"c3_Pm{p}"], out=Pm1[p][:, 1:CH], in_=Pc[p][:, 0:CH - 1])
            P.call("act", "activation", reads=[f"c3_Pc{p}"], writes=[k("PC")], out=PCc[p][b][:], in_=Pc[p][:, CH - 1:CH], func=AF.Copy)
            if d == 1:
                P.call("dve", "reciprocal", reads=[f"c3_Pm{p}"], writes=[f"c3_tv{p}"], out=tmpv[p][:], in_=Pm1[p][:])
                P.call("dve", "reciprocal", reads=[f"c3_Pc{p}"], writes=[f"c3_rP{p}"], out=rP[p][:], in_=Pc[p][:])
                P.call("dve", "tensor_scalar", reads=[f"c3_tv{p}", k("PC")], writes=[f"c3_Pc{p}"], out=Pc[p][:], in0=tmpv[p][:],
                       scalar1=PCc[p][b][:, 0:1], scalar2=None, op0=ALU.mult)
                P.call("dve", "tensor_scalar", reads=[f"c3_rP{p}", k("PC")], writes=[f"c3_Pm{p}"], out=Pm1[p][:], in0=rP[p][:],
                       scalar1=PCc[p][b][:, 0:1], scalar2=None, op0=ALU.mult)
            P.call("dve", "reciprocal", reads=[f"c3_Pc{p}"], writes=[f"c3_rP{p}"], out=rP[p][:], in_=Pc[p][:])
            for j in range(2):
                hs = slice(64 * j, 64 * j + 64)
                eng = "dve" if j == 0 else "pool"
                P.call(eng, "tensor_tensor", reads=[k("kk"), f"c3_Pm{p}"], writes=[k("KH")], out=bd["KH"][p][b][hs, hs],
                       in0=tin["kk"][p][b][hs, :], in1=Pm1[p][hs, :], op=ALU.mult)
                P.call(eng, "tensor_tensor", reads=[k("nk"), f"c3_rP{p}"], writes=[k("AH")], out=bd["AH"][p][b][hs, hs],
                       in0=tin["nk"][p][b][hs, :], in1=rP[p][hs, :], op=ALU.mult)
                P.call(eng, "tensor_tensor", reads=[k("kd"), f"c3_rP{p}"], writes=[k("KD")], out=bd["KD"][p][b][hs, hs],
                       in0=tin["kd"][p][b][hs, :], in1=rP[p][hs, :], op=ALU.mult)
                P.call(eng, "tensor_tensor", reads=[k("r"), f"c3_Pc{p}"], writes=[k("RH")], out=bd["RH"][p][b][hs, hs],
                       in0=tin["r"][p][b][hs, :], in1=Pc[p][hs, :], op=ALU.mult)
                P.call("dve", "tensor_scalar", reads=[k("AH"), k("PC")], writes=[k("ANs")], out=bd["ANs"][p][b][hs, hs],
                       in0=bd["AH"][p][b][hs, hs], scalar1=PCc[p][b][hs, 0:1], scalar2=None, op0=ALU.mult)
                P.call("dve", "tensor_scalar", reads=[k("KD"), k("PC")], writes=[k("KDs")], out=bd["KDs"][p][b][hs, hs],
                       in0=bd["KD"][p][b][hs, hs], scalar1=PCc[p][b][hs, 0:1], scalar2=None, op0=ALU.mult)

        for p in range(4):
            stage_a(0, p)
        for ci in range(nchunk):
            b = ci % NBUF
            kf = lambda p, nm: f"c3_{nm}{p}{b}"
            for p in range(4):
                d = p // 2
                k = lambda nm, p=p: f"c3_{nm}{p}{b}"
                for nm_s, nm_d in (("ANs", "ANtm"), ("KDs", "KDtm")):
                    pt_, pk = ps()
                    P.call("pe", "transpose", reads=[k(nm_s), "ident"], writes=[pk], out=pt_[:, 0:128], in_=bd[nm_s][p][b][:], identity=K.ident[:])
                    P.call("act", "activation", reads=[pk], writes=[k(nm_d)], out=sq[nm_d][p][b][:], in_=pt_[:, 0:128], func=AF.Copy)
                ms, mi = (0, 1) if d == 0 else (2, 3)
                for (dst, lh, rh, mk) in (("An", "AH", "KH", ms), ("B", "KD", "KH", ms), ("Apn", "AH", "RH", mi), ("Bp", "KD", "RH", mi)):
                    pt_, pk = ps()
                    mm(pt_[:, 0:128], bd[lh][p][b][:], bd[rh][p][b][:], [k(lh), k(rh)], pk)
                    P.call("dve", "tensor_tensor", reads=[pk, "c3_msk"], writes=[k(dst)], out=sq[dst][p][b][:], in0=pt_[:, 0:128],
                           in1=msk[:, mk, :], op=ALU.mult)
            for p in range(4):
                k = lambda nm, p=p: f"c3_{nm}{p}{b}"
                pt_, pk = ps()
                P.call("pe", "transpose", reads=[k("An"), "ident"], writes=[pk], out=pt_[:, 0:128], in_=sq["An"][p][b][:], identity=K.ident[:])
                P.call("act", "activation", reads=[pk], writes=[k("AnT")], out=sq["AnT"][p][b][:], in_=pt_[:, 0:128], func=AF.Copy)
                P.call("dve", "tensor_tensor", reads=[k("An"), "ident"], writes=[k("N")], out=sq["N"][p][b][:], in0=sq["An"][p][b][:], in1=K.ident[:], op=ALU.add)
            cur = {p: (sq["An"][p][b], sq["AnT"][p][b], kf(p, "An"), kf(p, "AnT")) for p in range(4)}
            nround = 5
            for rnd in range(nround):
                lastr = rnd == nround - 1
                nxt = {}
                for p in range(4):
                    Xc, XTc, xk, xtk = cur[p]
                    pt_, pk = ps()
                    mm(pt_[:, 0:128], Xc[:], XTc[:], [xk, xtk], pk)
                    x2t = X2T[p] if rnd % 2 == 0 else sq["XT"][p][b]
                    x2tk = f"c3_X2T_{p}" if rnd % 2 == 0 else kf(p, "XT")
                    P.call("act", "activation", reads=[pk], writes=[x2tk], out=x2t[:], in_=pt_[:, 0:128], func=AF.Copy)
                    nxt[p] = (x2t, x2tk)
                for p in range(4):
                    x2t, x2tk = nxt[p]
                    Nt = sq["N"][p][b]
                    pt3, pk3 = ps()
                    mm(pt3[:, 0:128], x2t[:], Nt[:], [x2tk, kf(p, "N")], pk3)
                    P.call("dve", "tensor_tensor", reads=[pk3, kf(p, "N")], writes=[kf(p, "N")], out=Nt[:], in0=pt3[:, 0:128], in1=Nt[:], op=ALU.add)
                    if not lastr:
                        pt2, pk2 = ps()
                        P.call("pe", "transpose", reads=[x2tk, "ident"], writes=[pk2], out=pt2[:, 0:128], in_=x2t[:], identity=K.ident[:])
                        x2 = X2[p] if rnd % 2 == 0 else sq["X"][p][b]
                        x2k = f"c3_X2_{p}" if rnd % 2 == 0 else kf(p, "X")
                        P.call("act", "activation", reads=[pk2], writes=[x2k], out=x2[:], in_=pt2[:, 0:128], func=AF.Copy)
                        cur[p] = (x2, x2t, x2k, x2tk)
                if rnd < 4 and ci + 1 < nchunk:
                    stage_a(ci + 1, rnd)
            if ci == nchunk - 1:
                P.mark(f"L{l}_scan_pre_last")
            kf = lambda p, nm: f"c3_{nm}{p}{b}"
            held = {}
            for p in range(4):
                pt_, pk = ps()
                mm(pt_[:, 0:64], bd["KH"][p][b][:], S[p][:], [kf(p, "KH"), f"c3_S{p}"], pk, start=True, stop=False, inc=False)
                mm(pt_[:, 0:64], sq["B"][p][b][:], Vtm[p][b][:], [kf(p, "B"), kf(p, "V")], pk, start=False, stop=True)
                P.call("act", "activation", reads=[pk], writes=[f"c3_Rt{p}"], out=Rt[p][:], in_=pt_[:, 0:64], func=AF.Copy)
            for p in range(4):
                pt2, pk2 = ps()
                mm(pt2[:, 0:64], sq["N"][p][b][:], Rt[p][:], [kf(p, "N"), f"c3_Rt{p}"], pk2)
                P.call("dve", "tensor_copy", reads=[pk2], writes=[f"c3_Ut{p}"], out=Ut[p][:], in_=pt2[:, 0:64])
            for p in range(4):
                sk_ = f"c3_S{p}"
                pt3, pk3 = ps()
                mm(pt3[:, 0:64], bd["RH"][p][b][:], S[p][:], [kf(p, "RH"), sk_], pk3, start=True, stop=False, inc=False)
                mm(pt3[:, 0:64], sq["Apn"][p][b][:], Ut[p][:], [kf(p, "Apn"), f"c3_Ut{p}"], pk3, start=False, stop=False, inc=False)
                mm(pt3[:, 0:64], sq["Bp"][p][b][:], Vtm[p][b][:], [kf(p, "Bp"), kf(p, "V")], pk3, start=False, stop=True)
                P.call("act", "activation", reads=[pk3], writes=[kf(p, "Ot")], out=Ot[p][b][:], in_=pt3[:, 0:64], func=AF.Copy)
                pt4, pk4 = ps()
                mm(pt4[:, 0:64], sq["ANtm"][p][b][:], Ut[p][:], [kf(p, "ANtm"), f"c3_Ut{p}"], pk4, start=True, stop=False, inc=False)
                mm(pt4[:, 0:64], sq["KDtm"][p][b][:], Vtm[p][b][:], [kf(p, "KDtm"), kf(p, "V")], pk4, start=False, stop=True)
                P.call("dve", "scalar_tensor_tensor", reads=[sk_, kf(p, "PC"), pk4], writes=[sk_], out=S[p][:], in0=S[p][:],
                       scalar=PCc[p][b][:, 0:1], in1=pt4[:, 0:64], op0=ALU.mult, op1=ALU.add)
            for p in range(4):
                d = p // 2; hp = p % 2
                t0 = order[d][ci] * CH
                for j in range(2):
                    f0 = (2 * hp + j) * 64
                    P.dma("sp", K.o_tm[l][t0:t0 + CH, d, f0:f0 + 64], Ot[p][b][64 * j:64 * j + 64, :], reads=[kf(p, "Ot")], writes=["o_tm"])


def phase_readout(P, nc, K, l):
    with ExitStack() as es:
        sb = lambda name, shape, dt=F32: es.enter_context(nc.sbuf_tensor(name, shape, dt))
        lng = sb("p4_lng", [128, 256]); lnb = sb("p4_lnb", [128, 256]); rkr = sb("p4_rkr", [128, 256])
        o2 = [sb(f"p4_o2{b}", [128, 2, 256]) for b in range(2)]
        tm = [sb(f"p4_tm{b}", [128, 7, 256]) for b in range(2)]
        o = sb("p4_o", [128, 4, 64]); xc = sb("p4_xc", [128, 4, 64]); sq = sb("p4_sq", [128, 4, 64])
        mu = sb("p4_mu", [128, 4]); var = sb("p4_var", [128, 4]); bs = sb("p4_bs", [128, 4])
        kds = sb("p4_kds", [128, 256]); y = [sb(f"p4_y{b}", [128, 256]) for b in range(2)]
        P.dma("sp", lng[:], K.rw_ln_g[l].partition_broadcast(128), writes=["p4_lng"])
        P.dma("sp", lnb[:], K.rw_ln_b[l].partition_broadcast(128), writes=["p4_lnb"])
        P.dma("sp", rkr[:], K.rw_r_k[l].partition_broadcast(128), writes=["p4_rkr"])
        f3 = lambda ap: ap.rearrange("p (h n) -> p h n", h=4)
        f2 = lambda ap: ap.rearrange("p h n -> p (h n)")
        bc = lambda ap: ap.unsqueeze(2).broadcast_to([128, 4, 64])
        for bi in range(T // 128):
            tb = bi * 128
            b = bi % 2
            ok = f"p4_o2{b}"; tk = f"p4_tm{b}"; yk = f"p4_y{b}"
            P.dma("sp", o2[b][:], K.o_tm[l][tb:tb + 128], reads=["o_tm"], writes=[ok])
            P.dma("sp", tm[b][:], K.rw_tm[l][tb:tb + 128], reads=["rw_tm"], writes=[tk])
            P.call("dve", "tensor_tensor", reads=[ok], writes=["p4_o"], out=f2(o[:]), in0=o2[b][:, 0, :], in1=o2[b][:, 1, :], op=ALU.add)
            P.call("dve", "tensor_reduce", reads=["p4_o"], writes=["p4_mu"], out=mu[:], in_=o[:], axis=AX.X, op=ALU.add)
            P.call("dve", "tensor_scalar", reads=["p4_mu"], writes=["p4_mu"], out=mu[:], in0=mu[:], scalar1=1.0 / 64, scalar2=None, op0=ALU.mult)
            P.call("dve", "tensor_tensor", reads=["p4_o", "p4_mu"], writes=["p4_xc"], out=xc[:], in0=o[:], in1=bc(mu[:]), op=ALU.subtract)
            P.call("pool", "tensor_tensor", reads=["p4_xc"], writes=["p4_sq"], out=sq[:], in0=xc[:], in1=xc[:], op=ALU.mult)
            P.call("dve", "tensor_reduce", reads=["p4_sq"], writes=["p4_var"], out=var[:], in_=sq[:], axis=AX.X, op=ALU.add)
            P.call("dve", "tensor_scalar", reads=["p4_var"], writes=["p4_var"], out=var[:], in0=var[:], scalar1=1.0 / 64, scalar2=64e-5,
                   op0=ALU.mult, op1=ALU.add)
            P.call("act", "activation", reads=["p4_var"], writes=["p4_var"], out=var[:], in_=var[:], func=AF.Sqrt)
            P.call("dve", "reciprocal", reads=["p4_var"], writes=["p4_var"], out=var[:], in_=var[:])
            P.call("dve", "tensor_tensor", reads=["p4_xc", "p4_var"], writes=["p4_xc"], out=xc[:], in0=xc[:], in1=bc(var[:]), op=ALU.mult)
            P.call("dve", "tensor_tensor", reads=["p4_xc", "p4_lng"], writes=["p4_xc"], out=f2(xc[:]), in0=f2(xc[:]), in1=lng[:], op=ALU.mult)
            P.call("dve", "tensor_tensor", reads=["p4_xc", "p4_lnb"], writes=["p4_xc"], out=f2(xc[:]), in0=f2(xc[:]), in1=lnb[:], op=ALU.add)
            P.call("pool", "tensor_tensor", reads=[tk], writes=["p4_kds"], out=kds[:], in0=tm[b][:, 2, :], in1=tm[b][:, 3, :], op=ALU.add)
            P.call("pool", "tensor_tensor", reads=[tk, "p4_kds"], writes=["p4_kds"], out=kds[:], in0=kds[:], in1=tm[b][:, 5, :], op=ALU.mult)
            P.call("pool", "tensor_tensor", reads=["p4_kds", "p4_rkr"], writes=["p4_kds"], out=kds[:], in0=kds[:], in1=rkr[:], op=ALU.mult)
            P.call("dve", "tensor_reduce", reads=["p4_kds"], writes=["p4_bs"], out=bs[:], in_=f3(kds[:]), axis=AX.X, op=ALU.add)
            P.call("dve", "tensor_tensor", reads=[tk, "p4_bs"], writes=["p4_sq"], out=sq[:], in0=f3(tm[b][:, 4, :]), in1=bc(bs[:]), op=ALU.mult)
            P.call("dve", "tensor_tensor", reads=["p4_sq", "p4_xc"], writes=["p4_sq"], out=sq[:], in0=sq[:], in1=xc[:], op=ALU.add)
            P.call("dve", "tensor_tensor", reads=["p4_sq", tk], writes=[yk], out=y[b][:], in0=f2(sq[:]), in1=tm[b][:, 6, :], op=ALU.mult)
            P.dma("sp", K.ytm[l][tb:tb + 128, 0:256], y[b][:], reads=[yk], writes=["ytm"])


def phase_attn(P, nc, K, l):
    lam_init = 0.8 - 0.6 * math.exp(-0.3 * l)
    NBLK = T // 128
    with ExitStack() as es:
        sb = lambda name, shape, dt=F32: es.enter_context(nc.sbuf_tensor(name, shape, dt))
        pp = lambda name: es.enter_context(nc.psum_tensor(name, [128, 512], F32))
        kdf = sb("p5_kdf", [128, 2, T], BF16)
        Vaug = sb("p5_V", [128, NBLK, 6, 65], BF16)
        Kd = [sb(f"p5_Kd{g}", [128, T], BF16) for g in range(2)]
        Qm = [[sb(f"p5_Qm{b}_{u}", [128, 512], BF16) for u in range(8)] for b in range(2)]
        Eb = [sb(f"p5_E{i}", [128, 512], BF16) for i in range(3)]
        oT = [sb(f"p5_oT{m}", [65, 512]) for m in range(2)]
        rz = [sb(f"p5_rz{m}", [64, 512]) for m in range(2)]
        dd = sb("p5_dd", [64, 512]); sqt = sb("p5_sq", [64, 512]); rs = sb("p5_rs", [64, 512])
        ybt = [sb(f"p5_yb{i}", [64, 512], BF16) for i in range(2)]
        sel = sb("p5_sel", [65, 64]); ones64 = sb("p5_ones", [64, 64])
        lamt = sb("p5_lamt", [64, 128]); lp = sb("p5_lp", [64, 64]); e12 = sb("p5_e12", [64, 2]); neglam = sb("p5_nl", [64, 1])
        gdf = sb("p5_gdf", [64, 1])
        msk = sb("p5_msk", [128, 2, 128])
        exps = sb("p5_exps", [128, 8])
        qg = [sb(f"p5_qg{b}", [128, 4, 128], BF16) for b in range(2)]
        Eg = [sb(f"p5_Eg{b}", [128, 5, 128], BF16) for b in range(2)]
        zt = sb("p5_zt", [128, 1]); ycst = [sb(f"p5_yc{b}", [128, 512]) for b in range(2)]
        ps_s = [pp(f"p5_ps{i}") for i in range(3)]
        ps_o = [pp(f"p5_po{m}") for m in range(2)]
        ps_z = [pp(f"p5_pz{m}") for m in range(2)]
        ps_g = pp("p5_pg")
        NS = dict(allow_slow_non_contiguous=True)
        for c in range(2):
            P.dma("sp", kdf[:, c, :], K.ropeT[l][256 + c * 128:256 + (c + 1) * 128, :], reads=["ropeT"], writes=["p5_kdf"])
        for g in range(2):
            for half in range(2):
                P.dma("sp", Kd[g][64 * half:64 * half + 64, :], K.ropeT[l][1024 + 64 * g:1024 + 64 * g + 64, :],
                      reads=["ropeT"], writes=[f"p5_Kd{g}"])
        P.call("pool", "memset", writes=["p5_V"], ap=Vaug[:], constant=1.0)
        for kb in range(NBLK):
            P.dma("sp", Vaug[:, kb, :, 0:64], K.vtm[l][kb * 128:(kb + 1) * 128, :].rearrange("p (h d) -> p h d", h=6),
                  reads=["vtm"], writes=["p5_V"])
        for b in range(2):
            for u in range(8):
                P.call("pool", "memset", writes=[f"p5_Qm{b}"], ap=Qm[b][u][:], constant=0.0)
        P.dma("sp", sel[:], K.sel_d, writes=["p5_sel"])
        P.dma("sp", ones64[:], K.bones_d[0:64, 0:64], writes=["p5_ones"])
        P.dma("sp", msk[:], K.msk_d.rearrange("m a b -> a m b"), writes=["p5_msk"])
        P.dma("sp", lamt[:], K.df_lambda[l].partition_broadcast(64), writes=["p5_lamt"])
        P.dma("sp", gdf[:], K.df_norm_g[l].rearrange("(p o) -> p o", o=1), writes=["p5_gdf"], **NS)
        P.dma("sp", exps[:], K.gq_sink[l].partition_broadcast(128), writes=["p5_exps"])
        P.call("act", "activation", reads=["p5_exps"], writes=["p5_exps"], out=exps[:], in_=exps[:], func=AF.Exp)
        P.call("dve", "tensor_scalar", reads=["p5_gdf"], writes=["p5_gdf"], out=gdf[:], in0=gdf[:], scalar1=1.0 - lam_init,
               scalar2=None, op0=ALU.mult)
        for i in range(2):
            P.call("dve", "tensor_tensor", reads=["p5_lamt"], writes=["p5_lp"], out=lp[:, 32 * i:32 * i + 32],
                   in0=lamt[:, 64 * i:64 * i + 32], in1=lamt[:, 64 * i + 32:64 * i + 64], op=ALU.mult)
        P.call("dve", "tensor_reduce", reads=["p5_lp"], writes=["p5_e12"], out=e12[:],
               in_=lp[:].rearrange("p (a b) -> p a b", a=2), axis=AX.X, op=ALU.add)
        P.call("act", "activation", reads=["p5_e12"], writes=["p5_e12"], out=e12[:], in_=e12[:], func=AF.Exp)
        P.call("dve", "tensor_tensor", reads=["p5_e12"], writes=["p5_nl"], out=neglam[:], in0=e12[:, 1:2], in1=e12[:, 0:1], op=ALU.subtract)
        P.call("dve", "tensor_scalar", reads=["p5_nl"], writes=["p5_nl"], out=neglam[:], in0=neglam[:], scalar1=-lam_init,
               scalar2=None, op0=ALU.add)
        cnt = {"s": 0, "yb": 0}
        for ti, (t0, nq) in enumerate(tiles_512()):
            b = ti % 2
            kbs = list(range(0, CTX // 128)) if t0 < CTX else list(range(NBLK))
            for u in range(8):
                r0 = 32 * (u % 4)
                P.dma("sp", Qm[b][u][r0:r0 + 32, :nq], K.ropeT[l][(u // 4) * 128 + r0:(u // 4) * 128 + r0 + 32, t0:t0 + nq],
                      reads=["ropeT"], writes=[f"p5_Qm{b}"])
            seq = [(h, m, ki, kb) for h in range(4) for m in range(2) for ki, kb in enumerate(kbs)]

            def emit_S(i):
                h, m, ki, kb = seq[i]
                u = 2 * h + m
                i3 = i % 3
                P.call("pe", "matmul", reads=["p5_kdf", f"p5_Qm{b}"], writes=[f"p5_ps{i3}"], out=ps_s[i3][:, :nq],
                       lhsT=kdf[:, u // 4, kb * 128:(kb + 1) * 128], rhs=Qm[b][u][:, :nq], start=True, stop=True)
                P.call("act", "activation", reads=[f"p5_ps{i3}"], writes=[f"p5_E{i3}"], out=Eb[i3][:, :nq],
                       in_=ps_s[i3][:, :nq], func=AF.Exp, scale=32 ** -0.5)

            def emit_PV(i):
                h, m, ki, kb = seq[i]
                i3 = i % 3
                P.call("pe", "matmul", reads=["p5_V", f"p5_E{i3}"], writes=[f"p5_po{m}"], out=ps_o[m][0:65, :nq],
                       lhsT=Vaug[:, kb, h, :], rhs=Eb[i3][:, :nq], start=(ki == 0), stop=(ki == len(kbs) - 1))

            def post(h):
                for m in range(2):
                    P.call("act", "activation", reads=[f"p5_po{m}"], writes=[f"p5_oT{m}"], out=oT[m][:, :nq], in_=ps_o[m][0:65, :nq],
                           func=AF.Copy)
                    P.call("pe", "matmul", reads=["p5_sel", f"p5_oT{m}"], writes=[f"p5_pz{m}"], out=ps_z[m][0:64, :nq],
                           lhsT=sel[:], rhs=oT[m][:, :nq], start=True, stop=True)
                    P.call("dve", "reciprocal", reads=[f"p5_pz{m}"], writes=[f"p5_rz{m}"], out=rz[m][:, :nq], in_=ps_z[m][0:64, :nq])
                    P.call("dve", "tensor_tensor", reads=[f"p5_oT{m}", f"p5_rz{m}"], writes=[f"p5_rz{m}"], out=rz[m][:, :nq],
                           in0=oT[m][0:64, :nq], in1=rz[m][:, :nq], op=ALU.mult)
                P.call("dve", "scalar_tensor_tensor", reads=["p5_rz0", "p5_rz1", "p5_nl"], writes=["p5_dd"], out=dd[:, :nq],
                       in0=rz[1][:, :nq], scalar=neglam[:, 0:1], in1=rz[0][:, :nq], op0=ALU.mult, op1=ALU.add)
                P.call("act", "activation", reads=["p5_dd"], writes=["p5_sq"], out=sqt[:, :nq], in_=dd[:, :nq], func=AF.Square)
                P.call("pe", "matmul", reads=["p5_ones", "p5_sq"], writes=["p5_pz0"], out=ps_z[0][0:64, :nq], lhsT=ones64[:],
                       rhs=sqt[:, :nq], start=True, stop=True)
                P.call("dve", "tensor_scalar", reads=["p5_pz0"], writes=["p5_rs"], out=rs[:, :nq], in0=ps_z[0][0:64, :nq],
                       scalar1=1.0 / 64, scalar2=EPS, op0=ALU.mult, op1=ALU.add)
                P.call("act", "activation", reads=["p5_rs"], writes=["p5_rs"], out=rs[:, :nq], in_=rs[:, :nq], func=AF.Sqrt)
                P.call("dve", "reciprocal", reads=["p5_rs"], writes=["p5_rs"], out=rs[:, :nq], in_=rs[:, :nq])
                i2 = cnt["yb"] % 2
                cnt["yb"] += 1
                P.call("dve", "scalar_tensor_tensor", reads=["p5_dd", "p5_gdf", "p5_rs"], writes=[f"p5_yb{i2}"], out=ybt[i2][:, :nq],
                       in0=dd[:, :nq], scalar=gdf[:, 0:1], in1=rs[:, :nq], op0=ALU.mult, op1=ALU.mult)
                P.dma("sp", K.yT[l][256 + 64 * h:256 + 64 * h + 64, t0:t0 + nq], ybt[i2][:, :nq], reads=[f"p5_yb{i2}"], writes=["yT"])

            LOOK = 2
            for i in range(min(LOOK, len(seq))):
                emit_S(i)
            for i in range(len(seq)):
                if i + LOOK < len(seq):
                    emit_S(i + LOOK)
                emit_PV(i)
                h, m, ki, kb = seq[i]
                if m == 1 and ki == len(kbs) - 1:
                    post(h)
        P.mark(f"L{l}_diff_end")
        for tb in range(NBLK):
            b = tb % 2
            P.dma("sp", qg[b][:], K.ropeT[l][512:1024, tb * 128:(tb + 1) * 128].rearrange("(c p) t -> p c t", p=128),
                  reads=["ropeT"], writes=[f"p5_qg{b}"])
            keyblocks = [0, 1]
            if tb >= 2:
                keyblocks += [kb for kb in (tb - 1, tb, tb + 1) if 2 <= kb < NBLK]
            nk = len(keyblocks)
            psA = [(ps_s[0], "p5_ps0", ps_s[1], "p5_ps1"), (ps_s[2], "p5_ps2", ps_o[0], "p5_po0")]
            psG = [(ps_g, "p5_pg"), (ps_o[1], "p5_po1")]

            def g_scores(hd):
                g = hd // 4; c = hd // 2; base = 64 * (hd % 2)
                bs = slice(base, base + 64)
                pa, pak, pb, pbk = psA[hd % 2]
                for j, kb in enumerate(keyblocks):
                    pst, pstk = (pa, pak) if j < 4 else (pb, pbk)
                    P.call("pe", "matmul", reads=[f"p5_Kd{g}", f"p5_qg{b}"], writes=[pstk], inc=(j == nk - 1 or j == 3),
                           out=pst[:, (j % 4) * 128:(j % 4 + 1) * 128], lhsT=Kd[g][bs, kb * 128:(kb + 1) * 128], rhs=qg[b][bs, c, :],
                           start=True, stop=True)
                eg = Eg[hd % 2]; ek = f"p5_Eg{hd % 2}"
                n0 = min(nk, 4)
                P.call("act", "activation", reads=[pak], writes=[ek], out=eg[:, 0:n0, :].rearrange("p a b -> p (a b)"),
                       in_=pa[:, 0:n0 * 128], func=AF.Exp, scale=64 ** -0.5)
                if nk > 4:
                    P.call("act", "activation", reads=[pbk], writes=[ek], out=eg[:, 4, :], in_=pb[:, 0:128],
                           func=AF.Exp, scale=64 ** -0.5)
                for j, kb in enumerate(keyblocks):
                    if tb >= 2 and kb == tb - 1 and kb >= 2:
                        P.call("pool", "tensor_tensor", reads=[ek, "p5_msk"], writes=[ek], out=eg[:, j, :], in0=eg[:, j, :], in1=msk[:, 0, :], op=ALU.mult)
                    if tb >= 2 and kb == tb + 1:
                        P.call("pool", "tensor_tensor", reads=[ek, "p5_msk"], writes=[ek], out=eg[:, j, :], in0=eg[:, j, :], in1=msk[:, 1, :], op=ALU.mult)

            def g_pv(hd):
                g = hd // 4
                eg = Eg[hd % 2]; ek = f"p5_Eg{hd % 2}"
                pgt, pgk = psG[hd % 2]
                for j, kb in enumerate(keyblocks):
                    P.call("pe", "matmul", reads=[ek, "p5_V"], writes=[pgk], inc=(j == nk - 1), out=pgt[:, 0:65], lhsT=eg[:, j, :],
                           rhs=Vaug[:, kb, 4 + g, :], start=(j == 0), stop=(j == nk - 1))
                P.call("dve", "tensor_scalar", reads=[pgk, "p5_exps"], writes=["p5_zt"], out=zt[:], in0=pgt[:, 64:65],
                       scalar1=exps[:, hd:hd + 1], scalar2=None, op0=ALU.add)
                P.call("dve", "reciprocal", reads=["p5_zt"], writes=["p5_zt"], out=zt[:], in_=zt[:])
                P.call("dve", "tensor_scalar", reads=[pgk, "p5_zt"], writes=[f"p5_yc{b}"], out=ycst[b][:, hd * 64:(hd + 1) * 64],
                       in0=pgt[:, 0:64], scalar1=zt[:, 0:1], scalar2=None, op0=ALU.mult)

            g_scores(0)
            for hd in range(8):
                if hd + 1 < 8:
                    g_scores(hd + 1)
                g_pv(hd)
            P.dma("sp", K.ytm[l][tb * 128:(tb + 1) * 128, 512:1024], ycst[b][:], reads=[f"p5_yc{b}"], writes=["ytm"])


def row_gain(P, nc, K, l, gi, jga, G, tmp, name):
    for who in range(2):
        P.dma("sp", G[who][:], K.norm_g[l, gi].partition_broadcast(128), writes=[f"{name}_G{who}"])
        P.dma("sp", tmp[:], K.modrow[l, who, jga].partition_broadcast(128), reads=["modrow"], writes=[name + "_tmp"])
        P.call("dve", "tensor_tensor", reads=[f"{name}_G{who}", name + "_tmp"], writes=[f"{name}_G{who}"],
               out=G[who][:], in0=G[who][:], in1=tmp[:], op=ALU.mult)


def norm_rows(P, ps2, pkeys, ss, ss2, rs, junk, pfx):
    P.call("act", "activation", reads=[pkeys[0]], writes=[pfx + "_junk", pfx + "_ss"], out=junk[:, 0:512], in_=ps2[0][:, :],
           func=AF.Square, accum_out=ss[:])
    P.call("act", "activation", reads=[pkeys[1]], writes=[pfx + "_junk", pfx + "_ss2"], out=junk[:, 512:1024], in_=ps2[1][:, :],
           func=AF.Square, accum_out=ss2[:])
    P.call("dve", "tensor_tensor", reads=[pfx + "_ss", pfx + "_ss2"], writes=[pfx + "_rs"], out=rs[:], in0=ss[:], in1=ss2[:], op=ALU.add)
    P.call("dve", "tensor_scalar", reads=[pfx + "_rs"], writes=[pfx + "_rs"], out=rs[:], in0=rs[:], scalar1=1.0 / D, scalar2=EPS,
           op0=ALU.mult, op1=ALU.add)
    P.call("act", "activation", reads=[pfx + "_rs"], writes=[pfx + "_rs"], out=rs[:], in_=rs[:], func=AF.Sqrt)
    P.call("dve", "reciprocal", reads=[pfx + "_rs"], writes=[pfx + "_rs"], out=rs[:], in_=rs[:])


def phase_outproj(P, nc, K, l, xsrc):
    NBLK = T // 128
    with ExitStack() as es:
        sb = lambda name, shape, dt=F32: es.enter_context(nc.sbuf_tensor(name, shape, dt))
        pp = lambda name: es.enter_context(nc.psum_tensor(name, [128, 512], F32))
        W = sb("p6_w", [128, 8, D], BF16)
        stg = [sb(f"p6_stg{i}", [128, 512]) for i in range(6)]
        G = [sb(f"p6_G{who}", [128, D]) for who in range(2)]
        tmp = sb("p6_tmp", [128, D])
        A = sb("p6_A", [128, 8, 2]); Bv = sb("p6_B", [128, 8, 2]); gcol = sb("p6_g", [128, 8])
        yt = [sb(f"p6_yt{b}", [128, D]) for b in range(2)]
        yTb = [sb(f"p6_yT{b}", [128, 8, 128], BF16) for b in range(2)]
        xt = [sb(f"p6_x{b}", [128, D]) for b in range(2)]
        xm = [sb(f"p6_xm{b}", [128, D]) for b in range(2)]
        xn = sb("p6_xn", [128, D]); junk = sb("p6_junk", [128, D])
        ss = sb("p6_ss", [128, 1]); ss2 = sb("p6_ss2", [128, 1]); rs = sb("p6_rs", [128, 1])
        hT = [sb(f"p6_hT{b}", [128, 8, 128], BF16) for b in range(2)]
        pt = es.enter_context(nc.psum_tensor("p6_pt", [128, 8, 128], F32))
        po4 = [pp(f"p6_po{i}") for i in range(4)]
        load_weight_bf16(P, nc, K.w_out[l], W, "p6_w", 8, D, stg, [f"p6_stg{i}" for i in range(6)])
        row_gain(P, nc, K, l, 1, 2, G, tmp, "p6")
        mod_AB(P, nc, K, l, 2, 4, 3, A, Bv, gcol, "p6")
        pt2 = es.enter_context(nc.psum_tensor("p6_pt2", [128, 8, 128], F32))
        ssb = sb("p6_ssb", [128, 1]); rsb = sb("p6_rsb", [128, 1]); junkb = sb("p6_junkb", [128, D])

        def part1(tb):
            b = tb % 2
            who = 1 if tb * 128 < CTX else 0
            ts = slice(tb * 128, (tb + 1) * 128)
            po = po4[2 * b:2 * b + 2]; pok = [f"p6_po{2 * b}", f"p6_po{2 * b + 1}"]
            P.dma("sp", yt[b][:], K.ytm[l][ts, :], reads=["ytm"], writes=[f"p6_yt{b}"])
            P.dma("sp", xt[b][:], xsrc[ts, :], reads=["xres"], writes=[f"p6_x{b}"])
            P.dma("sp", yTb[b][:, 2:4, :], K.yT[l][256:512, ts].rearrange("(c p) t -> p c t", p=128), reads=["yT"], writes=[f"p6_yT{b}"])
            for c in (0, 1, 4, 5, 6, 7):
                P.call("pe", "transpose", reads=[f"p6_yt{b}", "ident"], writes=["p6_pt"], inc=(c == 7), out=pt[:, c, :],
                       in_=yt[b][:, c * 128:(c + 1) * 128], identity=K.ident[:])
            P.call("act", "activation", reads=["p6_pt"], writes=[f"p6_yT{b}"], out=yTb[b][:, 0:2, :], in_=pt[:, 0:2, :], func=AF.Copy)
            P.call("dve", "tensor_copy", reads=["p6_pt"], writes=[f"p6_yT{b}"], out=yTb[b][:, 4:8, :], in_=pt[:, 4:8, :])
            for half in range(2):
                for kc in range(8):
                    P.call("pe", "matmul", reads=[f"p6_yT{b}", f"p6_w_{half}"], writes=[pok[half]], inc=(kc == 7), out=po[half][:, :],
                           lhsT=yTb[b][:, kc, :], rhs=W[:, kc, half * 512:(half + 1) * 512], start=(kc == 0), stop=(kc == 7))
            norm_rows(P, po, pok, ss, ss2, rs, junk, "p6")
            for half in range(2):
                hs = slice(half * 512, (half + 1) * 512)
                P.call("dve", "scalar_tensor_tensor", reads=[pok[half], "p6_rs", f"p6_G{who}"], writes=[f"p6_xm{b}"],
                       out=xm[b][:, hs], in0=po[half][:, :], scalar=rs[:, 0:1], in1=G[who][:, hs], op0=ALU.mult, op1=ALU.mult)
                P.call("pool", "tensor_tensor", reads=[f"p6_xm{b}", f"p6_x{b}"], writes=[f"p6_xm{b}"], out=xm[b][:, hs],
                       in0=xm[b][:, hs], in1=xt[b][:, hs], op=ALU.add)
            P.dma("sp", K.xmid[l][ts, :], xm[b][:], reads=[f"p6_xm{b}"], writes=["xmid"])

        def part2(tb):
            b = tb % 2
            who = 1 if tb * 128 < CTX else 0
            ts = slice(tb * 128, (tb + 1) * 128)
            norm_block(P, xm[b], f"p6_xm{b}", ssb, rsb, junkb, xn, "p6_xn", pfx="p6b")
            for dc in range(8):
                P.call("pe", "transpose", reads=["p6_xn", "ident"], writes=["p6_pt2"], inc=(dc == 7), out=pt2[:, dc, :],
                       in_=xn[:, dc * 128:(dc + 1) * 128], identity=K.ident[:])
            for dc in range(8):
                P.call("act", "activation", reads=["p6_pt2", "p6_A", "p6_B"], writes=[f"p6_hT{b}"], out=hT[b][:, dc, :],
                       in_=pt2[:, dc, :], func=AF.Identity, scale=A[:, dc, who:who + 1], bias=Bv[:, dc, who:who + 1])
            P.dma("sp", K.h2T[l][:, ts].rearrange("(c p) t -> p c t", p=128), hT[b][:], reads=[f"p6_hT{b}"], writes=["h2T"])

        part1(0)
        for tb in range(NBLK):
            if tb + 1 < NBLK:
                part1(tb + 1)
            part2(tb)


def phase_ffn_up(P, nc, K, l):
    NF = DFF // 128
    with ExitStack() as es:
        sb = lambda name, shape, dt=F32: es.enter_context(nc.sbuf_tensor(name, shape, dt))
        pp = lambda name: es.enter_context(nc.psum_tensor(name, [128, 512], F32))
        Wg = sb("p7_wg", [128, 8, DFF], BF16); Wu = sb("p7_wu", [128, 8, DFF], BF16)
        stg = [sb(f"p7_stg{i}", [128, 512]) for i in range(6)]
        cw = sb("p7_cw", [128, NF, 3]); cb = sb("p7_cb", [128, NF])
        hT = [sb(f"p7_hT{b}", [128, 8, 514], BF16) for b in range(2)]
        gsb = sb("p7_g", [128, 514]); tt = sb("p7_t", [128, 512]); sg = sb("p7_s", [128, 512])
        zt = [sb(f"p7_z{i}", [128, 512], BF16) for i in range(3)]
        pg = [pp(f"p7_pg{i}") for i in range(3)]; ph = [pp(f"p7_ph{i}") for i in range(2)]; pu = [pp(f"p7_pu{i}") for i in range(3)]
        NS = dict(allow_slow_non_contiguous=True)
        load_weight_bf16(P, nc, K.ff_w_gate[l], Wg, "p7_wg", 8, DFF, stg, [f"p7_stg{i}" for i in range(6)])
        load_weight_bf16(P, nc, K.ff_w_up[l], Wu, "p7_wu", 8, DFF, stg, [f"p7_stg{i}" for i in range(6)])
        for j in range(3):
            P.dma("sp", cw[:, :, j], K.ff_conv_w[l, j].rearrange("(c p) -> p c", p=128), writes=["p7_cw"], **NS)
        P.dma("sp", cb[:], K.ff_conv_b[l].rearrange("(c p) -> p c", p=128), writes=["p7_cb"], **NS)
        NSEAM = SEQ // 512 - 1
        hH = sb("p7_hH", [128, 8, 2 * NSEAM], BF16); HG = sb("p7_HG", [128, NF, 2 * NSEAM])
        for k_ in range(NSEAM):
            tk = CTX + 512 * (k_ + 1)
            P.dma("sp", hH[:, :, 2 * k_:2 * k_ + 2], K.h2T[l][:, tk - 1:tk + 1].rearrange("(c p) t -> p c t", p=128),
                  reads=["h2T"], writes=["p7_hH"], **NS)
        for fc in range(NF):
            fs = slice(fc * 128, (fc + 1) * 128)
            i2 = fc % 2
            for kc in range(8):
                P.call("pe", "matmul", reads=[f"p7_wg_{fc // 4}", "p7_hH"], writes=[f"p7_ph{i2}"], out=ph[i2][:, 0:2 * NSEAM], lhsT=Wg[:, kc, fs],
                       rhs=hH[:, kc, :], start=(kc == 0), stop=(kc == 7))
            P.call("act", "activation", reads=[f"p7_ph{i2}"], writes=["p7_HG"], out=HG[:, fc, :], in_=ph[i2][:, 0:2 * NSEAM], func=AF.Copy)
        cnt = {"i": 0}
        for ti, (t0, n) in enumerate(tiles_512()):
            b = ti % 2
            xi = ti - 1
            hk = f"p7_hT{b}"
            s0, s1 = (0, CTX) if t0 < CTX else (CTX, T)
            src = lambda a, b_: K.h2T[l][:, a:b_].rearrange("(c p) t -> p c t", p=128)
            P.dma("sp", hT[b][:, :, 1:n + 1], src(t0, t0 + n), reads=["h2T"], writes=[hk])
            if t0 > s0:
                P.dma("sp", hT[b][:, :, 0:1], src(t0 - 1, t0), reads=["h2T"], writes=[hk], **NS)
            else:
                P.call("pool", "memset", writes=[hk], ap=hT[b][:, :, 0:1], constant=0.0)
            if t0 + n < s1:
                P.dma("sp", hT[b][:, :, n + 1:n + 2], src(t0 + n, t0 + n + 1), reads=["h2T"], writes=[hk], **NS)
            else:
                P.call("pool", "memset", writes=[hk], ap=hT[b][:, :, n + 1:n + 2], constant=0.0)
            for fc in range(NF):
                i2 = cnt["i"] % 3
                cnt["i"] += 1
                fs = slice(fc * 128, (fc + 1) * 128)
                for kc in range(8):
                    P.call("pe", "matmul", reads=[f"p7_wg_{fc // 4}", hk], writes=[f"p7_pg{i2}"], inc=(kc == 7), out=pg[i2][:, :n], lhsT=Wg[:, kc, fs],
                           rhs=hT[b][:, kc, 1:n + 1], start=(kc == 0), stop=(kc == 7))
                for kc in range(8):
                    P.call("pe", "matmul", reads=[f"p7_wu_{fc // 4}", hk], writes=[f"p7_pu{i2}"], inc=(kc == 7), out=pu[i2][:, :n], lhsT=Wu[:, kc, fs],
                           rhs=hT[b][:, kc, 1:n + 1], start=(kc == 0), stop=(kc == 7))
                P.call("act", "activation", reads=[f"p7_pg{i2}"], writes=["p7_g"], out=gsb[:, 1:n + 1], in_=pg[i2][:, :n], func=AF.Copy)
                if t0 > s0:
                    P.call("dve", "tensor_copy", reads=["p7_HG"], writes=["p7_g"], out=gsb[:, 0:1], in_=HG[:, fc, 2 * (xi - 1):2 * (xi - 1) + 1])
                else:
                    P.call("pool", "memset", writes=["p7_g"], ap=gsb[:, 0:1], constant=0.0)
                if t0 + n < s1:
                    P.call("dve", "tensor_copy", reads=["p7_HG"], writes=["p7_g"], out=gsb[:, n + 1:n + 2], in_=HG[:, fc, 2 * xi + 1:2 * xi + 2])
                else:
                    P.call("pool", "memset", writes=["p7_g"], ap=gsb[:, n + 1:n + 2], constant=0.0)
                P.call("act", "activation", reads=["p7_g", "p7_cw", "p7_cb"], writes=["p7_t"], out=tt[:, :n], in_=gsb[:, 1:n + 1],
                       func=AF.Identity, scale=cw[:, fc, 1:2], bias=cb[:, fc:fc + 1])
                P.call("dve", "scalar_tensor_tensor", reads=["p7_g", "p7_cw", "p7_t"], writes=["p7_t"], out=tt[:, :n],
                       in0=gsb[:, 0:n], scalar=cw[:, fc, 0:1], in1=tt[:, :n], op0=ALU.mult, op1=ALU.add)
                P.call("dve", "scalar_tensor_tensor", reads=["p7_g", "p7_cw", "p7_t"], writes=["p7_t"], out=tt[:, :n],
                       in0=gsb[:, 2:n + 2], scalar=cw[:, fc, 2:3], in1=tt[:, :n], op0=ALU.mult, op1=ALU.add)
                P.call("act", "activation", reads=["p7_t"], writes=["p7_s"], out=sg[:, :n], in_=tt[:, :n], func=AF.Silu)
                P.call("dve", "tensor_tensor", reads=["p7_s", f"p7_pu{i2}"], writes=[f"p7_z{i2}"], out=zt[i2][:, :n], in0=sg[:, :n],
                       in1=pu[i2][:, :n], op=ALU.mult)
                P.dma("sp", K.zT[l][fs, t0:t0 + n], zt[i2][:, :n], reads=[f"p7_z{i2}"], writes=["zT"])


def phase_ffn_down(P, nc, K, l, xdst, last):
    NF = DFF // 128
    NBLK = T // 128
    with ExitStack() as es:
        sb = lambda name, shape, dt=F32: es.enter_context(nc.sbuf_tensor(name, shape, dt))
        pp = lambda name: es.enter_context(nc.psum_tensor(name, [128, 512], F32))
        Wd = sb("p8_wd", [128, NF, D], BF16)
        stg = [sb(f"p8_stg{i}", [128, 512]) for i in range(6)]
        G = [sb(f"p8_G{who}", [128, D]) for who in range(2)]
        tmp = sb("p8_tmp", [128, D]); junk = sb("p8_junk", [128, D])
        zb = [sb(f"p8_z{b}", [128, NF, 128], BF16) for b in range(2)]
        xm = [sb(f"p8_xm{b}", [128, D]) for b in range(2)]
        xo = [sb(f"p8_xo{b}", [128, D]) for b in range(2)]
        ss = sb("p8_ss", [128, 1]); ss2 = sb("p8_ss2", [128, 1]); rs = sb("p8_rs", [128, 1])
        po4 = [pp(f"p8_po{i}") for i in range(4)]
        load_weight_bf16(P, nc, K.ff_w_down[l], Wd, "p8_wd", NF, D, stg, [f"p8_stg{i}" for i in range(6)])
        row_gain(P, nc, K, l, 3, 5, G, tmp, "p8")
        for tb in range(NBLK):
            if last and tb * 128 < CTX:
                continue
            b = tb % 2
            who = 1 if tb * 128 < CTX else 0
            ts = slice(tb * 128, (tb + 1) * 128)
            po = po4[2 * b:2 * b + 2]; pok = [f"p8_po{2 * b}", f"p8_po{2 * b + 1}"]
            P.dma("sp", zb[b][:], K.zT[l][:, ts].rearrange("(c p) t -> p c t", p=128), reads=["zT"], writes=[f"p8_z{b}"])
            P.dma("sp", xm[b][:], K.xmid[l][ts, :], reads=["xmid"], writes=[f"p8_xm{b}"])
            for half in range(2):
                for fc in range(NF):
                    P.call("pe", "matmul", reads=[f"p8_z{b}", f"p8_wd_{half}"], writes=[pok[half]], inc=(fc == NF - 1), out=po[half][:, :],
                           lhsT=zb[b][:, fc, :], rhs=Wd[:, fc, half * 512:(half + 1) * 512], start=(fc == 0), stop=(fc == NF - 1))
            norm_rows(P, po, pok, ss, ss2, rs, junk, "p8")
            for half in range(2):
                hs = slice(half * 512, (half + 1) * 512)
                P.call("dve", "scalar_tensor_tensor", reads=[pok[half], "p8_rs", f"p8_G{who}"], writes=[f"p8_xo{b}"],
                       out=xo[b][:, hs], in0=po[half][:, :], scalar=rs[:, 0:1], in1=G[who][:, hs], op0=ALU.mult, op1=ALU.mult)
                P.call("pool", "tensor_tensor", reads=[f"p8_xo{b}", f"p8_xm{b}"], writes=[f"p8_xo{b}"], out=xo[b][:, hs],
                       in0=xo[b][:, hs], in1=xm[b][:, hs], op=ALU.add)
            if last:
                P.dma("sp", K.out[tb * 128 - CTX:(tb + 1) * 128 - CTX, :], xo[b][:], reads=[f"p8_xo{b}"], writes=["out"])
            else:
                P.dma("sp", xdst[ts, :], xo[b][:], reads=[f"p8_xo{b}"], writes=["xres"])


def build(dbg=(), upto=99, nlayers=L, skip=()):
    nc = bass.Bass("TRN2", target_bir_lowering=False)
    K = Ctx()
    K.scan_dbg = 'scandbg' in dbg
    K.chunked = 'seqscan' not in dbg
    K.f32r = False
    dt = lambda name, shape, dtype=F32, kind="ExternalInput": nc.dram_tensor(name, shape, dtype, kind=kind).ap()
    scr = lambda name, shape, dtype=F32: dt(name, shape, dtype, "ExternalOutput" if name in dbg else "Internal")
    K.xin = dt("xin", [T, D])
    K.c_in = dt("c_in", [D])
    K.cctx_in = dt("cctx_in", [D])
    K.ada_w = dt("ada_w", [L, D, 6 * D])
    K.ada_b = dt("ada_b", [L, 6 * D])
    K.norm_g = dt("norm_g", [L, 4, D])
    K.w_in = dt("w_in", [L, D, WCOLS])
    K.rope = dt("rope", [4, 128, T])
    K.ident_d = dt("ident", [128, 128])
    K.out = dt("out", [SEQ, D], kind="ExternalOutput")
    K.fm32 = [scr(f"fm32_{l}", [1024, T]) for l in range(L)]
    K.ropeT = [scr(f"ropeT_{l}", [1152, T], BF16) for l in range(L)]
    K.vtm = [scr(f"vtm_{l}", [T, 384], BF16) for l in range(L)]
    for nm, shp in (("rw_conv", [L, 3, 768]), ("rw_w0", [L, 2, 256]), ("rw_w_up", [L, 2, 32, 256]), ("rw_a0", [L, 2, 256]),
                    ("rw_a_up", [L, 2, 32, 256]), ("rw_g_up", [L, 64, 256]), ("rw_k_k", [L, 256]), ("rw_k_a", [L, 256]),
                    ("rw_r_k", [L, 256]), ("rw_ln_g", [L, 256]), ("rw_ln_b", [L, 256])):
        setattr(K, nm, dt(nm, shp))
    K.bones_d = dt("bones", [128, 128])
    K.col_w = [[scr(f"col_w_{l}_{i}", [128, T]) for i in range(4)] for l in range(L)]
    K.col_kr = [[scr(f"col_kr_{l}_{i}", [128, T]) for i in range(4)] for l in range(L)]
    K.rw_tm = [scr(f"rw_tm_{l}", [T, 7, 256]) for l in range(L)]
    K.col_nk = [[scr(f"col_nk_{l}_{i}", [128, T]) for i in range(4)] for l in range(L)]
    K.col_kd = [[scr(f"col_kd_{l}_{i}", [128, T]) for i in range(4)] for l in range(L)]
    K.cmask_d = dt("cmask", [4, 128, 128])
    K.o_tm = [scr(f"o_tm_{l}", [T, 2, 256]) for l in range(L)]
    K.ytm = [scr(f"ytm_{l}", [T, 1024]) for l in range(L)]
    K.yT = [scr(f"yT_{l}", [1024, T], BF16) for l in range(L)]
    K.modrow = scr("modrow", [L, 2, 6, D])
    K.xmid = [scr(f"xmid_{l}", [T, D]) for l in range(L)]
    K.h2T = [scr(f"h2T_{l}", [D, T], BF16) for l in range(L)]
    K.zT = [scr(f"zT_{l}", [DFF, T], BF16) for l in range(L)]
    K.xres = [scr(f"xres_{l}", [T, D]) for l in range(L)]
    K.w_out = dt("w_out", [L, D, D])
    K.ff_w_gate = dt("ff_w_gate", [L, D, DFF]); K.ff_w_up = dt("ff_w_up", [L, D, DFF]); K.ff_w_down = dt("ff_w_down", [L, DFF, D])
    K.ff_conv_w = dt("ff_conv_w", [L, 3, DFF]); K.ff_conv_b = dt("ff_conv_b", [L, DFF])
    K.df_lambda = dt("df_lambda", [L, 128])
    K.df_norm_g = dt("df_norm_g", [L, 64])
    K.gq_sink = dt("gq_sink", [L, 8])
    K.sel_d = dt("sel65", [65, 64])
    K.msk_d = dt("msk", [2, 128, 128])

    P = Prog(nc)
    with (
        nc.sbuf_tensor("modcol0", [128, 48, 2], F32) as mc0,
        nc.sbuf_tensor("modcol1", [128, 48, 2], F32) as mc1,
        nc.sbuf_tensor("ident_sb", [128, 128], F32) as ident,
    ):
        K.modcol = [mc0, mc1]
        K.ident = ident
        P.dma("sp", ident[:], K.ident_d, writes=["ident"])
        phase_mod(P, nc, K)
        P.barrier()
        for l in range(nlayers):
            xsrc = K.xin if l == 0 else K.xres[l - 1]
            last = (l == L - 1)
            nc0 = nc
            nc = Uniq(nc0, f"_L{l}")
            phases = [lambda: phase_inproj(P, nc, K, l, xsrc), lambda: phase_rwprep(P, nc, K, l), lambda: (phase_scan_chunked if K.chunked else phase_scan)(P, nc, K, l),
                      lambda: phase_readout(P, nc, K, l), lambda: phase_attn(P, nc, K, l), lambda: phase_outproj(P, nc, K, l, xsrc),
                      lambda: phase_ffn_up(P, nc, K, l), lambda: phase_ffn_down(P, nc, K, l, K.xres[l], last)]
            for pi, ph in enumerate(phases):
                if upto >= pi + 1 and pi + 1 not in skip:
                    ph()
                    P.mark(f"L{l}_ph{pi + 1}")
                    P.barrier()
            nc = nc0
        P.finish(["out"])
    return nc, P


def make_in_maps(inp):
    f = lambda a: np.ascontiguousarray(np.asarray(a, dtype=np.float32))
    cols = w_in_cols()
    shared = {
        "cctx_in": f(inp["c_ctx"]),
        "ada_w": f(inp["ada_w"]),
        "ada_b": f(inp["ada_b"]),
        "norm_g": f(inp["norm_g"]),
        "w_in": f(np.asarray(inp["w_in"])[:, :, cols]),
        "rope": rope_tables(),
        "ident": np.eye(128, dtype=np.float32),
        "bones": np.kron(np.eye(2, dtype=np.float32), np.ones((64, 64), np.float32)),
    }
    for nm in ("rw_conv", "rw_w0", "rw_w_up", "rw_a0", "rw_a_up", "rw_g_up", "rw_k_k", "rw_k_a", "rw_ln_g", "rw_ln_b"):
        shared[nm] = f(inp[nm])
    shared["rw_r_k"] = f(np.asarray(inp["rw_r_k"]).reshape(L, 256))
    shared["df_lambda"] = f(np.asarray(inp["df_lambda"]).reshape(L, 128))
    tau = np.arange(128) % 64
    shared["cmask"] = np.stack([tau[:, None] < tau[None, :], tau[:, None] <= tau[None, :],
                                tau[:, None] > tau[None, :], tau[:, None] >= tau[None, :]]).astype(np.float32)
    shared["df_norm_g"] = f(inp["df_norm_g"])
    for nm in ("w_out", "ff_w_gate", "ff_w_up", "ff_w_down", "ff_conv_w", "ff_conv_b"):
        shared[nm] = f(inp[nm])
    shared["gq_sink"] = f(inp["gq_sink"])
    sel = np.zeros((65, 64), np.float32); sel[64, :] = 1.0
    shared["sel65"] = sel
    a = np.arange(128)
    shared["msk"] = np.stack([(a[:, None] >= a[None, :]), (a[:, None] <= a[None, :])]).astype(np.float32)
    maps = []
    for core in range(8):
        b = core % NB
        m = dict(shared)
        m["xin"] = f(np.concatenate([inp["ctx"][b], inp["x"][b]], axis=0))
        m["c_in"] = f(inp["c"][b])
        maps.append(m)
    return maps


def kernel(**inputs):
    inp = {k: np.asarray(v) for k, v in inputs.items()}
    nc, _ = build()
    maps = make_in_maps(inp)
    res = run_bass_kernel_spmd(nc, maps, core_ids=list(range(8)))
    out = np.stack([np.asarray(res.results[b]["out"], dtype=np.float32) for b in range(NB)], axis=0)
    return out
```

```python
import math
from contextlib import ExitStack
import numpy as np
import concourse.bass as bass
import concourse.mybir as mybir
from concourse.bass_utils import run_bass_kernel_spmd

F32 = mybir.dt.float32
BF16 = mybir.dt.bfloat16
AF = mybir.ActivationFunctionType
ALU = mybir.AluOpType
AX = mybir.AxisListType

D = 1024
NB = 4
SEQ = 4096
CTX = 256
T = CTX + SEQ
L = 2
DFF = 2816
GRID_W = 64
EPS = 1e-6

ENG_NAMES = ("pe", "act", "dve", "pool", "sp")


class Prog:
    N_DMA_SEMS = 12

    def __init__(self, nc):
        self.nc = nc
        self.streams = {e: [] for e in ENG_NAMES}
        self.cnt = {e: 0 for e in ENG_NAMES}
        self.sems = {e: nc.alloc_semaphore(f"c_{e}") for e in ENG_NAMES}
        self.dsems = [nc.alloc_semaphore(f"d_{i}") for i in range(self.N_DMA_SEMS)]
        self.dcnt = [0] * self.N_DMA_SEMS
        self.dnext = 0
        self.waited = {e: {} for e in ENG_NAMES}
        self.bufs = {}
        self.n_instr = 0
        self.split_stores = True

    def _sem(self, key):
        return self.sems[key] if isinstance(key, str) else self.dsems[key]

    def _need(self, eng, tok):
        if tok is None:
            return
        key, val = tok
        if key == eng and val > self.cnt[eng]:
            return
        if self.waited[eng].get(key, 0) >= val:
            return
        self.waited[eng][key] = val
        sem = self._sem(key)
        self.streams[eng].append(lambda e, sem=sem, val=val: e.wait_ge(sem, val))
        self.n_instr += 1

    def _deps(self, eng, reads, writes):
        for r in reads:
            st = self.bufs.get(r)
            if st is not None:
                self._need(eng, st["w"])
        for w in writes:
            st = self.bufs.get(w)
            if st is not None:
                self._need(eng, st["w"])
                for t in st["r"]:
                    self._need(eng, t)

    def _commit(self, tok, reads, writes):
        for r in reads:
            st = self.bufs.setdefault(r, {"w": None, "r": []})
            st["r"].append(tok)
            if len(st["r"]) > 24:
                best = {}
                for k, v in st["r"]:
                    best[k] = max(best.get(k, 0), v)
                st["r"] = list(best.items())
        for w in writes:
            self.bufs[w] = {"w": tok, "r": []}

    def op(self, eng, fn, reads=(), writes=(), inc=True):
        self._deps(eng, reads, writes)
        sem = self.sems[eng]
        if inc:
            self.cnt[eng] += 1
            self.streams[eng].append(lambda e, fn=fn, sem=sem: fn(e).then_inc(sem, 1))
            tok = (eng, self.cnt[eng])
        else:
            self.streams[eng].append(lambda e, fn=fn: fn(e))
            tok = (eng, self.cnt[eng] + 1)
        self.n_instr += 1
        self._commit(tok, reads, writes)

    def call(self, eng, method, reads=(), writes=(), inc=True, **kw):
        self.op(eng, lambda e: getattr(e, method)(**kw), reads, writes, inc=inc)

    def dma(self, q, out, in_, reads=(), writes=(), **kw):
        if q == "sp" and self.split_stores and str(out.space) == "DRAM" and str(in_.space) != "DRAM":
            q = "pool"
        i = self.dnext
        self.dnext = (self.dnext + 1) % self.N_DMA_SEMS
        if self.dcnt[i] > 0:
            self._need(q, (i, 16 * self.dcnt[i]))
        self._deps(q, reads, writes)
        self.dcnt[i] += 1
        sem = self.dsems[i]
        self.streams[q].append(
            lambda e, out=out, in_=in_, sem=sem, kw=kw: e.dma_start(out=out, in_=in_, **kw).then_inc(sem, 16))
        self.n_instr += 1
        self._commit((i, 16 * self.dcnt[i]), reads, writes)

    def mark(self, name):
        if not hasattr(self, "marks"):
            self.marks = []
        self.marks.append((name, dict(self.cnt)))

    def barrier(self):
        toks = [(e, self.cnt[e]) for e in ENG_NAMES if self.cnt[e] > 0]
        toks += [(i, 16 * self.dcnt[i]) for i in range(self.N_DMA_SEMS) if self.dcnt[i] > 0]
        for e in ENG_NAMES:
            for tok in toks:
                if tok[0] != e:
                    self._need(e, tok)

    def finish(self, final_keys):
        for k in final_keys:
            st = self.bufs.get(k)
            if st is not None:
                self._need("sp", st["w"])
        for i in range(self.N_DMA_SEMS):
            if self.dcnt[i] > 0:
                self._need("sp", (i, 16 * self.dcnt[i]))
        nc = self.nc
        with nc.Block() as block:
            @block.tensor
            def _(e):
                for f in self.streams["pe"]:
                    f(e)

            @block.scalar
            def _(e):
                for f in self.streams["act"]:
                    f(e)

            @block.vector
            def _(e):
                for f in self.streams["dve"]:
                    f(e)

            @block.gpsimd
            def _(e):
                for f in self.streams["pool"]:
                    f(e)

            @block.sync
            def _(e):
                for f in self.streams["sp"]:
                    f(e)


class Ctx:
    pass


class Uniq:
    def __init__(self, nc, suffix):
        self._nc = nc
        self._sfx = suffix

    def sbuf_tensor(self, name, shape, dtype):
        return self._nc.sbuf_tensor(name + self._sfx, shape, dtype)

    def psum_tensor(self, name, shape, dtype):
        return self._nc.psum_tensor(name + self._sfx, shape, dtype)

    def __getattr__(self, k):
        return getattr(self._nc, k)


def phase_mod(P, nc, K):
    GW = 768
    NG = 6144 // GW
    with (
        nc.sbuf_tensor("m_c", [128, 8, 2], F32) as craw,
        nc.sbuf_tensor("m_cs", [128, 8, 2], F32) as cs,
        nc.sbuf_tensor("m_w0", [128, 8, GW], F32) as w0,
        nc.sbuf_tensor("m_w1", [128, 8, GW], F32) as w1,
        nc.sbuf_tensor("m_b", [128, 48], F32) as bcol,
        nc.psum_tensor("m_ps", [128, 256, 2], F32) as ps,
        nc.psum_tensor("m_pst", [128, 512], F32) as pst,
        nc.sbuf_tensor("m_mcw", [128, 48], F32) as mcw,
        nc.sbuf_tensor("m_mrow", [48, 128], F32) as mrow,
    ):
        wb = [w0, w1]
        P.dma("sp", craw[:, :, 0], K.c_in.rearrange("(c p) -> p c", p=128), writes=["m_c"],
              allow_slow_non_contiguous=True)
        P.dma("sp", craw[:, :, 1], K.cctx_in.rearrange("(c p) -> p c", p=128), writes=["m_c"],
              allow_slow_non_contiguous=True)
        P.op("act", lambda e: e.activation(out=cs[:], in_=craw[:], func=AF.Silu), reads=["m_c"], writes=["m_cs"])
        for l in range(L):
            P.dma("sp", bcol[:], K.ada_b[l].rearrange("(j p) -> p j", p=128), writes=["m_b"],
                  allow_slow_non_contiguous=True)
            for gi in range(NG):
                wt = wb[gi % 2]
                wk = f"m_w{gi % 2}"
                src = K.ada_w[l, :, gi * GW:(gi + 1) * GW].rearrange("(kc p) n -> p kc n", p=128)
                P.dma("sp", wt[:], src, writes=[wk])
                for jj in range(GW // 128):
                    j = gi * (GW // 128) + jj
                    for kc in range(8):
                        P.op("pe", lambda e, wt=wt, jj=jj, kc=kc, j=j: e.matmul(
                            ps[:, j, :], lhsT=wt[:, kc, jj * 128:(jj + 1) * 128], rhs=cs[:, kc, :],
                            start=(kc == 0), stop=(kc == 7)),
                            reads=[wk, "m_cs"], writes=["m_ps"])
            mc = K.modcol[l]
            for who in range(2):
                P.op("dve", lambda e, mc=mc, who=who: e.tensor_tensor(
                    out=mc[:, :, who], in0=ps[:, 0:48, who], in1=bcol[:], op=ALU.add),
                    reads=["m_ps", "m_b"], writes=[f"modcol{l}"])
            for who in range(2):
                P.call("dve", "tensor_copy", reads=[f"modcol{l}"], writes=["m_mcw"], out=mcw[:], in_=mc[:, :, who])
                P.call("pe", "transpose", reads=["m_mcw", "ident"], writes=["m_pst"], out=pst[0:48, 0:128], in_=mcw[:], identity=K.ident[:])
                P.call("act", "activation", reads=["m_pst"], writes=["m_mrow"], out=mrow[:], in_=pst[0:48, 0:128], func=AF.Copy)
                P.dma("sp", K.modrow[l, who].rearrange("j (c p) -> (j c) p", p=128), mrow[:], reads=["m_mrow"], writes=["modrow"])


def _swap_idx(du):
    nf = du // 4
    idx = np.arange(du)
    axis = idx // (2 * nf); half = (idx // nf) % 2; f = idx % nf
    return axis * 2 * nf + (1 - half) * nf + f


def w_in_cols():
    sw32 = _swap_idx(32); sw64 = _swap_idx(64)
    cols = list(range(0, 768))
    cols += list(range(832, 896)) + list(range(768, 832))
    cols += list(range(896, 960)) * 2
    dfq = np.arange(960, 1216); dfk = np.arange(1216, 1472)
    sw = lambda base, du: np.concatenate([base[u * du:(u + 1) * du][_swap_idx(du)] for u in range(len(base) // du)])
    cols += list(dfq) + list(sw(dfq, 32)) + list(dfk) + list(sw(dfk, 32))
    gqq = np.arange(1728, 2240); gqk = np.arange(2240, 2368)
    cols += list(gqq) + list(sw(gqq, 64)) + list(gqk) + list(sw(gqk, 64))
    cols += list(range(1472, 1728)) + list(range(2368, 2496))
    return np.asarray(cols, dtype=np.int64)


NFM = 26
WCOLS = NFM * 128 + 384


def rope_tables():
    out = np.zeros((4, 128, T), np.float32)
    tt = np.arange(SEQ)
    pos = np.stack([(tt // GRID_W).astype(np.float32), (tt % GRID_W).astype(np.float32)], 0)
    for ti, du in ((0, 32), (2, 64)):
        nf = du // 4
        inv = (np.float32(10000.0) ** (-np.arange(nf, dtype=np.float32) / np.float32(nf))).astype(np.float32)
        i = np.arange(du)
        axis = i // (2 * nf); half = (i // nf) % 2; f = i % nf
        ang = (pos[axis] * inv[f][:, None]).astype(np.float32)
        c = np.cos(ang).astype(np.float32); s_ = np.sin(ang).astype(np.float32)
        s_ = np.where(half[:, None] == 0, -s_, s_)
        rep = 128 // du
        out[ti, :, :CTX] = 1.0
        out[ti, :, CTX:] = np.tile(c, (rep, 1))
        out[ti + 1, :, CTX:] = np.tile(s_, (rep, 1))
    return out


def tiles_512():
    return [(0, CTX)] + [(CTX + i * 512, 512) for i in range(SEQ // 512)]


def norm_block(P, xt, xk, ss, rs, junk, xn, xnk, pfx="nb"):
    P.call("act", "activation", reads=[xk], writes=[pfx + "_junk", pfx + "_ss"],
           out=junk[:], in_=xt[:], func=AF.Square, accum_out=ss[:])
    P.call("dve", "tensor_scalar", reads=[pfx + "_ss"], writes=[pfx + "_rs"],
           out=rs[:], in0=ss[:], scalar1=1.0 / D, scalar2=EPS, op0=ALU.mult, op1=ALU.add)
    P.call("act", "activation", reads=[pfx + "_rs"], writes=[pfx + "_rs"], out=rs[:], in_=rs[:], func=AF.Sqrt)
    P.call("dve", "reciprocal", reads=[pfx + "_rs"], writes=[pfx + "_rs"], out=rs[:], in_=rs[:])
    P.call("dve", "tensor_scalar", reads=[xk, pfx + "_rs"], writes=[xnk],
           out=xn[:], in0=xt[:], scalar1=rs[:], scalar2=None, op0=ALU.mult)


def mod_AB(P, nc, K, l, gi, jsc, jsh, A, Bv, gcol, name):
    P.dma("sp", gcol[:], K.norm_g[l, gi].rearrange("(c p) -> p c", p=128), writes=[name + "_g"],
          allow_slow_non_contiguous=True)
    mc = K.modcol[l]
    for who in range(2):
        P.call("dve", "scalar_tensor_tensor", reads=[f"modcol{l}", name + "_g"], writes=[name + "_A"],
               out=A[:, :, who], in0=mc[:, jsc * 8:(jsc + 1) * 8, who], scalar=1.0, in1=gcol[:],
               op0=ALU.add, op1=ALU.mult)
        P.call("dve", "tensor_copy", reads=[f"modcol{l}"], writes=[name + "_B"],
               out=Bv[:, :, who], in_=mc[:, jsh * 8:(jsh + 1) * 8, who])


def load_weight_bf16(P, nc, src, W, wkey, nk, ncols, stages, skeys):
    i = 0
    for c0 in range(0, ncols, 512):
        c1 = min(ncols, c0 + 512)
        key = f"{wkey}_{c0 // 512}"
        for kc in range(nk):
            st = stages[i % len(stages)]; sk = skeys[i % len(stages)]
            P.dma("sp", st[:, :c1 - c0], src[kc * 128:(kc + 1) * 128, c0:c1], writes=[sk])
            eng = ("pool", "dve", "act")[i % 3]
            i += 1
            if eng == "act":
                P.call("act", "activation", reads=[sk], writes=[key], out=W[:, kc, c0:c1], in_=st[:, :c1 - c0], func=AF.Copy)
            else:
                P.call(eng, "tensor_copy", reads=[sk], writes=[key], out=W[:, kc, c0:c1], in_=st[:, :c1 - c0])


def wkeys(wkey, c0, c1):
    return [f"{wkey}_{c}" for c in range(c0 // 512, (c1 - 1) // 512 + 1)]


def phase_inproj(P, nc, K, l, xsrc):
    with ExitStack() as es:
        W = es.enter_context(nc.sbuf_tensor("p1_w", [128, 8, WCOLS], BF16))
        x0 = es.enter_context(nc.sbuf_tensor("p1_x0", [128, D], F32))
        x1 = es.enter_context(nc.sbuf_tensor("p1_x1", [128, D], F32))
        junk = es.enter_context(nc.sbuf_tensor("p1_junk", [128, D], F32))
        xn0 = es.enter_context(nc.sbuf_tensor("p1_xn0", [128, D], F32))
        xn1 = es.enter_context(nc.sbuf_tensor("p1_xn1", [128, D], F32))
        ss = es.enter_context(nc.sbuf_tensor("p1_ss", [128, 1], F32))
        rs = es.enter_context(nc.sbuf_tensor("p1_rs", [128, 1], F32))
        hT0 = es.enter_context(nc.sbuf_tensor("p1_hT0", [128, 8, 512], BF16))
        hT1 = es.enter_context(nc.sbuf_tensor("p1_hT1", [128, 8, 512], BF16))
        A = es.enter_context(nc.sbuf_tensor("p1_A", [128, 8, 2], F32))
        Bv = es.enter_context(nc.sbuf_tensor("p1_B", [128, 8, 2], F32))
        gcol = es.enter_context(nc.sbuf_tensor("p1_g", [128, 8], F32))
        tab = es.enter_context(nc.sbuf_tensor("p1_tab", [128, 4, 512], F32))
        st0 = es.enter_context(nc.sbuf_tensor("p1_st0", [128, 512], F32))
        st1 = es.enter_context(nc.sbuf_tensor("p1_st1", [128, 512], F32))
        tm1 = es.enter_context(nc.sbuf_tensor("p1_t1", [128, 512], F32))
        tm2 = es.enter_context(nc.sbuf_tensor("p1_t2", [128, 512], F32))
        ro0 = es.enter_context(nc.sbuf_tensor("p1_ro0", [128, 512], BF16))
        ro1 = es.enter_context(nc.sbuf_tensor("p1_ro1", [128, 512], BF16))
        vs0 = es.enter_context(nc.sbuf_tensor("p1_vs0", [128, 384], BF16))
        vs1 = es.enter_context(nc.sbuf_tensor("p1_vs1", [128, 384], BF16))
        pt = es.enter_context(nc.psum_tensor("p1_pt", [128, 8, 128], F32))
        pf0 = es.enter_context(nc.psum_tensor("p1_pf0", [128, 512], F32))
        pf1 = es.enter_context(nc.psum_tensor("p1_pf1", [128, 512], F32))
        pf2 = es.enter_context(nc.psum_tensor("p1_pf2", [128, 512], F32))
        pf3 = es.enter_context(nc.psum_tensor("p1_pf3", [128, 512], F32))
        pv = es.enter_context(nc.psum_tensor("p1_pv", [128, 512], F32))
        xb = [x0, x1]; xnb = [xn0, xn1]; hTb = [hT0, hT1]; stb = [st0, st1]; rob = [ro0, ro1]; vsb = [vs0, vs1]
        pfb = [pf0, pf1, pf2, pf3]
        load_weight_bf16(P, nc, K.w_in[l], W, "p1_w", 8, WCOLS, [st0, st1, tm1, tm2],
                         ["p1_st0", "p1_st1", "p1_t1", "p1_t2"])
        mod_AB(P, nc, K, l, 0, 1, 0, A, Bv, gcol, "p1")
        cnt = {"blk": 0, "pf": 0, "st": 0, "ro": 0}
        xnt = [[es.enter_context(nc.sbuf_tensor(f"p1_xnt{i}_{j}", [128, D], F32)) for j in range(4)] for i in range(2)]
        tl = tiles_512()

        def prepA(ti):
            t0, n = tl[ti]
            for blk in range(n // 128):
                tb = t0 + blk * 128
                i2 = cnt["blk"] % 2
                cnt["blk"] += 1
                xt = xb[i2]; xk = f"p1_x{i2}"
                P.dma("sp", xt[:], xsrc[tb:tb + 128, :], reads=["xres"], writes=[xk])
                norm_block(P, xt, xk, ss, rs, junk, xnt[ti % 2][blk], f"p1_xnt{ti % 2}_{blk}")

        def prepB(ti):
            t0, n = tl[ti]
            who = 1 if t0 < CTX else 0
            hT = hTb[ti % 2]; hk = f"p1_hT{ti % 2}"
            for blk in range(n // 128):
                xn = xnt[ti % 2][blk]; xnk = f"p1_xnt{ti % 2}_{blk}"
                for dc in range(8):
                    P.call("pe", "transpose", reads=[xnk, "ident"], writes=["p1_pt"], inc=(dc == 7),
                           out=pt[:, dc, :], in_=xn[:, dc * 128:(dc + 1) * 128], identity=K.ident[:])
                for dc in range(8):
                    P.call("act", "activation", reads=["p1_pt", "p1_A", "p1_B"], writes=[hk],
                           out=hT[:, dc, blk * 128:(blk + 1) * 128], in_=pt[:, dc, :], func=AF.Identity,
                           scale=A[:, dc, who:who + 1], bias=Bv[:, dc, who:who + 1])

        def proj(ti):
            t0, n = tl[ti]
            hT = hTb[ti % 2]; hk = f"p1_hT{ti % 2}"
            P.dma("sp", tab[:, :, :n], K.rope[:, :, t0:t0 + n].rearrange("f p t -> p f t"), writes=["p1_tab"])

            def fm(m):
                i4 = cnt["pf"] % 4
                cnt["pf"] += 1
                ps = pfb[i4]; pk = f"p1_pf{i4}"
                for kc in range(8):
                    P.call("pe", "matmul", reads=wkeys("p1_w", m * 128, (m + 1) * 128) + [hk], writes=[pk], inc=(kc == 7),
                           out=ps[:, :n], lhsT=W[:, kc, m * 128:(m + 1) * 128], rhs=hT[:, kc, :n],
                           start=(kc == 0), stop=(kc == 7))
                return ps, pk

            for m in range(8):
                ps, pk = fm(m)
                i2 = cnt["st"] % 2
                cnt["st"] += 1
                st = stb[i2]; sk = f"p1_st{i2}"
                P.call("act", "activation", reads=[pk], writes=[sk], out=st[:, :n], in_=ps[:, :n], func=AF.Copy)
                P.dma("sp", K.fm32[l][m * 128:(m + 1) * 128, t0:t0 + n], st[:, :n], reads=[sk], writes=["fm32"])
            pairs = [(8, 10, 0), (9, 11, 0), (12, 14, 0), (13, 15, 0), (16, 20, 2), (17, 21, 2), (18, 22, 2),
                     (19, 23, 2), (24, 25, 2)]
            for oi, (ma, mb, tbi) in enumerate(pairs):
                psa, pka = fm(ma)
                psb, pkb = fm(mb)
                i2 = cnt["ro"] % 2
                cnt["ro"] += 1
                ro = rob[i2]; rk = f"p1_ro{i2}"
                P.call("dve", "tensor_tensor", reads=[pka, "p1_tab"], writes=["p1_t1"],
                       out=tm1[:, :n], in0=psa[:, :n], in1=tab[:, tbi, :n], op=ALU.mult)
                P.call("dve", "tensor_tensor", reads=[pkb, "p1_tab"], writes=["p1_t2"],
                       out=tm2[:, :n], in0=psb[:, :n], in1=tab[:, tbi + 1, :n], op=ALU.mult)
                P.call("pool", "tensor_tensor", reads=["p1_t1", "p1_t2"], writes=[rk],
                       out=ro[:, :n], in0=tm1[:, :n], in1=tm2[:, :n], op=ALU.add)
                P.dma("sp", K.ropeT[l][oi * 128:(oi + 1) * 128, t0:t0 + n], ro[:, :n], reads=[rk], writes=["ropeT"])
            for blk in range(n // 128):
                tb = t0 + blk * 128
                vs = vsb[blk % 2]; vk = f"p1_vs{blk % 2}"
                for kc in range(8):
                    P.call("pe", "matmul", reads=wkeys("p1_w", NFM * 128, WCOLS) + [hk], writes=["p1_pv"], inc=(kc == 7),
                           out=pv[:, 0:384], lhsT=hT[:, kc, blk * 128:(blk + 1) * 128], rhs=W[:, kc, NFM * 128:WCOLS],
                           start=(kc == 0), stop=(kc == 7))
                P.call("act", "activation", reads=["p1_pv"], writes=[vk], out=vs[:], in_=pv[:, 0:384], func=AF.Copy)
                P.dma("sp", K.vtm[l][tb:tb + 128, :], vs[:], reads=[vk], writes=["vtm"])

        prepA(0)
        prepB(0)
        for ti in range(len(tl)):
            if ti + 1 < len(tl):
                prepA(ti + 1)
            proj(ti)
            if ti + 1 < len(tl):
                prepB(ti + 1)


def phase_rwprep(P, nc, K, l):
    with ExitStack() as es:
        sb = lambda name, shape, dt=F32: es.enter_context(nc.sbuf_tensor(name, shape, dt))
        pp = lambda name, shape, dt=F32: es.enter_context(nc.psum_tensor(name, shape, dt))
        cw = sb("p2_cw", [128, 6, 3]); kkc = sb("p2_kkc", [128, 2]); kac = sb("p2_kac", [128, 2])
        omka = sb("p2_omka", [128, 2]); w0c = sb("p2_w0", [128, 2, 2]); a0c = sb("p2_a0", [128, 2, 2])
        wup = sb("p2_wup", [64, 256]); aup = sb("p2_aup", [64, 256]); gup = sb("p2_gup", [128, 256])
        bones = sb("p2_bones", [128, 128])
        xr = sb("p2_xr", [128, 6, 514]); cv = sb("p2_cv", [128, 6, 512])
        lg = sb("p2_lg", [128, 512]); la = sb("p2_la", [64, 512]); thw = sb("p2_thw", [64, 512]); sgd = sb("p2_sgd", [128, 512])
        kkr = sb("p2_kkr", [128, 512]); sq = sb("p2_sq", [128, 512]); nr = sb("p2_nr", [128, 512])
        kk = sb("p2_kk", [128, 2, 512]); sgw = sb("p2_sgw", [128, 512])
        dec0 = sb("p2_dec0", [128, 512]); dec1 = sb("p2_dec1", [128, 512])
        av = sb("p2_a", [128, 512]); tt = sb("p2_t", [128, 512])
        NKf = sb("p2_NKf", [128, 2, 2, 512]); KDf = sb("p2_KDf", [128, 2, 2, 512])
        tm0 = sb("p2_tm0", [128, 7, 256]); tm1 = sb("p2_tm1", [128, 7, 256])
        pT = pp("p2_pT", [128, 12, 128]); pg_full = pp("p2_pg", [128, 512]); pg = pg_full[:, 0:256]
        px0 = pp("p2_px0", [128, 512]); px1 = pp("p2_px1", [128, 512]); px2 = pp("p2_px2", [128, 512])
        pxb = [px0, px1, px2]; decb = [dec0, dec1]; tmb = [tm0, tm1]
        NS = dict(allow_slow_non_contiguous=True)
        for j in range(3):
            P.dma("sp", cw[:, :, j], K.rw_conv[l, j].rearrange("(c p) -> p c", p=128), writes=["p2_cw"], **NS)
        P.dma("sp", kkc[:], K.rw_k_k[l].rearrange("(h p) -> p h", p=128), writes=["p2_kkc"], **NS)
        P.dma("sp", kac[:], K.rw_k_a[l].rearrange("(h p) -> p h", p=128), writes=["p2_kac"], **NS)
        for d in range(2):
            P.dma("sp", w0c[:, d, :], K.rw_w0[l, d].rearrange("(h p) -> p h", p=128), writes=["p2_w0"], **NS)
            P.dma("sp", a0c[:, d, :], K.rw_a0[l, d].rearrange("(h p) -> p h", p=128), writes=["p2_a0"], **NS)
        for d in range(2):
            P.dma("sp", wup[32 * d:32 * d + 32, :], K.rw_w_up[l, d], writes=["p2_wup"])
            P.dma("sp", aup[32 * d:32 * d + 32, :], K.rw_a_up[l, d], writes=["p2_aup"])
        P.dma("sp", gup[64:128, :], K.rw_g_up[l], writes=["p2_gup"])
        P.dma("sp", bones[:], K.bones_d, writes=["p2_bones"])
        P.call("dve", "tensor_scalar", reads=["p2_kac"], writes=["p2_omka"], out=omka[:], in0=kac[:],
               scalar1=-1.0, scalar2=1.0, op0=ALU.mult, op1=ALU.add)
        cnt = {"px": 0, "dec": 0, "tm": 0}

        def px():
            i = cnt["px"] % 3
            cnt["px"] += 1
            return pxb[i], f"p2_px{i}"

        for (t0, n) in tiles_512():
            s0, s1 = (0, CTX) if t0 < CTX else (CTX, T)
            src = lambda a, b: K.fm32[l][0:768, a:b].rearrange("(c p) t -> p c t", p=128)
            P.dma("sp", xr[:, :, 1:n + 1], src(t0, t0 + n), reads=["fm32"], writes=["p2_xr"])
            if t0 > s0:
                P.dma("sp", xr[:, :, 0:1], src(t0 - 1, t0), reads=["fm32"], writes=["p2_xr"], **NS)
            else:
                P.call("pool", "memset", writes=["p2_xr"], ap=xr[:, :, 0:1], constant=0.0)
            if t0 + n < s1:
                P.dma("sp", xr[:, :, n + 1:n + 2], src(t0 + n, t0 + n + 1), reads=["fm32"], writes=["p2_xr"], **NS)
            else:
                P.call("pool", "memset", writes=["p2_xr"], ap=xr[:, :, n + 1:n + 2], constant=0.0)
            P.dma("sp", lg[:, :n], K.fm32[l][768:896, t0:t0 + n], reads=["fm32"], writes=["p2_lg"])
            P.dma("sp", la[:, :n], K.fm32[l][896:960, t0:t0 + n], reads=["fm32"], writes=["p2_la"])
            for c in range(6):
                P.call("act", "activation", reads=["p2_xr", "p2_cw"], writes=["p2_cv"],
                       out=cv[:, c, :n], in_=xr[:, c, 1:n + 1], func=AF.Identity, scale=cw[:, c, 1:2])
                P.call("dve", "scalar_tensor_tensor", reads=["p2_xr", "p2_cw", "p2_cv"], writes=["p2_cv"],
                       out=cv[:, c, :n], in0=xr[:, c, 0:n], scalar=cw[:, c, 0:1], in1=cv[:, c, :n],
                       op0=ALU.mult, op1=ALU.add)
                P.call("dve", "scalar_tensor_tensor", reads=["p2_xr", "p2_cw", "p2_cv"], writes=["p2_cv"],
                       out=cv[:, c, :n], in0=xr[:, c, 2:n + 2], scalar=cw[:, c, 2:3], in1=cv[:, c, :n],
                       op0=ALU.mult, op1=ALU.add)
            P.call("act", "activation", reads=["p2_lg"], writes=["p2_thw"], out=thw[:, :n], in_=lg[0:64, :n], func=AF.Tanh)
            P.call("act", "activation", reads=["p2_lg"], writes=["p2_sgd"], out=sgd[64:128, :n], in_=lg[64:128, :n], func=AF.Sigmoid)
            for hp in range(2):
                kf = cv[:, 2 + hp, :n]
                P.call("dve", "tensor_scalar", reads=["p2_cv", "p2_kkc"], writes=["p2_kkr"], out=kkr[:, :n], in0=kf,
                       scalar1=kkc[:, hp:hp + 1], scalar2=None, op0=ALU.mult)
                P.call("act", "activation", reads=["p2_kkr"], writes=["p2_sq"], out=sq[:, :n], in_=kkr[:, :n], func=AF.Square)
                ps, pk = px()
                P.call("pe", "matmul", reads=["p2_bones", "p2_sq"], writes=[pk], out=ps[:, :n], lhsT=bones[:], rhs=sq[:, :n],
                       start=True, stop=True)
                P.call("act", "activation", reads=[pk], writes=["p2_nr"], out=nr[:, :n], in_=ps[:, :n], func=AF.Sqrt)
                P.call("dve", "tensor_scalar", reads=["p2_nr"], writes=["p2_nr"], out=nr[:, :n], in0=nr[:, :n],
                       scalar1=1e-12, scalar2=None, op0=ALU.max)
                P.call("dve", "reciprocal", reads=["p2_nr"], writes=["p2_nr"], out=nr[:, :n], in_=nr[:, :n])
                P.call("dve", "tensor_tensor", reads=["p2_kkr", "p2_nr"], writes=["p2_kk"], out=kk[:, hp, :n],
                       in0=kkr[:, :n], in1=nr[:, :n], op=ALU.mult)
                P.dma("sp", K.col_kr[l][hp][:, t0:t0 + n], kk[:, hp, :n], reads=["p2_kk"], writes=["col_kr"])
                P.dma("sp", K.col_kr[l][2 + hp][:, t0:t0 + n], cv[:, hp, :n], reads=["p2_cv"], writes=["col_kr"])
                for d in range(2):
                    ps, pk = px()
                    P.call("pe", "matmul", reads=["p2_wup", "p2_thw"], writes=[pk], out=ps[:, :n],
                           lhsT=wup[32 * d:32 * d + 32, hp * 128:(hp + 1) * 128], rhs=thw[32 * d:32 * d + 32, :n],
                           start=True, stop=True)
                    P.call("act", "activation", reads=[pk, "p2_w0"], writes=["p2_sgw"], out=sgw[:, :n], in_=ps[:, :n],
                           func=AF.Sigmoid, bias=w0c[:, d, hp:hp + 1])
                    i2 = cnt["dec"] % 2
                    cnt["dec"] += 1
                    dec = decb[i2]; dk = f"p2_dec{i2}"
                    P.call("act", "activation", reads=["p2_sgw"], writes=[dk], out=dec[:, :n], in_=sgw[:, :n],
                           func=AF.Exp, scale=-math.exp(-0.5))
                    P.dma("sp", K.col_w[l][2 * d + hp][:, t0:t0 + n], dec[:, :n], reads=[dk], writes=["col_w"])
                    ps, pk = px()
                    P.call("pe", "matmul", reads=["p2_aup", "p2_la"], writes=[pk], out=ps[:, :n],
                           lhsT=aup[32 * d:32 * d + 32, hp * 128:(hp + 1) * 128], rhs=la[32 * d:32 * d + 32, :n],
                           start=True, stop=True)
                    P.call("act", "activation", reads=[pk, "p2_a0"], writes=["p2_a"], out=av[:, :n], in_=ps[:, :n],
                           func=AF.Sigmoid, bias=a0c[:, d, hp:hp + 1])
                    P.call("dve", "tensor_scalar", reads=["p2_a", "p2_kac", "p2_omka"], writes=["p2_t"], out=tt[:, :n],
                           in0=av[:, :n], scalar1=kac[:, hp:hp + 1], scalar2=omka[:, hp:hp + 1], op0=ALU.mult, op1=ALU.add)
                    P.call("dve", "tensor_tensor", reads=["p2_cv", "p2_t"], writes=["p2_KDf"], out=KDf[:, d, hp, :n],
                           in0=kf, in1=tt[:, :n], op=ALU.mult)
                    P.call("dve", "scalar_tensor_tensor", reads=["p2_kk", "p2_a"], writes=["p2_NKf"], out=NKf[:, d, hp, :n],
                           in0=kk[:, hp, :n], scalar=-1.0, in1=av[:, :n], op0=ALU.mult, op1=ALU.mult)
                    P.dma("sp", K.col_nk[l][2 * d + hp][:, t0:t0 + n], NKf[:, d, hp, :n], reads=["p2_NKf"], writes=["col_nk"])
                    P.dma("sp", K.col_kd[l][2 * d + hp][:, t0:t0 + n], KDf[:, d, hp, :n], reads=["p2_KDf"], writes=["col_kd"])
            for blk in range(n // 128):
                tb = t0 + blk * 128
                bs = slice(blk * 128, (blk + 1) * 128)
                srcs = []
                for d in range(2):
                    for hp in range(2):
                        srcs.append((NKf[:, d, hp, bs], "p2_NKf"))
                for d in range(2):
                    for hp in range(2):
                        srcs.append((KDf[:, d, hp, bs], "p2_KDf"))
                for hp in range(2):
                    srcs.append((cv[:, 4 + hp, bs], "p2_cv"))
                for hp in range(2):
                    srcs.append((cv[:, hp, bs], "p2_cv"))
                for j, (sap, skey) in enumerate(srcs):
                    P.call("pe", "transpose", reads=[skey, "ident"], writes=["p2_pT"], inc=(j == len(srcs) - 1), out=pT[:, j, :], in_=sap,
                           identity=K.ident[:])
                P.call("pe", "matmul", reads=["p2_sgd", "p2_gup"], writes=["p2_pg"], out=pg, lhsT=sgd[64:128, bs], rhs=gup[64:128, :],
                       start=True, stop=True)
                i2 = cnt["tm"] % 2
                cnt["tm"] += 1
                tm = tmb[i2]; tk = f"p2_tm{i2}"
                for q in range(3):
                    eng = "act" if q == 1 else "dve"
                    o_ap = tm[:, 2 * q:2 * q + 2, :].rearrange("p a b -> p (a b)")
                    i_ap = pT[:, 4 * q:4 * q + 4, :].rearrange("p a b -> p (a b)")
                    if eng == "act":
                        P.call("act", "activation", reads=["p2_pT"], writes=[tk], out=o_ap, in_=i_ap, func=AF.Copy)
                    else:
                        P.call("dve", "tensor_copy", reads=["p2_pT"], writes=[tk], out=o_ap, in_=i_ap)
                P.call("act", "activation", reads=["p2_pg"], writes=[tk], out=tm[:, 6, :], in_=pg, func=AF.Copy)
                P.dma("sp", K.rw_tm[l][tb:tb + 128], tm[:], reads=[tk], writes=["rw_tm"])


TC = 32


def phase_scan(P, nc, K, l):
    nchunk = T // TC
    nctx = CTX // TC
    fwd = list(range(nchunk))
    bwd = list(range(nctx - 1, -1, -1)) + list(range(nchunk - 1, nctx - 1, -1))
    with ExitStack() as es:
        sb = lambda name, shape, dt=F32: es.enter_context(nc.sbuf_tensor(name, shape, dt))
        pp = lambda name, shape, dt=F32: es.enter_context(nc.psum_tensor(name, shape, dt))
        wcol = [sb(f"p3_w{b}", [128, 4, TC]) for b in range(2)]
        kkc = [sb(f"p3_kkc{b}", [128, 4, TC]) for b in range(2)]
        rc = [sb(f"p3_rc{b}", [128, 4, TC]) for b in range(2)]
        KKbd = [sb(f"p3_KK{b}", [128, 4, TC, 8]) for b in range(2)]
        Rbd = [sb(f"p3_R{b}", [128, 4, TC, 8]) for b in range(2)]
        LH = [[sb(f"p3_LH{b}{hp}", [128, TC, 128]) for hp in range(2)] for b in range(2)]
        Vr = [sb(f"p3_V{b}", [128, TC, 64]) for b in range(2)]
        Orows = [sb(f"p3_O{b}", [128, TC, 64]) for b in range(2)]
        SKV = sb("p3_SKV", [128, 64])
        S = sb("p3_S", [128, 4, 64])
        ps_sk = pp("p3_psk", [128, 512])[:, 0:64]; ps_o = pp("p3_po", [128, 512])[:, 0:64]
        ps_u = [pp(f"p3_pu{p}", [128, 512])[:, 0:64] for p in range(4)]
        for b in range(2):
            tiles = [(KKbd[b], f"p3_KK{b}"), (Rbd[b], f"p3_R{b}"), (Vr[b], f"p3_V{b}"), (Orows[b], f"p3_O{b}"),
                     (LH[b][0], f"p3_LH{b}"), (LH[b][1], f"p3_LH{b}")]
            for t_, nm in tiles:
                P.call("pool", "memset", writes=[nm + "_g0", nm + "_g1"], ap=t_[:], constant=0.0)
        P.call("pool", "memset", writes=["p3_S0", "p3_S1", "p3_S2", "p3_S3"], ap=S[:], constant=0.0)
        P.call("pool", "memset", writes=["p3_SKV_g0", "p3_SKV_g1"], ap=SKV[:], constant=0.0)
        row = lambda ap: ap.rearrange("(o t) k -> o t k", o=1)
        for c in range(nchunk):
            b = c % 2
            t0s = (fwd[c] * TC, bwd[c] * TC)
            for g in range(2):
                t0 = t0s[g]
                for hp in range(2):
                    p = 2 * g + hp
                    r0 = 64 * g + 4 * hp
                    P.dma("sp", wcol[b][:, p, :], K.col_w[l][p][:, t0:t0 + TC], reads=["col_w"], writes=[f"p3_w{b}_g{g}"])
                    P.dma("sp", kkc[b][:, p, :], K.col_kr[l][hp][:, t0:t0 + TC], reads=["col_kr"], writes=[f"p3_kkc{b}_g{g}"])
                    P.dma("sp", rc[b][:, p, :], K.col_kr[l][2 + hp][:, t0:t0 + TC], reads=["col_kr"], writes=[f"p3_rc{b}_g{g}"])
                    for j in range(2):
                        f0 = (2 * hp + j) * 64
                        P.dma("sp", LH[b][hp][r0 + j:r0 + j + 1, :, 64 * j:64 * j + 64],
                              row(K.rw_tm[l][t0:t0 + TC, g, f0:f0 + 64]), reads=["rw_tm"], writes=[f"p3_LH{b}_g{g}"])
                        P.dma("sp", LH[b][hp][r0 + 2 + j:r0 + 3 + j, :, 64 * j:64 * j + 64],
                              row(K.rw_tm[l][t0:t0 + TC, 2 + g, f0:f0 + 64]), reads=["rw_tm"], writes=[f"p3_LH{b}_g{g}"])
                        P.dma("sp", Vr[b][r0 + 2 + j:r0 + 3 + j, :, :],
                              row(K.rw_tm[l][t0:t0 + TC, 4, f0:f0 + 64]), reads=["rw_tm"], writes=[f"p3_V{b}_g{g}"])
                    for half in range(2):
                        hs = slice(64 * half, 64 * half + 64)
                        P.call("pool", "tensor_copy", reads=[f"p3_kkc{b}_g{g}"], writes=[f"p3_KK{b}_g{g}"],
                               out=KKbd[b][hs, p, :, 4 * hp + half], in_=kkc[b][hs, p, :])
                        P.call("pool", "tensor_copy", reads=[f"p3_rc{b}_g{g}"], writes=[f"p3_R{b}_g{g}"],
                               out=Rbd[b][hs, p, :, 4 * hp + half], in_=rc[b][hs, p, :])
            if getattr(K, "scan_dbg", False) and c == 0:
                for nm, t_, keys in (("dbg_LH0", LH[0][0], ["p3_LH0_g0", "p3_LH0_g1"]), ("dbg_LH1", LH[0][1], ["p3_LH0_g0", "p3_LH0_g1"]),
                                     ("dbg_V", Vr[0], ["p3_V0_g0", "p3_V0_g1"]), ("dbg_KK", KKbd[0], ["p3_KK0_g0", "p3_KK0_g1"]),
                                     ("dbg_R", Rbd[0], ["p3_R0_g0", "p3_R0_g1"]), ("dbg_w", wcol[0], ["p3_w0_g0", "p3_w0_g1"])):
                    shp = list(t_.shape)
                    o_ = nc.dram_tensor(nm, shp, F32, kind="ExternalOutput").ap()
                    P.dma("sp", o_, t_[:], reads=keys, writes=[nm])
            for i in range(TC):
                idxs = (i, TC - 1 - i)
                for g in range(2):
                    idx = idxs[g]
                    gs = slice(64 * g, 64 * g + 8)
                    for hp in range(2):
                        p = 2 * g + hp
                        P.call("pe", "matmul", reads=[f"p3_KK{b}_g{g}", f"p3_S{p}"], writes=[f"p3_psk_g{g}"],
                               out=ps_sk[gs, :], lhsT=KKbd[b][:, p, idx, :], rhs=S[:, p, :],
                               start=(hp == 0), stop=(hp == 1))
                    P.call("dve", "tensor_tensor", reads=[f"p3_psk_g{g}", f"p3_V{b}_g{g}"], writes=[f"p3_SKV_g{g}"],
                           out=SKV[gs, :], in0=ps_sk[gs, :], in1=Vr[b][gs, idx, :], op=ALU.add)
                    for hp in range(2):
                        p = 2 * g + hp
                        P.call("pe", "matmul", reads=[f"p3_LH{b}_g{g}", f"p3_SKV_g{g}"], writes=[f"p3_pu{p}"],
                               out=ps_u[p], lhsT=LH[b][hp][gs, idx, :], rhs=SKV[gs, :], start=True, stop=True)
                    for hp in range(2):
                        p = 2 * g + hp
                        P.call("dve", "scalar_tensor_tensor", reads=[f"p3_S{p}", f"p3_w{b}_g{g}", f"p3_pu{p}"],
                               writes=[f"p3_S{p}"], out=S[:, p, :], in0=S[:, p, :], scalar=wcol[b][:, p, idx:idx + 1],
                               in1=ps_u[p], op0=ALU.mult, op1=ALU.add)
                    for hp in range(2):
                        p = 2 * g + hp
                        P.call("pe", "matmul", reads=[f"p3_R{b}_g{g}", f"p3_S{p}"], writes=[f"p3_po_g{g}"],
                               out=ps_o[gs, :], lhsT=Rbd[b][:, p, idx, :], rhs=S[:, p, :],
                               start=(hp == 0), stop=(hp == 1))
                    P.call("act", "activation", reads=[f"p3_po_g{g}"], writes=[f"p3_O{b}_g{g}"],
                           out=Orows[b][gs, idx, :], in_=ps_o[gs, :], func=AF.Copy)
            if getattr(K, "scan_dbg", False) and c == 0:
                o_ = nc.dram_tensor("dbg_O", list(Orows[0].shape), F32, kind="ExternalOutput").ap()
                P.dma("sp", o_, Orows[0][:], reads=["p3_O0_g0", "p3_O0_g1"], writes=["dbg_O"])
                o_ = nc.dram_tensor("dbg_S", list(S.shape), F32, kind="ExternalOutput").ap()
                P.dma("sp", o_, S[:], reads=["p3_S0", "p3_S1", "p3_S2", "p3_S3"], writes=["dbg_S"])
                return
            for g in range(2):
                t0 = t0s[g]
                for hp in range(2):
                    r0 = 64 * g + 4 * hp
                    for j in range(2):
                        f0 = (2 * hp + j) * 64
                        P.dma("sp", row(K.o_tm[l][t0:t0 + TC, g, f0:f0 + 64]), Orows[b][r0 + j:r0 + j + 1, :, :],
                              reads=[f"p3_O{b}_g{g}"], writes=["o_tm"])


CH = 64


def phase_scan_chunked(P, nc, K, l):
    nchunk = T // CH
    nctx = CTX // CH
    order = [list(range(nchunk)), list(range(nctx - 1, -1, -1)) + list(range(nchunk - 1, nctx - 1, -1))]
    with ExitStack() as es:
        sb = lambda name, shape, dt=F32: es.enter_context(nc.sbuf_tensor(name, shape, dt))
        NBUF = 2
        names_in = ["w", "kk", "nk", "kd", "r"]
        tin = {nm: [[sb(f"c3_{nm}{p}{b}", [128, CH]) for b in range(NBUF)] for p in range(4)] for nm in names_in}
        Vtm = [[sb(f"c3_V{p}{b}", [128, 64]) for b in range(NBUF)] for p in range(4)]
        bdn = ["KH", "AH", "KD", "RH", "ANs", "KDs"]
        bd = {nm: [[sb(f"c3_{nm}{p}{b}", [128, 128]) for b in range(NBUF)] for p in range(4)] for nm in bdn}
        sqn = ["An", "AnT", "B", "Apn", "Bp", "X", "XT", "N", "ANtm", "KDtm"]
        sq = {nm: [[sb(f"c3_{nm}{p}{b}", [128, 128]) for b in range(NBUF)] for p in range(4)] for nm in sqn}
        X2 = [sb(f"c3_X2_{p}", [128, 128]) for p in range(4)]; X2T = [sb(f"c3_X2T_{p}", [128, 128]) for p in range(4)]
        Pc = [sb(f"c3_Pc{p}", [128, CH]) for p in range(4)]; Pm1 = [sb(f"c3_Pm{p}", [128, CH]) for p in range(4)]
        rP = [sb(f"c3_rP{p}", [128, CH]) for p in range(4)]; tmpv = [sb(f"c3_tv{p}", [128, CH]) for p in range(4)]
        PCc = [[sb(f"c3_PC{p}{b}", [128, 1]) for b in range(NBUF)] for p in range(4)]
        zer = sb("c3_zero", [128, CH])
        S = [sb(f"c3_S{p}", [128, 64]) for p in range(4)]
        Rt = [sb(f"c3_Rt{p}", [128, 64]) for p in range(4)]; Ut = [sb(f"c3_Ut{p}", [128, 64]) for p in range(4)]
        Ot = [[sb(f"c3_Ot{p}{b}", [128, 64]) for b in range(NBUF)] for p in range(4)]
        msk = sb("c3_msk", [128, 4, 128])
        psb = [es.enter_context(nc.psum_tensor(f"c3_ps{i}", [128, 512], F32)) for i in range(8)]
        pcnt = {"i": 0}

        def ps():
            i = pcnt["i"] % 8
            pcnt["i"] += 1
            return psb[i], f"c3_ps{i}"

        P.dma("sp", msk[:], K.cmask_d.rearrange("m a b -> a m b"), writes=["c3_msk"])
        P.call("pool", "memset", writes=["c3_zero"], ap=zer[:], constant=0.0)
        for p in range(4):
            P.call("pool", "memset", writes=[f"c3_S{p}"], ap=S[p][:], constant=0.0)
            for b in range(NBUF):
                for nm in bdn:
                    P.call("pool", "memset", writes=[f"c3_{nm}{p}{b}"], ap=bd[nm][p][b][:], constant=0.0)

        def mm(out, lhsT, rhs, reads, pk, start=True, stop=True, fast=False, inc=True):
            P.call("pe", "matmul", reads=reads, writes=[pk], inc=inc, out=out, lhsT=lhsT, rhs=rhs, start=start, stop=stop)

        def stage_a(ci, p):
            b = ci % NBUF
            d = p // 2; hp = p % 2
            t0 = order[d][ci] * CH
            k = lambda nm: f"c3_{nm}{p}{b}"
            srcs = {"w": K.col_w[l][p], "kk": K.col_kr[l][hp], "nk": K.col_nk[l][p], "kd": K.col_kd[l][p], "r": K.col_kr[l][2 + hp]}
            rkeys = {"w": "col_w", "kk": "col_kr", "nk": "col_nk", "kd": "col_kd", "r": "col_kr"}
            for nm in names_in:
                P.dma("sp", tin[nm][p][b][:], srcs[nm][:, t0:t0 + CH], reads=[rkeys[nm]], writes=[k(nm)])
            for j in range(2):
                f0 = (2 * hp + j) * 64
                P.dma("sp", Vtm[p][b][64 * j:64 * j + 64, :], K.rw_tm[l][t0:t0 + CH, 4, f0:f0 + 64], reads=["rw_tm"], writes=[k("V")])
            w_ = tin["w"][p][b]
            P.call("dve", "tensor_tensor_scan", reads=[k("w"), "c3_zero"], writes=[f"c3_Pc{p}"], out=Pc[p][:], data0=w_[:], data1=zer[:],
                   initial=1.0, op0=ALU.mult, op1=ALU.add)
            P.call("pool", "memset", writes=[f"c3_Pm{p}"], ap=Pm1[p][:, 0:1], constant=1.0)
            P.call("pool", "tensor_copy", reads=[f"c3_Pc{p}"], writes=[f"c3_Pm{p}"], out=Pm1[p][:, 1:CH], in_=Pc[p][:, 0:CH - 1])
            P.call("act", "activation", reads=[f"c3_Pc{p}"], writes=[k("PC")], out=PCc[p][b][:], in_=Pc[p][:, CH - 1:CH], func=AF.Copy)
            if d == 1:
                P.call("dve", "reciprocal", reads=[f"c3_Pm{p}"], writes=[f"c3_tv{p}"], out=tmpv[p][:], in_=Pm1[p][:])
                P.call("dve", "reciprocal", reads=[f"c3_Pc{p}"], writes=[f"c3_rP{p}"], out=rP[p][:], in_=Pc[p][:])
                P.call("dve", "tensor_scalar", reads=[f"c3_tv{p}", k("PC")], writes=[f"c3_Pc{p}"], out=Pc[p][:], in0=tmpv[p][:],
                       scalar1=PCc[p][b][:, 0:1], scalar2=None, op0=ALU.mult)
                P.call("dve", "tensor_scalar", reads=[f"c3_rP{p}", k("PC")], writes=[f"c3_Pm{p}"], out=Pm1[p][:], in0=rP[p][:],
                       scalar1=PCc[p][b][:, 0:1], scalar2=None, op0=ALU.mult)
            P.call("dve", "reciprocal", reads=[f"c3_Pc{p}"], writes=[f"c3_rP{p}"], out=rP[p][:], in_=Pc[p][:])
            for j in range(2):
                hs = slice(64 * j, 64 * j + 64)
                eng = "dve" if j == 0 else "pool"
                P.call(eng, "tensor_tensor", reads=[k("kk"), f"c3_Pm{p}"], writes=[k("KH")], out=bd["KH"][p][b][hs, hs],
                       in0=tin["kk"][p][b][hs, :], in1=Pm1[p][hs, :], op=ALU.mult)
                P.call(eng, "tensor_tensor", reads=[k("nk"), f"c3_rP{p}"], writes=[k("AH")], out=bd["AH"][p][b][hs, hs],
                       in0=tin["nk"][p][b][hs, :], in1=rP[p][hs, :], op=ALU.mult)
                P.call(eng, "tensor_tensor", reads=[k("kd"), f"c3_rP{p}"], writes=[k("KD")], out=bd["KD"][p][b][hs, hs],
                       in0=tin["kd"][p][b][hs, :], in1=rP[p][hs, :], op=ALU.mult)
                P.call(eng, "tensor_tensor", reads=[k("r"), f"c3_Pc{p}"], writes=[k("RH")], out=bd["RH"][p][b][hs, hs],
                       in0=tin["r"][p][b][hs, :], in1=Pc[p][hs, :], op=ALU.mult)
                P.call("dve", "tensor_scalar", reads=[k("AH"), k("PC")], writes=[k("ANs")], out=bd["ANs"][p][b][hs, hs],
                       in0=bd["AH"][p][b][hs, hs], scalar1=PCc[p][b][hs, 0:1], scalar2=None, op0=ALU.mult)
                P.call("dve", "tensor_scalar", reads=[k("KD"), k("PC")], writes=[k("KDs")], out=bd["KDs"][p][b][hs, hs],
                       in0=bd["KD"][p][b][hs, hs], scalar1=PCc[p][b][hs, 0:1], scalar2=None, op0=ALU.mult)

        for p in range(4):
            stage_a(0, p)
        for ci in range(nchunk):
            b = ci % NBUF
            kf = lambda p, nm: f"c3_{nm}{p}{b}"
            for p in range(4):
                d = p // 2
                k = lambda nm, p=p: f"c3_{nm}{p}{b}"
                for nm_s, nm_d in (("ANs", "ANtm"), ("KDs", "KDtm")):
                    pt_, pk = ps()
                    P.call("pe", "transpose", reads=[k(nm_s), "ident"], writes=[pk], inc=False, out=pt_[:, 0:128], in_=bd[nm_s][p][b][:], identity=K.ident[:])
                    P.call("act", "activation", reads=[pk], writes=[k(nm_d)], out=sq[nm_d][p][b][:], in_=pt_[:, 0:128], func=AF.Copy)
                ms, mi = (0, 1) if d == 0 else (2, 3)
                for (dst, lh, rh, mk) in (("An", "AH", "KH", ms), ("B", "KD", "KH", ms), ("Apn", "AH", "RH", mi), ("Bp", "KD", "RH", mi)):
                    pt_, pk = ps()
                    mm(pt_[:, 0:128], bd[lh][p][b][:], bd[rh][p][b][:], [k(lh), k(rh)], pk)
                    P.call("dve", "tensor_tensor", reads=[pk, "c3_msk"], writes=[k(dst)], out=sq[dst][p][b][:], in0=pt_[:, 0:128],
                           in1=msk[:, mk, :], op=ALU.mult)
            for p in range(4):
                k = lambda nm, p=p: f"c3_{nm}{p}{b}"
                pt_, pk = ps()
                P.call("pe", "transpose", reads=[k("An"), "ident"], writes=[pk], out=pt_[:, 0:128], in_=sq["An"][p][b][:], identity=K.ident[:])
                P.call("act", "activation", reads=[pk], writes=[k("AnT")], out=sq["AnT"][p][b][:], in_=pt_[:, 0:128], func=AF.Copy)
                P.call("dve", "tensor_tensor", reads=[k("An"), "ident"], writes=[k("N")], out=sq["N"][p][b][:], in0=sq["An"][p][b][:], in1=K.ident[:], op=ALU.add)
            cur = {p: (sq["An"][p][b], sq["AnT"][p][b], kf(p, "An"), kf(p, "AnT")) for p in range(4)}
            nround = 5
            for rnd in range(nround):
                lastr = rnd == nround - 1
                nxt = {}
                for p in range(4):
                    Xc, XTc, xk, xtk = cur[p]
                    pt_, pk = ps()
                    mm(pt_[:, 0:128], Xc[:], XTc[:], [xk, xtk], pk)
                    x2t = X2T[p] if rnd % 2 == 0 else sq["XT"][p][b]
                    x2tk = f"c3_X2T_{p}" if rnd % 2 == 0 else kf(p, "XT")
                    P.call("act", "activation", reads=[pk], writes=[x2tk], out=x2t[:], in_=pt_[:, 0:128], func=AF.Copy)
                    nxt[p] = (x2t, x2tk)
                for p in range(4):
                    x2t, x2tk = nxt[p]
                    Nt = sq["N"][p][b]
                    pt3, pk3 = ps()
                    mm(pt3[:, 0:128], x2t[:], Nt[:], [x2tk, kf(p, "N")], pk3)
                    P.call("dve", "tensor_tensor", reads=[pk3, kf(p, "N")], writes=[kf(p, "N")], out=Nt[:], in0=pt3[:, 0:128], in1=Nt[:], op=ALU.add)
                    if not lastr:
                        pt2, pk2 = ps()
                        P.call("pe", "transpose", reads=[x2tk, "ident"], writes=[pk2], out=pt2[:, 0:128], in_=x2t[:], identity=K.ident[:])
                        x2 = X2[p] if rnd % 2 == 0 else sq["X"][p][b]
                        x2k = f"c3_X2_{p}" if rnd % 2 == 0 else kf(p, "X")
                        P.call("act", "activation", reads=[pk2], writes=[x2k], out=x2[:], in_=pt2[:, 0:128], func=AF.Copy)
                        cur[p] = (x2, x2t, x2k, x2tk)
                if rnd < 4 and ci + 1 < nchunk:
                    stage_a(ci + 1, rnd)
            if ci == nchunk - 1:
                P.mark(f"L{l}_scan_pre_last")
            kf = lambda p, nm: f"c3_{nm}{p}{b}"
            held = {}
            for p in range(4):
                pt_, pk = ps()
                mm(pt_[:, 0:64], bd["KH"][p][b][:], S[p][:], [kf(p, "KH"), f"c3_S{p}"], pk, start=True, stop=False, inc=False)
                mm(pt_[:, 0:64], sq["B"][p][b][:], Vtm[p][b][:], [kf(p, "B"), kf(p, "V")], pk, start=False, stop=True)
                P.call("act", "activation", reads=[pk], writes=[f"c3_Rt{p}"], out=Rt[p][:], in_=pt_[:, 0:64], func=AF.Copy)
            for p in range(4):
                pt2, pk2 = ps()
                mm(pt2[:, 0:64], sq["N"][p][b][:], Rt[p][:], [kf(p, "N"), f"c3_Rt{p}"], pk2)
                P.call("dve", "tensor_copy", reads=[pk2], writes=[f"c3_Ut{p}"], out=Ut[p][:], in_=pt2[:, 0:64])
            for p in range(4):
                sk_ = f"c3_S{p}"
                pt3, pk3 = ps()
                mm(pt3[:, 0:64], bd["RH"][p][b][:], S[p][:], [kf(p, "RH"), sk_], pk3, start=True, stop=False, inc=False)
                mm(pt3[:, 0:64], sq["Apn"][p][b][:], Ut[p][:], [kf(p, "Apn"), f"c3_Ut{p}"], pk3, start=False, stop=False, inc=False)
                mm(pt3[:, 0:64], sq["Bp"][p][b][:], Vtm[p][b][:], [kf(p, "Bp"), kf(p, "V")], pk3, start=False, stop=True)
                P.call("act", "activation", reads=[pk3], writes=[kf(p, "Ot")], out=Ot[p][b][:], in_=pt3[:, 0:64], func=AF.Copy)
                pt4, pk4 = ps()
                mm(pt4[:, 0:64], sq["ANtm"][p][b][:], Ut[p][:], [kf(p, "ANtm"), f"c3_Ut{p}"], pk4, start=True, stop=False, inc=False)
                mm(pt4[:, 0:64], sq["KDtm"][p][b][:], Vtm[p][b][:], [kf(p, "KDtm"), kf(p, "V")], pk4, start=False, stop=True)
                P.call("dve", "scalar_tensor_tensor", reads=[sk_, kf(p, "PC"), pk4], writes=[sk_], out=S[p][:], in0=S[p][:],
                       scalar=PCc[p][b][:, 0:1], in1=pt4[:, 0:64], op0=ALU.mult, op1=ALU.add)
            for p in range(4):
                d = p // 2; hp = p % 2
                t0 = order[d][ci] * CH
                for j in range(2):
                    f0 = (2 * hp + j) * 64
                    P.dma("sp", K.o_tm[l][t0:t0 + CH, d, f0:f0 + 64], Ot[p][b][64 * j:64 * j + 64, :], reads=[kf(p, "Ot")], writes=["o_tm"])


def phase_readout(P, nc, K, l):
    with ExitStack() as es:
        sb = lambda name, shape, dt=F32: es.enter_context(nc.sbuf_tensor(name, shape, dt))
        lng = sb("p4_lng", [128, 256]); lnb = sb("p4_lnb", [128, 256]); rkr = sb("p4_rkr", [128, 256])
        o2 = [sb(f"p4_o2{b}", [128, 2, 256]) for b in range(2)]
        tm = [sb(f"p4_tm{b}", [128, 7, 256]) for b in range(2)]
        o = sb("p4_o", [128, 4, 64]); xc = sb("p4_xc", [128, 4, 64]); sq = sb("p4_sq", [128, 4, 64])
        mu = sb("p4_mu", [128, 4]); var = sb("p4_var", [128, 4]); bs = sb("p4_bs", [128, 4])
        kds = sb("p4_kds", [128, 256]); y = [sb(f"p4_y{b}", [128, 256]) for b in range(2)]
        P.dma("sp", lng[:], K.rw_ln_g[l].partition_broadcast(128), writes=["p4_lng"])
        P.dma("sp", lnb[:], K.rw_ln_b[l].partition_broadcast(128), writes=["p4_lnb"])
        P.dma("sp", rkr[:], K.rw_r_k[l].partition_broadcast(128), writes=["p4_rkr"])
        f3 = lambda ap: ap.rearrange("p (h n) -> p h n", h=4)
        f2 = lambda ap: ap.rearrange("p h n -> p (h n)")
        bc = lambda ap: ap.unsqueeze(2).broadcast_to([128, 4, 64])
        for bi in range(T // 128):
            tb = bi * 128
            b = bi % 2
            ok = f"p4_o2{b}"; tk = f"p4_tm{b}"; yk = f"p4_y{b}"
            P.dma("sp", o2[b][:], K.o_tm[l][tb:tb + 128], reads=["o_tm"], writes=[ok])
            P.dma("sp", tm[b][:], K.rw_tm[l][tb:tb + 128], reads=["rw_tm"], writes=[tk])
            P.call("dve", "tensor_tensor", reads=[ok], writes=["p4_o"], out=f2(o[:]), in0=o2[b][:, 0, :], in1=o2[b][:, 1, :], op=ALU.add)
            P.call("dve", "tensor_reduce", reads=["p4_o"], writes=["p4_mu"], out=mu[:], in_=o[:], axis=AX.X, op=ALU.add)
            P.call("dve", "tensor_scalar", reads=["p4_mu"], writes=["p4_mu"], out=mu[:], in0=mu[:], scalar1=1.0 / 64, scalar2=None, op0=ALU.mult)
            P.call("dve", "tensor_tensor", reads=["p4_o", "p4_mu"], writes=["p4_xc"], out=xc[:], in0=o[:], in1=bc(mu[:]), op=ALU.subtract)
            P.call("pool", "tensor_tensor", reads=["p4_xc"], writes=["p4_sq"], out=sq[:], in0=xc[:], in1=xc[:], op=ALU.mult)
            P.call("dve", "tensor_reduce", reads=["p4_sq"], writes=["p4_var"], out=var[:], in_=sq[:], axis=AX.X, op=ALU.add)
            P.call("dve", "tensor_scalar", reads=["p4_var"], writes=["p4_var"], out=var[:], in0=var[:], scalar1=1.0 / 64, scalar2=64e-5,
                   op0=ALU.mult, op1=ALU.add)
            P.call("act", "activation", reads=["p4_var"], writes=["p4_var"], out=var[:], in_=var[:], func=AF.Sqrt)
            P.call("dve", "reciprocal", reads=["p4_var"], writes=["p4_var"], out=var[:], in_=var[:])
            P.call("dve", "tensor_tensor", reads=["p4_xc", "p4_var"], writes=["p4_xc"], out=xc[:], in0=xc[:], in1=bc(var[:]), op=ALU.mult)
            P.call("dve", "tensor_tensor", reads=["p4_xc", "p4_lng"], writes=["p4_xc"], out=f2(xc[:]), in0=f2(xc[:]), in1=lng[:], op=ALU.mult)
            P.call("dve", "tensor_tensor", reads=["p4_xc", "p4_lnb"], writes=["p4_xc"], out=f2(xc[:]), in0=f2(xc[:]), in1=lnb[:], op=ALU.add)
            P.call("pool", "tensor_tensor", reads=[tk], writes=["p4_kds"], out=kds[:], in0=tm[b][:, 2, :], in1=tm[b][:, 3, :], op=ALU.add)
            P.call("pool", "tensor_tensor", reads=[tk, "p4_kds"], writes=["p4_kds"], out=kds[:], in0=kds[:], in1=tm[b][:, 5, :], op=ALU.mult)
            P.call("pool", "tensor_tensor", reads=["p4_kds", "p4_rkr"], writes=["p4_kds"], out=kds[:], in0=kds[:], in1=rkr[:], op=ALU.mult)
            P.call("dve", "tensor_reduce", reads=["p4_kds"], writes=["p4_bs"], out=bs[:], in_=f3(kds[:]), axis=AX.X, op=ALU.add)
            P.call("dve", "tensor_tensor", reads=[tk, "p4_bs"], writes=["p4_sq"], out=sq[:], in0=f3(tm[b][:, 4, :]), in1=bc(bs[:]), op=ALU.mult)
            P.call("dve", "tensor_tensor", reads=["p4_sq", "p4_xc"], writes=["p4_sq"], out=sq[:], in0=sq[:], in1=xc[:], op=ALU.add)
            P.call("dve", "tensor_tensor", reads=["p4_sq", tk], writes=[yk], out=y[b][:], in0=f2(sq[:]), in1=tm[b][:, 6, :], op=ALU.mult)
            P.dma("sp", K.ytm[l][tb:tb + 128, 0:256], y[b][:], reads=[yk], writes=["ytm"])


def phase_attn(P, nc, K, l):
    lam_init = 0.8 - 0.6 * math.exp(-0.3 * l)
    NBLK = T // 128
    with ExitStack() as es:
        sb = lambda name, shape, dt=F32: es.enter_context(nc.sbuf_tensor(name, shape, dt))
        pp = lambda name: es.enter_context(nc.psum_tensor(name, [128, 512], F32))
        kdf = sb("p5_kdf", [128, 2, T], BF16)
        Vaug = sb("p5_V", [128, NBLK, 6, 65], BF16)
        Kd = [sb(f"p5_Kd{g}", [128, T], BF16) for g in range(2)]
        Qm = [[sb(f"p5_Qm{b}_{u}", [128, 512], BF16) for u in range(8)] for b in range(2)]
        Eb = [sb(f"p5_E{i}", [128, 512], BF16) for i in range(3)]
        oT = [sb(f"p5_oT{m}", [65, 512]) for m in range(2)]
        rz = [sb(f"p5_rz{m}", [64, 512]) for m in range(2)]
        dd = sb("p5_dd", [64, 512]); sqt = sb("p5_sq", [64, 512]); rs = sb("p5_rs", [64, 512])
        ybt = [sb(f"p5_yb{i}", [64, 512], BF16) for i in range(2)]
        sel = sb("p5_sel", [65, 64]); ones64 = sb("p5_ones", [64, 64])
        lamt = sb("p5_lamt", [64, 128]); lp = sb("p5_lp", [64, 64]); e12 = sb("p5_e12", [64, 2]); neglam = sb("p5_nl", [64, 1])
        gdf = sb("p5_gdf", [64, 1])
        msk = sb("p5_msk", [128, 2, 128])
        exps = sb("p5_exps", [128, 8])
        qg = [sb(f"p5_qg{b}", [128, 4, 128], BF16) for b in range(2)]
        Eg = [sb(f"p5_Eg{b}", [128, 5, 128], BF16) for b in range(2)]
        zt = sb("p5_zt", [128, 1]); ycst = [sb(f"p5_yc{b}", [128, 512]) for b in range(2)]
        ps_s = [pp(f"p5_ps{i}") for i in range(3)]
        ps_o = [pp(f"p5_po{m}") for m in range(2)]
        ps_z = [pp(f"p5_pz{m}") for m in range(2)]
        ps_g = pp("p5_pg")
        NS = dict(allow_slow_non_contiguous=True)
        for c in range(2):
            P.dma("sp", kdf[:, c, :], K.ropeT[l][256 + c * 128:256 + (c + 1) * 128, :], reads=["ropeT"], writes=["p5_kdf"])
        for g in range(2):
            for half in range(2):
                P.dma("sp", Kd[g][64 * half:64 * half + 64, :], K.ropeT[l][1024 + 64 * g:1024 + 64 * g + 64, :],
                      reads=["ropeT"], writes=[f"p5_Kd{g}"])
        P.call("pool", "memset", writes=["p5_V"], ap=Vaug[:], constant=1.0)
        for kb in range(NBLK):
            P.dma("sp", Vaug[:, kb, :, 0:64], K.vtm[l][kb * 128:(kb + 1) * 128, :].rearrange("p (h d) -> p h d", h=6),
                  reads=["vtm"], writes=["p5_V"])
        for b in range(2):
            for u in range(8):
                P.call("pool", "memset", writes=[f"p5_Qm{b}"], ap=Qm[b][u][:], constant=0.0)
        P.dma("sp", sel[:], K.sel_d, writes=["p5_sel"])
        P.dma("sp", ones64[:], K.bones_d[0:64, 0:64], writes=["p5_ones"])
        P.dma("sp", msk[:], K.msk_d.rearrange("m a b -> a m b"), writes=["p5_msk"])
        P.dma("sp", lamt[:], K.df_lambda[l].partition_broadcast(64), writes=["p5_lamt"])
        P.dma("sp", gdf[:], K.df_norm_g[l].rearrange("(p o) -> p o", o=1), writes=["p5_gdf"], **NS)
        P.dma("sp", exps[:], K.gq_sink[l].partition_broadcast(128), writes=["p5_exps"])
        P.call("act", "activation", reads=["p5_exps"], writes=["p5_exps"], out=exps[:], in_=exps[:], func=AF.Exp)
        P.call("dve", "tensor_scalar", reads=["p5_gdf"], writes=["p5_gdf"], out=gdf[:], in0=gdf[:], scalar1=1.0 - lam_init,
               scalar2=None, op0=ALU.mult)
        for i in range(2):
            P.call("dve", "tensor_tensor", reads=["p5_lamt"], writes=["p5_lp"], out=lp[:, 32 * i:32 * i + 32],
                   in0=lamt[:, 64 * i:64 * i + 32], in1=lamt[:, 64 * i + 32:64 * i + 64], op=ALU.mult)
        P.call("dve", "tensor_reduce", reads=["p5_lp"], writes=["p5_e12"], out=e12[:],
               in_=lp[:].rearrange("p (a b) -> p a b", a=2), axis=AX.X, op=ALU.add)
        P.call("act", "activation", reads=["p5_e12"], writes=["p5_e12"], out=e12[:], in_=e12[:], func=AF.Exp)
        P.call("dve", "tensor_tensor", reads=["p5_e12"], writes=["p5_nl"], out=neglam[:], in0=e12[:, 1:2], in1=e12[:, 0:1], op=ALU.subtract)
        P.call("dve", "tensor_scalar", reads=["p5_nl"], writes=["p5_nl"], out=neglam[:], in0=neglam[:], scalar1=-lam_init,
               scalar2=None, op0=ALU.add)
        cnt = {"s": 0, "yb": 0}
        for ti, (t0, nq) in enumerate(tiles_512()):
            b = ti % 2
            kbs = list(range(0, CTX // 128)) if t0 < CTX else list(range(NBLK))
            for u in range(8):
                r0 = 32 * (u % 4)
                P.dma("sp", Qm[b][u][r0:r0 + 32, :nq], K.ropeT[l][(u // 4) * 128 + r0:(u // 4) * 128 + r0 + 32, t0:t0 + nq],
                      reads=["ropeT"], writes=[f"p5_Qm{b}"])
            seq = [(h, m, ki, kb) for h in range(4) for m in range(2) for ki, kb in enumerate(kbs)]

            def emit_S(i):
                h, m, ki, kb = seq[i]
                u = 2 * h + m
                i3 = i % 3
                P.call("pe", "matmul", reads=["p5_kdf", f"p5_Qm{b}"], writes=[f"p5_ps{i3}"], out=ps_s[i3][:, :nq],
                       lhsT=kdf[:, u // 4, kb * 128:(kb + 1) * 128], rhs=Qm[b][u][:, :nq], start=True, stop=True)
                P.call("act", "activation", reads=[f"p5_ps{i3}"], writes=[f"p5_E{i3}"], out=Eb[i3][:, :nq],
                       in_=ps_s[i3][:, :nq], func=AF.Exp, scale=32 ** -0.5)

            def emit_PV(i):
                h, m, ki, kb = seq[i]
                i3 = i % 3
                P.call("pe", "matmul", reads=["p5_V", f"p5_E{i3}"], writes=[f"p5_po{m}"], out=ps_o[m][0:65, :nq],
                       lhsT=Vaug[:, kb, h, :], rhs=Eb[i3][:, :nq], start=(ki == 0), stop=(ki == len(kbs) - 1))

            def post(h):
                for m in range(2):
                    P.call("act", "activation", reads=[f"p5_po{m}"], writes=[f"p5_oT{m}"], out=oT[m][:, :nq], in_=ps_o[m][0:65, :nq],
                           func=AF.Copy)
                    P.call("pe", "matmul", reads=["p5_sel", f"p5_oT{m}"], writes=[f"p5_pz{m}"], out=ps_z[m][0:64, :nq],
                           lhsT=sel[:], rhs=oT[m][:, :nq], start=True, stop=True)
                    P.call("dve", "reciprocal", reads=[f"p5_pz{m}"], writes=[f"p5_rz{m}"], out=rz[m][:, :nq], in_=ps_z[m][0:64, :nq])
                    P.call("dve", "tensor_tensor", reads=[f"p5_oT{m}", f"p5_rz{m}"], writes=[f"p5_rz{m}"], out=rz[m][:, :nq],
                           in0=oT[m][0:64, :nq], in1=rz[m][:, :nq], op=ALU.mult)
                P.call("dve", "scalar_tensor_tensor", reads=["p5_rz0", "p5_rz1", "p5_nl"], writes=["p5_dd"], out=dd[:, :nq],
                       in0=rz[1][:, :nq], scalar=neglam[:, 0:1], in1=rz[0][:, :nq], op0=ALU.mult, op1=ALU.add)
                P.call("act", "activation", reads=["p5_dd"], writes=["p5_sq"], out=sqt[:, :nq], in_=dd[:, :nq], func=AF.Square)
                P.call("pe", "matmul", reads=["p5_ones", "p5_sq"], writes=["p5_pz0"], out=ps_z[0][0:64, :nq], lhsT=ones64[:],
                       rhs=sqt[:, :nq], start=True, stop=True)
                P.call("dve", "tensor_scalar", reads=["p5_pz0"], writes=["p5_rs"], out=rs[:, :nq], in0=ps_z[0][0:64, :nq],
                       scalar1=1.0 / 64, scalar2=EPS, op0=ALU.mult, op1=ALU.add)
                P.call("act", "activation", reads=["p5_rs"], writes=["p5_rs"], out=rs[:, :nq], in_=rs[:, :nq], func=AF.Sqrt)
                P.call("dve", "reciprocal", reads=["p5_rs"], writes=["p5_rs"], out=rs[:, :nq], in_=rs[:, :nq])
                i2 = cnt["yb"] % 2
                cnt["yb"] += 1
                P.call("dve", "scalar_tensor_tensor", reads=["p5_dd", "p5_gdf", "p5_rs"], writes=[f"p5_yb{i2}"], out=ybt[i2][:, :nq],
                       in0=dd[:, :nq], scalar=gdf[:, 0:1], in1=rs[:, :nq], op0=ALU.mult, op1=ALU.mult)
                P.dma("sp", K.yT[l][256 + 64 * h:256 + 64 * h + 64, t0:t0 + nq], ybt[i2][:, :nq], reads=[f"p5_yb{i2}"], writes=["yT"])

            LOOK = 2
            for i in range(min(LOOK, len(seq))):
                emit_S(i)
            for i in range(len(seq)):
                if i + LOOK < len(seq):
                    emit_S(i + LOOK)
                emit_PV(i)
                h, m, ki, kb = seq[i]
                if m == 1 and ki == len(kbs) - 1:
                    post(h)
        P.mark(f"L{l}_diff_end")
        for tb in range(NBLK):
            b = tb % 2
            P.dma("sp", qg[b][:], K.ropeT[l][512:1024, tb * 128:(tb + 1) * 128].rearrange("(c p) t -> p c t", p=128),
                  reads=["ropeT"], writes=[f"p5_qg{b}"])
            keyblocks = [0, 1]
            if tb >= 2:
                keyblocks += [kb for kb in (tb - 1, tb, tb + 1) if 2 <= kb < NBLK]
            nk = len(keyblocks)
            psA = [(ps_s[0], "p5_ps0", ps_s[1], "p5_ps1"), (ps_s[2], "p5_ps2", ps_o[0], "p5_po0")]
            psG = [(ps_g, "p5_pg"), (ps_o[1], "p5_po1")]

            def g_scores(hd):
                g = hd // 4; c = hd // 2; base = 64 * (hd % 2)
                bs = slice(base, base + 64)
                pa, pak, pb, pbk = psA[hd % 2]
                for j, kb in enumerate(keyblocks):
                    pst, pstk = (pa, pak) if j < 4 else (pb, pbk)
                    P.call("pe", "matmul", reads=[f"p5_Kd{g}", f"p5_qg{b}"], writes=[pstk], inc=(j == nk - 1 or j == 3),
                           out=pst[:, (j % 4) * 128:(j % 4 + 1) * 128], lhsT=Kd[g][bs, kb * 128:(kb + 1) * 128], rhs=qg[b][bs, c, :],
                           start=True, stop=True)
                eg = Eg[hd % 2]; ek = f"p5_Eg{hd % 2}"
                n0 = min(nk, 4)
                P.call("act", "activation", reads=[pak], writes=[ek], out=eg[:, 0:n0, :].rearrange("p a b -> p (a b)"),
                       in_=pa[:, 0:n0 * 128], func=AF.Exp, scale=64 ** -0.5)
                if nk > 4:
                    P.call("act", "activation", reads=[pbk], writes=[ek], out=eg[:, 4, :], in_=pb[:, 0:128],
                           func=AF.Exp, scale=64 ** -0.5)
                for j, kb in enumerate(keyblocks):
                    if tb >= 2 and kb == tb - 1 and kb >= 2:
                        P.call("pool", "tensor_tensor", reads=[ek, "p5_msk"], writes=[ek], out=eg[:, j, :], in0=eg[:, j, :], in1=msk[:, 0, :], op=ALU.mult)
                    if tb >= 2 and kb == tb + 1:
                        P.call("pool", "tensor_tensor", reads=[ek, "p5_msk"], writes=[ek], out=eg[:, j, :], in0=eg[:, j, :], in1=msk[:, 1, :], op=ALU.mult)

            def g_pv(hd):
                g = hd // 4
                eg = Eg[hd % 2]; ek = f"p5_Eg{hd % 2}"
                pgt, pgk = psG[hd % 2]
                for j, kb in enumerate(keyblocks):
                    P.call("pe", "matmul", reads=[ek, "p5_V"], writes=[pgk], inc=(j == nk - 1), out=pgt[:, 0:65], lhsT=eg[:, j, :],
                           rhs=Vaug[:, kb, 4 + g, :], start=(j == 0), stop=(j == nk - 1))
                P.call("dve", "tensor_scalar", reads=[pgk, "p5_exps"], writes=["p5_zt"], out=zt[:], in0=pgt[:, 64:65],
                       scalar1=exps[:, hd:hd + 1], scalar2=None, op0=ALU.add)
                P.call("dve", "reciprocal", reads=["p5_zt"], writes=["p5_zt"], out=zt[:], in_=zt[:])
                P.call("dve", "tensor_scalar", reads=[pgk, "p5_zt"], writes=[f"p5_yc{b}"], out=ycst[b][:, hd * 64:(hd + 1) * 64],
                       in0=pgt[:, 0:64], scalar1=zt[:, 0:1], scalar2=None, op0=ALU.mult)

            g_scores(0)
            for hd in range(8):
                if hd + 1 < 8:
                    g_scores(hd + 1)
                g_pv(hd)
            P.dma("sp", K.ytm[l][tb * 128:(tb + 1) * 128, 512:1024], ycst[b][:], reads=[f"p5_yc{b}"], writes=["ytm"])


def row_gain(P, nc, K, l, gi, jga, G, tmp, name):
    for who in range(2):
        P.dma("sp", G[who][:], K.norm_g[l, gi].partition_broadcast(128), writes=[f"{name}_G{who}"])
        P.dma("sp", tmp[:], K.modrow[l, who, jga].partition_broadcast(128), reads=["modrow"], writes=[name + "_tmp"])
        P.call("dve", "tensor_tensor", reads=[f"{name}_G{who}", name + "_tmp"], writes=[f"{name}_G{who}"],
               out=G[who][:], in0=G[who][:], in1=tmp[:], op=ALU.mult)


def norm_rows(P, ps2, pkeys, ss, ss2, rs, junk, pfx):
    P.call("act", "activation", reads=[pkeys[0]], writes=[pfx + "_junk", pfx + "_ss"], out=junk[:, 0:512], in_=ps2[0][:, :],
           func=AF.Square, accum_out=ss[:])
    P.call("act", "activation", reads=[pkeys[1]], writes=[pfx + "_junk", pfx + "_ss2"], out=junk[:, 512:1024], in_=ps2[1][:, :],
           func=AF.Square, accum_out=ss2[:])
    P.call("dve", "tensor_tensor", reads=[pfx + "_ss", pfx + "_ss2"], writes=[pfx + "_rs"], out=rs[:], in0=ss[:], in1=ss2[:], op=ALU.add)
    P.call("dve", "tensor_scalar", reads=[pfx + "_rs"], writes=[pfx + "_rs"], out=rs[:], in0=rs[:], scalar1=1.0 / D, scalar2=EPS,
           op0=ALU.mult, op1=ALU.add)
    P.call("act", "activation", reads=[pfx + "_rs"], writes=[pfx + "_rs"], out=rs[:], in_=rs[:], func=AF.Sqrt)
    P.call("dve", "reciprocal", reads=[pfx + "_rs"], writes=[pfx + "_rs"], out=rs[:], in_=rs[:])


def phase_outproj(P, nc, K, l, xsrc):
    NBLK = T // 128
    with ExitStack() as es:
        sb = lambda name, shape, dt=F32: es.enter_context(nc.sbuf_tensor(name, shape, dt))
        pp = lambda name: es.enter_context(nc.psum_tensor(name, [128, 512], F32))
        W = sb("p6_w", [128, 8, D], BF16)
        stg = [sb(f"p6_stg{i}", [128, 512]) for i in range(6)]
        G = [sb(f"p6_G{who}", [128, D]) for who in range(2)]
        tmp = sb("p6_tmp", [128, D])
        A = sb("p6_A", [128, 8, 2]); Bv = sb("p6_B", [128, 8, 2]); gcol = sb("p6_g", [128, 8])
        yt = [sb(f"p6_yt{b}", [128, D]) for b in range(2)]
        yTb = [sb(f"p6_yT{b}", [128, 8, 128], BF16) for b in range(2)]
        xt = [sb(f"p6_x{b}", [128, D]) for b in range(2)]
        xm = [sb(f"p6_xm{b}", [128, D]) for b in range(2)]
        xn = sb("p6_xn", [128, D]); junk = sb("p6_junk", [128, D])
        ss = sb("p6_ss", [128, 1]); ss2 = sb("p6_ss2", [128, 1]); rs = sb("p6_rs", [128, 1])
        hT = [sb(f"p6_hT{b}", [128, 8, 128], BF16) for b in range(2)]
        pt = es.enter_context(nc.psum_tensor("p6_pt", [128, 8, 128], F32))
        po4 = [pp(f"p6_po{i}") for i in range(4)]
        load_weight_bf16(P, nc, K.w_out[l], W, "p6_w", 8, D, stg, [f"p6_stg{i}" for i in range(6)])
        row_gain(P, nc, K, l, 1, 2, G, tmp, "p6")
        mod_AB(P, nc, K, l, 2, 4, 3, A, Bv, gcol, "p6")
        pt2 = es.enter_context(nc.psum_tensor("p6_pt2", [128, 8, 128], F32))
        ssb = sb("p6_ssb", [128, 1]); rsb = sb("p6_rsb", [128, 1]); junkb = sb("p6_junkb", [128, D])

        def part1(tb):
            b = tb % 2
            who = 1 if tb * 128 < CTX else 0
            ts = slice(tb * 128, (tb + 1) * 128)
            po = po4[2 * b:2 * b + 2]; pok = [f"p6_po{2 * b}", f"p6_po{2 * b + 1}"]
            P.dma("sp", yt[b][:], K.ytm[l][ts, :], reads=["ytm"], writes=[f"p6_yt{b}"])
            P.dma("sp", xt[b][:], xsrc[ts, :], reads=["xres"], writes=[f"p6_x{b}"])
            P.dma("sp", yTb[b][:, 2:4, :], K.yT[l][256:512, ts].rearrange("(c p) t -> p c t", p=128), reads=["yT"], writes=[f"p6_yT{b}"])
            for c in (0, 1, 4, 5, 6, 7):
                P.call("pe", "transpose", reads=[f"p6_yt{b}", "ident"], writes=["p6_pt"], inc=(c == 7), out=pt[:, c, :],
                       in_=yt[b][:, c * 128:(c + 1) * 128], identity=K.ident[:])
            P.call("act", "activation", reads=["p6_pt"], writes=[f"p6_yT{b}"], out=yTb[b][:, 0:2, :], in_=pt[:, 0:2, :], func=AF.Copy)
            P.call("dve", "tensor_copy", reads=["p6_pt"], writes=[f"p6_yT{b}"], out=yTb[b][:, 4:8, :], in_=pt[:, 4:8, :])
            for half in range(2):
                for kc in range(8):
                    P.call("pe", "matmul", reads=[f"p6_yT{b}", f"p6_w_{half}"], writes=[pok[half]], inc=(kc == 7), out=po[half][:, :],
                           lhsT=yTb[b][:, kc, :], rhs=W[:, kc, half * 512:(half + 1) * 512], start=(kc == 0), stop=(kc == 7))
            norm_rows(P, po, pok, ss, ss2, rs, junk, "p6")
            for half in range(2):
                hs = slice(half * 512, (half + 1) * 512)
                P.call("dve", "scalar_tensor_tensor", reads=[pok[half], "p6_rs", f"p6_G{who}"], writes=[f"p6_xm{b}"],
                       out=xm[b][:, hs], in0=po[half][:, :], scalar=rs[:, 0:1], in1=G[who][:, hs], op0=ALU.mult, op1=ALU.mult)
                P.call("pool", "tensor_tensor", reads=[f"p6_xm{b}", f"p6_x{b}"], writes=[f"p6_xm{b}"], out=xm[b][:, hs],
                       in0=xm[b][:, hs], in1=xt[b][:, hs], op=ALU.add)
            P.dma("sp", K.xmid[l][ts, :], xm[b][:], reads=[f"p6_xm{b}"], writes=["xmid"])

        def part2(tb):
            b = tb % 2
            who = 1 if tb * 128 < CTX else 0
            ts = slice(tb * 128, (tb + 1) * 128)
            norm_block(P, xm[b], f"p6_xm{b}", ssb, rsb, junkb, xn, "p6_xn", pfx="p6b")
            for dc in range(8):
                P.call("pe", "transpose", reads=["p6_xn", "ident"], writes=["p6_pt2"], inc=(dc == 7), out=pt2[:, dc, :],
                       in_=xn[:, dc * 128:(dc + 1) * 128], identity=K.ident[:])
            for dc in range(8):
                P.call("act", "activation", reads=["p6_pt2", "p6_A", "p6_B"], writes=[f"p6_hT{b}"], out=hT[b][:, dc, :],
                       in_=pt2[:, dc, :], func=AF.Identity, scale=A[:, dc, who:who + 1], bias=Bv[:, dc, who:who + 1])
            P.dma("sp", K.h2T[l][:, ts].rearrange("(c p) t -> p c t", p=128), hT[b][:], reads=[f"p6_hT{b}"], writes=["h2T"])

        part1(0)
        for tb in range(NBLK):
            if tb + 1 < NBLK:
                part1(tb + 1)
            part2(tb)


def phase_ffn_up(P, nc, K, l):
    NF = DFF // 128
    with ExitStack() as es:
        sb = lambda name, shape, dt=F32: es.enter_context(nc.sbuf_tensor(name, shape, dt))
        pp = lambda name: es.enter_context(nc.psum_tensor(name, [128, 512], F32))
        Wg = sb("p7_wg", [128, 8, DFF], BF16); Wu = sb("p7_wu", [128, 8, DFF], BF16)
        stg = [sb(f"p7_stg{i}", [128, 512]) for i in range(6)]
        cw = sb("p7_cw", [128, NF, 3]); cb = sb("p7_cb", [128, NF])
        hT = [sb(f"p7_hT{b}", [128, 8, 514], BF16) for b in range(2)]
        gsb = sb("p7_g", [128, 514]); tt = sb("p7_t", [128, 512]); sg = sb("p7_s", [128, 512])
        zt = [sb(f"p7_z{i}", [128, 512], BF16) for i in range(3)]
        pg = [pp(f"p7_pg{i}") for i in range(3)]; ph = [pp(f"p7_ph{i}") for i in range(2)]; pu = [pp(f"p7_pu{i}") for i in range(3)]
        NS = dict(allow_slow_non_contiguous=True)
        load_weight_bf16(P, nc, K.ff_w_gate[l], Wg, "p7_wg", 8, DFF, stg, [f"p7_stg{i}" for i in range(6)])
        load_weight_bf16(P, nc, K.ff_w_up[l], Wu, "p7_wu", 8, DFF, stg, [f"p7_stg{i}" for i in range(6)])
        for j in range(3):
            P.dma("sp", cw[:, :, j], K.ff_conv_w[l, j].rearrange("(c p) -> p c", p=128), writes=["p7_cw"], **NS)
        P.dma("sp", cb[:], K.ff_conv_b[l].rearrange("(c p) -> p c", p=128), writes=["p7_cb"], **NS)
        NSEAM = SEQ // 512 - 1
        hH = sb("p7_hH", [128, 8, 2 * NSEAM], BF16); HG = sb("p7_HG", [128, NF, 2 * NSEAM])
        for k_ in range(NSEAM):
            tk = CTX + 512 * (k_ + 1)
            P.dma("sp", hH[:, :, 2 * k_:2 * k_ + 2], K.h2T[l][:, tk - 1:tk + 1].rearrange("(c p) t -> p c t", p=128),
                  reads=["h2T"], writes=["p7_hH"], **NS)
        for fc in range(NF):
            fs = slice(fc * 128, (fc + 1) * 128)
            i2 = fc % 2
            for kc in range(8):
                P.call("pe", "matmul", reads=[f"p7_wg_{fc // 4}", "p7_hH"], writes=[f"p7_ph{i2}"], out=ph[i2][:, 0:2 * NSEAM], lhsT=Wg[:, kc, fs],
                       rhs=hH[:, kc, :], start=(kc == 0), stop=(kc == 7))
            P.call("act", "activation", reads=[f"p7_ph{i2}"], writes=["p7_HG"], out=HG[:, fc, :], in_=ph[i2][:, 0:2 * NSEAM], func=AF.Copy)
        cnt = {"i": 0}
        for ti, (t0, n) in enumerate(tiles_512()):
            b = ti % 2
            xi = ti - 1
            hk = f"p7_hT{b}"
            s0, s1 = (0, CTX) if t0 < CTX else (CTX, T)
            src = lambda a, b_: K.h2T[l][:, a:b_].rearrange("(c p) t -> p c t", p=128)
            P.dma("sp", hT[b][:, :, 1:n + 1], src(t0, t0 + n), reads=["h2T"], writes=[hk])
            if t0 > s0:
                P.dma("sp", hT[b][:, :, 0:1], src(t0 - 1, t0), reads=["h2T"], writes=[hk], **NS)
            else:
                P.call("pool", "memset", writes=[hk], ap=hT[b][:, :, 0:1], constant=0.0)
            if t0 + n < s1:
                P.dma("sp", hT[b][:, :, n + 1:n + 2], src(t0 + n, t0 + n + 1), reads=["h2T"], writes=[hk], **NS)
            else:
                P.call("pool", "memset", writes=[hk], ap=hT[b][:, :, n + 1:n + 2], constant=0.0)
            for fc in range(NF):
                i2 = cnt["i"] % 3
                cnt["i"] += 1
                fs = slice(fc * 128, (fc + 1) * 128)
                for kc in range(8):
                    P.call("pe", "matmul", reads=[f"p7_wg_{fc // 4}", hk], writes=[f"p7_pg{i2}"], inc=(kc == 7), out=pg[i2][:, :n], lhsT=Wg[:, kc, fs],
                           rhs=hT[b][:, kc, 1:n + 1], start=(kc == 0), stop=(kc == 7))
                for kc in range(8):
                    P.call("pe", "matmul", reads=[f"p7_wu_{fc // 4}", hk], writes=[f"p7_pu{i2}"], inc=(kc == 7), out=pu[i2][:, :n], lhsT=Wu[:, kc, fs],
                           rhs=hT[b][:, kc, 1:n + 1], start=(kc == 0), stop=(kc == 7))
                P.call("act", "activation", reads=[f"p7_pg{i2}"], writes=["p7_g"], out=gsb[:, 1:n + 1], in_=pg[i2][:, :n], func=AF.Copy)
                if t0 > s0:
                    P.call("dve", "tensor_copy", reads=["p7_HG"], writes=["p7_g"], out=gsb[:, 0:1], in_=HG[:, fc, 2 * (xi - 1):2 * (xi - 1) + 1])
                else:
                    P.call("pool", "memset", writes=["p7_g"], ap=gsb[:, 0:1], constant=0.0)
                if t0 + n < s1:
                    P.call("dve", "tensor_copy", reads=["p7_HG"], writes=["p7_g"], out=gsb[:, n + 1:n + 2], in_=HG[:, fc, 2 * xi + 1:2 * xi + 2])
                else:
                    P.call("pool", "memset", writes=["p7_g"], ap=gsb[:, n + 1:n + 2], constant=0.0)
                P.call("act", "activation", reads=["p7_g", "p7_cw", "p7_cb"], writes=["p7_t"], out=tt[:, :n], in_=gsb[:, 1:n + 1],
                       func=AF.Identity, scale=cw[:, fc, 1:2], bias=cb[:, fc:fc + 1])
                P.call("dve", "scalar_tensor_tensor", reads=["p7_g", "p7_cw", "p7_t"], writes=["p7_t"], out=tt[:, :n],
                       in0=gsb[:, 0:n], scalar=cw[:, fc, 0:1], in1=tt[:, :n], op0=ALU.mult, op1=ALU.add)
                P.call("dve", "scalar_tensor_tensor", reads=["p7_g", "p7_cw", "p7_t"], writes=["p7_t"], out=tt[:, :n],
                       in0=gsb[:, 2:n + 2], scalar=cw[:, fc, 2:3], in1=tt[:, :n], op0=ALU.mult, op1=ALU.add)
                P.call("act", "activation", reads=["p7_t"], writes=["p7_s"], out=sg[:, :n], in_=tt[:, :n], func=AF.Silu)
                P.call("dve", "tensor_tensor", reads=["p7_s", f"p7_pu{i2}"], writes=[f"p7_z{i2}"], out=zt[i2][:, :n], in0=sg[:, :n],
                       in1=pu[i2][:, :n], op=ALU.mult)
                P.dma("sp", K.zT[l][fs, t0:t0 + n], zt[i2][:, :n], reads=[f"p7_z{i2}"], writes=["zT"])


def phase_ffn_down(P, nc, K, l, xdst, last):
    NF = DFF // 128
    NBLK = T // 128
    with ExitStack() as es:
        sb = lambda name, shape, dt=F32: es.enter_context(nc.sbuf_tensor(name, shape, dt))
        pp = lambda name: es.enter_context(nc.psum_tensor(name, [128, 512], F32))
        Wd = sb("p8_wd", [128, NF, D], BF16)
        stg = [sb(f"p8_stg{i}", [128, 512]) for i in range(6)]
        G = [sb(f"p8_G{who}", [128, D]) for who in range(2)]
        tmp = sb("p8_tmp", [128, D]); junk = sb("p8_junk", [128, D])
        zb = [sb(f"p8_z{b}", [128, NF, 128], BF16) for b in range(2)]
        xm = [sb(f"p8_xm{b}", [128, D]) for b in range(2)]
        xo = [sb(f"p8_xo{b}", [128, D]) for b in range(2)]
        ss = sb("p8_ss", [128, 1]); ss2 = sb("p8_ss2", [128, 1]); rs = sb("p8_rs", [128, 1])
        po4 = [pp(f"p8_po{i}") for i in range(4)]
        load_weight_bf16(P, nc, K.ff_w_down[l], Wd, "p8_wd", NF, D, stg, [f"p8_stg{i}" for i in range(6)])
        row_gain(P, nc, K, l, 3, 5, G, tmp, "p8")
        for tb in range(NBLK):
            if last and tb * 128 < CTX:
                continue
            b = tb % 2
            who = 1 if tb * 128 < CTX else 0
            ts = slice(tb * 128, (tb + 1) * 128)
            po = po4[2 * b:2 * b + 2]; pok = [f"p8_po{2 * b}", f"p8_po{2 * b + 1}"]
            P.dma("sp", zb[b][:], K.zT[l][:, ts].rearrange("(c p) t -> p c t", p=128), reads=["zT"], writes=[f"p8_z{b}"])
            P.dma("sp", xm[b][:], K.xmid[l][ts, :], reads=["xmid"], writes=[f"p8_xm{b}"])
            for half in range(2):
                for fc in range(NF):
                    P.call("pe", "matmul", reads=[f"p8_z{b}", f"p8_wd_{half}"], writes=[pok[half]], inc=(fc == NF - 1), out=po[half][:, :],
                           lhsT=zb[b][:, fc, :], rhs=Wd[:, fc, half * 512:(half + 1) * 512], start=(fc == 0), stop=(fc == NF - 1))
            norm_rows(P, po, pok, ss, ss2, rs, junk, "p8")
            for half in range(2):
                hs = slice(half * 512, (half + 1) * 512)
                P.call("dve", "scalar_tensor_tensor", reads=[pok[half], "p8_rs", f"p8_G{who}"], writes=[f"p8_xo{b}"],
                       out=xo[b][:, hs], in0=po[half][:, :], scalar=rs[:, 0:1], in1=G[who][:, hs], op0=ALU.mult, op1=ALU.mult)
                P.call("pool", "tensor_tensor", reads=[f"p8_xo{b}", f"p8_xm{b}"], writes=[f"p8_xo{b}"], out=xo[b][:, hs],
                       in0=xo[b][:, hs], in1=xm[b][:, hs], op=ALU.add)
            if last:
                P.dma("sp", K.out[tb * 128 - CTX:(tb + 1) * 128 - CTX, :], xo[b][:], reads=[f"p8_xo{b}"], writes=["out"])
            else:
                P.dma("sp", xdst[ts, :], xo[b][:], reads=[f"p8_xo{b}"], writes=["xres"])


def build(dbg=(), upto=99, nlayers=L, skip=()):
    nc = bass.Bass("TRN2", target_bir_lowering=False)
    K = Ctx()
    K.scan_dbg = 'scandbg' in dbg
    K.chunked = 'seqscan' not in dbg
    K.f32r = False
    dt = lambda name, shape, dtype=F32, kind="ExternalInput": nc.dram_tensor(name, shape, dtype, kind=kind).ap()
    scr = lambda name, shape, dtype=F32: dt(name, shape, dtype, "ExternalOutput" if name in dbg else "Internal")
    K.xin = dt("xin", [T, D])
    K.c_in = dt("c_in", [D])
    K.cctx_in = dt("cctx_in", [D])
    K.ada_w = dt("ada_w", [L, D, 6 * D])
    K.ada_b = dt("ada_b", [L, 6 * D])
    K.norm_g = dt("norm_g", [L, 4, D])
    K.w_in = dt("w_in", [L, D, WCOLS])
    K.rope = dt("rope", [4, 128, T])
    K.ident_d = dt("ident", [128, 128])
    K.out = dt("out", [SEQ, D], kind="ExternalOutput")
    K.fm32 = [scr(f"fm32_{l}", [1024, T]) for l in range(L)]
    K.ropeT = [scr(f"ropeT_{l}", [1152, T], BF16) for l in range(L)]
    K.vtm = [scr(f"vtm_{l}", [T, 384], BF16) for l in range(L)]
    for nm, shp in (("rw_conv", [L, 3, 768]), ("rw_w0", [L, 2, 256]), ("rw_w_up", [L, 2, 32, 256]), ("rw_a0", [L, 2, 256]),
                    ("rw_a_up", [L, 2, 32, 256]), ("rw_g_up", [L, 64, 256]), ("rw_k_k", [L, 256]), ("rw_k_a", [L, 256]),
                    ("rw_r_k", [L, 256]), ("rw_ln_g", [L, 256]), ("rw_ln_b", [L, 256])):
        setattr(K, nm, dt(nm, shp))
    K.bones_d = dt("bones", [128, 128])
    K.col_w = [[scr(f"col_w_{l}_{i}", [128, T]) for i in range(4)] for l in range(L)]
    K.col_kr = [[scr(f"col_kr_{l}_{i}", [128, T]) for i in range(4)] for l in range(L)]
    K.rw_tm = [scr(f"rw_tm_{l}", [T, 7, 256]) for l in range(L)]
    K.col_nk = [[scr(f"col_nk_{l}_{i}", [128, T]) for i in range(4)] for l in range(L)]
    K.col_kd = [[scr(f"col_kd_{l}_{i}", [128, T]) for i in range(4)] for l in range(L)]
    K.cmask_d = dt("cmask", [4, 128, 128])
    K.o_tm = [scr(f"o_tm_{l}", [T, 2, 256]) for l in range(L)]
    K.ytm = [scr(f"ytm_{l}", [T, 1024]) for l in range(L)]
    K.yT = [scr(f"yT_{l}", [1024, T], BF16) for l in range(L)]
    K.modrow = scr("modrow", [L, 2, 6, D])
    K.xmid = [scr(f"xmid_{l}", [T, D]) for l in range(L)]
    K.h2T = [scr(f"h2T_{l}", [D, T], BF16) for l in range(L)]
    K.zT = [scr(f"zT_{l}", [DFF, T], BF16) for l in range(L)]
    K.xres = [scr(f"xres_{l}", [T, D]) for l in range(L)]
    K.w_out = dt("w_out", [L, D, D])
    K.ff_w_gate = dt("ff_w_gate", [L, D, DFF]); K.ff_w_up = dt("ff_w_up", [L, D, DFF]); K.ff_w_down = dt("ff_w_down", [L, DFF, D])
    K.ff_conv_w = dt("ff_conv_w", [L, 3, DFF]); K.ff_conv_b = dt("ff_conv_b", [L, DFF])
    K.df_lambda = dt("df_lambda", [L, 128])
    K.df_norm_g = dt("df_norm_g", [L, 64])
    K.gq_sink = dt("gq_sink", [L, 8])
    K.sel_d = dt("sel65", [65, 64])
    K.msk_d = dt("msk", [2, 128, 128])

    P = Prog(nc)
    with (
        nc.sbuf_tensor("modcol0", [128, 48, 2], F32) as mc0,
        nc.sbuf_tensor("modcol1", [128, 48, 2], F32) as mc1,
        nc.sbuf_tensor("ident_sb", [128, 128], F32) as ident,
    ):
        K.modcol = [mc0, mc1]
        K.ident = ident
        P.dma("sp", ident[:], K.ident_d, writes=["ident"])
        phase_mod(P, nc, K)
        P.barrier()
        for l in range(nlayers):
            xsrc = K.xin if l == 0 else K.xres[l - 1]
            last = (l == L - 1)
            nc0 = nc
            nc = Uniq(nc0, f"_L{l}")
            phases = [lambda: phase_inproj(P, nc, K, l, xsrc), lambda: phase_rwprep(P, nc, K, l), lambda: (phase_scan_chunked if K.chunked else phase_scan)(P, nc, K, l),
                      lambda: phase_readout(P, nc, K, l), lambda: phase_attn(P, nc, K, l), lambda: phase_outproj(P, nc, K, l, xsrc),
                      lambda: phase_ffn_up(P, nc, K, l), lambda: phase_ffn_down(P, nc, K, l, K.xres[l], last)]
            for pi, ph in enumerate(phases):
                if upto >= pi + 1 and pi + 1 not in skip:
                    ph()
                    P.mark(f"L{l}_ph{pi + 1}")
                    P.barrier()
            nc = nc0
        P.finish(["out"])
    return nc, P


def make_in_maps(inp):
    f = lambda a: np.ascontiguousarray(np.asarray(a, dtype=np.float32))
    cols = w_in_cols()
    shared = {
        "cctx_in": f(inp["c_ctx"]),
        "ada_w": f(inp["ada_w"]),
        "ada_b": f(inp["ada_b"]),
        "norm_g": f(inp["norm_g"]),
        "w_in": f(np.asarray(inp["w_in"])[:, :, cols]),
        "rope": rope_tables(),
        "ident": np.eye(128, dtype=np.float32),
        "bones": np.kron(np.eye(2, dtype=np.float32), np.ones((64, 64), np.float32)),
    }
    for nm in ("rw_conv", "rw_w0", "rw_w_up", "rw_a0", "rw_a_up", "rw_g_up", "rw_k_k", "rw_k_a", "rw_ln_g", "rw_ln_b"):
        shared[nm] = f(inp[nm])
    shared["rw_r_k"] = f(np.asarray(inp["rw_r_k"]).reshape(L, 256))
    shared["df_lambda"] = f(np.asarray(inp["df_lambda"]).reshape(L, 128))
    tau = np.arange(128) % 64
    shared["cmask"] = np.stack([tau[:, None] < tau[None, :], tau[:, None] <= tau[None, :],
                                tau[:, None] > tau[None, :], tau[:, None] >= tau[None, :]]).astype(np.float32)
    shared["df_norm_g"] = f(inp["df_norm_g"])
    for nm in ("w_out", "ff_w_gate", "ff_w_up", "ff_w_down", "ff_conv_w", "ff_conv_b"):
        shared[nm] = f(inp[nm])
    shared["gq_sink"] = f(inp["gq_sink"])
    sel = np.zeros((65, 64), np.float32); sel[64, :] = 1.0
    shared["sel65"] = sel
    a = np.arange(128)
    shared["msk"] = np.stack([(a[:, None] >= a[None, :]), (a[:, None] <= a[None, :])]).astype(np.float32)
    maps = []
    for core in range(8):
        b = core % NB
        m = dict(shared)
        m["xin"] = f(np.concatenate([inp["ctx"][b], inp["x"][b]], axis=0))
        m["c_in"] = f(inp["c"][b])
        maps.append(m)
    return maps


def kernel(**inputs):
    inp = {k: np.asarray(v) for k, v in inputs.items()}
    nc, _ = build()
    maps = make_in_maps(inp)
    res = run_bass_kernel_spmd(nc, maps, core_ids=list(range(8)))
    out = np.stack([np.asarray(res.results[b]["out"], dtype=np.float32) for b in range(NB)], axis=0)
    return out
```
